# Optimizing a Trainium2 kernel written in Bass

```python
import math
import jax, jax.numpy as jnp
from jax import lax
import numpy as np


D_MODEL = 1024
BATCH = 4
SEQ = 4096
DEPTH = 4
DEC_BATCH = 128
DEC_SEQ = 1
PAST_LEN = 2048
PAGE_SIZE = 128

N_A_LAYERS = DEPTH // 2
N_B_LAYERS = DEPTH - N_A_LAYERS
HEAD_DIM = 64
MIX_W = 3 * D_MODEL // 4
MEM_W = D_MODEL // 4
MEM_HEADS = MEM_W // HEAD_DIM
MEM_TOKENS = 256
SSM_GROUP = 16
SSM_GROUPS = MIX_W // SSM_GROUP
SSM_STATE = 64
DT_MIN = 0.001
DT_MAX = 0.1
MOBA_HEADS = MIX_W // HEAD_DIM
MOBA_KV_HEADS = 4
KV_W = MOBA_KV_HEADS * HEAD_DIM
MOBA_BLOCK = 256
MOBA_TOPK = 3
Q_BLOCK = 64
D_FF = ((-(-8 * D_MODEL // 3) + 255) // 256) * 256
EPS = 1e-6

kernel_name = 'yoco_s5_moba_memory_decoder_step'


def _rmsnorm(x, g):
    x32 = x.astype(jnp.float32)
    y = x32 * lax.rsqrt(jnp.mean(x32 * x32, axis=-1, keepdims=True) + EPS)
    return (y * g.astype(jnp.float32)).astype(x.dtype)


def _swiglu(a, w_gu, w_down):
    g, u = jnp.split(a @ w_gu, 2, axis=-1)
    return (jax.nn.silu(g) * u) @ w_down


def _complex_affine_combine(e1, e2):
    a1r, a1i, b1r, b1i = e1
    a2r, a2i, b2r, b2i = e2
    return (a2r * a1r - a2i * a1i, a2r * a1i + a2i * a1r,
            a2r * b1r - a2i * b1i + b2r, a2r * b1i + a2i * b1r + b2i)


def _s5_mixer(u, h0_re, h0_im, a_re, a_im, log_dt, b_re, b_im, c_re, c_im, d_skip, w_glu):
    f32 = jnp.float32
    bsz, seq, _ = u.shape
    uf = u.astype(f32).reshape(bsz, seq, SSM_GROUPS, SSM_GROUP)
    dt = jnp.exp(log_dt.astype(f32))[:, None]
    lam_re = jnp.minimum(a_re.astype(f32), -1e-4)
    lam_im = a_im.astype(f32)
    mag = jnp.exp(dt * lam_re)
    ang = dt * lam_im
    ab_re, ab_im = mag * jnp.cos(ang), mag * jnp.sin(ang)
    den = lam_re * lam_re + lam_im * lam_im
    num_re = ab_re - 1.0
    f_re = (num_re * lam_re + ab_im * lam_im) / den
    f_im = (ab_im * lam_re - num_re * lam_im) / den
    b_re32, b_im32 = b_re.astype(f32), b_im.astype(f32)
    bb_re = f_re[..., None] * b_re32 - f_im[..., None] * b_im32
    bb_im = f_re[..., None] * b_im32 + f_im[..., None] * b_re32
    bu_re = jnp.einsum('blgh,gph->lbgp', uf, bb_re)
    bu_im = jnp.einsum('blgh,gph->lbgp', uf, bb_im)
    a_l_re = jnp.broadcast_to(ab_re[None, None], (seq, 1) + ab_re.shape)
    a_l_im = jnp.broadcast_to(ab_im[None, None], (seq, 1) + ab_im.shape)
    cum_re, cum_im, x_re, x_im = lax.associative_scan(
        _complex_affine_combine, (a_l_re, a_l_im, bu_re, bu_im), axis=0)
    h_re = h0_re.astype(f32)[None]
    h_im = h0_im.astype(f32)[None]
    xr = x_re + cum_re * h_re - cum_im * h_im
    xi = x_im + cum_re * h_im + cum_im * h_re
    y = (jnp.einsum('lbgp,ghp->blgh', xr, c_re.astype(f32))
         - jnp.einsum('lbgp,ghp->blgh', xi, c_im.astype(f32))
         + d_skip.astype(f32) * uf)
    y = jax.nn.gelu(y.reshape(bsz, seq, MIX_W))
    y = y * jax.nn.sigmoid(y @ w_glu.astype(f32))
    return y.astype(u.dtype), xr[-1], xi[-1]


def _moba_attend(q, k, v, q_pos0):
    f32 = jnp.float32
    bq, lq, nh, hd = q.shape
    t_len = k.shape[1]
    nb = -(-t_len // MOBA_BLOCK)
    pad = nb * MOBA_BLOCK - t_len
    kp = jnp.pad(k, ((0, 0), (0, pad), (0, 0), (0, 0))).reshape(bq, nb, MOBA_BLOCK, MOBA_KV_HEADS, hd)
    vp = jnp.pad(v, ((0, 0), (0, pad), (0, 0), (0, 0))).reshape(bq, nb, MOBA_BLOCK, MOBA_KV_HEADS, hd)
    kmean = jnp.mean(kp.astype(f32), axis=2)
    kv_of_h = jnp.arange(nh, dtype=jnp.int32) // (nh // MOBA_KV_HEADS)
    qb = math.gcd(lq, Q_BLOCK)
    nq = lq // qb
    k_sel = min(MOBA_TOPK, nb)
    q_items = q.reshape(bq * nq, qb, nh, hd)
    b_items = jnp.repeat(jnp.arange(bq, dtype=jnp.int32), nq)
    p_items = q_pos0 + jnp.tile(jnp.arange(nq, dtype=jnp.int32) * qb, bq)
    scale = HEAD_DIM ** -0.5

    def one(item):
        qi, b, p0 = item
        qi32 = qi.astype(f32)
        pos = p0 + jnp.arange(qb, dtype=jnp.int32)
        blk = pos // MOBA_BLOCK
        gate = jnp.einsum('qhd,jhd->qhj', qi32, kmean[b][:, kv_of_h])
        past = jnp.arange(nb, dtype=jnp.int32)[None, None, :] < blk[:, None, None]
        gate = jnp.where(past, gate, -jnp.inf)
        _, top = lax.top_k(gate, k_sel)
        slot_ok = jnp.broadcast_to(jnp.arange(k_sel, dtype=jnp.int32)[None, None, :] < blk[:, None, None], (qb, nh, k_sel))
        own = jnp.broadcast_to(blk[:, None, None], (qb, nh, 1))
        idx = jnp.concatenate([top.astype(jnp.int32), own], axis=-1)
        ok = jnp.concatenate([slot_ok, jnp.ones((qb, nh, 1), bool)], axis=-1)
        kh = kv_of_h[None, :, None]
        kg = kp[b, idx, :, kh].astype(f32)
        vg = vp[b, idx, :, kh].astype(f32)
        s = jnp.einsum('qhd,qhsnd->qhsn', qi32, kg) * scale
        kpos = idx[..., None] * MOBA_BLOCK + jnp.arange(MOBA_BLOCK, dtype=jnp.int32)
        mask = ok[..., None] & (kpos <= pos[:, None, None, None])
        s = jnp.where(mask, s, -jnp.inf)
        pr = jax.nn.softmax(s.reshape(qb, nh, -1), axis=-1).reshape(s.shape)
        return jnp.einsum('qhsn,qhsnd->qhd', pr, vg).astype(q.dtype)

    out = lax.map(one, (q_items, b_items, p_items))
    return out.reshape(bq, lq, nh, hd)


def _mem_kv(mem, g_mem, w_mem_kv, g_mk):
    m = _rmsnorm(mem, g_mem)
    mk, mv = jnp.split(m @ w_mem_kv, 2, axis=-1)
    bsz, nm, _ = mem.shape
    mk = _rmsnorm(mk.reshape(bsz, nm, MEM_HEADS, HEAD_DIM), g_mk)
    return mk, mv.reshape(bsz, nm, MEM_HEADS, HEAD_DIM)


def _mem_attend(q, mk, mv):
    s = jnp.einsum('blhd,bmhd->bhlm', q.astype(jnp.float32), mk.astype(jnp.float32)) * (HEAD_DIM ** -0.5)
    pr = jax.nn.softmax(s, axis=-1)
    return jnp.einsum('bhlm,bmhd->blhd', pr, mv.astype(jnp.float32)).astype(q.dtype)


def _shared_kv(h, g_kv, w_kv, g_k):
    a = _rmsnorm(h, g_kv)
    k, v = jnp.split(a @ w_kv, 2, axis=-1)
    bsz, seq, _ = h.shape
    k = _rmsnorm(k.reshape(bsz, seq, MOBA_KV_HEADS, HEAD_DIM), g_k)
    return k, v.reshape(bsz, seq, MOBA_KV_HEADS, HEAD_DIM)


def _run_trunk(x, pos0, h0_re, h0_im, mem_k, mem_v, k_past, v_past, p):
    bsz, seq, _ = x.shape
    h = x
    fin_re, fin_im = [], []
    k_new = v_new = k_all = v_all = None
    for l in range(DEPTH):
        a = _rmsnorm(h, p['g_mix'][l])
        proj = a @ p['w_in'][l]
        mix_in, mq = proj[..., :MIX_W], proj[..., MIX_W:]
        if l < N_A_LAYERS:
            mix_out, fr, fi = _s5_mixer(mix_in, h0_re[l], h0_im[l], p['ssm_a_re'][l], p['ssm_a_im'][l],
                                        p['ssm_log_dt'][l], p['ssm_b_re'][l], p['ssm_b_im'][l],
                                        p['ssm_c_re'][l], p['ssm_c_im'][l], p['ssm_d'][l], p['w_glu'][l])
            fin_re.append(fr.astype(h0_re.dtype))
            fin_im.append(fi.astype(h0_im.dtype))
        else:
            j = l - N_A_LAYERS
            q = _rmsnorm(mix_in.reshape(bsz, seq, MOBA_HEADS, HEAD_DIM), p['g_q'][j])
            mix_out = _moba_attend(q, k_all, v_all, pos0).reshape(bsz, seq, MIX_W)
        mqh = _rmsnorm(mq.reshape(bsz, seq, MEM_HEADS, HEAD_DIM), p['g_mq'][l])
        mem_out = _mem_attend(mqh, mem_k[l], mem_v[l]).reshape(bsz, seq, MEM_W)
        h = h + jnp.concatenate([mix_out, mem_out], axis=-1) @ p['w_out'][l]
        h = h + _swiglu(_rmsnorm(h, p['g_ffn'][l]), p['w_gu'][l], p['w_down'][l])
        if l == N_A_LAYERS - 1:
            k_new, v_new = _shared_kv(h, p['g_kv'], p['w_kv'], p['g_k'])
            k_all = jnp.concatenate([k_past.astype(k_new.dtype), k_new], axis=1)
            v_all = jnp.concatenate([v_past.astype(v_new.dtype), v_new], axis=1)
    return h, jnp.stack(fin_re), jnp.stack(fin_im), k_new, v_new


def setup_inputs(seed: int = 0) -> dict:
    key = jax.random.key(seed)
    ks = jax.random.split(key, 40)
    f32 = jnp.float32

    def nrm(i, shape, scale):
        return jax.random.normal(ks[i], shape, f32) * scale

    n_pages = PAST_LEN // PAGE_SIZE
    n_used = DEC_BATCH * n_pages
    n_phys = n_used + max(1, n_used // 4)
    page_table = jax.random.permutation(ks[0], n_phys)[:n_used].reshape(DEC_BATCH, n_pages).astype(jnp.int32)
    mix_total = MIX_W + MEM_W
    return {
        'x_prompt': nrm(1, (BATCH, SEQ, D_MODEL), 1.0),
        'x_sample': nrm(2, (DEC_BATCH, DEC_SEQ, D_MODEL), 1.0),
        'mem_prompt': nrm(3, (BATCH, MEM_TOKENS, D_MODEL), 1.0),
        'state_ssm_re': nrm(4, (N_A_LAYERS, DEC_BATCH, SSM_GROUPS, SSM_STATE), 0.1),
        'state_ssm_im': nrm(5, (N_A_LAYERS, DEC_BATCH, SSM_GROUPS, SSM_STATE), 0.1),
        'cache_k': nrm(6, (n_phys, PAGE_SIZE, MOBA_KV_HEADS, HEAD_DIM), 1.0),
        'cache_v': nrm(7, (n_phys, PAGE_SIZE, MOBA_KV_HEADS, HEAD_DIM), 1.0),
        'page_table': page_table,
        'cache_mem_k': nrm(8, (DEPTH, DEC_BATCH, MEM_TOKENS, MEM_HEADS, HEAD_DIM), 1.0),
        'cache_mem_v': nrm(9, (DEPTH, DEC_BATCH, MEM_TOKENS, MEM_HEADS, HEAD_DIM), 1.0),
        'g_mix': 1.0 + nrm(10, (DEPTH, D_MODEL), 0.02),
        'w_in': nrm(11, (DEPTH, D_MODEL, mix_total), D_MODEL ** -0.5),
        'w_out': nrm(12, (DEPTH, mix_total, D_MODEL), mix_total ** -0.5),
        'g_ffn': 1.0 + nrm(13, (DEPTH, D_MODEL), 0.02),
        'w_gu': nrm(14, (DEPTH, D_MODEL, 2 * D_FF), D_MODEL ** -0.5),
        'w_down': nrm(15, (DEPTH, D_FF, D_MODEL), D_FF ** -0.5),
        'ssm_a_re': -0.5 + nrm(16, (N_A_LAYERS, SSM_GROUPS, SSM_STATE), 0.01),
        'ssm_a_im': jnp.pi * jnp.arange(SSM_STATE, dtype=f32) + nrm(17, (N_A_LAYERS, SSM_GROUPS, SSM_STATE), 0.01),
        'ssm_log_dt': jax.random.uniform(ks[18], (N_A_LAYERS, SSM_GROUPS), f32, math.log(DT_MIN), math.log(DT_MAX)),
        'ssm_b_re': nrm(19, (N_A_LAYERS, SSM_GROUPS, SSM_STATE, SSM_GROUP), (2.0 * SSM_GROUP) ** -0.5),
        'ssm_b_im': nrm(20, (N_A_LAYERS, SSM_GROUPS, SSM_STATE, SSM_GROUP), (2.0 * SSM_GROUP) ** -0.5),
        'ssm_c_re': nrm(21, (N_A_LAYERS, SSM_GROUPS, SSM_GROUP, SSM_STATE), SSM_STATE ** -0.25),
        'ssm_c_im': nrm(22, (N_A_LAYERS, SSM_GROUPS, SSM_GROUP, SSM_STATE), SSM_STATE ** -0.25),
        'ssm_d': 1.0 + nrm(23, (N_A_LAYERS, SSM_GROUPS, SSM_GROUP), 0.02),
        'w_glu': nrm(24, (N_A_LAYERS, MIX_W, MIX_W), MIX_W ** -0.5),
        'g_q': 1.0 + nrm(25, (N_B_LAYERS, HEAD_DIM), 0.02),
        'g_mq': 1.0 + nrm(26, (DEPTH, HEAD_DIM), 0.02),
        'g_mem': 1.0 + nrm(27, (DEPTH, D_MODEL), 0.02),
        'w_mem_kv': nrm(28, (DEPTH, D_MODEL, 2 * MEM_W), D_MODEL ** -0.5),
        'g_mk': 1.0 + nrm(29, (DEPTH, HEAD_DIM), 0.02),
        'g_kv': 1.0 + nrm(30, (D_MODEL,), 0.02),
        'w_kv': nrm(31, (D_MODEL, 2 * KV_W), D_MODEL ** -0.5),
        'g_k': 1.0 + nrm(32, (HEAD_DIM,), 0.02),
    }


def reference(x_prompt, x_sample, mem_prompt, state_ssm_re, state_ssm_im, cache_k, cache_v, page_table,
              cache_mem_k, cache_mem_v, g_mix, w_in, w_out, g_ffn, w_gu, w_down, ssm_a_re, ssm_a_im,
              ssm_log_dt, ssm_b_re, ssm_b_im, ssm_c_re, ssm_c_im, ssm_d, w_glu, g_q, g_mq, g_mem,
              w_mem_kv, g_mk, g_kv, w_kv, g_k):
    p = dict(g_mix=g_mix, w_in=w_in, w_out=w_out, g_ffn=g_ffn, w_gu=w_gu, w_down=w_down,
             ssm_a_re=ssm_a_re, ssm_a_im=ssm_a_im, ssm_log_dt=ssm_log_dt, ssm_b_re=ssm_b_re,
             ssm_b_im=ssm_b_im, ssm_c_re=ssm_c_re, ssm_c_im=ssm_c_im, ssm_d=ssm_d, w_glu=w_glu,
             g_q=g_q, g_mq=g_mq, g_kv=g_kv, w_kv=w_kv, g_k=g_k)
    mks, mvs = [], []
    for l in range(DEPTH):
        mk, mv = _mem_kv(mem_prompt, g_mem[l], w_mem_kv[l], g_mk[l])
        mks.append(mk)
        mvs.append(mv)
    p_mem_k = jnp.stack(mks)
    p_mem_v = jnp.stack(mvs)
    bp = x_prompt.shape[0]
    h0 = jnp.zeros((N_A_LAYERS, bp, SSM_GROUPS, SSM_STATE), state_ssm_re.dtype)
    kv_empty = jnp.zeros((bp, 0, MOBA_KV_HEADS, HEAD_DIM), x_prompt.dtype)
    y_prompt, p_ssm_re, p_ssm_im, p_k, p_v = _run_trunk(
        x_prompt, 0, h0, h0, p_mem_k, p_mem_v, kv_empty, kv_empty, p)
    bs = x_sample.shape[0]
    n_pages = page_table.shape[1]
    past_len = n_pages * cache_k.shape[1]
    k_past = cache_k[page_table].reshape(bs, past_len, MOBA_KV_HEADS, HEAD_DIM)
    v_past = cache_v[page_table].reshape(bs, past_len, MOBA_KV_HEADS, HEAD_DIM)
    y_sample, s_ssm_re, s_ssm_im, s_k, s_v = _run_trunk(
        x_sample, past_len, state_ssm_re, state_ssm_im, cache_mem_k, cache_mem_v, k_past, v_past, p)
    return (y_prompt, y_sample, p_ssm_re, p_ssm_im, p_k, p_v, p_mem_k, p_mem_v, s_ssm_re, s_ssm_im, s_k, s_v)
```

```python
import numpy as np
import concourse.bass as bass
import concourse.mybir as mybir
from concourse.bass_utils import run_bass_kernel_spmd
from contextlib import ExitStack

F32 = mybir.dt.float32
BF16 = mybir.dt.bfloat16
I32 = mybir.dt.int32
U32 = mybir.dt.uint32
AF = mybir.ActivationFunctionType
ALU = mybir.AluOpType
AX = mybir.AxisListType


class Buf:
    __slots__ = ('t', 'w', 'r', 'dsem', 'dcnt', 'name', 'es', 'uid')
    _next = [0]

    def __init__(self, t, name=None):
        self.t = t
        self.w = None
        self.r = {}
        self.dsem = None
        self.dcnt = 0
        self.name = name
        self.es = None
        Buf._next[0] += 1
        self.uid = 'b%d' % Buf._next[0]

    def __getitem__(self, k):
        return self.t[k]


class Prog:
    ENG = ('pe', 'act', 'dve', 'pool', 'sp')

    def __init__(self, nc, es):
        self.nc = nc
        self.es = es
        self.q = {e: [] for e in self.ENG}
        self.cnt = {e: 0 for e in self.ENG}
        self.sem = {e: es.enter_context(nc.semaphore('s_' + e)) for e in self.ENG}
        self.seen = {e: {} for e in self.ENG}
        self.nsem = 0
        self.out_events = []
        self.all_dma = []
        self.sem_es = es

    def sb(self, name, shape, dt):
        b = Buf(self.es.enter_context(self.nc.sbuf_tensor(name, list(shape), dt)), name)
        b.es = self.es
        return b

    def ps(self, name, shape, dt=F32):
        return Buf(self.es.enter_context(self.nc.psum_tensor(name, list(shape), dt)), name)

    def view(self, t, name=None):
        return Buf(t, name)

    def _waits(self, eng, reads, writes):
        need = {}

        def add(ev):
            if ev is None:
                return
            k, v = ev
            if k == 'pe' and eng == 'pe':
                return
            if need.get(k, 0) < v:
                need[k] = v
        for b in reads:
            add(b.w)
        for b in writes:
            add(b.w)
            for k, v in b.r.items():
                add((k, v))
        out = []
        for k, v in need.items():
            if self.seen[eng].get(k, 0) >= v:
                continue
            self.seen[eng][k] = v
            out.append((k, v))
        return out

    def op(self, eng, fn, reads=(), writes=()):
        waits = self._waits(eng, reads, writes)
        self.cnt[eng] += 1
        t = self.cnt[eng]
        for b in reads:
            if b.r.get(eng, 0) < t:
                b.r[eng] = t
        for b in writes:
            b.w = (eng, t)
            b.r = {}
        sem = self.sem

        def run(h):
            for k, v in waits:
                h.wait_ge(sem[k], v)
            fn(h).then_inc(sem[eng], 1)
        self.q[eng].append(run)

    def _dsem(self, owner):
        if owner.dsem is None:
            self.nsem += 1
            owner.dsem = (owner.es or self.sem_es).enter_context(self.nc.semaphore('d%d' % self.nsem))
            self.sem[owner.uid] = owner.dsem
        return owner.dsem

    def dma(self, q, fn, reads=(), writes=(), owner=None, is_out=False, inc=16):
        waits = self._waits(q, reads, writes)
        if owner is None:
            owner = writes[0] if writes else reads[0]
        ds = self._dsem(owner)
        owner.dcnt += inc
        ev = (owner.uid, owner.dcnt)
        for b in reads:
            b.r[ev[0]] = ev[1]
        for b in writes:
            b.w = ev
            b.r = {}
        if is_out:
            self.out_events.append(ev)
        self.all_dma.append(ev)
        sem = self.sem

        def run(h):
            for k, v in waits:
                h.wait_ge(sem[k], v)
            fn(h).then_inc(ds, inc)
        self.q[q].append(run)

    def phase_begin(self):
        self._saved_es = self.es
        self.es = ExitStack()
        self.es.__enter__()
        self._sems_es = self._saved_es

    def phase_end(self):
        last = {}
        for k, v in self.all_dma:
            last[k] = max(last.get(k, 0), v)
        self.all_dma = []
        self.out_events = []
        waits = list(last.items())
        sem = self.sem

        def run(h):
            for k, v in waits:
                h.wait_ge(sem[k], v)
        self.q['sp'].append(run)
        self.build()
        self.q = {e: [] for e in self.ENG}
        self.es.__exit__(None, None, None)
        self.es = self._saved_es

    def finish(self):
        last = {}
        for k, v in self.out_events:
            last[k] = max(last.get(k, 0), v)
        waits = list(last.items())
        sem = self.sem

        def run(h):
            for k, v in waits:
                h.wait_ge(sem[k], v)
        self.q['sp'].append(run)

    def build(self):
        q = self.q
        with self.nc.Block() as blk:
            @blk.tensor
            def _(h):
                for f in q['pe']:
                    f(h)

            @blk.scalar
            def _(h):
                for f in q['act']:
                    f(h)

            @blk.vector
            def _(h):
                for f in q['dve']:
                    f(h)

            @blk.gpsimd
            def _(h):
                for f in q['pool']:
                    f(h)

            @blk.sync
            def _(h):
                for f in q['sp']:
                    f(h)


import math
from types import SimpleNamespace

D = 1024
NPT = 2048
NS = 16
NT = NPT + NS
DFF = 2816
DEPTH = 4
CHUNKS = [(0, 512), (512, 512), (1024, 512), (1536, 512), (2048, 16)]
EPS = 1e-6
PI = math.pi
STAGE = 9


def build_program(stage=STAGE):
    nc = bass.Bass("TRN2", target_bir_lowering=False)

    def din(name, shape, dt=F32):
        return nc.dram_tensor(name, list(shape), dt, kind="ExternalInput").ap()

    def dout(name, shape, dt=F32):
        return nc.dram_tensor(name, list(shape), dt, kind="ExternalOutput").ap()

    xp = din("xp", [NPT, D])
    xs = din("xs", [NS, D])
    mem = din("mem", [256, D])
    w_in = din("w_in", [DEPTH, D, D])
    w_out = din("w_out", [DEPTH, D, D])
    w_gu = din("w_gu", [DEPTH, D, 2 * DFF])
    w_down = din("w_down", [DEPTH, DFF, D])
    w_mem_kv = din("w_mem_kv", [DEPTH, D, 512])
    w_kv = din("w_kv", [D, 512])
    w_glu = din("w_glu", [2, 768, 768])
    gcols = din("gcols", [128, 13, 8])
    ghead = din("ghead", [128, 12, 64])
    gheadp = din("gheadp", [128, 12])
    cmk = din("cmk", [DEPTH, NS, 256, 256])
    cmv = din("cmv", [DEPTH, NS, 256, 256])
    s5a = din("s5a", [2, 128, 3, 24])
    s5b = din("s5b", [2, 2, 128, 24, 16])
    s5c = din("s5c", [2, 2, 128, 24, 16])
    s5d = din("s5d", [2, 96, 8])
    st0 = din("st0", [2, 2, NS, 3072])
    flagb = din("flagb", [128, 1])
    cache_k = din("cache_k", [2560, 128, 256])
    cache_v = din("cache_v", [2560, 128, 256])
    ptab = din("ptab", [1, NS * 16], I32)

    y_p = dout("y_p", [NPT, D])
    y_s = dout("y_s", [NS, D])
    o_pmk = dout("o_pmk", [DEPTH, 256, 256])
    o_pmv = dout("o_pmv", [DEPTH, 256, 256])
    o_pk = dout("o_pk", [NPT, 256])
    o_pv = dout("o_pv", [NPT, 256])
    o_sk = dout("o_sk", [NS, 256])
    o_sv = dout("o_sv", [NS, 256])
    o_pssm = dout("o_pssm", [2, 2, 24, 128])
    o_sssm = dout("o_sssm", [2, 2, NS, 3072])

    xin_d = nc.dram_tensor("xin_d", [128, 48], F32)
    xout_d = nc.dram_tensor("xout_d", [256, 48], F32)
    kv_d = nc.dram_tensor("kv_d", [512, 2048], BF16)
    kvall_d = nc.dram_tensor("kvall_d", [1024, 2048], BF16)
    snew_d = nc.dram_tensor("snew_d", [NS, 512], F32)
    knT_d = nc.dram_tensor("knT_d", [128, 2, NS], BF16)

    with ExitStack() as es:
        P = Prog(nc, es)
        dxin = P.view(xin_d, "xin_d")
        dxout = P.view(xout_d, "xout_d")
        dkv = P.view(kv_d, "kv_d")
        dkvall = P.view(kvall_d, "kvall_d")
        dsnew = P.view(snew_d, "snew_d")
        dknT = P.view(knT_d, "knT_d")

        def tt(out, a, b, op, R, W, eng='dve'):
            P.op(eng, lambda h: h.tensor_tensor(out=out, in0=a, in1=b, op=op), reads=R, writes=W)

        def ts(out, a, s1, op0, R, W, s2=None, op1=None, eng='dve'):
            if op1 is None:
                P.op(eng, lambda h: h.tensor_scalar(out=out, in0=a, scalar1=s1, scalar2=None, op0=op0), reads=R, writes=W)
            else:
                P.op(eng, lambda h: h.tensor_scalar(out=out, in0=a, scalar1=s1, scalar2=s2, op0=op0, op1=op1), reads=R, writes=W)

        def stt(out, a, sc, b, op0, op1, R, W):
            P.op('dve', lambda h: h.scalar_tensor_tensor(out=out, in0=a, scalar=sc, in1=b, op0=op0, op1=op1), reads=R, writes=W)

        def act(out, a, func, R, W, scale=1.0, bias=None):
            if bias is None:
                P.op('act', lambda h: h.activation(out=out, in_=a, func=func, scale=scale), reads=R, writes=W)
            else:
                P.op('act', lambda h: h.activation(out=out, in_=a, func=func, scale=scale, bias=bias), reads=R, writes=W)

        def cp(out, a, R, W, eng='dve'):
            P.op(eng, lambda h: h.tensor_copy(out=out, in_=a), reads=R, writes=W)

        def mset(ap, val, W, eng='pool'):
            P.op(eng, lambda h: h.memset(ap, val), writes=W)

        def mm(out, lhsT, rhs, start, stop, R, W):
            P.op('pe', lambda h: h.matmul(out, lhsT=lhsT, rhs=rhs, start=start, stop=stop), reads=R, writes=W)

        def tr(out, in_, idn, R, W):
            P.op('pe', lambda h: h.transpose(out, in_, idn), reads=R, writes=W)

        def dma(q, out, in_, R=(), W=(), is_out=False):
            P.dma(q, lambda h: h.dma_start(out=out, in_=in_), reads=list(R), writes=list(W), is_out=is_out)

        ident = P.sb("ident", [128, 128], F32)
        onesb = P.sb("onesb", [128, 128], BF16)
        blk64 = P.sb("blk64", [128, 128], BF16)
        epsc = P.sb("epsc", [128, 1], F32)
        hpic = P.sb("hpic", [128, 1], F32)
        gc = P.sb("gc", [128, 13, 8], F32)
        gh = P.sb("gh", [128, 12, 64], F32)
        ghp = P.sb("ghp", [128, 12], F32)
        selrow = P.sb("selrow", [128, 128], F32)
        flg = P.sb("flg", [128, 1], F32)
        pidx = P.sb("pidx", [128, NS * 16], I32)
        hT = [P.sb("hT%d" % i, [128, 8, w], F32) for i, (_, w) in enumerate(CHUNKS)]
        Bm_all = es.enter_context(nc.sbuf_tensor("Bm_all", [128, 10, NT], BF16))
        Bm = [P.view(Bm_all[:, :, c0_:c0_ + w_], "Bm%d" % i_) for i_, (c0_, w_) in enumerate(CHUNKS)]
        wb0 = P.sb("wb0", [128, 8192], BF16)
        MKT = P.sb("MKT", [128, DEPTH, 2, 256], BF16)
        MVA = P.sb("MVA", [128, DEPTH, 2, 4, 128], BF16)
        banks = [P.ps("bank%d" % i, [128, 512], F32) for i in range(8)]
        st = {'bank': 0, 'pT': 0, 'ffn': 0}

        def bank():
            b = banks[st['bank'] % 8]
            st['bank'] += 1
            return b

        def wb0v(k):
            return wb0[:, :].rearrange("p (k n) -> p k n", k=k)

        def common(pfx):
            T = SimpleNamespace()
            T.aT = P.sb(pfx + "aT", [128, 8, 512], BF16)
            T.sqs = [P.sb(pfx + "sq%d" % i, [128, 512], BF16) for i in range(2)]
            T.rs = P.sb(pfx + "rs", [128, 512], F32)
            T.stg = P.sb(pfx + "stg", [128, 1024], F32)
            T.kvtm = P.sb(pfx + "kvtm", [128, 512], F32)
            T.ssq = P.sb(pfx + "ssq", [128, 256], F32)
            T.hs4 = P.sb(pfx + "hs4", [128, 4], F32)
            T.kn = P.sb(pfx + "kn", [128, 256], F32)
            T.qsq = P.sb(pfx + "qsq", [128, 512], BF16)
            T.qrs = P.sb(pfx + "qrs", [128, 512], F32)
            T.pT = [P.sb(pfx + "pT%d" % i, [128, 512], BF16) for i in range(4)]
            T.osb = P.sb(pfx + "osb", [128, 512], F32)
            T.rden = P.sb(pfx + "rden", [128, 512], F32)
            T.qn = P.sb(pfx + "qn", [128, 2, 512], BF16)
            return T

        def rmsnorm_chunk(T, srcbuf, w, gidx, dstbuf, dst_tile0=0):
            pb = bank()
            for k in range(8):
                sq = T.sqs[k % 2]
                act(sq[:, :w], srcbuf[:, k, :w], AF.Square, [srcbuf], [sq])
                mm(pb[:, :w], onesb[:], sq[:, :w], k == 0, k == 7, [onesb, sq], [pb])
            act(T.rs[:, :w], pb[:, :w], AF.Ln, [pb, epsc], [T.rs], scale=1.0 / D, bias=epsc[:, 0:1])
            act(T.rs[:, :w], T.rs[:, :w], AF.Exp, [T.rs], [T.rs], scale=-0.5)
            for k in range(8):
                stt(dstbuf[:, dst_tile0 + k, :w], srcbuf[:, k, :w], gc[:, gidx, k:k + 1], T.rs[:, :w], ALU.mult, ALU.mult, [srcbuf, gc, T.rs], [dstbuf])

        def load_sq_weight(src, p, nt, ncols):
            v = wb0[:, 0:nt * ncols].rearrange("p (t n) -> p t n", t=nt)
            dma('pool', v[0:p, :, :], src.rearrange("(t p) n -> p t n", p=p), W=[wb0])
            return v

        def headnorm_tokmajor(T, src, srcbufs, gidx, dst, nrows=128):
            tt(T.ssq[0:nrows, :], src, src, ALU.mult, srcbufs, [T.ssq])
            P.op('dve', lambda h: h.tensor_reduce(out=T.hs4[0:nrows, :], in_=T.ssq[0:nrows, :].rearrange("p (h d) -> p h d", h=4), axis=AX.X, op=ALU.add), reads=[T.ssq], writes=[T.hs4])
            act(T.hs4[0:nrows, :], T.hs4[0:nrows, :], AF.Sqrt, [T.hs4, epsc], [T.hs4], scale=1.0 / 64, bias=epsc[0:nrows, 0:1])
            P.op('dve', lambda h: h.reciprocal(out=T.hs4[0:nrows, :], in_=T.hs4[0:nrows, :]), reads=[T.hs4], writes=[T.hs4])
            for hd in range(4):
                stt(dst[0:nrows, hd * 64:(hd + 1) * 64], src[:, hd * 64:(hd + 1) * 64], T.hs4[0:nrows, hd:hd + 1], gh[0:nrows, gidx, :], ALU.mult, ALU.mult, srcbufs + [T.hs4, gh], [dst])

        def headnorm_feat(T, buf, tile, w, gidx, dst, dtile):
            act(T.qsq[:, :w], buf[:, tile, :w], AF.Square, [buf], [T.qsq])
            pb = bank()
            mm(pb[:, :w], blk64[:], T.qsq[:, :w], True, True, [blk64, T.qsq], [pb])
            act(T.qrs[:, :w], pb[:, :w], AF.Ln, [pb, epsc], [T.qrs], scale=1.0 / 64, bias=epsc[:, 0:1])
            act(T.qrs[:, :w], T.qrs[:, :w], AF.Exp, [T.qrs], [T.qrs], scale=-0.5)
            stt(dst[:, dtile, :w], buf[:, tile, :w], ghp[:, gidx:gidx + 1], T.qrs[:, :w], ALU.mult, ALU.mult, [buf, ghp, T.qrs], [dst])

        def attend(T, keys, q_ap, qbufs, w, par, dst, dtile, kbufs, dc0=0):
            po = bank()
            n = len(keys)
            for i, (kap, vap) in enumerate(keys):
                ps = bank()
                mm(ps[:, :w], kap, q_ap, True, True, kbufs + qbufs, [ps])
                pt = T.pT[st['pT'] % 4]
                st['pT'] += 1
                act(pt[:, :w], ps[:, :w], AF.Exp, [ps], [pt])
                mm(po[:, :w], vap, pt[:, :w], i == 0, i == n - 1, kbufs + [pt], [po])
            lo = par * 64
            drow = 64 if par == 0 else 0
            act(T.osb[lo:lo + 64, :w], po[lo:lo + 64, :w], AF.Copy, [po], [T.osb])
            P.op('dve', lambda h: h.reciprocal(out=T.rden[drow:drow + 1, :w], in_=po[drow:drow + 1, :w]), reads=[po], writes=[T.rden])
            pbc = bank()
            mm(pbc[:, :w], selrow[drow:drow + 1, :], T.rden[drow:drow + 1, :w], True, True, [selrow, T.rden], [pbc])
            tt(dst[lo:lo + 64, dtile, dc0:dc0 + w], T.osb[lo:lo + 64, :w], pbc[lo:lo + 64, :w], ALU.mult, [T.osb, pbc], [dst])

        P.phase_begin()
        T = common("A")
        stg2 = P.sb("Astg2", [128, 1024], F32)
        memT = P.sb("memT", [128, 8, 128], F32)
        mnT = P.sb("mnT", [128, 8, 128], BF16)
        mset(ident[:], 0.0, [ident])
        P.op('pool', lambda h: h.affine_select(out=ident[:], in_=ident[:], pattern=[[-1, 128]], compare_op=ALU.not_equal, fill=1.0, base=0, channel_multiplier=1), reads=[ident], writes=[ident])
        mset(onesb[:], 1.0, [onesb])
        mset(blk64[:], 0.0, [blk64])
        mset(blk64[0:64, 0:64], 1.0, [blk64])
        mset(blk64[64:128, 64:128], 1.0, [blk64])
        mset(epsc[:], EPS, [epsc])
        mset(hpic[:], PI / 2, [hpic])
        mset(selrow[:], 0.0, [selrow])
        mset(selrow[64:65, 0:64], 1.0, [selrow])
        mset(selrow[0:1, 64:128], 1.0, [selrow])
        dma('sp', gc[:], gcols, W=[gc])
        dma('sp', gh[:], ghead, W=[gh])
        dma('sp', ghp[:], gheadp, W=[ghp])
        dma('sp', flg[:], flagb, W=[flg])
        pti = P.sb("pti", [128, NS * 16], I32)
        ptf = P.sb("ptf", [128, NS * 16], F32)
        iot = P.sb("iot", [128, 1], I32)
        iof = P.sb("iof", [128, 1], F32)
        dma('sp', pti[:], ptab.partition_broadcast(128), W=[pti])
        P.op('pool', lambda h: h.iota(iot[:], pattern=[[0, 1]], base=0, channel_multiplier=1), writes=[iot])
        cp(iof[:], iot[:], [iot], [iof])
        cp(ptf[:], pti[:], [pti], [ptf])
        ts(ptf[:], ptf[:], 128.0, ALU.mult, [ptf, iof], [ptf], s2=iof[:, 0:1], op1=ALU.add)
        cp(pidx[:], ptf[:], [ptf], [pidx])
        mset(MVA[:], 0.0, [MVA])
        for hd in range(4):
            dcol = 64 if hd % 2 == 0 else 0
            mset(MVA[:, :, :, hd, dcol:dcol + 1], 1.0, [MVA])

        for ti in range(NPT // 128 + 1):
            ntok = 128 if ti < NPT // 128 else NS
            s = T.stg if ti % 2 == 0 else stg2
            src = xp[ti * 128:(ti + 1) * 128, :] if ti < NPT // 128 else xs
            dma('sp', s[0:ntok, :], src, W=[s])
            ch = min(ti // 4, 4)
            c0 = (ti % 4) * 128 if ch < 4 else 0
            for half in range(2):
                pb = bank()
                for kk in range(4):
                    k = half * 4 + kk
                    tr(pb[:, kk * 128:kk * 128 + ntok], s[0:ntok, k * 128:(k + 1) * 128], ident[0:ntok, 0:ntok], [s, ident], [pb])
                src_ap = pb[:, :].rearrange("p (k t) -> p k t", k=4)[:, :, 0:ntok]
                dst_ap = hT[ch][:, half * 4:half * 4 + 4, c0:c0 + ntok]
                if half == 0:
                    act(dst_ap, src_ap, AF.Copy, [pb], [hT[ch]])
                else:
                    cp(dst_ap, src_ap, [pb], [hT[ch]])

        for ti in range(2):
            s = T.stg
            dma('sp', s[:, :], mem[ti * 128:(ti + 1) * 128, :], W=[s])
            for half in range(2):
                pb = bank()
                for kk in range(4):
                    k = half * 4 + kk
                    tr(pb[:, kk * 128:(kk + 1) * 128], s[:, k * 128:(k + 1) * 128], ident[:], [s, ident], [pb])
                act(memT[:, half * 4:half * 4 + 4, :], pb[:, :].rearrange("p (k t) -> p k t", k=4), AF.Copy, [pb], [memT])
            for l in range(DEPTH):
                rmsnorm_chunk(T, memT, 128, 8 + l, mnT)
                wv = load_sq_weight(w_mem_kv[l], 128, 8, 512)
                pb = bank()
                for k in range(8):
                    mm(pb[:, :], mnT[:, k, :], wv[:, k, :], k == 0, k == 7, [mnT, wb0], [pb])
                act(T.kvtm[:], pb[:], AF.Copy, [pb], [T.kvtm])
                headnorm_tokmajor(T, T.kvtm[:, 0:256], [T.kvtm], 4 + l, T.kn)
                dma('sp', o_pmk[l, ti * 128:(ti + 1) * 128, :], T.kn[:, :], R=[T.kn], is_out=True)
                dma('sp', o_pmv[l, ti * 128:(ti + 1) * 128, :], T.kvtm[:, 256:512], R=[T.kvtm], is_out=True)
                for hd in range(4):
                    c0 = (hd % 2) * 64
                    cp(MVA[:, l, ti, hd, c0:c0 + 64], T.kvtm[:, 256 + hd * 64:256 + (hd + 1) * 64], [T.kvtm], [MVA], eng='pool')
                pb2 = bank()
                for hp in range(2):
                    tr(pb2[:, hp * 128:(hp + 1) * 128], T.kn[:, hp * 128:(hp + 1) * 128], ident[:], [T.kn, ident], [pb2])
                act(MKT[:, l, :, ti * 128:(ti + 1) * 128], pb2[:, 0:256].rearrange("p (a t) -> p a t", a=2), AF.Copy, [pb2], [MKT], scale=0.125)
        P.phase_end()

        def inproj(l, T):
            is_s5 = l < 2
            wv = load_sq_weight(w_in[l], 128, 8, 1024)
            for ch, (c0, w) in enumerate(CHUNKS):
                rmsnorm_chunk(T, hT[ch], w, l, T.aT)
                if is_s5:
                    outs = [(t, 96 * t, 96) for t in range(8)]
                else:
                    outs = [(t, 128 * t, 128) for t in range(6)]
                outs += [(8, 768, 128), (9, 896, 128)]
                for i, (slot, col0, mw) in enumerate(outs):
                    pb = bank()
                    for k in range(8):
                        mm(pb[0:mw, :w], wv[:, k, col0:col0 + mw], T.aT[:, k, :w], k == 0, k == 7, [wb0, T.aT], [pb])
                    if i % 2 == 0:
                        act(Bm[ch][0:mw, slot, :w], pb[0:mw, :w], AF.Copy, [pb], [Bm[ch]])
                    else:
                        cp(Bm[ch][0:mw, slot, :w], pb[0:mw, :w], [pb], [Bm[ch]])

        def s5_layer(l):
            P.phase_begin()
            nm = lambda n: "s5%s_%d" % (n, l)
            prm = P.sb(nm("prm"), [128, 3, 24], F32)
            PB = [P.sb(nm("PB%d" % i), [128, 24, 16], F32) for i in range(2)]
            PC = [P.sb(nm("PC%d" % i), [128, 24, 16], F32) for i in range(2)]
            dcol = P.sb(nm("dcol"), [128, 8], F32)
            V = {n: P.sb(nm(n), [128, 24], F32) for n in
                 ["dt", "lre", "mag", "ang", "r", "kf", "m1", "c1", "s1", "abr", "abi", "nabi", "den", "nre", "t1", "t2", "fre", "fim", "m8", "c8", "s8"]}
            ki = P.sb(nm("ki"), [128, 24], I32)
            PWr = P.sb(nm("PWr"), [128, 9, 24], F32)
            PWi = P.sb(nm("PWi"), [128, 9, 24], F32)
            P2r = P.sb(nm("P2r"), [128, 8, 24], F32)
            P2i = P.sb(nm("P2i"), [128, 8, 24], F32)
            H0 = [P.sb(nm("H0%d" % i), [128, 24, 16], F32) for i in range(2)]
            XN = [P.sb(nm("XN%d" % i), [128, 24, 16], F32) for i in range(2)]
            H0b = P.sb(nm("H0b"), [128, 24, 2, 16], BF16)
            FIN = P.sb(nm("FIN"), [128, 24, 2], F32)
            XI = P.sb(nm("XI"), [128, 24, 2], F32)
            PFIN = P.sb(nm("PFIN"), [128, 2, 24], F32)
            z0 = P.sb(nm("z0"), [128, 4], F32)
            BsT = P.sb(nm("BsT"), [128, 8, 2, 128], BF16)
            Kf = P.sb(nm("Kf"), [128, 8, 96], BF16)
            K0f = P.sb(nm("K0f"), [128, 96], F32)
            CjT = P.sb(nm("CjT"), [128, 3, 8, 2, 96], BF16)
            BBw = P.sb(nm("BBw"), [128, 3, 2, 96], F32)
            Cw = P.sb(nm("Cw"), [128, 3, 2, 96], F32)
            Rc3 = P.sb(nm("Rc3"), [128, 3, 256], F32)
            Rs3 = P.sb(nm("Rs3"), [128, 3, 256], F32)
            SpB = P.sb(nm("SpB"), [128, 1536], F32)
            ZB = P.sb(nm("ZB"), [128, 1728], F32)
            taB = P.sb(nm("ta"), [128, 768], F32)
            rtmp = XN[0]
            u9 = SpB
            u8 = P.sb(nm("u8"), [128, 8, 256], BF16)
            Xprev = P.sb(nm("Xprev"), [128, 3, 2, 256], BF16)
            TS8 = SpB[:, :].rearrange("p (k r n) -> p k r n", k=8, r=2)
            Sp = [SpB[:, 0:256], SpB[:, 256:512]]
            sstg = SpB
            ACw = ZB[:, :].rearrange("p (k r n) -> p k r n", k=9, r=2)
            Z = [ZB[:, 0:256], ZB[:, 256:512]]
            t8a = taB[:, :].rearrange("p (k n) -> p k n", k=8)
            ta, tb, M8t = taB[:, 0:256], taB[:, 256:512], taB[:, 512:768]
            rtab_d = nc.dram_tensor(nm("rtab"), [24, 128, 2, 256], F32)
            bst_d = nc.dram_tensor(nm("bst"), [8, 128, 8 * 2 * 128], BF16)
            drtab = P.view(rtab_d, "rtab")
            dbst = P.view(bst_d, "bst")
            cnt = {'ac': 0}

            dma('sp', prm[:], s5a[l], W=[prm])
            for i in range(2):
                dma('sp', PB[i][:], s5b[l, i], W=[PB[i]])
                dma('sp', PC[i][:], s5c[l, i], W=[PC[i]])
            dma('sp', dcol[0:96, :], s5d[l], W=[dcol])
            mset(K0f[:], 0.0, [K0f])
            mset(Kf[:], 0.0, [Kf])
            are, aim, ldt = prm[:, 0, :], prm[:, 1, :], prm[:, 2, :]
            v = lambda n: V[n][:, :]
            act(v("dt"), ldt, AF.Exp, [prm], [V["dt"]])
            ts(v("lre"), are, -1e-4, ALU.min, [prm], [V["lre"]])
            tt(v("t1"), v("dt"), v("lre"), ALU.mult, [V["dt"], V["lre"]], [V["t1"]])
            act(v("mag"), v("t1"), AF.Exp, [V["t1"]], [V["mag"]])
            tt(v("ang"), v("dt"), aim, ALU.mult, [V["dt"], prm], [V["ang"]])
            ts(v("kf"), v("ang"), 1.0 / (2 * PI), ALU.mult, [V["ang"]], [V["kf"]])
            cp(ki[:, :], v("kf"), [V["kf"]], [ki])
            cp(v("kf"), ki[:, :], [ki], [V["kf"]])
            stt(v("r"), v("kf"), -2 * PI, v("ang"), ALU.mult, ALU.add, [V["kf"], V["ang"]], [V["r"]])
            ts(v("m1"), v("r"), PI, ALU.is_gt, [V["r"]], [V["m1"]])
            stt(v("r"), v("m1"), -2 * PI, v("r"), ALU.mult, ALU.add, [V["m1"], V["r"]], [V["r"]])
            ts(v("m1"), v("r"), -PI, ALU.is_lt, [V["r"]], [V["m1"]])
            stt(v("r"), v("m1"), 2 * PI, v("r"), ALU.mult, ALU.add, [V["m1"], V["r"]], [V["r"]])
            ts(v("r"), v("r"), PI, ALU.min, [V["r"]], [V["r"]], s2=-PI, op1=ALU.max)
            act(v("s1"), v("r"), AF.Sin, [V["r"]], [V["s1"]])
            stt(v("t1"), v("r"), -1.0, v("r"), ALU.mult, ALU.max, [V["r"]], [V["t1"]])
            act(v("c1"), v("t1"), AF.Sin, [V["t1"], hpic], [V["c1"]], scale=-1.0, bias=hpic[:, 0:1])
            tt(v("abr"), v("mag"), v("c1"), ALU.mult, [V["mag"], V["c1"]], [V["abr"]])
            tt(v("abi"), v("mag"), v("s1"), ALU.mult, [V["mag"], V["s1"]], [V["abi"]])
            ts(v("nabi"), v("abi"), -1.0, ALU.mult, [V["abi"]], [V["nabi"]])
            tt(v("den"), v("lre"), v("lre"), ALU.mult, [V["lre"]], [V["den"]])
            tt(v("t1"), aim, aim, ALU.mult, [prm], [V["t1"]])
            tt(v("den"), v("den"), v("t1"), ALU.add, [V["den"], V["t1"]], [V["den"]])
            P.op('dve', lambda h: h.reciprocal(out=v("den"), in_=v("den")), reads=[V["den"]], writes=[V["den"]])
            ts(v("nre"), v("abr"), -1.0, ALU.add, [V["abr"]], [V["nre"]])
            tt(v("t1"), v("nre"), v("lre"), ALU.mult, [V["nre"], V["lre"]], [V["t1"]])
            tt(v("t2"), v("abi"), aim, ALU.mult, [V["abi"], prm], [V["t2"]])
            tt(v("t1"), v("t1"), v("t2"), ALU.add, [V["t1"], V["t2"]], [V["t1"]])
            tt(v("fre"), v("t1"), v("den"), ALU.mult, [V["t1"], V["den"]], [V["fre"]])
            tt(v("t1"), v("abi"), v("lre"), ALU.mult, [V["abi"], V["lre"]], [V["t1"]])
            tt(v("t2"), v("nre"), aim, ALU.mult, [V["nre"], prm], [V["t2"]])
            tt(v("t1"), v("t1"), v("t2"), ALU.subtract, [V["t1"], V["t2"]], [V["t1"]])
            tt(v("fim"), v("t1"), v("den"), ALU.mult, [V["t1"], V["den"]], [V["fim"]])
            frb = v("fre").unsqueeze(2).broadcast_to([128, 24, 16])
            fib = v("fim").unsqueeze(2).broadcast_to([128, 24, 16])
            tt(XN[0][:], PB[1][:], fib, ALU.mult, [PB[1], V["fim"]], [XN[0]])
            tt(XN[1][:], PB[0][:], fib, ALU.mult, [PB[0], V["fim"]], [XN[1]])
            tt(PB[0][:], PB[0][:], frb, ALU.mult, [PB[0], V["fre"]], [PB[0]])
            tt(PB[0][:], PB[0][:], XN[0][:], ALU.subtract, [PB[0], XN[0]], [PB[0]])
            tt(PB[1][:], PB[1][:], frb, ALU.mult, [PB[1], V["fre"]], [PB[1]])
            tt(PB[1][:], PB[1][:], XN[1][:], ALU.add, [PB[1], XN[1]], [PB[1]])
            mset(PWr[:, 0, :], 1.0, [PWr])
            mset(PWi[:, 0, :], 0.0, [PWi])
            for k in range(1, 9):
                tt(v("t1"), PWr[:, k - 1, :], v("abr"), ALU.mult, [PWr, V["abr"]], [V["t1"]])
                tt(v("t2"), PWi[:, k - 1, :], v("abi"), ALU.mult, [PWi, V["abi"]], [V["t2"]])
                tt(PWr[:, k, :], v("t1"), v("t2"), ALU.subtract, [V["t1"], V["t2"]], [PWr])
                tt(v("t1"), PWr[:, k - 1, :], v("abi"), ALU.mult, [PWr, V["abi"]], [V["t1"]])
                tt(v("t2"), PWi[:, k - 1, :], v("abr"), ALU.mult, [PWi, V["abr"]], [V["t2"]])
                tt(PWi[:, k, :], v("t1"), v("t2"), ALU.add, [V["t1"], V["t2"]], [PWi])
            tt(v("m8"), v("mag"), v("mag"), ALU.mult, [V["mag"]], [V["m8"]])
            tt(v("m8"), v("m8"), v("m8"), ALU.mult, [V["m8"]], [V["m8"]])
            tt(v("m8"), v("m8"), v("m8"), ALU.mult, [V["m8"]], [V["m8"]])

            def csquare(cr_o, ci_o, cr_i, ci_i, R, W):
                tt(v("t1"), cr_i, cr_i, ALU.mult, R, [V["t1"]])
                tt(v("t2"), ci_i, ci_i, ALU.mult, R, [V["t2"]])
                tt(v("den"), cr_i, ci_i, ALU.mult, R, [V["den"]])
                tt(cr_o, v("t1"), v("t2"), ALU.subtract, [V["t1"], V["t2"]], W)
                ts(ci_o, v("den"), 2.0, ALU.mult, [V["den"]], W)
            csquare(v("c8"), v("s8"), v("c1"), v("s1"), [V["c1"], V["s1"]], [V["c8"], V["s8"]])
            csquare(v("c8"), v("s8"), v("c8"), v("s8"), [V["c8"], V["s8"]], [V["c8"], V["s8"]])
            csquare(v("c8"), v("s8"), v("c8"), v("s8"), [V["c8"], V["s8"]], [V["c8"], V["s8"]])
            cp(P2r[:, 0, :], v("c8"), [V["c8"]], [P2r])
            cp(P2i[:, 0, :], v("s8"), [V["s8"]], [P2i])
            for k in range(1, 8):
                csquare(P2r[:, k, :], P2i[:, k, :], P2r[:, k - 1, :], P2i[:, k - 1, :], [P2r, P2i], [P2r, P2i])

            for ri in range(2):
                for blk in range(6):
                    dma('sp', sstg[0:NS, 0:512], st0[l, ri, :, blk * 512:(blk + 1) * 512], W=[sstg])
                    pb = bank()
                    for i in range(4):
                        tr(pb[:, i * 16:(i + 1) * 16], sstg[0:NS, i * 128:(i + 1) * 128], ident[0:NS, 0:NS], [sstg, ident], [pb])
                    cp(H0[ri][:, 4 * blk:4 * blk + 4, :], pb[:, 0:64].rearrange("p (a s) -> p a s", a=4), [pb], [H0[ri]])
                cp(H0b[:, :, ri, :], H0[ri][:], [H0[ri]], [H0b])

            def build_wide(q, qi):
                for ri in range(2):
                    for g2 in range(2):
                        c0 = 32 * qi + 16 * g2
                        cp(BBw[g2 * 64:(g2 + 1) * 64, qi, ri, c0:c0 + 16], PB[ri][g2 * 64:(g2 + 1) * 64, q, :], [PB[ri]], [BBw], eng='pool')
                        cp(Cw[g2 * 64:(g2 + 1) * 64, qi, ri, c0:c0 + 16], PC[ri][g2 * 64:(g2 + 1) * 64, q, :], [PC[ri]], [Cw], eng='pool')

            def build_Bs(q, qi):
                pr = PWr[:, 0:8, q:q + 1].broadcast_to([128, 8, 96])
                pi = PWi[:, 0:8, q:q + 1].broadcast_to([128, 8, 96])
                Br = BBw[:, qi, 0, :].unsqueeze(1).broadcast_to([128, 8, 96])
                Bi = BBw[:, qi, 1, :].unsqueeze(1).broadcast_to([128, 8, 96])
                tt(t8a, Bi, pi, ALU.mult, [BBw, PWi], [taB])
                tt(TS8[:, :, 0, :], Br, pr, ALU.mult, [BBw, PWr], [SpB])
                tt(TS8[:, :, 0, :], TS8[:, :, 0, :], t8a, ALU.subtract, [SpB, taB], [SpB])
                tt(t8a, Br, pi, ALU.mult, [BBw, PWi], [taB])
                tt(TS8[:, :, 1, :], Bi, pr, ALU.mult, [BBw, PWr], [SpB])
                tt(TS8[:, :, 1, :], TS8[:, :, 1, :], t8a, ALU.add, [SpB, taB], [SpB])
                for k in range(8):
                    pb = bank()
                    tr(pb[0:96, 0:128], TS8[:, k, 0, :], ident[:], [SpB, ident], [pb])
                    tr(pb[0:96, 128:256], TS8[:, k, 1, :], ident[:], [SpB, ident], [pb])
                    act(BsT[32 * qi:32 * qi + 32, k, :, :], pb[32 * qi:32 * qi + 32, 0:256].rearrange("p (r n) -> p r n", r=2), AF.Copy, [pb], [BsT])

            def build_AC(q, qi):
                pr = PWr[:, 0:9, q:q + 1].broadcast_to([128, 9, 96])
                pi = PWi[:, 0:9, q:q + 1].broadcast_to([128, 9, 96])
                Cr = Cw[:, qi, 0, :].unsqueeze(1).broadcast_to([128, 9, 96])
                Ci = Cw[:, qi, 1, :].unsqueeze(1).broadcast_to([128, 9, 96])
                t9 = u9[:, 672:1536].rearrange("p (k n) -> p k n", k=9)
                tt(t9, Ci, pi, ALU.mult, [Cw, PWi], [u9], eng='pool')
                tt(ACw[:, :, 0, :], Cr, pr, ALU.mult, [Cw, PWr], [ZB], eng='pool')
                tt(ACw[:, :, 0, :], ACw[:, :, 0, :], t9, ALU.subtract, [ZB, u9], [ZB], eng='pool')
                tt(t9, Cr, pi, ALU.mult, [Cw, PWi], [u9], eng='pool')
                tt(ACw[:, :, 1, :], Ci, pr, ALU.mult, [Cw, PWr], [ZB], eng='pool')
                tt(ACw[:, :, 1, :], ACw[:, :, 1, :], t9, ALU.add, [ZB, u9], [ZB], eng='pool')
                ts(ACw[:, :, 1, :], ACw[:, :, 1, :], -1.0, ALU.mult, [ZB], [ZB], eng='pool')

            def build_wide_triple(t):
                mset(BBw[:], 0.0, [BBw])
                mset(Cw[:], 0.0, [Cw])
                for qi in range(3):
                    build_wide(3 * t + qi, qi)

            def build_conv_tables(t):
                for qi in range(3):
                    q = 3 * t + qi
                    build_AC(q, qi)
                    cp(CjT[:, qi, :, :, :], ACw[:, 1:9, :, :], [ZB], [CjT], eng='pool')
                    pK = bank()
                    for tau in range(8):
                        for ri in range(2):
                            mm(pK[0:96, tau * 32:(tau + 1) * 32], BBw[:, qi, ri, :], ACw[:, tau, ri, 32 * qi:32 * qi + 32], ri == 0, ri == 1, [BBw, ZB], [pK])
                    cp(Kf[32 * qi:32 * qi + 32, 1:8, 32 * qi:32 * qi + 32], pK[32 * qi:32 * qi + 32, 32:256].rearrange("p (k n) -> p k n", k=7), [pK], [Kf])
                    cp(K0f[32 * qi:32 * qi + 32, 32 * qi:32 * qi + 32], pK[32 * qi:32 * qi + 32, 0:32], [pK], [K0f])
                stt(Kf[0:96, 0, :], ident[0:96, 0:96], dcol[0:96, t:t + 1], K0f[0:96, :], ALU.mult, ALU.add, [ident, dcol, K0f], [Kf])

            def deint(t):
                act(u8[0:96, :, :], Bm_all[0:96, t, 0:NPT].rearrange("p (c s) -> p s c", s=8), AF.Copy, [Bm[0], Bm[1], Bm[2], Bm[3]], [u8])

            def build_rot_triple(t):
                q0 = 3 * t
                mset(Rc3[:, :, 0:1], 1.0, [Rc3])
                mset(Rs3[:, :, 0:1], 0.0, [Rs3])
                for k in range(8):
                    n = 1 << k
                    pr = P2r[:, k, q0:q0 + 3].unsqueeze(2).broadcast_to([128, 3, n])
                    pi = P2i[:, k, q0:q0 + 3].unsqueeze(2).broadcast_to([128, 3, n])
                    tmp = rtmp[:, :, :].rearrange("p a b -> p (a b)")[:, 0:3 * n].rearrange("p (a n) -> p a n", a=3)
                    tt(tmp, Rs3[:, :, 0:n], pi, ALU.mult, [Rs3, P2i], [rtmp], eng='pool')
                    tt(Rc3[:, :, n:2 * n], Rc3[:, :, 0:n], pr, ALU.mult, [Rc3, P2r], [Rc3], eng='pool')
                    tt(Rc3[:, :, n:2 * n], Rc3[:, :, n:2 * n], tmp, ALU.subtract, [Rc3, rtmp], [Rc3], eng='pool')
                    tt(tmp, Rc3[:, :, 0:n], pi, ALU.mult, [Rc3, P2i], [rtmp], eng='pool')
                    tt(Rs3[:, :, n:2 * n], Rs3[:, :, 0:n], pr, ALU.mult, [Rs3, P2r], [Rs3], eng='pool')
                    tt(Rs3[:, :, n:2 * n], Rs3[:, :, n:2 * n], tmp, ALU.add, [Rs3, rtmp], [Rs3], eng='pool')

            def scan_pair(t, qi, use_init, full):
                q = 3 * t + qi
                Rc, Rs = Rc3[:, qi, :], Rs3[:, qi, :]
                cp(M8t, V["m8"][:, q:q + 1].broadcast_to([128, 256]), [V["m8"]], [taB])
                pS = bank()
                for ri in range(2):
                    for s_ in range(8):
                        mm(pS[:, ri * 256:(ri + 1) * 256], BsT[32 * qi:32 * qi + 32, 7 - s_, ri, :], u8[32 * qi:32 * qi + 32, s_, :], s_ == 0, s_ == 7, [BsT, u8], [pS])
                Sr, Si = pS[:, 0:256], pS[:, 256:512]
                tt(ta, Sr, Rc, ALU.mult, [pS, Rc3], [taB])
                tt(tb, Si, Rs, ALU.mult, [pS, Rs3], [taB])
                tt(Sp[0], ta, tb, ALU.add, [taB], [SpB])
                tt(ta, Si, Rc, ALU.mult, [pS, Rc3], [taB])
                tt(tb, Sr, Rs, ALU.mult, [pS, Rs3], [taB])
                tt(Sp[1], ta, tb, ALU.subtract, [taB], [SpB])
                if use_init:
                    xr, xi = XI[:, q, 0:1], XI[:, q, 1:2]
                    c8q, s8q = V["c8"][:, q:q + 1], V["s8"][:, q:q + 1]
                    ts(z0[:, 2:3], xi, s8q, ALU.mult, [XI, V["s8"]], [z0])
                    stt(z0[:, 0:1], xr, c8q, z0[:, 2:3], ALU.mult, ALU.subtract, [XI, V["c8"], z0], [z0])
                    ts(z0[:, 2:3], xr, s8q, ALU.mult, [XI, V["s8"]], [z0])
                    stt(z0[:, 1:2], xi, c8q, z0[:, 2:3], ALU.mult, ALU.add, [XI, V["c8"], z0], [z0])
                for ri in range(2):
                    init = z0[:, ri:ri + 1] if use_init else 0.0
                    P.op('dve', lambda h, ri=ri, init=init: h.tensor_tensor_scan(out=Z[ri], data0=M8t, data1=Sp[ri], initial=init, op0=ALU.mult, op1=ALU.add), reads=[taB, SpB, z0], writes=[ZB])
                if not full:
                    tt(ta[:, 0:1], Z[1][:, 255:256], Rs[:, 255:256], ALU.mult, [ZB, Rs3], [taB])
                    tt(tb[:, 0:1], Z[0][:, 255:256], Rc[:, 255:256], ALU.mult, [ZB, Rc3], [taB])
                    tt(FIN[:, q, 0:1], tb[:, 0:1], ta[:, 0:1], ALU.subtract, [taB], [FIN])
                    tt(ta[:, 0:1], Z[0][:, 255:256], Rs[:, 255:256], ALU.mult, [ZB, Rs3], [taB])
                    tt(tb[:, 0:1], Z[1][:, 255:256], Rc[:, 255:256], ALU.mult, [ZB, Rc3], [taB])
                    tt(FIN[:, q, 1:2], tb[:, 0:1], ta[:, 0:1], ALU.add, [taB], [FIN])
                    return
                tt(ta, Z[0], Rc, ALU.mult, [ZB, Rc3], [taB])
                tt(tb, Z[1], Rs, ALU.mult, [ZB, Rs3], [taB])
                tt(Sp[0], ta, tb, ALU.subtract, [taB], [SpB])
                tt(ta, Z[1], Rc, ALU.mult, [ZB, Rc3], [taB])
                tt(tb, Z[0], Rs, ALU.mult, [ZB, Rs3], [taB])
                tt(Sp[1], ta, tb, ALU.add, [taB], [SpB])
                for ri in range(2):
                    cp(Xprev[:, qi, ri, 1:256], Sp[ri][:, 0:255], [SpB], [Xprev], eng='pool')
                    cp(Xprev[:, qi, ri, 0:1], XI[:, q, ri:ri + 1], [XI], [Xprev], eng='pool')
                    cp(PFIN[:, ri, q:q + 1], Sp[ri][:, 255:256], [SpB], [PFIN], eng='pool')

            for t in range(8):
                deint(t)
                build_wide_triple(t)
                build_rot_triple(t)
                for qi in range(3):
                    build_Bs(3 * t + qi, qi)
                dma('sp', bst_d.ap()[t], BsT[:, :, :, :].rearrange("p k r n -> p (k r n)"), R=[BsT], W=[dbst])
                for qi in range(3):
                    dma('sp', rtab_d.ap()[3 * t + qi, :, 0, :], Rc3[:, qi, :], R=[Rc3], W=[drtab])
                    dma('sp', rtab_d.ap()[3 * t + qi, :, 1, :], Rs3[:, qi, :], R=[Rs3], W=[drtab])
                for qi in range(3):
                    scan_pair(t, qi, False, False)
            dma('sp', xin_d.ap(), FIN[:, :, :].rearrange("p q r -> p (q r)"), R=[FIN], W=[dxin])
            P.dma('pool', lambda h: h.collective_compute("AllGather", ALU.bypass, replica_groups=[[0, 1], [2, 3], [4, 5], [6, 7]], ins=[xin_d.ap().opt()], outs=[xout_d.ap().opt()]), reads=[dxin], writes=[dxout], inc=1)
            dma('sp', XI[:, :, :].rearrange("p q r -> p (q r)"), xout_d.ap()[0:128, :], R=[dxout], W=[XI])
            ts(XI[:, :, :], XI[:, :, :], flg[:, 0:1], ALU.mult, [XI, flg], [XI])
            for t in range(8):
                deint(t)
                dma('sp', BsT[:, :, :, :].rearrange("p k r n -> p (k r n)"), bst_d.ap()[t], R=[dbst], W=[BsT])
                for qi in range(3):
                    dma('sp', Rc3[:, qi, :], rtab_d.ap()[3 * t + qi, :, 0, :], R=[drtab], W=[Rc3])
                    dma('sp', Rs3[:, qi, :], rtab_d.ap()[3 * t + qi, :, 1, :], R=[drtab], W=[Rs3])
                build_wide_triple(t)
                build_conv_tables(t)
                for qi in range(3):
                    scan_pair(t, qi, True, True)
                pos = [bank() for _ in range(4)]
                for j in range(8):
                    po = pos[j // 2]
                    o = po[0:96, (j % 2) * 256:(j % 2 + 1) * 256]
                    nmm = (j + 1) + 6
                    i = 0
                    for tau in range(j + 1):
                        mm(o, Kf[0:96, tau, :], u8[0:96, j - tau, :], i == 0, i == nmm - 1, [Kf, u8], [po])
                        i += 1
                    for qi in range(3):
                        for ri in range(2):
                            mm(o, CjT[:, qi, j, ri, :], Xprev[:, qi, ri, :], i == 0, i == nmm - 1, [CjT, Xprev], [po])
                            i += 1
                for ch in range(4):
                    for b in range(4):
                        src = pos[b][0:96, :].rearrange("p (j c) -> p j c", j=2)[:, :, ch * 64:(ch + 1) * 64]
                        dst = Bm[ch][0:96, t, :].rearrange("p (c j) -> p j c", j=8)[:, 2 * b:2 * b + 2, :]
                        act(dst, src, AF.Gelu, [pos[b]], [Bm[ch]])
                pq = bank()
                i = 0
                mm(pq[0:96, 0:NS], Kf[0:96, 0, :], Bm[4][0:96, t, 0:NS], True, False, [Kf, Bm[4]], [pq])
                for qi in range(3):
                    for ri in range(2):
                        i += 1
                        mm(pq[0:96, 0:NS], CjT[:, qi, 0, ri, :], H0b[:, 3 * t + qi, ri, :], False, i == 6, [CjT, H0b], [pq])
                for qi in range(3):
                    q = 3 * t + qi
                    pu = bank()
                    for ri in range(2):
                        mm(pu[:, ri * NS:(ri + 1) * NS], BsT[32 * qi:32 * qi + 32, 0, ri, :], Bm[4][32 * qi:32 * qi + 32, t, 0:NS], True, True, [BsT, Bm[4]], [pu])
                    abr_q, abi_q, nabi_q = V["abr"][:, q:q + 1], V["abi"][:, q:q + 1], V["nabi"][:, q:q + 1]
                    ts(ta[:, 0:NS], H0[0][:, q, :], abr_q, ALU.mult, [H0[0], V["abr"]], [taB])
                    tt(ta[:, 0:NS], ta[:, 0:NS], pu[:, 0:NS], ALU.add, [taB, pu], [taB])
                    stt(XN[0][:, q, :], H0[1][:, q, :], nabi_q, ta[:, 0:NS], ALU.mult, ALU.add, [H0[1], V["nabi"], taB], [XN[0]])
                    ts(tb[:, 0:NS], H0[1][:, q, :], abr_q, ALU.mult, [H0[1], V["abr"]], [taB])
                    tt(tb[:, 0:NS], tb[:, 0:NS], pu[:, NS:2 * NS], ALU.add, [taB, pu], [taB])
                    stt(XN[1][:, q, :], H0[0][:, q, :], abi_q, tb[:, 0:NS], ALU.mult, ALU.add, [H0[0], V["abi"], taB], [XN[1]])
                act(Bm[4][0:96, t, 0:NS], pq[0:96, 0:NS], AF.Gelu, [pq], [Bm[4]])
            for ri in range(2):
                pb = bank()
                tr(pb[0:24, 0:128], PFIN[:, ri, :], ident[:], [PFIN, ident], [pb])
                cp(sstg[0:24, 0:128], pb[0:24, 0:128], [pb], [sstg])
                dma('sp', o_pssm[l, ri], sstg[0:24, 0:128], R=[sstg], is_out=True)
                for blk in range(6):
                    pb = bank()
                    for i in range(4):
                        tr(pb[0:NS, i * 128:(i + 1) * 128], XN[ri][:, 4 * blk + i, :], ident[:], [XN[ri], ident], [pb])
                    cp(sstg[0:NS, 0:512], pb[0:NS, :], [pb], [sstg])
                    dma('sp', o_sssm[l, ri, :, blk * 512:(blk + 1) * 512], sstg[0:NS, 0:512], R=[sstg], is_out=True)
            P.phase_end()

        def moba_layer(l):
            j = l - 2
            P.phase_begin()
            nm = lambda n: "mb%s_%d" % (n, l)
            KTaug = P.sb(nm("KT"), [128, 4096], BF16)
            VZ = P.sb(nm("VZ"), [128, 32, 192], BF16)
            KMf = P.sb(nm("KMf"), [128, 16], F32)
            KM = P.sb(nm("KM"), [128, 16], BF16)
            qaug = [P.sb(nm("qa%d" % i), [128, 512], BF16) for i in range(2)]
            MBw = [P.sb(nm("MBw%d" % i), [128, 80], F32) for i in range(4)]
            g1 = P.sb(nm("g1"), [128, 16], F32)
            m8 = P.sb(nm("m8"), [128, 8], F32)
            sel = P.sb(nm("sel"), [128, 16], F32)
            VBg = P.sb(nm("VBg"), [128, 8, 16], F32)
            vm = P.sb(nm("vm"), [128, 8, 16], F32)
            pb1 = P.sb(nm("pb1"), [128, 1], F32)
            tri = P.sb(nm("tri"), [128, 128], BF16)
            shiftsel = P.sb(nm("shs"), [128, 64], BF16)
            pTs = [P.sb(nm("pT%d" % i), [128, 512], BF16) for i in range(4)]
            osb = P.sb(nm("osb"), [128, 512], F32)
            rden = P.sb(nm("rden"), [128, 512], F32)
            qsq = P.sb(nm("qsq"), [128, 512], BF16)
            qrs = P.sb(nm("qrs"), [128, 512], F32)
            T = SimpleNamespace(qsq=qsq, qrs=qrs)
            c = {'pt': 0, 'qa': 0, 'mb': 0, 'ps': 0, 'po': 0, 'mi': 0}

            def bank_ps():
                c['ps'] += 1
                return banks[c['ps'] % 3]

            def bank_po():
                c['po'] += 1
                return banks[3 + c['po'] % 2]

            def bank_mi():
                return banks[5]

            sd = sdec_gen(l) if stage >= 4 else None
            pend = {'e': None}

            def step_sd(k):
                nonlocal sd
                for _ in range(k):
                    if sd is None:
                        return
                    try:
                        next(sd)
                    except StopIteration:
                        sd = None

            mset(tri[:], 1.0, [tri])
            P.op('pool', lambda h: h.affine_select(out=tri[:], in_=tri[:], pattern=[[1, 128]], compare_op=ALU.is_ge, fill=0.0, base=0, channel_multiplier=-1), reads=[tri], writes=[tri])
            mset(shiftsel[:], 1.0, [shiftsel])
            P.op('pool', lambda h: h.affine_select(out=shiftsel[:], in_=shiftsel[:], pattern=[[-1, 64]], compare_op=ALU.is_equal, fill=0.0, base=-64, channel_multiplier=1), reads=[shiftsel], writes=[shiftsel])
            mset(KTaug[64:80, :], 1.0, [KTaug])
            P.op('pool', lambda h: h.affine_select(out=KTaug[64:80, :], in_=KTaug[64:80, :], pattern=[[1, 4096]], compare_op=ALU.is_ge, fill=0.0, base=0, channel_multiplier=-256), reads=[KTaug], writes=[KTaug])
            P.op('pool', lambda h: h.affine_select(out=KTaug[64:80, :], in_=KTaug[64:80, :], pattern=[[-1, 4096]], compare_op=ALU.is_ge, fill=0.0, base=255, channel_multiplier=256), reads=[KTaug], writes=[KTaug])
            mset(VZ[:], 0.0, [VZ])
            mset(VZ[:, :, 64:65], 1.0, [VZ])
            for i in range(4):
                mset(MBw[i][:], 0.0, [MBw[i]])
            ts(pb1[:, :], flg[:, :], -1.0, ALU.add, [flg], [pb1], s2=1e9, op1=ALU.mult)
            mset(VBg[:], -1e9, [VBg])
            mset(vm[:], 0.0, [vm])
            for lb in range(1, 8):
                mset(VBg[:, lb, 8:8 + lb], 0.0, [VBg])
                mset(vm[:, lb, 8:8 + lb], 1.0, [vm])
            cp(VBg[:, :, 0:8], pb1[:, 0:1].unsqueeze(2).broadcast_to([128, 8, 8]), [pb1], [VBg])
            cp(vm[:, :, 0:8], flg[:, 0:1].unsqueeze(2).broadcast_to([128, 8, 8]), [flg], [vm])

            step_sd(1)
            for ch in range(4):
                for slot in range(6):
                    headnorm_feat(T, Bm[ch], slot, 512, 8 + j, Bm[ch], slot)

            vall = kvall_d.ap()[256:512, :].rearrange("r (e c) -> (r e) c", c=256)
            vloc = kv_d.ap()[256:512, :].rearrange("r (e c) -> (r e) c", c=256)
            for kvh in range(4):
                dma('sp', KTaug[0:64, 0:2048], kvall_d.ap()[kvh * 64:(kvh + 1) * 64, :], R=[dkvall], W=[KTaug])
                dma('sp', KTaug[0:64, 2048:4096], kv_d.ap()[kvh * 64:(kvh + 1) * 64, :], R=[dkv], W=[KTaug])
                for (src, dsrc, t0) in ((vall, dkvall, 0), (vloc, dkv, 16)):
                    sv = src[:, kvh * 64:(kvh + 1) * 64].rearrange("(t p) c -> p t c", p=128)
                    dma('sp', VZ[:, t0:t0 + 16, 0:64], sv, R=[dsrc], W=[VZ])
                    dma('sp', VZ[:, t0:t0 + 16, 128:192], sv, R=[dsrc], W=[VZ])
                P.op('dve', lambda h: h.tensor_reduce(out=KMf[0:64, :], in_=KTaug[0:64, :].rearrange("p (b t) -> p b t", t=256), axis=AX.X, op=ALU.add), reads=[KTaug], writes=[KMf])
                cp(KM[0:64, :], KMf[0:64, :], [KMf], [KM])
                units = [(hh, ch) for hh in range(3) for ch in range(4)]

                def prep_gen(hh, ch, qa):
                    hd = 3 * kvh + hh
                    slot, par = hd // 2, hd % 2
                    if par == 0:
                        cp(qa[0:64, :], Bm[ch][0:64, slot, :], [Bm[ch]], [qa])
                    else:
                        pq = banks[3 + (c['po'] + 1) % 2]
                        mm(pq[0:64, :], shiftsel[:, :], Bm[ch][:, slot, :], True, True, [shiftsel, Bm[ch]], [pq])
                        act(qa[0:64, :], pq[0:64, :], AF.Copy, [pq], [qa])
                    pg = banks[5]
                    mbws = []
                    for tti in range(4):
                        lb = (ch * 4 + tti) // 2
                        mm(pg[:, tti * 16:(tti + 1) * 16], qa[0:64, tti * 128:(tti + 1) * 128], KM[0:64, :], True, True, [qa, KM], [pg])
                    for tti in range(4):
                        lb = (ch * 4 + tti) // 2
                        tt(g1[:, :], pg[:, tti * 16:(tti + 1) * 16], VBg[:, lb, :], ALU.add, [pg, VBg], [g1])
                        P.op('dve', lambda h: h.max(out=m8[:, :], in_=g1[:, :]), reads=[g1], writes=[m8])
                        ts(sel[:, :], g1[:, :], m8[:, 2:3], ALU.is_ge, [g1, m8], [sel])
                        tt(sel[:, :], sel[:, :], vm[:, lb, :], ALU.mult, [sel, vm], [sel])
                        mbw = MBw[c['mb'] % 4]
                        c['mb'] += 1
                        ts(mbw[:, 64:80], sel[:, :], -1.0, ALU.add, [sel], [mbw], s2=30000.0, op1=ALU.mult)
                        mset(mbw[:, 72 + lb:73 + lb], 0.0, [mbw], eng='dve')
                        mbws.append(mbw)
                    yield
                    for tti in range(4):
                        tr(pg[0:80, tti * 128:(tti + 1) * 128], mbws[tti][:, :], ident[:], [mbws[tti], ident], [pg])
                    cp(qa[64:80, :], pg[64:80, :], [pg], [qa])
                    yield

                def drain(g):
                    for _ in g:
                        pass

                qa_cur = qaug[c['qa'] % 2]
                c['qa'] += 1
                drain(prep_gen(units[0][0], units[0][1], qa_cur))
                for ui, (hh, ch) in enumerate(units):
                    hd = 3 * kvh + hh
                    slot, par = hd // 2, hd % 2
                    lo = par * 64
                    zoff = par * 64
                    drow = 64 if par == 0 else 0
                    qa = qa_cur
                    gnext = None
                    if ui + 1 < len(units):
                        qa_next = qaug[c['qa'] % 2]
                        c['qa'] += 1
                        gnext = prep_gen(units[ui + 1][0], units[ui + 1][1], qa_next)
                    b0 = 2 * ch
                    steps = [(kt, 0, 512, False) for kt in range(16 + 2 * b0)]
                    steps += [(16 + 2 * b0, 0, 512, True), (17 + 2 * b0, 128, 384, True), (18 + 2 * b0, 256, 256, True), (19 + 2 * b0, 384, 128, True)]
                    po = bank_po()
                    n = len(steps)
                    LA = 2
                    live = {}
                    for i in range(n + LA):
                        if i % 8 == 5:
                            step_sd(1)
                        if i == 4 and pend['e'] is not None:
                            pend['e']()
                            pend['e'] = None
                        if gnext is not None and (i == 6 or i == 14):
                            next(gnext)
                        if i < n:
                            kt, q0, nq, dotri = steps[i]
                            ps = bank_ps()
                            pt = pTs[c['pt'] % 4]
                            c['pt'] += 1
                            mm(ps[:, 0:nq], KTaug[0:80, kt * 128:(kt + 1) * 128], qa[0:80, q0:q0 + nq], True, True, [KTaug, qa], [ps])
                            act(pt[:, 0:nq], ps[:, 0:nq], AF.Exp, [ps], [pt])
                            if dotri:
                                tt(pt[:, 0:128], pt[:, 0:128], tri[:, :], ALU.mult, [pt, tri], [pt])
                            live[i] = pt
                        k = i - LA
                        if k >= 0:
                            kt, q0, nq, dotri = steps[k]
                            pt = live.pop(k)
                            mm(po[:, q0:q0 + nq], VZ[:, kt, zoff:zoff + 128], pt[:, 0:nq], k == 0, k == n - 1, [VZ, pt], [po])
                    def epi1(po=po, lo=lo, drow=drow):
                        act(osb[lo:lo + 64, :], po[lo:lo + 64, :], AF.Copy, [po], [osb])
                        act(rden[drow:drow + 1, :], po[drow:drow + 1, :], AF.Ln, [po], [rden])
                        act(rden[drow:drow + 1, :], rden[drow:drow + 1, :], AF.Exp, [rden], [rden], scale=-1.0)

                    def epi2(po=po, lo=lo, drow=drow, ch=ch, slot=slot):
                        pbc = banks[5]
                        mm(pbc[:, :], selrow[drow:drow + 1, :], rden[drow:drow + 1, :], True, True, [selrow, rden], [pbc])
                        tt(Bm[ch][lo:lo + 64, slot, :], osb[lo:lo + 64, :], pbc[lo:lo + 64, :], ALU.mult, [osb, pbc], [Bm[ch]])
                    epi1()
                    pend['e'] = epi2
                    step_sd(1)
                    if gnext is not None:
                        drain(gnext)
                        qa_cur = qa_next
                if pend['e'] is not None:
                    pend['e']()
                    pend['e'] = None
            if stage < 4:
                mset(Bm[4][:, 0:6, :], 0.0, [Bm[4]])
            step_sd(100000)
            P.phase_end()

        def sdec_gen(l):
            j = l - 2
            nm = lambda n: "sd%s_%d" % (n, l)
            Kpg = [P.sb(nm("Kpg%d" % i), [128, 2, 256], F32) for i in range(3)]
            Vpg = [P.sb(nm("Vpg%d" % i), [128, 2, 256], F32) for i in range(3)]
            KTp = [P.sb(nm("KTp%d" % i), [128, 2, 256], BF16) for i in range(2)]
            VA = [P.sb(nm("VA%d" % i), [128, 2, 4, 65], BF16) for i in range(3)]
            KMs2 = [P.sb(nm("KMs%d" % i), [128, 2, 8], F32) for i in range(2)]
            KMb = P.sb(nm("KMb"), [128, 2, 8], BF16)
            QS = P.sb(nm("QS"), [128, 12, NS], BF16)
            Em = P.sb(nm("Em"), [128, 4, 128], BF16)
            Ep1 = P.sb(nm("Ep1"), [128, 128], BF16)
            identb = P.sb(nm("identb"), [128, 128], BF16)
            ones6 = P.sb(nm("ones6"), [128, 65], BF16)
            KTn = P.sb(nm("KTn"), [128, 2, NS], BF16)
            VAn = P.sb(nm("VAn"), [128, 4, 65], BF16)
            vnf = P.sb(nm("vnf"), [128, 256], F32)
            PT = [P.sb(nm("PT%d" % i), [128, 24], BF16) for i in range(2)]
            PTn = P.sb(nm("PTn"), [128, 12], BF16)
            PARTS = P.sb(nm("PARTS"), [128, 9, 12], F32)
            g = P.sb(nm("g"), [128, 2, 8], F32)
            m8 = P.sb(nm("m8"), [128, 2, 8], F32)
            sel = P.sb(nm("sel"), [128, 2, 8], F32)
            selx = P.sb(nm("selx"), [128, 2, 8, 6], BF16)
            tmp = P.sb(nm("tmp"), [128, 8, 6], F32)
            acc = P.sb(nm("acc"), [128, 12], F32)
            rden = P.sb(nm("rden"), [128, 12], F32)
            Osb = P.sb(nm("Osb"), [128, 12], BF16)
            qsq = P.sb(nm("qsq"), [128, 512], BF16)
            qrs = P.sb(nm("qrs"), [128, 512], F32)
            T = SimpleNamespace(qsq=qsq, qrs=qrs)
            c = {'k': 0}
            ckv = cache_k.rearrange("n t c -> (n t) c")
            cvv = cache_v.rearrange("n t c -> (n t) c")

            cp(identb[:], ident[:], [ident], [identb], eng='pool')
            mset(Em[:], 0.0, [Em])
            cp(Em[0:64, 0, 0:64], ident[0:64, 0:64], [ident], [Em], eng='pool')
            cp(Em[64:128, 1, 64:128], ident[64:128, 64:128], [ident], [Em], eng='pool')
            cp(Em[0:64, 2, 64:128], ident[0:64, 0:64], [ident], [Em], eng='pool')
            cp(Em[64:128, 3, 0:64], ident[64:128, 64:128], [ident], [Em], eng='pool')
            mset(Ep1[:], 0.0, [Ep1])
            cp(Ep1[0:64, 64:128], ident[0:64, 0:64], [ident], [Ep1], eng='pool')
            mset(ones6[:], 1.0, [ones6])
            for i in range(3):
                mset(VA[i][:, :, :, 64:65], 1.0, [VA[i]])
            mset(VAn[:, :, 64:65], 1.0, [VAn])
            dma('sp', KTn[:, :, :], knT_d.ap(), R=[dknT], W=[KTn])
            dma('sp', vnf[0:NS, :], snew_d.ap()[:, 256:512], R=[dsnew], W=[vnf])
            cp(VAn[0:NS, :, 0:64], vnf[0:NS, :].rearrange("p (k d) -> p k d", k=4), [vnf], [VAn], eng='pool')

            for slot in range(6):
                headnorm_feat(T, Bm[4], slot, NS, 8 + j, Bm[4], slot)
            for hd in range(12):
                par, kp = hd % 2, (hd // 3) % 2
                vsel = {(0, 0): 0, (1, 1): 1, (0, 1): 2, (1, 0): 3}[(par, kp)]
                pq = bank()
                mm(pq[:, 0:NS], Em[:, vsel, :], Bm[4][:, hd // 2, 0:NS], True, True, [Em, Bm[4]], [pq])
                cp(QS[:, hd, :], pq[:, 0:NS], [pq], [QS])

            yield
            B6 = banks[6]

            def issue_gather(step):
                if step >= NS * 8:
                    return
                kpg, vpg = Kpg[step % 3], Vpg[step % 3]
                for pg in range(2):
                    col = step * 2 + pg
                    P.dma('pool', lambda h, kpg=kpg, pg=pg, col=col: h.indirect_dma_start(out=kpg[:, pg, :], out_offset=None, in_=ckv, in_offset=bass.IndirectOffsetOnAxis(ap=pidx[:, col:col + 1], axis=0)), reads=[pidx], writes=[kpg])
                    P.dma('pool', lambda h, vpg=vpg, pg=pg, col=col: h.indirect_dma_start(out=vpg[:, pg, :], out_offset=None, in_=cvv, in_offset=bass.IndirectOffsetOnAxis(ap=pidx[:, col:col + 1], axis=0)), reads=[pidx], writes=[vpg])
            def combine(s):
                pparts = banks[6]
                KMs = KMs2[s % 2]
                for kvh in range(4):
                    mm(B6[0:NS, 160 + 3 * kvh:160 + 3 * kvh + 3], KTn[:, kvh // 2, :], QS[:, 3 * kvh:3 * kvh + 3, s], True, True, [KTn, QS], [B6])
                act(PTn[0:NS, :], B6[0:NS, 160:172], AF.Exp, [B6], [PTn])
                ts(PTn[0:NS, :], PTn[0:NS, :], ident[0:NS, s:s + 1], ALU.mult, [PTn, ident], [PTn])
                for kvh in range(4):
                    mm(pparts[0:65, 96 + 3 * kvh:96 + 3 * kvh + 3], VAn[0:NS, kvh, :], PTn[0:NS, 3 * kvh:3 * kvh + 3], True, True, [VAn, PTn], [pparts])
                cp(PARTS[0:65, :, :], pparts[0:65, 0:108].rearrange("p (b h) -> p b h", h=12), [pparts], [PARTS])
                cp(KMb[:, :, :], KMs[:, :, :], [KMs], [KMb])
                for pr in range(2):
                    mm(B6[0:6, 176 + pr * 8:176 + pr * 8 + 8], QS[:, 6 * pr:6 * pr + 6, s], KMb[:, pr, :], True, True, [QS, KMb], [B6])
                cp(g[0:6, :, :], B6[0:6, 176:192].rearrange("p (a b) -> p a b", a=2), [B6], [g])
                for pr in range(2):
                    P.op('dve', lambda h, pr=pr: h.max(out=m8[0:6, pr, :], in_=g[0:6, pr, :]), reads=[g], writes=[m8])
                    ts(sel[0:6, pr, :], g[0:6, pr, :], m8[0:6, pr, 2:3], ALU.is_ge, [g, m8], [sel])
                    tt(selx[0:6, pr, :, :], sel[0:6, pr, :].unsqueeze(2).broadcast_to([6, 8, 6]), ident[0:6, 0:6].unsqueeze(1).broadcast_to([6, 8, 6]), ALU.mult, [sel, ident], [selx])
                for pr in range(2):
                    mm(B6[0:65, 256 + pr * 48:256 + pr * 48 + 48], ones6[0:6, :], selx[0:6, pr, :, :].rearrange("p b h -> p (b h)"), True, True, [ones6, selx], [B6])
                for pr in range(2):
                    tt(tmp[0:65, :, :], PARTS[0:65, 0:8, 6 * pr:6 * pr + 6], B6[0:65, 256 + pr * 48:256 + pr * 48 + 48].rearrange("p (b h) -> p b h", h=6), ALU.mult, [PARTS, B6], [tmp])
                    P.op('dve', lambda h, pr=pr: h.tensor_reduce(out=acc[0:65, 6 * pr:6 * pr + 6], in_=tmp[0:65, :, :].rearrange("p b h -> p h b"), axis=AX.X, op=ALU.add), reads=[tmp], writes=[acc])
                tt(acc[0:65, :], acc[0:65, :], PARTS[0:65, 8, :], ALU.add, [acc, PARTS], [acc])
                P.op('dve', lambda h: h.reciprocal(out=rden[64:65, :], in_=acc[64:65, :]), reads=[acc], writes=[rden])
                mm(B6[:, 384:396], selrow[64:65, :], rden[64:65, :], True, True, [selrow, rden], [B6])
                tt(Osb[0:64, :], acc[0:64, :], B6[0:64, 384:396], ALU.mult, [acc, B6], [Osb])
                mm(B6[:, 400:406], identb[0:64, :], Osb[0:64, 0:12:2], True, False, [identb, Osb], [B6])
                mm(B6[:, 400:406], Ep1[0:64, :], Osb[0:64, 1:12:2], False, True, [Ep1, Osb], [B6])
                cp(Bm[4][:, 0:6, s], B6[:, 400:406], [B6], [Bm[4]])

            NST = NS * 8
            pparts = banks[6]

            def stageT(step):
                s_, blk = step // 8, step % 8
                kpg, vpg, ktp, va = Kpg[step % 3], Vpg[step % 3], KTp[step % 2], VA[step % 3]
                KMs = KMs2[s_ % 2]
                pk = banks[7]
                for pg in range(2):
                    for pr in range(2):
                        tr(pk[:, (pr * 2 + pg) * 128:(pr * 2 + pg + 1) * 128], kpg[:, pg, pr * 128:(pr + 1) * 128], ident[:], [kpg, ident], [pk])
                ts(ktp[:, :, :], pk[:, :].rearrange("p (a t) -> p a t", a=2), 0.125, ALU.mult, [pk], [ktp])
                P.op('dve', lambda h, ktp=ktp, blk=blk, KMs=KMs: h.tensor_reduce(out=KMs[:, :, blk], in_=ktp[:, :, :], axis=AX.X, op=ALU.add), reads=[ktp], writes=[KMs])
                for pg in range(2):
                    cp(va[:, pg, :, 0:64], vpg[:, pg, :].rearrange("p (k d) -> p k d", k=4), [vpg], [va], eng='pool')

            def stageQ(step):
                s_ = step // 8
                ktp = KTp[step % 2]
                psS = B6[:, 128:152]
                for pg in range(2):
                    for kvh in range(4):
                        mm(psS[:, pg * 12 + 3 * kvh:pg * 12 + 3 * kvh + 3], ktp[:, kvh // 2, pg * 128:(pg + 1) * 128], QS[:, 3 * kvh:3 * kvh + 3, s_], True, True, [ktp, QS], [B6])
                pt = PT[step % 2]
                act(pt[:, 0:24], psS[:, 0:24], AF.Exp, [B6], [pt])

            def stageV(step):
                s_, blk = step // 8, step % 8
                va, pt = VA[step % 3], PT[step % 2]
                for kvh in range(4):
                    for pg in range(2):
                        mm(pparts[0:65, blk * 12 + 3 * kvh:blk * 12 + 3 * kvh + 3], va[:, pg, kvh, :], pt[:, pg * 12 + 3 * kvh:pg * 12 + 3 * kvh + 3], pg == 0, pg == 1, [va, pt], [pparts])

            for g_ in range(2):
                issue_gather(g_)
            for e in range(-2, NST):
                issue_gather(e + 4)
                if 0 <= e + 2 < NST:
                    stageT(e + 2)
                if 0 <= e + 1 < NST:
                    stageQ(e + 1)
                if 0 <= e < NST:
                    stageV(e)
                    if e % 8 == 7:
                        combine(e // 8)
                yield

        for l in range(DEPTH):
            is_s5 = l < 2
            if l == 0:
                P.phase_begin()
                T = common("i%d" % l)
                inproj(l, T)
                P.phase_end()

            if is_s5 and stage >= 2:
                s5_layer(l)
            elif (not is_s5) and stage >= 3:
                moba_layer(l)
            else:
                P.phase_begin()
                for ch in range(5):
                    mset(Bm[ch][:, 0:8, :], 0.0, [Bm[ch]])
                P.phase_end()

            P.phase_begin()
            T = common("o%d" % l)
            wb1 = P.sb("wb1_%d" % l, [128, 6144], BF16)
            wb2 = P.sb("wb2_%d" % l, [128, 2, 1024], BF16)
            SK = P.sb("SK_%d" % l, [128, 4, 2, 256], BF16)
            SV = P.sb("SV_%d" % l, [128, 4, 2, 256], BF16)
            Os = P.sb("Os_%d" % l, [128, 4, NS], BF16)
            identb3 = P.sb("identb3_%d" % l, [128, 128], BF16)
            Ep13 = P.sb("Ep13_%d" % l, [128, 128], BF16)
            cp(identb3[:], ident[:], [ident], [identb3], eng='pool')
            mset(Ep13[:], 0.0, [Ep13])
            cp(Ep13[0:64, 64:128], ident[0:64, 0:64], [ident], [Ep13], eng='pool')
            nmix = 8 if is_s5 else 6
            pmix = 96 if is_s5 else 128
            if is_s5 and stage >= 2:
                gv = load_sq_weight(w_glu[l], 96, 8, 768)
                for ch, (c0, w) in enumerate(CHUNKS):
                    for m in range(8):
                        pb = bank()
                        for k in range(8):
                            mm(pb[0:96, :w], gv[0:96, k, m * 96:(m + 1) * 96], Bm[ch][0:96, k, :w], k == 0, k == 7, [wb0, Bm[ch]], [pb])
                        act(T.osb[0:96, :w], pb[0:96, :w], AF.Sigmoid, [pb], [T.osb])
                        tt(T.aT[0:96, m, :w], T.osb[0:96, :w], Bm[ch][0:96, m, :w], ALU.mult, [T.osb, Bm[ch]], [T.aT])
                    for m in range(8):
                        cp(Bm[ch][0:96, m, :w], T.aT[0:96, m, :w], [T.aT], [Bm[ch]], eng='pool')
            def mem_part1(ch, hd):
                par, hp = hd % 2, hd // 2
                w = CHUNKS[ch][1]
                pts = []
                for kt in range(2):
                    ps = bank()
                    mm(ps[:, :w], MKT[par * 64:(par + 1) * 64, l, hp, kt * 128:(kt + 1) * 128], T.qn[par * 64:(par + 1) * 64, hp, :w], True, True, [MKT, T.qn], [ps])
                    pt = T.pT[st['pT'] % 4]
                    st['pT'] += 1
                    act(pt[:, :w], ps[:, :w], AF.Exp, [ps], [pt])
                    pts.append(pt)
                return pts

            def mem_part2(ch, hd, pts):
                par, hp = hd % 2, hd // 2
                w = CHUNKS[ch][1]
                po = bank()
                for kt in range(2):
                    mm(po[:, :w], MVA[:, l, kt, hd, :], pts[kt][:, :w], kt == 0, kt == 1, [MVA, pts[kt]], [po])
                lo = par * 64
                drow = 64 if par == 0 else 0
                act(T.osb[lo:lo + 64, :w], po[lo:lo + 64, :w], AF.Copy, [po], [T.osb])
                act(T.rden[drow:drow + 1, :w], po[drow:drow + 1, :w], AF.Ln, [po], [T.rden])
                act(T.rden[drow:drow + 1, :w], T.rden[drow:drow + 1, :w], AF.Exp, [T.rden], [T.rden], scale=-1.0)
                pbc = bank()
                mm(pbc[:, :w], selrow[drow:drow + 1, :], T.rden[drow:drow + 1, :w], True, True, [selrow, T.rden], [pbc])
                tt(Bm[ch][lo:lo + 64, 8 + hp, :w], T.osb[lo:lo + 64, :w], pbc[lo:lo + 64, :w], ALU.mult, [T.osb, pbc], [Bm[ch]])

            prev = None
            for ch in range(4):
                for hp in range(2):
                    headnorm_feat(T, Bm[ch], 8 + hp, 512, l, T.qn, hp)
                for hd in range(4):
                    pts = mem_part1(ch, hd)
                    if prev is not None:
                        mem_part2(*prev)
                    prev = (ch, hd, pts)
            mem_part2(*prev)
            for hp in range(2):
                headnorm_feat(T, Bm[4], 8 + hp, NS, l, T.qn, hp)
            for grp in range(4):
                for si in range(4):
                    sq_ = grp * 4 + si
                    for ti in range(2):
                        sbuf_, o0 = ((T.stg, 0) if (si * 2 + ti) % 2 == 0 else (T.kvtm, 0))
                        dma('sp', sbuf_[:, 0:256], cmk[l, sq_, ti * 128:(ti + 1) * 128, :], W=[sbuf_])
                        dma('sp', sbuf_[:, 256:512], cmv[l, sq_, ti * 128:(ti + 1) * 128, :], W=[sbuf_])
                        pb2 = bank()
                        for hp in range(2):
                            tr(pb2[:, hp * 128:(hp + 1) * 128], sbuf_[:, hp * 128:(hp + 1) * 128], ident[:], [sbuf_, ident], [pb2])
                        act(SK[:, si, :, ti * 128:(ti + 1) * 128], pb2[:, 0:256].rearrange("p (a t) -> p a t", a=2), AF.Copy, [pb2], [SK], scale=0.125)
                        cp(SV[:, si, ti, :], sbuf_[:, 256:512], [sbuf_], [SV], eng='pool')
                for hd in range(4):
                    par, hp = hd % 2, hd // 2
                    ps = bank()
                    for si in range(4):
                        sq_ = grp * 4 + si
                        for kt in range(2):
                            mm(ps[:, kt * 4 + si:kt * 4 + si + 1], SK[par * 64:(par + 1) * 64, si, hp, kt * 128:(kt + 1) * 128], T.qn[par * 64:(par + 1) * 64, hp, sq_:sq_ + 1], True, True, [SK, T.qn], [ps])
                    pt = T.pT[st['pT'] % 4]
                    st['pT'] += 1
                    act(pt[:, 0:8], ps[:, 0:8], AF.Exp, [ps], [pt])
                    po = bank()
                    for si in range(4):
                        for kt in range(2):
                            mm(po[0:64, si:si + 1], SV[:, si, kt, hd * 64:(hd + 1) * 64], pt[:, kt * 4 + si:kt * 4 + si + 1], kt == 0, kt == 1, [SV, pt], [po])
                    for kt in range(2):
                        mm(po[0:64, 8:12], onesb[:, 0:64], pt[:, kt * 4:(kt + 1) * 4], kt == 0, kt == 1, [onesb, pt], [po])
                    P.op('dve', lambda h, po=po: h.reciprocal(out=T.rden[0:64, 0:4], in_=po[0:64, 8:12]), reads=[po], writes=[T.rden])
                    act(T.osb[0:64, 0:4], po[0:64, 0:4], AF.Copy, [po], [T.osb])
                    tt(Os[0:64, hd, grp * 4:(grp + 1) * 4], T.osb[0:64, 0:4], T.rden[0:64, 0:4], ALU.mult, [T.osb, T.rden], [Os])
            for hp in range(2):
                po2 = bank()
                mm(po2[:, 0:NS], identb3[0:64, :], Os[0:64, 2 * hp, :], True, False, [identb3, Os], [po2])
                mm(po2[:, 0:NS], Ep13[0:64, :], Os[0:64, 2 * hp + 1, :], False, True, [Ep13, Os], [po2])
                cp(Bm[4][:, 8 + hp, 0:NS], po2[:, 0:NS], [po2], [Bm[4]])
            ov = load_sq_weight(w_out[l, 0:768, :], pmix, nmix, 1024)
            dma('pool', wb2[:, :, :], w_out[l, 768:1024, :].rearrange("(t p) n -> p t n", p=128), W=[wb2])
            for ch, (c0, w) in enumerate(CHUNKS):
                for m in range(8):
                    pb = bank()
                    for k in range(nmix):
                        mm(pb[:, :w], ov[0:pmix, k, m * 128:(m + 1) * 128], Bm[ch][0:pmix, k, :w], k == 0, False, [wb0, Bm[ch]], [pb])
                    for k in range(2):
                        mm(pb[:, :w], wb2[:, k, m * 128:(m + 1) * 128], Bm[ch][:, 8 + k, :w], False, k == 1, [wb2, Bm[ch]], [pb])
                    tt(hT[ch][:, m, :w], hT[ch][:, m, :w], pb[:, :w], ALU.add, [hT[ch], pb], [hT[ch]])
            for ch, (c0, w) in enumerate(CHUNKS):
                rmsnorm_chunk(T, hT[ch], w, 4 + l, Bm[ch])
            for j in range(DFF // 256):
                wbuf = wb0 if st['ffn'] % 2 == 0 else wb1
                st['ffn'] += 1
                gu = wbuf[:, 0:4096].rearrange("p (a k n) -> p a k n", a=2, k=8)
                dn = wbuf[:, 4096:6144].rearrange("p (t n) -> p t n", t=2)
                dma('pool', gu[:, 0, :, :], w_gu[l, :, 256 * j:256 * (j + 1)].rearrange("(t p) n -> p t n", p=128), W=[wbuf])
                dma('pool', gu[:, 1, :, :], w_gu[l, :, DFF + 256 * j:DFF + 256 * (j + 1)].rearrange("(t p) n -> p t n", p=128), W=[wbuf])
                dma('pool', dn[:, :, :], w_down[l, 256 * j:256 * (j + 1), :].rearrange("(t p) n -> p t n", p=128), W=[wbuf])
                for ch, (c0, w) in enumerate(CHUNKS):
                    hids = [T.pT[st['pT'] % 4], T.pT[(st['pT'] + 1) % 4]]
                    st['pT'] += 2
                    for t in range(2):
                        pg = bank()
                        pu = bank()
                        for k in range(8):
                            mm(pg[:, :w], gu[:, 0, k, t * 128:(t + 1) * 128], Bm[ch][:, k, :w], k == 0, k == 7, [wbuf, Bm[ch]], [pg])
                        for k in range(8):
                            mm(pu[:, :w], gu[:, 1, k, t * 128:(t + 1) * 128], Bm[ch][:, k, :w], k == 0, k == 7, [wbuf, Bm[ch]], [pu])
                        act(T.osb[:, :w], pg[:, :w], AF.Silu, [pg], [T.osb])
                        tt(hids[t][:, :w], T.osb[:, :w], pu[:, :w], ALU.mult, [T.osb, pu], [hids[t]])
                    for m in range(8):
                        pb = bank()
                        for t in range(2):
                            mm(pb[:, :w], dn[:, t, m * 128:(m + 1) * 128], hids[t][:, :w], t == 0, t == 1, [wbuf, hids[t]], [pb])
                        tt(hT[ch][:, m, :w], hT[ch][:, m, :w], pb[:, :w], ALU.add, [hT[ch], pb], [hT[ch]])
            if l == 1:
                kvw = load_sq_weight(w_kv, 128, 8, 512)
                for ch, (c0, w) in enumerate(CHUNKS):
                    rmsnorm_chunk(T, hT[ch], w, 12, T.aT)
                    for ti in range((w + 127) // 128):
                        nt = min(128, w - ti * 128)
                        pb = bank()
                        for k in range(8):
                            mm(pb[0:nt, :], T.aT[:, k, ti * 128:ti * 128 + nt], kvw[:, k, :], k == 0, k == 7, [T.aT, wb0], [pb])
                        act(T.kvtm[0:nt, :], pb[0:nt, :], AF.Copy, [pb], [T.kvtm])
                        headnorm_tokmajor(T, T.kvtm[0:nt, 0:256], [T.kvtm], 10, T.kn, nrows=nt)
                        if ch < 4:
                            r0 = c0 + ti * 128
                            dma('sp', o_pk[r0:r0 + 128, :], T.kn[:, :], R=[T.kn], is_out=True)
                            dma('sp', o_pv[r0:r0 + 128, :], T.kvtm[:, 256:512], R=[T.kvtm], is_out=True)
                            pb2 = bank()
                            for hp in range(2):
                                tr(pb2[:, hp * 128:(hp + 1) * 128], T.kn[:, hp * 128:(hp + 1) * 128], ident[:], [T.kn, ident], [pb2])
                            act(T.qn[:, :, 0:128], pb2[:, 0:256].rearrange("p (a t) -> p a t", a=2), AF.Copy, [pb2], [T.qn], scale=0.125)
                            dma('sp', kv_d.ap()[0:256, r0:r0 + 128].rearrange("(a p) t -> p a t", p=128), T.qn[:, :, 0:128], R=[T.qn], W=[dkv])
                            cp(T.qsq[:, 0:256], T.kvtm[:, 256:512], [T.kvtm], [T.qsq], eng='pool')
                            dma('sp', kv_d.ap()[256:512, :].rearrange("r (e c) -> (r e) c", c=256)[r0:r0 + 128, :], T.qsq[:, 0:256], R=[T.qsq], W=[dkv])
                        else:
                            dma('sp', o_sk[:, :], T.kn[0:NS, :], R=[T.kn], is_out=True)
                            dma('sp', o_sv[:, :], T.kvtm[0:NS, 256:512], R=[T.kvtm], is_out=True)
                            dma('sp', snew_d.ap()[:, 0:256], T.kn[0:NS, :], R=[T.kn], W=[dsnew])
                            dma('sp', snew_d.ap()[:, 256:512], T.kvtm[0:NS, 256:512], R=[T.kvtm], W=[dsnew])
                            pb2 = bank()
                            for hp in range(2):
                                tr(pb2[:, hp * NS:(hp + 1) * NS], T.kn[0:NS, hp * 128:(hp + 1) * 128], ident[0:NS, 0:NS], [T.kn, ident], [pb2])
                            act(T.qn[:, :, 0:NS], pb2[:, 0:2 * NS].rearrange("p (a t) -> p a t", a=2), AF.Copy, [pb2], [T.qn], scale=0.125)
                            dma('sp', knT_d.ap(), T.qn[:, :, 0:NS], R=[T.qn], W=[dknT])
                P.dma('pool', lambda h: h.collective_compute("AllGather", ALU.bypass, replica_groups=[[0, 1], [2, 3], [4, 5], [6, 7]], ins=[kv_d.ap().opt()], outs=[kvall_d.ap().opt()]), reads=[dkv], writes=[dkvall], inc=1)
            if l + 1 < DEPTH:
                inproj(l + 1, T)
            P.phase_end()

        P.phase_begin()
        T = common("Z")
        for ti in range(NPT // 128 + 1):
            ntok = 128 if ti < NPT // 128 else NS
            ch = min(ti // 4, 4)
            c0 = (ti % 4) * 128 if ch < 4 else 0
            s = T.stg
            for half in range(2):
                pb = bank()
                for kk in range(4):
                    k = half * 4 + kk
                    tr(pb[0:ntok, kk * 128:(kk + 1) * 128], hT[ch][:, k, c0:c0 + ntok], ident[:], [hT[ch], ident], [pb])
                if half == 0:
                    act(s[0:ntok, 0:512], pb[0:ntok, :], AF.Copy, [pb], [s])
                else:
                    cp(s[0:ntok, 512:1024], pb[0:ntok, :], [pb], [s])
            dst = y_p[ti * 128:(ti + 1) * 128, :] if ti < NPT // 128 else y_s
            dma('sp', dst, s[0:ntok, :], R=[s], is_out=True)
        P.finish()
        P.phase_end()
    return nc


_NC_CACHE = {}


def _prep_inputs(inp):
    f = np.float32
    g = lambda k: np.asarray(inp[k])
    gcols = np.zeros((128, 13, 8), f)
    vecs = [g('g_mix')[l] for l in range(4)] + [g('g_ffn')[l] for l in range(4)] + [g('g_mem')[l] for l in range(4)] + [g('g_kv')]
    for i, v in enumerate(vecs):
        gcols[:, i, :] = v.reshape(8, 128).T
    hv = [g('g_mq')[l] for l in range(4)] + [g('g_mk')[l] for l in range(4)] + [g('g_q')[j] for j in range(2)] + [g('g_k')] + [np.ones(64, f)]
    ghead = np.zeros((128, 12, 64), f)
    gheadp = np.zeros((128, 12), f)
    for i, v in enumerate(hv):
        ghead[:, i, :] = v[None, :]
        gheadp[:, i] = np.concatenate([v, v])

    def pair_layout(a):
        sh = a.shape
        a = a.reshape((24, 2, 64) + sh[2:])
        a = np.moveaxis(a, 0, 2)
        return np.ascontiguousarray(a.reshape((128, 24) + sh[2:]))
    s5a = np.zeros((2, 128, 3, 24), f)
    s5b = np.zeros((2, 2, 128, 24, 16), f)
    s5c = np.zeros((2, 2, 128, 24, 16), f)
    s5d = np.zeros((2, 96, 8), f)
    for l in range(2):
        s5a[l, :, 0, :] = pair_layout(g('ssm_a_re')[l])
        s5a[l, :, 1, :] = pair_layout(g('ssm_a_im')[l])
        s5a[l, :, 2, :] = pair_layout(np.repeat(g('ssm_log_dt')[l][:, None], 64, axis=1))
        s5b[l, 0] = pair_layout(g('ssm_b_re')[l])
        s5b[l, 1] = pair_layout(g('ssm_b_im')[l])
        s5c[l, 0] = pair_layout(np.swapaxes(g('ssm_c_re')[l], 1, 2))
        s5c[l, 1] = pair_layout(np.swapaxes(g('ssm_c_im')[l], 1, 2))
        s5d[l] = g('ssm_d')[l].reshape(8, 96).T
    shared = dict(w_in=g('w_in'), w_out=g('w_out'), w_gu=g('w_gu'), w_down=g('w_down'), w_mem_kv=g('w_mem_kv'),
                  w_kv=g('w_kv'), w_glu=g('w_glu'), gcols=gcols, ghead=ghead, gheadp=gheadp, s5a=s5a, s5b=s5b, s5c=s5c, s5d=s5d)
    maps = []
    ck = np.ascontiguousarray(g('cache_k').reshape(2560, 128, 256))
    cv = np.ascontiguousarray(g('cache_v').reshape(2560, 128, 256))
    for c in range(8):
        b, half = c // 2, c % 2
        m = dict(shared)
        m['xp'] = np.ascontiguousarray(g('x_prompt')[b, half * NPT:(half + 1) * NPT, :])
        m['xs'] = np.ascontiguousarray(g('x_sample')[c * NS:(c + 1) * NS, 0, :])
        m['mem'] = np.ascontiguousarray(g('mem_prompt')[b])
        m['cmk'] = np.ascontiguousarray(g('cache_mem_k')[:, c * NS:(c + 1) * NS].reshape(4, NS, 256, 256))
        m['cmv'] = np.ascontiguousarray(g('cache_mem_v')[:, c * NS:(c + 1) * NS].reshape(4, NS, 256, 256))
        m['st0'] = np.ascontiguousarray(np.stack([g('state_ssm_re')[:, c * NS:(c + 1) * NS].reshape(2, NS, 3072),
                                                  g('state_ssm_im')[:, c * NS:(c + 1) * NS].reshape(2, NS, 3072)], axis=1))
        m['flagb'] = np.full((128, 1), float(half), f)
        m['cache_k'] = ck
        m['cache_v'] = cv
        m['ptab'] = np.ascontiguousarray(g('page_table')[c * NS:(c + 1) * NS].reshape(1, NS * 16).astype(np.int32))
        maps.append(m)
    return maps


def kernel(**inp):
    if 'nc' not in _NC_CACHE:
        _NC_CACHE['nc'] = build_program()
    nc = _NC_CACHE['nc']
    maps = _prep_inputs(inp)
    res = run_bass_kernel_spmd(nc, maps, core_ids=list(range(8))).results
    f = np.float32
    y_prompt = np.zeros((4, 4096, D), f)
    y_sample = np.zeros((128, 1, D), f)
    p_k = np.zeros((4, 4096, 4, 64), f)
    p_v = np.zeros((4, 4096, 4, 64), f)
    p_mem_k = np.zeros((4, 4, 256, 4, 64), f)
    p_mem_v = np.zeros((4, 4, 256, 4, 64), f)
    s_k = np.zeros((128, 1, 4, 64), f)
    s_v = np.zeros((128, 1, 4, 64), f)
    p_re = np.zeros((2, 4, 48, 64), f)
    p_im = np.zeros((2, 4, 48, 64), f)
    s_re = np.zeros((2, 128, 48, 64), f)
    s_im = np.zeros((2, 128, 48, 64), f)
    for c in range(8):
        b, half = c // 2, c % 2
        r = res[c]
        y_prompt[b, half * NPT:(half + 1) * NPT] = r['y_p']
        y_sample[c * NS:(c + 1) * NS, 0] = r['y_s']
        p_k[b, half * NPT:(half + 1) * NPT] = r['o_pk'].reshape(NPT, 4, 64)
        p_v[b, half * NPT:(half + 1) * NPT] = r['o_pv'].reshape(NPT, 4, 64)
        s_k[c * NS:(c + 1) * NS, 0] = r['o_sk'].reshape(NS, 4, 64)
        s_v[c * NS:(c + 1) * NS, 0] = r['o_sv'].reshape(NS, 4, 64)
        s_re[:, c * NS:(c + 1) * NS] = r['o_sssm'][:, 0].reshape(2, NS, 48, 64)
        s_im[:, c * NS:(c + 1) * NS] = r['o_sssm'][:, 1].reshape(2, NS, 48, 64)
        if half == 0:
            p_mem_k[:, b] = r['o_pmk'].reshape(4, 256, 4, 64)
            p_mem_v[:, b] = r['o_pmv'].reshape(4, 256, 4, 64)
        else:
            p_re[:, b] = r['o_pssm'][:, 0].reshape(2, 48, 64)
            p_im[:, b] = r['o_pssm'][:, 1].reshape(2, 48, 64)
    return (y_prompt, y_sample, p_re, p_im, p_k, p_v, p_mem_k, p_mem_v, s_re, s_im, s_k, s_v)
```

```python
import numpy as np
import concourse.bass as bass
import concourse.mybir as mybir
from concourse.bass_utils import run_bass_kernel_spmd
from contextlib import ExitStack

F32 = mybir.dt.float32
BF16 = mybir.dt.bfloat16
I32 = mybir.dt.int32
U32 = mybir.dt.uint32
AF = mybir.ActivationFunctionType
ALU = mybir.AluOpType
AX = mybir.AxisListType


class Buf:
    __slots__ = ('t', 'w', 'r', 'dsem', 'dcnt', 'name', 'es', 'uid')
    _next = [0]

    def __init__(self, t, name=None):
        self.t = t
        self.w = None
        self.r = {}
        self.dsem = None
        self.dcnt = 0
        self.name = name
        self.es = None
        Buf._next[0] += 1
        self.uid = 'b%d' % Buf._next[0]

    def __getitem__(self, k):
        return self.t[k]


class Prog:
    ENG = ('pe', 'act', 'dve', 'pool', 'sp')

    def __init__(self, nc, es):
        self.nc = nc
        self.es = es
        self.q = {e: [] for e in self.ENG}
        self.cnt = {e: 0 for e in self.ENG}
        self.sem = {e: es.enter_context(nc.semaphore('s_' + e)) for e in self.ENG}
        self.seen = {e: {} for e in self.ENG}
        self.nsem = 0
        self.out_events = []
        self.all_dma = []
        self.sem_es = es

    def sb(self, name, shape, dt):
        b = Buf(self.es.enter_context(self.nc.sbuf_tensor(name, list(shape), dt)), name)
        b.es = self.es
        return b

    def ps(self, name, shape, dt=F32):
        return Buf(self.es.enter_context(self.nc.psum_tensor(name, list(shape), dt)), name)

    def view(self, t, name=None):
        return Buf(t, name)

    def _waits(self, eng, reads, writes):
        need = {}

        def add(ev):
            if ev is None:
                return
            k, v = ev
            if k == 'pe' and eng == 'pe':
                return
            if need.get(k, 0) < v:
                need[k] = v
        for b in reads:
            add(b.w)
        for b in writes:
            add(b.w)
            for k, v in b.r.items():
                add((k, v))
        out = []
        for k, v in need.items():
            if self.seen[eng].get(k, 0) >= v:
                continue
            self.seen[eng][k] = v
            out.append((k, v))
        return out

    def op(self, eng, fn, reads=(), writes=()):
        waits = self._waits(eng, reads, writes)
        self.cnt[eng] += 1
        t = self.cnt[eng]
        for b in reads:
            if b.r.get(eng, 0) < t:
                b.r[eng] = t
        for b in writes:
            b.w = (eng, t)
            b.r = {}
        sem = self.sem

        def run(h):
            for k, v in waits:
                h.wait_ge(sem[k], v)
            fn(h).then_inc(sem[eng], 1)
        self.q[eng].append(run)

    def _dsem(self, owner):
        if owner.dsem is None:
            self.nsem += 1
            owner.dsem = (owner.es or self.sem_es).enter_context(self.nc.semaphore('d%d' % self.nsem))
            self.sem[owner.uid] = owner.dsem
        return owner.dsem

    def dma(self, q, fn, reads=(), writes=(), owner=None, is_out=False, inc=16):
        waits = self._waits(q, reads, writes)
        if owner is None:
            owner = writes[0] if writes else reads[0]
        ds = self._dsem(owner)
        owner.dcnt += inc
        ev = (owner.uid, owner.dcnt)
        for b in reads:
            b.r[ev[0]] = ev[1]
        for b in writes:
            b.w = ev
            b.r = {}
        if is_out:
            self.out_events.append(ev)
        self.all_dma.append(ev)
        sem = self.sem

        def run(h):
            for k, v in waits:
                h.wait_ge(sem[k], v)
            fn(h).then_inc(ds, inc)
        self.q[q].append(run)

    def phase_begin(self):
        self._saved_es = self.es
        self.es = ExitStack()
        self.es.__enter__()
        self._sems_es = self._saved_es

    def phase_end(self):
        last = {}
        for k, v in self.all_dma:
            last[k] = max(last.get(k, 0), v)
        self.all_dma = []
        self.out_events = []
        waits = list(last.items())
        sem = self.sem

        def run(h):
            for k, v in waits:
                h.wait_ge(sem[k], v)
        self.q['sp'].append(run)
        self.build()
        self.q = {e: [] for e in self.ENG}
        self.es.__exit__(None, None, None)
        self.es = self._saved_es

    def finish(self):
        last = {}
        for k, v in self.out_events:
            last[k] = max(last.get(k, 0), v)
        waits = list(last.items())
        sem = self.sem

        def run(h):
            for k, v in waits:
                h.wait_ge(sem[k], v)
        self.q['sp'].append(run)

    def build(self):
        q = self.q
        with self.nc.Block() as blk:
            @blk.tensor
            def _(h):
                for f in q['pe']:
                    f(h)

            @blk.scalar
            def _(h):
                for f in q['act']:
                    f(h)

            @blk.vector
            def _(h):
                for f in q['dve']:
                    f(h)

            @blk.gpsimd
            def _(h):
                for f in q['pool']:
                    f(h)

            @blk.sync
            def _(h):
                for f in q['sp']:
                    f(h)


import math
from types import SimpleNamespace

D = 1024
NPT = 2048
NS = 16
NT = NPT + NS
DFF = 2816
DEPTH = 4
CHUNKS = [(0, 512), (512, 512), (1024, 512), (1536, 512), (2048, 16)]
EPS = 1e-6
PI = math.pi
STAGE = 9


def build_program(stage=STAGE):
    nc = bass.Bass("TRN2", target_bir_lowering=False)

    def din(name, shape, dt=F32):
        return nc.dram_tensor(name, list(shape), dt, kind="ExternalInput").ap()

    def dout(name, shape, dt=F32):
        return nc.dram_tensor(name, list(shape), dt, kind="ExternalOutput").ap()

    xp = din("xp", [NPT, D])
    xs = din("xs", [NS, D])
    mem = din("mem", [256, D])
    w_in = din("w_in", [DEPTH, D, D])
    w_out = din("w_out", [DEPTH, D, D])
    w_gu = din("w_gu", [DEPTH, D, 2 * DFF])
    w_down = din("w_down", [DEPTH, DFF, D])
    w_mem_kv = din("w_mem_kv", [DEPTH, D, 512])
    w_kv = din("w_kv", [D, 512])
    w_glu = din("w_glu", [2, 768, 768])
    gcols = din("gcols", [128, 13, 8])
    ghead = din("ghead", [128, 12, 64])
    gheadp = din("gheadp", [128, 12])
    cmk = din("cmk", [DEPTH, NS, 256, 256])
    cmv = din("cmv", [DEPTH, NS, 256, 256])
    s5a = din("s5a", [2, 128, 3, 24])
    s5b = din("s5b", [2, 2, 128, 24, 16])
    s5c = din("s5c", [2, 2, 128, 24, 16])
    s5d = din("s5d", [2, 96, 8])
    st0 = din("st0", [2, 2, NS, 3072])
    flagb = din("flagb", [128, 1])
    cache_k = din("cache_k", [2560, 128, 256])
    cache_v = din("cache_v", [2560, 128, 256])
    ptab = din("ptab", [1, NS * 16], I32)

    y_p = dout("y_p", [NPT, D])
    y_s = dout("y_s", [NS, D])
    o_pmk = dout("o_pmk", [DEPTH, 256, 256])
    o_pmv = dout("o_pmv", [DEPTH, 256, 256])
    o_pk = dout("o_pk", [NPT, 256])
    o_pv = dout("o_pv", [NPT, 256])
    o_sk = dout("o_sk", [NS, 256])
    o_sv = dout("o_sv", [NS, 256])
    o_pssm = dout("o_pssm", [2, 2, 24, 128])
    o_sssm = dout("o_sssm", [2, 2, NS, 3072])

    xin_d = nc.dram_tensor("xin_d", [128, 48], F32)
    xout_d = nc.dram_tensor("xout_d", [256, 48], F32)
    kv_d = nc.dram_tensor("kv_d", [512, 2048], BF16)
    kvall_d = nc.dram_tensor("kvall_d", [1024, 2048], BF16)
    snew_d = nc.dram_tensor("snew_d", [NS, 512], F32)
    knT_d = nc.dram_tensor("knT_d", [128, 2, NS], BF16)

    with ExitStack() as es:
        P = Prog(nc, es)
        dxin = P.view(xin_d, "xin_d")
        dxout = P.view(xout_d, "xout_d")
        dkv = P.view(kv_d, "kv_d")
        dkvall = P.view(kvall_d, "kvall_d")
        dsnew = P.view(snew_d, "snew_d")
        dknT = P.view(knT_d, "knT_d")

        def tt(out, a, b, op, R, W, eng='dve'):
            P.op(eng, lambda h: h.tensor_tensor(out=out, in0=a, in1=b, op=op), reads=R, writes=W)

        def ts(out, a, s1, op0, R, W, s2=None, op1=None, eng='dve'):
            if op1 is None:
                P.op(eng, lambda h: h.tensor_scalar(out=out, in0=a, scalar1=s1, scalar2=None, op0=op0), reads=R, writes=W)
            else:
                P.op(eng, lambda h: h.tensor_scalar(out=out, in0=a, scalar1=s1, scalar2=s2, op0=op0, op1=op1), reads=R, writes=W)

        def stt(out, a, sc, b, op0, op1, R, W):
            P.op('dve', lambda h: h.scalar_tensor_tensor(out=out, in0=a, scalar=sc, in1=b, op0=op0, op1=op1), reads=R, writes=W)

        def act(out, a, func, R, W, scale=1.0, bias=None):
            if bias is None:
                P.op('act', lambda h: h.activation(out=out, in_=a, func=func, scale=scale), reads=R, writes=W)
            else:
                P.op('act', lambda h: h.activation(out=out, in_=a, func=func, scale=scale, bias=bias), reads=R, writes=W)

        def cp(out, a, R, W, eng='dve'):
            P.op(eng, lambda h: h.tensor_copy(out=out, in_=a), reads=R, writes=W)

        def mset(ap, val, W, eng='pool'):
            P.op(eng, lambda h: h.memset(ap, val), writes=W)

        def mm(out, lhsT, rhs, start, stop, R, W):
            P.op('pe', lambda h: h.matmul(out, lhsT=lhsT, rhs=rhs, start=start, stop=stop), reads=R, writes=W)

        def tr(out, in_, idn, R, W):
            P.op('pe', lambda h: h.transpose(out, in_, idn), reads=R, writes=W)

        def dma(q, out, in_, R=(), W=(), is_out=False):
            P.dma(q, lambda h: h.dma_start(out=out, in_=in_), reads=list(R), writes=list(W), is_out=is_out)

        ident = P.sb("ident", [128, 128], F32)
        onesb = P.sb("onesb", [128, 128], BF16)
        blk64 = P.sb("blk64", [128, 128], BF16)
        epsc = P.sb("epsc", [128, 1], F32)
        hpic = P.sb("hpic", [128, 1], F32)
        gc = P.sb("gc", [128, 13, 8], F32)
        gh = P.sb("gh", [128, 12, 64], F32)
        ghp = P.sb("ghp", [128, 12], F32)
        selrow = P.sb("selrow", [128, 128], F32)
        flg = P.sb("flg", [128, 1], F32)
        pidx = P.sb("pidx", [128, NS * 16], I32)
        hT = [P.sb("hT%d" % i, [128, 8, w], F32) for i, (_, w) in enumerate(CHUNKS)]
        Bm_all = es.enter_context(nc.sbuf_tensor("Bm_all", [128, 10, NT], BF16))
        Bm = [P.view(Bm_all[:, :, c0_:c0_ + w_], "Bm%d" % i_) for i_, (c0_, w_) in enumerate(CHUNKS)]
        wb0 = P.sb("wb0", [128, 8192], BF16)
        MKT = P.sb("MKT", [128, DEPTH, 2, 256], BF16)
        MVA = P.sb("MVA", [128, DEPTH, 2, 4, 128], BF16)
        banks = [P.ps("bank%d" % i, [128, 512], F32) for i in range(8)]
        st = {'bank': 0, 'pT': 0, 'ffn': 0}

        def bank():
            b = banks[st['bank'] % 8]
            st['bank'] += 1
            return b

        def wb0v(k):
            return wb0[:, :].rearrange("p (k n) -> p k n", k=k)

        def common(pfx):
            T = SimpleNamespace()
            T.aT = P.sb(pfx + "aT", [128, 8, 512], BF16)
            T.sqs = [P.sb(pfx + "sq%d" % i, [128, 512], BF16) for i in range(2)]
            T.rs = P.sb(pfx + "rs", [128, 512], F32)
            T.stg = P.sb(pfx + "stg", [128, 1024], F32)
            T.kvtm = P.sb(pfx + "kvtm", [128, 512], F32)
            T.ssq = P.sb(pfx + "ssq", [128, 256], F32)
            T.hs4 = P.sb(pfx + "hs4", [128, 4], F32)
            T.kn = P.sb(pfx + "kn", [128, 256], F32)
            T.qsq = P.sb(pfx + "qsq", [128, 512], BF16)
            T.qrs = P.sb(pfx + "qrs", [128, 512], F32)
            T.pT = [P.sb(pfx + "pT%d" % i, [128, 512], BF16) for i in range(4)]
            T.osb = P.sb(pfx + "osb", [128, 512], F32)
            T.rden = P.sb(pfx + "rden", [128, 512], F32)
            T.qn = P.sb(pfx + "qn", [128, 2, 512], BF16)
            return T

        def rmsnorm_chunk(T, srcbuf, w, gidx, dstbuf, dst_tile0=0):
            pb = bank()
            for k in range(8):
                sq = T.sqs[k % 2]
                act(sq[:, :w], srcbuf[:, k, :w], AF.Square, [srcbuf], [sq])
                mm(pb[:, :w], onesb[:], sq[:, :w], k == 0, k == 7, [onesb, sq], [pb])
            act(T.rs[:, :w], pb[:, :w], AF.Ln, [pb, epsc], [T.rs], scale=1.0 / D, bias=epsc[:, 0:1])
            act(T.rs[:, :w], T.rs[:, :w], AF.Exp, [T.rs], [T.rs], scale=-0.5)
            for k in range(8):
                stt(dstbuf[:, dst_tile0 + k, :w], srcbuf[:, k, :w], gc[:, gidx, k:k + 1], T.rs[:, :w], ALU.mult, ALU.mult, [srcbuf, gc, T.rs], [dstbuf])

        def load_sq_weight(src, p, nt, ncols):
            v = wb0[:, 0:nt * ncols].rearrange("p (t n) -> p t n", t=nt)
            dma('pool', v[0:p, :, :], src.rearrange("(t p) n -> p t n", p=p), W=[wb0])
            return v

        def headnorm_tokmajor(T, src, srcbufs, gidx, dst, nrows=128):
            tt(T.ssq[0:nrows, :], src, src, ALU.mult, srcbufs, [T.ssq])
            P.op('dve', lambda h: h.tensor_reduce(out=T.hs4[0:nrows, :], in_=T.ssq[0:nrows, :].rearrange("p (h d) -> p h d", h=4), axis=AX.X, op=ALU.add), reads=[T.ssq], writes=[T.hs4])
            act(T.hs4[0:nrows, :], T.hs4[0:nrows, :], AF.Sqrt, [T.hs4, epsc], [T.hs4], scale=1.0 / 64, bias=epsc[0:nrows, 0:1])
            P.op('dve', lambda h: h.reciprocal(out=T.hs4[0:nrows, :], in_=T.hs4[0:nrows, :]), reads=[T.hs4], writes=[T.hs4])
            for hd in range(4):
                stt(dst[0:nrows, hd * 64:(hd + 1) * 64], src[:, hd * 64:(hd + 1) * 64], T.hs4[0:nrows, hd:hd + 1], gh[0:nrows, gidx, :], ALU.mult, ALU.mult, srcbufs + [T.hs4, gh], [dst])

        def headnorm_feat(T, buf, tile, w, gidx, dst, dtile):
            act(T.qsq[:, :w], buf[:, tile, :w], AF.Square, [buf], [T.qsq])
            pb = bank()
            mm(pb[:, :w], blk64[:], T.qsq[:, :w], True, True, [blk64, T.qsq], [pb])
            act(T.qrs[:, :w], pb[:, :w], AF.Ln, [pb, epsc], [T.qrs], scale=1.0 / 64, bias=epsc[:, 0:1])
            act(T.qrs[:, :w], T.qrs[:, :w], AF.Exp, [T.qrs], [T.qrs], scale=-0.5)
            stt(dst[:, dtile, :w], buf[:, tile, :w], ghp[:, gidx:gidx + 1], T.qrs[:, :w], ALU.mult, ALU.mult, [buf, ghp, T.qrs], [dst])

        def attend(T, keys, q_ap, qbufs, w, par, dst, dtile, kbufs, dc0=0):
            po = bank()
            n = len(keys)
            for i, (kap, vap) in enumerate(keys):
                ps = bank()
                mm(ps[:, :w], kap, q_ap, True, True, kbufs + qbufs, [ps])
                pt = T.pT[st['pT'] % 4]
                st['pT'] += 1
                act(pt[:, :w], ps[:, :w], AF.Exp, [ps], [pt])
                mm(po[:, :w], vap, pt[:, :w], i == 0, i == n - 1, kbufs + [pt], [po])
            lo = par * 64
            drow = 64 if par == 0 else 0
            act(T.osb[lo:lo + 64, :w], po[lo:lo + 64, :w], AF.Copy, [po], [T.osb])
            P.op('dve', lambda h: h.reciprocal(out=T.rden[drow:drow + 1, :w], in_=po[drow:drow + 1, :w]), reads=[po], writes=[T.rden])
            pbc = bank()
            mm(pbc[:, :w], selrow[drow:drow + 1, :], T.rden[drow:drow + 1, :w], True, True, [selrow, T.rden], [pbc])
            tt(dst[lo:lo + 64, dtile, dc0:dc0 + w], T.osb[lo:lo + 64, :w], pbc[lo:lo + 64, :w], ALU.mult, [T.osb, pbc], [dst])

        P.phase_begin()
        T = common("A")
        stg2 = P.sb("Astg2", [128, 1024], F32)
        memT = P.sb("memT", [128, 8, 128], F32)
        mnT = P.sb("mnT", [128, 8, 128], BF16)
        mset(ident[:], 0.0, [ident])
        P.op('pool', lambda h: h.affine_select(out=ident[:], in_=ident[:], pattern=[[-1, 128]], compare_op=ALU.not_equal, fill=1.0, base=0, channel_multiplier=1), reads=[ident], writes=[ident])
        mset(onesb[:], 1.0, [onesb])
        mset(blk64[:], 0.0, [blk64])
        mset(blk64[0:64, 0:64], 1.0, [blk64])
        mset(blk64[64:128, 64:128], 1.0, [blk64])
        mset(epsc[:], EPS, [epsc])
        mset(hpic[:], PI / 2, [hpic])
        mset(selrow[:], 0.0, [selrow])
        mset(selrow[64:65, 0:64], 1.0, [selrow])
        mset(selrow[0:1, 64:128], 1.0, [selrow])
        dma('sp', gc[:], gcols, W=[gc])
        dma('sp', gh[:], ghead, W=[gh])
        dma('sp', ghp[:], gheadp, W=[ghp])
        dma('sp', flg[:], flagb, W=[flg])
        pti = P.sb("pti", [128, NS * 16], I32)
        ptf = P.sb("ptf", [128, NS * 16], F32)
        iot = P.sb("iot", [128, 1], I32)
        iof = P.sb("iof", [128, 1], F32)
        dma('sp', pti[:], ptab.partition_broadcast(128), W=[pti])
        P.op('pool', lambda h: h.iota(iot[:], pattern=[[0, 1]], base=0, channel_multiplier=1), writes=[iot])
        cp(iof[:], iot[:], [iot], [iof])
        cp(ptf[:], pti[:], [pti], [ptf])
        ts(ptf[:], ptf[:], 128.0, ALU.mult, [ptf, iof], [ptf], s2=iof[:, 0:1], op1=ALU.add)
        cp(pidx[:], ptf[:], [ptf], [pidx])
        mset(MVA[:], 0.0, [MVA])
        for hd in range(4):
            dcol = 64 if hd % 2 == 0 else 0
            mset(MVA[:, :, :, hd, dcol:dcol + 1], 1.0, [MVA])

        for ti in range(NPT // 128 + 1):
            ntok = 128 if ti < NPT // 128 else NS
            s = T.stg if ti % 2 == 0 else stg2
            src = xp[ti * 128:(ti + 1) * 128, :] if ti < NPT // 128 else xs
            dma('sp', s[0:ntok, :], src, W=[s])
            ch = min(ti // 4, 4)
            c0 = (ti % 4) * 128 if ch < 4 else 0
            for half in range(2):
                pb = bank()
                for kk in range(4):
                    k = half * 4 + kk
                    tr(pb[:, kk * 128:kk * 128 + ntok], s[0:ntok, k * 128:(k + 1) * 128], ident[0:ntok, 0:ntok], [s, ident], [pb])
                src_ap = pb[:, :].rearrange("p (k t) -> p k t", k=4)[:, :, 0:ntok]
                dst_ap = hT[ch][:, half * 4:half * 4 + 4, c0:c0 + ntok]
                if half == 0:
                    act(dst_ap, src_ap, AF.Copy, [pb], [hT[ch]])
                else:
                    cp(dst_ap, src_ap, [pb], [hT[ch]])

        for ti in range(2):
            s = T.stg
            dma('sp', s[:, :], mem[ti * 128:(ti + 1) * 128, :], W=[s])
            for half in range(2):
                pb = bank()
                for kk in range(4):
                    k = half * 4 + kk
                    tr(pb[:, kk * 128:(kk + 1) * 128], s[:, k * 128:(k + 1) * 128], ident[:], [s, ident], [pb])
                act(memT[:, half * 4:half * 4 + 4, :], pb[:, :].rearrange("p (k t) -> p k t", k=4), AF.Copy, [pb], [memT])
            for l in range(DEPTH):
                rmsnorm_chunk(T, memT, 128, 8 + l, mnT)
                wv = load_sq_weight(w_mem_kv[l], 128, 8, 512)
                pb = bank()
                for k in range(8):
                    mm(pb[:, :], mnT[:, k, :], wv[:, k, :], k == 0, k == 7, [mnT, wb0], [pb])
                act(T.kvtm[:], pb[:], AF.Copy, [pb], [T.kvtm])
                headnorm_tokmajor(T, T.kvtm[:, 0:256], [T.kvtm], 4 + l, T.kn)
                dma('sp', o_pmk[l, ti * 128:(ti + 1) * 128, :], T.kn[:, :], R=[T.kn], is_out=True)
                dma('sp', o_pmv[l, ti * 128:(ti + 1) * 128, :], T.kvtm[:, 256:512], R=[T.kvtm], is_out=True)
                for hd in range(4):
                    c0 = (hd % 2) * 64
                    cp(MVA[:, l, ti, hd, c0:c0 + 64], T.kvtm[:, 256 + hd * 64:256 + (hd + 1) * 64], [T.kvtm], [MVA], eng='pool')
                pb2 = bank()
                for hp in range(2):
                    tr(pb2[:, hp * 128:(hp + 1) * 128], T.kn[:, hp * 128:(hp + 1) * 128], ident[:], [T.kn, ident], [pb2])
                act(MKT[:, l, :, ti * 128:(ti + 1) * 128], pb2[:, 0:256].rearrange("p (a t) -> p a t", a=2), AF.Copy, [pb2], [MKT], scale=0.125)
        P.phase_end()

        def inproj(l, T):
            is_s5 = l < 2
            wv = load_sq_weight(w_in[l], 128, 8, 1024)
            for ch, (c0, w) in enumerate(CHUNKS):
                rmsnorm_chunk(T, hT[ch], w, l, T.aT)
                if is_s5:
                    outs = [(t, 96 * t, 96) for t in range(8)]
                else:
                    outs = [(t, 128 * t, 128) for t in range(6)]
                outs += [(8, 768, 128), (9, 896, 128)]
                for i, (slot, col0, mw) in enumerate(outs):
                    pb = bank()
                    for k in range(8):
                        mm(pb[0:mw, :w], wv[:, k, col0:col0 + mw], T.aT[:, k, :w], k == 0, k == 7, [wb0, T.aT], [pb])
                    if i % 2 == 0:
                        act(Bm[ch][0:mw, slot, :w], pb[0:mw, :w], AF.Copy, [pb], [Bm[ch]])
                    else:
                        cp(Bm[ch][0:mw, slot, :w], pb[0:mw, :w], [pb], [Bm[ch]])

        def s5_layer(l):
            P.phase_begin()
            nm = lambda n: "s5%s_%d" % (n, l)
            prm = P.sb(nm("prm"), [128, 3, 24], F32)
            PB = [P.sb(nm("PB%d" % i), [128, 24, 16], F32) for i in range(2)]
            PC = [P.sb(nm("PC%d" % i), [128, 24, 16], F32) for i in range(2)]
            dcol = P.sb(nm("dcol"), [128, 8], F32)
            V = {n: P.sb(nm(n), [128, 24], F32) for n in
                 ["dt", "lre", "mag", "ang", "r", "kf", "m1", "c1", "s1", "abr", "abi", "nabi", "den", "nre", "t1", "t2", "fre", "fim", "m8", "c8", "s8"]}
            ki = P.sb(nm("ki"), [128, 24], I32)
            PWr = P.sb(nm("PWr"), [128, 9, 24], F32)
            PWi = P.sb(nm("PWi"), [128, 9, 24], F32)
            P2r = P.sb(nm("P2r"), [128, 8, 24], F32)
            P2i = P.sb(nm("P2i"), [128, 8, 24], F32)
            H0 = [P.sb(nm("H0%d" % i), [128, 24, 16], F32) for i in range(2)]
            XN = [P.sb(nm("XN%d" % i), [128, 24, 16], F32) for i in range(2)]
            H0b = P.sb(nm("H0b"), [128, 24, 2, 16], BF16)
            FIN = P.sb(nm("FIN"), [128, 24, 2], F32)
            XI = P.sb(nm("XI"), [128, 24, 2], F32)
            PFIN = P.sb(nm("PFIN"), [128, 2, 24], F32)
            z0 = P.sb(nm("z0"), [128, 4], F32)
            BsT = P.sb(nm("BsT"), [128, 8, 2, 128], BF16)
            Kf = P.sb(nm("Kf"), [128, 8, 96], BF16)
            K0f = P.sb(nm("K0f"), [128, 96], F32)
            CjT = P.sb(nm("CjT"), [128, 3, 8, 2, 96], BF16)
            BBw = P.sb(nm("BBw"), [128, 3, 2, 96], F32)
            Cw = P.sb(nm("Cw"), [128, 3, 2, 96], F32)
            Rc3 = P.sb(nm("Rc3"), [128, 3, 256], F32)
            Rs3 = P.sb(nm("Rs3"), [128, 3, 256], F32)
            SpB = P.sb(nm("SpB"), [128, 1536], F32)
            ZB = P.sb(nm("ZB"), [128, 1728], F32)
            taB = P.sb(nm("ta"), [128, 768], F32)
            rtmp = XN[0]
            u8 = P.sb(nm("u8"), [128, 8, 256], BF16)
            Xprev = P.sb(nm("Xprev"), [128, 3, 2, 256], BF16)
            TS8 = SpB[:, :].rearrange("p (k r n) -> p k r n", k=8, r=2)
            Sp = [SpB[:, 0:256], SpB[:, 256:512]]
            sstg = SpB
            ACw = ZB[:, :].rearrange("p (k r n) -> p k r n", k=9, r=2)
            Z = [ZB[:, 0:256], ZB[:, 256:512]]
            t8a = taB[:, :].rearrange("p (k n) -> p k n", k=8)
            ta, tb, M8t = taB[:, 0:256], taB[:, 256:512], taB[:, 512:768]
            rtab_d = nc.dram_tensor(nm("rtab"), [24, 128, 2, 256], F32)
            bst_d = nc.dram_tensor(nm("bst"), [8, 128, 8 * 2 * 128], BF16)
            drtab = P.view(rtab_d, "rtab")
            dbst = P.view(bst_d, "bst")
            cnt = {'ac': 0}

            dma('sp', prm[:], s5a[l], W=[prm])
            for i in range(2):
                dma('sp', PB[i][:], s5b[l, i], W=[PB[i]])
                dma('sp', PC[i][:], s5c[l, i], W=[PC[i]])
            dma('sp', dcol[0:96, :], s5d[l], W=[dcol])
            mset(K0f[:], 0.0, [K0f])
            mset(Kf[:], 0.0, [Kf])
            are, aim, ldt = prm[:, 0, :], prm[:, 1, :], prm[:, 2, :]
            v = lambda n: V[n][:, :]
            act(v("dt"), ldt, AF.Exp, [prm], [V["dt"]])
            ts(v("lre"), are, -1e-4, ALU.min, [prm], [V["lre"]])
            tt(v("t1"), v("dt"), v("lre"), ALU.mult, [V["dt"], V["lre"]], [V["t1"]])
            act(v("mag"), v("t1"), AF.Exp, [V["t1"]], [V["mag"]])
            tt(v("ang"), v("dt"), aim, ALU.mult, [V["dt"], prm], [V["ang"]])
            ts(v("kf"), v("ang"), 1.0 / (2 * PI), ALU.mult, [V["ang"]], [V["kf"]])
            cp(ki[:, :], v("kf"), [V["kf"]], [ki])
            cp(v("kf"), ki[:, :], [ki], [V["kf"]])
            stt(v("r"), v("kf"), -2 * PI, v("ang"), ALU.mult, ALU.add, [V["kf"], V["ang"]], [V["r"]])
            ts(v("m1"), v("r"), PI, ALU.is_gt, [V["r"]], [V["m1"]])
            stt(v("r"), v("m1"), -2 * PI, v("r"), ALU.mult, ALU.add, [V["m1"], V["r"]], [V["r"]])
            ts(v("m1"), v("r"), -PI, ALU.is_lt, [V["r"]], [V["m1"]])
            stt(v("r"), v("m1"), 2 * PI, v("r"), ALU.mult, ALU.add, [V["m1"], V["r"]], [V["r"]])
            ts(v("r"), v("r"), PI, ALU.min, [V["r"]], [V["r"]], s2=-PI, op1=ALU.max)
            act(v("s1"), v("r"), AF.Sin, [V["r"]], [V["s1"]])
            stt(v("t1"), v("r"), -1.0, v("r"), ALU.mult, ALU.max, [V["r"]], [V["t1"]])
            act(v("c1"), v("t1"), AF.Sin, [V["t1"], hpic], [V["c1"]], scale=-1.0, bias=hpic[:, 0:1])
            tt(v("abr"), v("mag"), v("c1"), ALU.mult, [V["mag"], V["c1"]], [V["abr"]])
            tt(v("abi"), v("mag"), v("s1"), ALU.mult, [V["mag"], V["s1"]], [V["abi"]])
            ts(v("nabi"), v("abi"), -1.0, ALU.mult, [V["abi"]], [V["nabi"]])
            tt(v("den"), v("lre"), v("lre"), ALU.mult, [V["lre"]], [V["den"]])
            tt(v("t1"), aim, aim, ALU.mult, [prm], [V["t1"]])
            tt(v("den"), v("den"), v("t1"), ALU.add, [V["den"], V["t1"]], [V["den"]])
            P.op('dve', lambda h: h.reciprocal(out=v("den"), in_=v("den")), reads=[V["den"]], writes=[V["den"]])
            ts(v("nre"), v("abr"), -1.0, ALU.add, [V["abr"]], [V["nre"]])
            tt(v("t1"), v("nre"), v("lre"), ALU.mult, [V["nre"], V["lre"]], [V["t1"]])
            tt(v("t2"), v("abi"), aim, ALU.mult, [V["abi"], prm], [V["t2"]])
            tt(v("t1"), v("t1"), v("t2"), ALU.add, [V["t1"], V["t2"]], [V["t1"]])
            tt(v("fre"), v("t1"), v("den"), ALU.mult, [V["t1"], V["den"]], [V["fre"]])
            tt(v("t1"), v("abi"), v("lre"), ALU.mult, [V["abi"], V["lre"]], [V["t1"]])
            tt(v("t2"), v("nre"), aim, ALU.mult, [V["nre"], prm], [V["t2"]])
            tt(v("t1"), v("t1"), v("t2"), ALU.subtract, [V["t1"], V["t2"]], [V["t1"]])
            tt(v("fim"), v("t1"), v("den"), ALU.mult, [V["t1"], V["den"]], [V["fim"]])
            frb = v("fre").unsqueeze(2).broadcast_to([128, 24, 16])
            fib = v("fim").unsqueeze(2).broadcast_to([128, 24, 16])
            tt(XN[0][:], PB[1][:], fib, ALU.mult, [PB[1], V["fim"]], [XN[0]])
            tt(XN[1][:], PB[0][:], fib, ALU.mult, [PB[0], V["fim"]], [XN[1]])
            tt(PB[0][:], PB[0][:], frb, ALU.mult, [PB[0], V["fre"]], [PB[0]])
            tt(PB[0][:], PB[0][:], XN[0][:], ALU.subtract, [PB[0], XN[0]], [PB[0]])
            tt(PB[1][:], PB[1][:], frb, ALU.mult, [PB[1], V["fre"]], [PB[1]])
            tt(PB[1][:], PB[1][:], XN[1][:], ALU.add, [PB[1], XN[1]], [PB[1]])
            mset(PWr[:, 0, :], 1.0, [PWr])
            mset(PWi[:, 0, :], 0.0, [PWi])
            for k in range(1, 9):
                tt(v("t1"), PWr[:, k - 1, :], v("abr"), ALU.mult, [PWr, V["abr"]], [V["t1"]])
                tt(v("t2"), PWi[:, k - 1, :], v("abi"), ALU.mult, [PWi, V["abi"]], [V["t2"]])
                tt(PWr[:, k, :], v("t1"), v("t2"), ALU.subtract, [V["t1"], V["t2"]], [PWr])
                tt(v("t1"), PWr[:, k - 1, :], v("abi"), ALU.mult, [PWr, V["abi"]], [V["t1"]])
                tt(v("t2"), PWi[:, k - 1, :], v("abr"), ALU.mult, [PWi, V["abr"]], [V["t2"]])
                tt(PWi[:, k, :], v("t1"), v("t2"), ALU.add, [V["t1"], V["t2"]], [PWi])
            tt(v("m8"), v("mag"), v("mag"), ALU.mult, [V["mag"]], [V["m8"]])
            tt(v("m8"), v("m8"), v("m8"), ALU.mult, [V["m8"]], [V["m8"]])
            tt(v("m8"), v("m8"), v("m8"), ALU.mult, [V["m8"]], [V["m8"]])

            def csquare(cr_o, ci_o, cr_i, ci_i, R, W):
                tt(v("t1"), cr_i, cr_i, ALU.mult, R, [V["t1"]])
                tt(v("t2"), ci_i, ci_i, ALU.mult, R, [V["t2"]])
                tt(v("den"), cr_i, ci_i, ALU.mult, R, [V["den"]])
                tt(cr_o, v("t1"), v("t2"), ALU.subtract, [V["t1"], V["t2"]], W)
                ts(ci_o, v("den"), 2.0, ALU.mult, [V["den"]], W)
            csquare(v("c8"), v("s8"), v("c1"), v("s1"), [V["c1"], V["s1"]], [V["c8"], V["s8"]])
            csquare(v("c8"), v("s8"), v("c8"), v("s8"), [V["c8"], V["s8"]], [V["c8"], V["s8"]])
            csquare(v("c8"), v("s8"), v("c8"), v("s8"), [V["c8"], V["s8"]], [V["c8"], V["s8"]])
            cp(P2r[:, 0, :], v("c8"), [V["c8"]], [P2r])
            cp(P2i[:, 0, :], v("s8"), [V["s8"]], [P2i])
            for k in range(1, 8):
                csquare(P2r[:, k, :], P2i[:, k, :], P2r[:, k - 1, :], P2i[:, k - 1, :], [P2r, P2i], [P2r, P2i])

            for ri in range(2):
                for blk in range(6):
                    dma('sp', sstg[0:NS, 0:512], st0[l, ri, :, blk * 512:(blk + 1) * 512], W=[sstg])
                    pb = bank()
                    for i in range(4):
                        tr(pb[:, i * 16:(i + 1) * 16], sstg[0:NS, i * 128:(i + 1) * 128], ident[0:NS, 0:NS], [sstg, ident], [pb])
                    cp(H0[ri][:, 4 * blk:4 * blk + 4, :], pb[:, 0:64].rearrange("p (a s) -> p a s", a=4), [pb], [H0[ri]])
                cp(H0b[:, :, ri, :], H0[ri][:], [H0[ri]], [H0b])

            def build_wide(q, qi):
                for ri in range(2):
                    for g2 in range(2):
                        c0 = 32 * qi + 16 * g2
                        cp(BBw[g2 * 64:(g2 + 1) * 64, qi, ri, c0:c0 + 16], PB[ri][g2 * 64:(g2 + 1) * 64, q, :], [PB[ri]], [BBw], eng='pool')
                        cp(Cw[g2 * 64:(g2 + 1) * 64, qi, ri, c0:c0 + 16], PC[ri][g2 * 64:(g2 + 1) * 64, q, :], [PC[ri]], [Cw], eng='pool')

            def build_Bs(q, qi):
                pr = PWr[:, 0:8, q:q + 1].broadcast_to([128, 8, 96])
                pi = PWi[:, 0:8, q:q + 1].broadcast_to([128, 8, 96])
                Br = BBw[:, qi, 0, :].unsqueeze(1).broadcast_to([128, 8, 96])
                Bi = BBw[:, qi, 1, :].unsqueeze(1).broadcast_to([128, 8, 96])
                tt(t8a, Bi, pi, ALU.mult, [BBw, PWi], [taB])
                tt(TS8[:, :, 0, :], Br, pr, ALU.mult, [BBw, PWr], [SpB])
                tt(TS8[:, :, 0, :], TS8[:, :, 0, :], t8a, ALU.subtract, [SpB, taB], [SpB])
                tt(t8a, Br, pi, ALU.mult, [BBw, PWi], [taB])
                tt(TS8[:, :, 1, :], Bi, pr, ALU.mult, [BBw, PWr], [SpB])
                tt(TS8[:, :, 1, :], TS8[:, :, 1, :], t8a, ALU.add, [SpB, taB], [SpB])
                for k in range(8):
                    pb = bank()
                    tr(pb[0:96, 0:128], TS8[:, k, 0, :], ident[:], [SpB, ident], [pb])
                    tr(pb[0:96, 128:256], TS8[:, k, 1, :], ident[:], [SpB, ident], [pb])
                    act(BsT[32 * qi:32 * qi + 32, k, :, :], pb[32 * qi:32 * qi + 32, 0:256].rearrange("p (r n) -> p r n", r=2), AF.Copy, [pb], [BsT])

            def build_AC(q, qi):
                pr = PWr[:, 0:9, q:q + 1].broadcast_to([128, 9, 96])
                pi = PWi[:, 0:9, q:q + 1].broadcast_to([128, 9, 96])
                Cr = Cw[:, qi, 0, :].unsqueeze(1).broadcast_to([128, 9, 96])
                Ci = Cw[:, qi, 1, :].unsqueeze(1).broadcast_to([128, 9, 96])
                t9 = SpB[:, 0:864].rearrange("p (k n) -> p k n", k=9)
                tt(t9, Ci, pi, ALU.mult, [Cw, PWi], [SpB])
                tt(ACw[:, :, 0, :], Cr, pr, ALU.mult, [Cw, PWr], [ZB])
                tt(ACw[:, :, 0, :], ACw[:, :, 0, :], t9, ALU.subtract, [ZB, SpB], [ZB])
                tt(t9, Cr, pi, ALU.mult, [Cw, PWi], [SpB])
                tt(ACw[:, :, 1, :], Ci, pr, ALU.mult, [Cw, PWr], [ZB])
                stt(ACw[:, :, 1, :], ACw[:, :, 1, :], -1.0, t9, ALU.mult, ALU.subtract, [ZB, SpB], [ZB])

            def build_wide_triple(t):
                mset(BBw[:], 0.0, [BBw])
                mset(Cw[:], 0.0, [Cw])
                for qi in range(3):
                    build_wide(3 * t + qi, qi)

            def build_conv_tables(t):
                for qi in range(3):
                    q = 3 * t + qi
                    build_AC(q, qi)
                    cp(CjT[:, qi, :, :, :], ACw[:, 1:9, :, :], [ZB], [CjT], eng='pool')
                    pK = bank()
                    for tau in range(8):
                        for ri in range(2):
                            mm(pK[0:96, tau * 32:(tau + 1) * 32], BBw[:, qi, ri, :], ACw[:, tau, ri, 32 * qi:32 * qi + 32], ri == 0, ri == 1, [BBw, ZB], [pK])
                    cp(Kf[32 * qi:32 * qi + 32, 1:8, 32 * qi:32 * qi + 32], pK[32 * qi:32 * qi + 32, 32:256].rearrange("p (k n) -> p k n", k=7), [pK], [Kf])
                    cp(K0f[32 * qi:32 * qi + 32, 32 * qi:32 * qi + 32], pK[32 * qi:32 * qi + 32, 0:32], [pK], [K0f])
                stt(Kf[0:96, 0, :], ident[0:96, 0:96], dcol[0:96, t:t + 1], K0f[0:96, :], ALU.mult, ALU.add, [ident, dcol, K0f], [Kf])

            def deint(t):
                act(u8[0:96, :, :], Bm_all[0:96, t, 0:NPT].rearrange("p (c s) -> p s c", s=8), AF.Copy, [Bm[0], Bm[1], Bm[2], Bm[3]], [u8])

            def build_rot_triple(t):
                q0 = 3 * t
                mset(Rc3[:, :, 0:1], 1.0, [Rc3], eng='dve')
                mset(Rs3[:, :, 0:1], 0.0, [Rs3], eng='dve')
                for k in range(8):
                    n = 1 << k
                    pr = P2r[:, k, q0:q0 + 3].unsqueeze(2).broadcast_to([128, 3, n])
                    pi = P2i[:, k, q0:q0 + 3].unsqueeze(2).broadcast_to([128, 3, n])
                    tmp = rtmp[:, :, :].rearrange("p a b -> p (a b)")[:, 0:3 * n].rearrange("p (a n) -> p a n", a=3)
                    tt(tmp, Rs3[:, :, 0:n], pi, ALU.mult, [Rs3, P2i], [rtmp])
                    tt(Rc3[:, :, n:2 * n], Rc3[:, :, 0:n], pr, ALU.mult, [Rc3, P2r], [Rc3])
                    tt(Rc3[:, :, n:2 * n], Rc3[:, :, n:2 * n], tmp, ALU.subtract, [Rc3, rtmp], [Rc3])
                    tt(tmp, Rc3[:, :, 0:n], pi, ALU.mult, [Rc3, P2i], [rtmp])
                    tt(Rs3[:, :, n:2 * n], Rs3[:, :, 0:n], pr, ALU.mult, [Rs3, P2r], [Rs3])
                    tt(Rs3[:, :, n:2 * n], Rs3[:, :, n:2 * n], tmp, ALU.add, [Rs3, rtmp], [Rs3])

            def scan_pair(t, qi, use_init, full):
                q = 3 * t + qi
                Rc, Rs = Rc3[:, qi, :], Rs3[:, qi, :]
                cp(M8t, V["m8"][:, q:q + 1].broadcast_to([128, 256]), [V["m8"]], [taB])
                pS = bank()
                for ri in range(2):
                    for s_ in range(8):
                        mm(pS[:, ri * 256:(ri + 1) * 256], BsT[32 * qi:32 * qi + 32, 7 - s_, ri, :], u8[32 * qi:32 * qi + 32, s_, :], s_ == 0, s_ == 7, [BsT, u8], [pS])
                Sr, Si = pS[:, 0:256], pS[:, 256:512]
                tt(ta, Sr, Rc, ALU.mult, [pS, Rc3], [taB])
                tt(tb, Si, Rs, ALU.mult, [pS, Rs3], [taB])
                tt(Sp[0], ta, tb, ALU.add, [taB], [SpB])
                tt(ta, Si, Rc, ALU.mult, [pS, Rc3], [taB])
                tt(tb, Sr, Rs, ALU.mult, [pS, Rs3], [taB])
                tt(Sp[1], ta, tb, ALU.subtract, [taB], [SpB])
                if use_init:
                    xr, xi = XI[:, q, 0:1], XI[:, q, 1:2]
                    c8q, s8q = V["c8"][:, q:q + 1], V["s8"][:, q:q + 1]
                    ts(z0[:, 2:3], xi, s8q, ALU.mult, [XI, V["s8"]], [z0])
                    stt(z0[:, 0:1], xr, c8q, z0[:, 2:3], ALU.mult, ALU.subtract, [XI, V["c8"], z0], [z0])
                    ts(z0[:, 2:3], xr, s8q, ALU.mult, [XI, V["s8"]], [z0])
                    stt(z0[:, 1:2], xi, c8q, z0[:, 2:3], ALU.mult, ALU.add, [XI, V["c8"], z0], [z0])
                for ri in range(2):
                    init = z0[:, ri:ri + 1] if use_init else 0.0
                    P.op('dve', lambda h, ri=ri, init=init: h.tensor_tensor_scan(out=Z[ri], data0=M8t, data1=Sp[ri], initial=init, op0=ALU.mult, op1=ALU.add), reads=[taB, SpB, z0], writes=[ZB])
                if not full:
                    tt(ta[:, 0:1], Z[1][:, 255:256], Rs[:, 255:256], ALU.mult, [ZB, Rs3], [taB])
                    tt(tb[:, 0:1], Z[0][:, 255:256], Rc[:, 255:256], ALU.mult, [ZB, Rc3], [taB])
                    tt(FIN[:, q, 0:1], tb[:, 0:1], ta[:, 0:1], ALU.subtract, [taB], [FIN])
                    tt(ta[:, 0:1], Z[0][:, 255:256], Rs[:, 255:256], ALU.mult, [ZB, Rs3], [taB])
                    tt(tb[:, 0:1], Z[1][:, 255:256], Rc[:, 255:256], ALU.mult, [ZB, Rc3], [taB])
                    tt(FIN[:, q, 1:2], tb[:, 0:1], ta[:, 0:1], ALU.add, [taB], [FIN])
                    return
                tt(ta, Z[0], Rc, ALU.mult, [ZB, Rc3], [taB])
                tt(tb, Z[1], Rs, ALU.mult, [ZB, Rs3], [taB])
                tt(Sp[0], ta, tb, ALU.subtract, [taB], [SpB])
                tt(ta, Z[1], Rc, ALU.mult, [ZB, Rc3], [taB])
                tt(tb, Z[0], Rs, ALU.mult, [ZB, Rs3], [taB])
                tt(Sp[1], ta, tb, ALU.add, [taB], [SpB])
                for ri in range(2):
                    cp(Xprev[:, qi, ri, 1:256], Sp[ri][:, 0:255], [SpB], [Xprev], eng='pool')
                    cp(Xprev[:, qi, ri, 0:1], XI[:, q, ri:ri + 1], [XI], [Xprev], eng='pool')
                    cp(PFIN[:, ri, q:q + 1], Sp[ri][:, 255:256], [SpB], [PFIN], eng='pool')

            for t in range(8):
                deint(t)
                build_wide_triple(t)
                build_rot_triple(t)
                for qi in range(3):
                    build_Bs(3 * t + qi, qi)
                dma('sp', bst_d.ap()[t], BsT[:, :, :, :].rearrange("p k r n -> p (k r n)"), R=[BsT], W=[dbst])
                for qi in range(3):
                    dma('sp', rtab_d.ap()[3 * t + qi, :, 0, :], Rc3[:, qi, :], R=[Rc3], W=[drtab])
                    dma('sp', rtab_d.ap()[3 * t + qi, :, 1, :], Rs3[:, qi, :], R=[Rs3], W=[drtab])
                for qi in range(3):
                    scan_pair(t, qi, False, False)
            dma('sp', xin_d.ap(), FIN[:, :, :].rearrange("p q r -> p (q r)"), R=[FIN], W=[dxin])
            P.dma('pool', lambda h: h.collective_compute("AllGather", ALU.bypass, replica_groups=[[0, 1], [2, 3], [4, 5], [6, 7]], ins=[xin_d.ap().opt()], outs=[xout_d.ap().opt()]), reads=[dxin], writes=[dxout], inc=1)
            dma('sp', XI[:, :, :].rearrange("p q r -> p (q r)"), xout_d.ap()[0:128, :], R=[dxout], W=[XI])
            ts(XI[:, :, :], XI[:, :, :], flg[:, 0:1], ALU.mult, [XI, flg], [XI])
            for t in range(8):
                deint(t)
                dma('sp', BsT[:, :, :, :].rearrange("p k r n -> p (k r n)"), bst_d.ap()[t], R=[dbst], W=[BsT])
                for qi in range(3):
                    dma('sp', Rc3[:, qi, :], rtab_d.ap()[3 * t + qi, :, 0, :], R=[drtab], W=[Rc3])
                    dma('sp', Rs3[:, qi, :], rtab_d.ap()[3 * t + qi, :, 1, :], R=[drtab], W=[Rs3])
                build_wide_triple(t)
                build_conv_tables(t)
                for qi in range(3):
                    scan_pair(t, qi, True, True)
                pos = [bank() for _ in range(4)]
                for j in range(8):
                    po = pos[j // 2]
                    o = po[0:96, (j % 2) * 256:(j % 2 + 1) * 256]
                    nmm = (j + 1) + 6
                    i = 0
                    for tau in range(j + 1):
                        mm(o, Kf[0:96, tau, :], u8[0:96, j - tau, :], i == 0, i == nmm - 1, [Kf, u8], [po])
                        i += 1
                    for qi in range(3):
                        for ri in range(2):
                            mm(o, CjT[:, qi, j, ri, :], Xprev[:, qi, ri, :], i == 0, i == nmm - 1, [CjT, Xprev], [po])
                            i += 1
                for ch in range(4):
                    for b in range(4):
                        src = pos[b][0:96, :].rearrange("p (j c) -> p j c", j=2)[:, :, ch * 64:(ch + 1) * 64]
                        dst = Bm[ch][0:96, t, :].rearrange("p (c j) -> p j c", j=8)[:, 2 * b:2 * b + 2, :]
                        act(dst, src, AF.Gelu, [pos[b]], [Bm[ch]])
                pq = bank()
                i = 0
                mm(pq[0:96, 0:NS], Kf[0:96, 0, :], Bm[4][0:96, t, 0:NS], True, False, [Kf, Bm[4]], [pq])
                for qi in range(3):
                    for ri in range(2):
                        i += 1
                        mm(pq[0:96, 0:NS], CjT[:, qi, 0, ri, :], H0b[:, 3 * t + qi, ri, :], False, i == 6, [CjT, H0b], [pq])
                for qi in range(3):
                    q = 3 * t + qi
                    pu = bank()
                    for ri in range(2):
                        mm(pu[:, ri * NS:(ri + 1) * NS], BsT[32 * qi:32 * qi + 32, 0, ri, :], Bm[4][32 * qi:32 * qi + 32, t, 0:NS], True, True, [BsT, Bm[4]], [pu])
                    abr_q, abi_q, nabi_q = V["abr"][:, q:q + 1], V["abi"][:, q:q + 1], V["nabi"][:, q:q + 1]
                    ts(ta[:, 0:NS], H0[0][:, q, :], abr_q, ALU.mult, [H0[0], V["abr"]], [taB])
                    tt(ta[:, 0:NS], ta[:, 0:NS], pu[:, 0:NS], ALU.add, [taB, pu], [taB])
                    stt(XN[0][:, q, :], H0[1][:, q, :], nabi_q, ta[:, 0:NS], ALU.mult, ALU.add, [H0[1], V["nabi"], taB], [XN[0]])
                    ts(tb[:, 0:NS], H0[1][:, q, :], abr_q, ALU.mult, [H0[1], V["abr"]], [taB])
                    tt(tb[:, 0:NS], tb[:, 0:NS], pu[:, NS:2 * NS], ALU.add, [taB, pu], [taB])
                    stt(XN[1][:, q, :], H0[0][:, q, :], abi_q, tb[:, 0:NS], ALU.mult, ALU.add, [H0[0], V["abi"], taB], [XN[1]])
                act(Bm[4][0:96, t, 0:NS], pq[0:96, 0:NS], AF.Gelu, [pq], [Bm[4]])
            for ri in range(2):
                pb = bank()
                tr(pb[0:24, 0:128], PFIN[:, ri, :], ident[:], [PFIN, ident], [pb])
                cp(sstg[0:24, 0:128], pb[0:24, 0:128], [pb], [sstg])
                dma('sp', o_pssm[l, ri], sstg[0:24, 0:128], R=[sstg], is_out=True)
                for blk in range(6):
                    pb = bank()
                    for i in range(4):
                        tr(pb[0:NS, i * 128:(i + 1) * 128], XN[ri][:, 4 * blk + i, :], ident[:], [XN[ri], ident], [pb])
                    cp(sstg[0:NS, 0:512], pb[0:NS, :], [pb], [sstg])
                    dma('sp', o_sssm[l, ri, :, blk * 512:(blk + 1) * 512], sstg[0:NS, 0:512], R=[sstg], is_out=True)
            P.phase_end()

        def moba_layer(l):
            j = l - 2
            P.phase_begin()
            nm = lambda n: "mb%s_%d" % (n, l)
            KTaug = P.sb(nm("KT"), [128, 4096], BF16)
            VZ = P.sb(nm("VZ"), [128, 32, 192], BF16)
            KMf = P.sb(nm("KMf"), [128, 16], F32)
            KM = P.sb(nm("KM"), [128, 16], BF16)
            qaug = [P.sb(nm("qa%d" % i), [128, 512], BF16) for i in range(2)]
            MBw = [P.sb(nm("MBw%d" % i), [128, 80], F32) for i in range(4)]
            g1 = P.sb(nm("g1"), [128, 16], F32)
            m8 = P.sb(nm("m8"), [128, 8], F32)
            sel = P.sb(nm("sel"), [128, 16], F32)
            VBg = P.sb(nm("VBg"), [128, 8, 16], F32)
            vm = P.sb(nm("vm"), [128, 8, 16], F32)
            pb1 = P.sb(nm("pb1"), [128, 1], F32)
            tri = P.sb(nm("tri"), [128, 128], BF16)
            shiftsel = P.sb(nm("shs"), [128, 64], BF16)
            pTs = [P.sb(nm("pT%d" % i), [128, 512], BF16) for i in range(4)]
            osb = P.sb(nm("osb"), [128, 512], F32)
            rden = P.sb(nm("rden"), [128, 512], F32)
            qsq = P.sb(nm("qsq"), [128, 512], BF16)
            qrs = P.sb(nm("qrs"), [128, 512], F32)
            T = SimpleNamespace(qsq=qsq, qrs=qrs)
            c = {'pt': 0, 'qa': 0, 'mb': 0, 'ps': 0, 'po': 0, 'mi': 0}

            def bank_ps():
                c['ps'] += 1
                return banks[c['ps'] % 3]

            def bank_po():
                c['po'] += 1
                return banks[3 + c['po'] % 2]

            def bank_mi():
                return banks[5]

            sd = sdec_gen(l) if stage >= 4 else None
            pend = {'e': None}

            def step_sd(k):
                nonlocal sd
                for _ in range(k):
                    if sd is None:
                        return
                    try:
                        next(sd)
                    except StopIteration:
                        sd = None

            mset(tri[:], 1.0, [tri])
            P.op('pool', lambda h: h.affine_select(out=tri[:], in_=tri[:], pattern=[[1, 128]], compare_op=ALU.is_ge, fill=0.0, base=0, channel_multiplier=-1), reads=[tri], writes=[tri])
            mset(shiftsel[:], 1.0, [shiftsel])
            P.op('pool', lambda h: h.affine_select(out=shiftsel[:], in_=shiftsel[:], pattern=[[-1, 64]], compare_op=ALU.is_equal, fill=0.0, base=-64, channel_multiplier=1), reads=[shiftsel], writes=[shiftsel])
            mset(KTaug[64:80, :], 1.0, [KTaug])
            P.op('pool', lambda h: h.affine_select(out=KTaug[64:80, :], in_=KTaug[64:80, :], pattern=[[1, 4096]], compare_op=ALU.is_ge, fill=0.0, base=0, channel_multiplier=-256), reads=[KTaug], writes=[KTaug])
            P.op('pool', lambda h: h.affine_select(out=KTaug[64:80, :], in_=KTaug[64:80, :], pattern=[[-1, 4096]], compare_op=ALU.is_ge, fill=0.0, base=255, channel_multiplier=256), reads=[KTaug], writes=[KTaug])
            mset(VZ[:], 0.0, [VZ])
            mset(VZ[:, :, 64:65], 1.0, [VZ])
            for i in range(4):
                mset(MBw[i][:], 0.0, [MBw[i]])
            ts(pb1[:, :], flg[:, :], -1.0, ALU.add, [flg], [pb1], s2=1e9, op1=ALU.mult)
            mset(VBg[:], -1e9, [VBg])
            mset(vm[:], 0.0, [vm])
            for lb in range(1, 8):
                mset(VBg[:, lb, 8:8 + lb], 0.0, [VBg])
                mset(vm[:, lb, 8:8 + lb], 1.0, [vm])
            cp(VBg[:, :, 0:8], pb1[:, 0:1].unsqueeze(2).broadcast_to([128, 8, 8]), [pb1], [VBg])
            cp(vm[:, :, 0:8], flg[:, 0:1].unsqueeze(2).broadcast_to([128, 8, 8]), [flg], [vm])

            step_sd(1)
            for ch in range(4):
                for slot in range(6):
                    headnorm_feat(T, Bm[ch], slot, 512, 8 + j, Bm[ch], slot)

            vall = kvall_d.ap()[256:512, :].rearrange("r (e c) -> (r e) c", c=256)
            vloc = kv_d.ap()[256:512, :].rearrange("r (e c) -> (r e) c", c=256)
            for kvh in range(4):
                dma('sp', KTaug[0:64, 0:2048], kvall_d.ap()[kvh * 64:(kvh + 1) * 64, :], R=[dkvall], W=[KTaug])
                dma('sp', KTaug[0:64, 2048:4096], kv_d.ap()[kvh * 64:(kvh + 1) * 64, :], R=[dkv], W=[KTaug])
                for (src, dsrc, t0) in ((vall, dkvall, 0), (vloc, dkv, 16)):
                    sv = src[:, kvh * 64:(kvh + 1) * 64].rearrange("(t p) c -> p t c", p=128)
                    dma('sp', VZ[:, t0:t0 + 16, 0:64], sv, R=[dsrc], W=[VZ])
                    dma('sp', VZ[:, t0:t0 + 16, 128:192], sv, R=[dsrc], W=[VZ])
                P.op('dve', lambda h: h.tensor_reduce(out=KMf[0:64, :], in_=KTaug[0:64, :].rearrange("p (b t) -> p b t", t=256), axis=AX.X, op=ALU.add), reads=[KTaug], writes=[KMf])
                cp(KM[0:64, :], KMf[0:64, :], [KMf], [KM])
                units = [(hh, ch) for hh in range(3) for ch in range(4)]

                def prep_gen(hh, ch, qa):
                    hd = 3 * kvh + hh
                    slot, par = hd // 2, hd % 2
                    if par == 0:
                        cp(qa[0:64, :], Bm[ch][0:64, slot, :], [Bm[ch]], [qa])
                    else:
                        pq = banks[3 + (c['po'] + 1) % 2]
                        mm(pq[0:64, :], shiftsel[:, :], Bm[ch][:, slot, :], True, True, [shiftsel, Bm[ch]], [pq])
                        act(qa[0:64, :], pq[0:64, :], AF.Copy, [pq], [qa])
                    pg = banks[5]
                    mbws = []
                    for tti in range(4):
                        lb = (ch * 4 + tti) // 2
                        mm(pg[:, tti * 16:(tti + 1) * 16], qa[0:64, tti * 128:(tti + 1) * 128], KM[0:64, :], True, True, [qa, KM], [pg])
                    for tti in range(4):
                        lb = (ch * 4 + tti) // 2
                        tt(g1[:, :], pg[:, tti * 16:(tti + 1) * 16], VBg[:, lb, :], ALU.add, [pg, VBg], [g1])
                        P.op('dve', lambda h: h.max(out=m8[:, :], in_=g1[:, :]), reads=[g1], writes=[m8])
                        ts(sel[:, :], g1[:, :], m8[:, 2:3], ALU.is_ge, [g1, m8], [sel])
                        tt(sel[:, :], sel[:, :], vm[:, lb, :], ALU.mult, [sel, vm], [sel])
                        mbw = MBw[c['mb'] % 4]
                        c['mb'] += 1
                        ts(mbw[:, 64:80], sel[:, :], -1.0, ALU.add, [sel], [mbw], s2=30000.0, op1=ALU.mult)
                        mset(mbw[:, 72 + lb:73 + lb], 0.0, [mbw], eng='dve')
                        mbws.append(mbw)
                    yield
                    for tti in range(4):
                        tr(pg[0:80, tti * 128:(tti + 1) * 128], mbws[tti][:, :], ident[:], [mbws[tti], ident], [pg])
                    cp(qa[64:80, :], pg[64:80, :], [pg], [qa])
                    yield

                def drain(g):
                    for _ in g:
                        pass

                qa_cur = qaug[c['qa'] % 2]
                c['qa'] += 1
                drain(prep_gen(units[0][0], units[0][1], qa_cur))
                for ui, (hh, ch) in enumerate(units):
                    hd = 3 * kvh + hh
                    slot, par = hd // 2, hd % 2
                    lo = par * 64
                    zoff = par * 64
                    drow = 64 if par == 0 else 0
                    qa = qa_cur
                    gnext = None
                    if ui + 1 < len(units):
                        qa_next = qaug[c['qa'] % 2]
                        c['qa'] += 1
                        gnext = prep_gen(units[ui + 1][0], units[ui + 1][1], qa_next)
                    b0 = 2 * ch
                    steps = [(kt, 0, 512, False) for kt in range(16 + 2 * b0)]
                    steps += [(16 + 2 * b0, 0, 512, True), (17 + 2 * b0, 128, 384, True), (18 + 2 * b0, 256, 256, True), (19 + 2 * b0, 384, 128, True)]
                    po = bank_po()
                    n = len(steps)
                    LA = 2
                    live = {}
                    for i in range(n + LA):
                        if i % 8 == 5:
                            step_sd(1)
                        if i == 4 and pend['e'] is not None:
                            pend['e']()
                            pend['e'] = None
                        if gnext is not None and (i == 6 or i == 14):
                            next(gnext)
                        if i < n:
                            kt, q0, nq, dotri = steps[i]
                            ps = bank_ps()
                            pt = pTs[c['pt'] % 4]
                            c['pt'] += 1
                            mm(ps[:, 0:nq], KTaug[0:80, kt * 128:(kt + 1) * 128], qa[0:80, q0:q0 + nq], True, True, [KTaug, qa], [ps])
                            act(pt[:, 0:nq], ps[:, 0:nq], AF.Exp, [ps], [pt])
                            if dotri:
                                tt(pt[:, 0:128], pt[:, 0:128], tri[:, :], ALU.mult, [pt, tri], [pt])
                            live[i] = pt
                        k = i - LA
                        if k >= 0:
                            kt, q0, nq, dotri = steps[k]
                            pt = live.pop(k)
                            mm(po[:, q0:q0 + nq], VZ[:, kt, zoff:zoff + 128], pt[:, 0:nq], k == 0, k == n - 1, [VZ, pt], [po])
                    def epi1(po=po, lo=lo, drow=drow):
                        act(osb[lo:lo + 64, :], po[lo:lo + 64, :], AF.Copy, [po], [osb])
                        act(rden[drow:drow + 1, :], po[drow:drow + 1, :], AF.Ln, [po], [rden])
                        act(rden[drow:drow + 1, :], rden[drow:drow + 1, :], AF.Exp, [rden], [rden], scale=-1.0)

                    def epi2(po=po, lo=lo, drow=drow, ch=ch, slot=slot):
                        pbc = banks[5]
                        mm(pbc[:, :], selrow[drow:drow + 1, :], rden[drow:drow + 1, :], True, True, [selrow, rden], [pbc])
                        tt(Bm[ch][lo:lo + 64, slot, :], osb[lo:lo + 64, :], pbc[lo:lo + 64, :], ALU.mult, [osb, pbc], [Bm[ch]])
                    epi1()
                    pend['e'] = epi2
                    step_sd(1)
                    if gnext is not None:
                        drain(gnext)
                        qa_cur = qa_next
                if pend['e'] is not None:
                    pend['e']()
                    pend['e'] = None
            if stage < 4:
                mset(Bm[4][:, 0:6, :], 0.0, [Bm[4]])
            step_sd(100000)
            P.phase_end()

        def sdec_gen(l):
            j = l - 2
            nm = lambda n: "sd%s_%d" % (n, l)
            Kpg = [P.sb(nm("Kpg%d" % i), [128, 2, 256], F32) for i in range(3)]
            Vpg = [P.sb(nm("Vpg%d" % i), [128, 2, 256], F32) for i in range(3)]
            KTp = [P.sb(nm("KTp%d" % i), [128, 2, 256], BF16) for i in range(2)]
            VA = [P.sb(nm("VA%d" % i), [128, 2, 4, 65], BF16) for i in range(3)]
            KMs2 = [P.sb(nm("KMs%d" % i), [128, 2, 8], F32) for i in range(2)]
            KMb = P.sb(nm("KMb"), [128, 2, 8], BF16)
            QS = P.sb(nm("QS"), [128, 12, NS], BF16)
            Em = P.sb(nm("Em"), [128, 4, 128], BF16)
            Ep1 = P.sb(nm("Ep1"), [128, 128], BF16)
            identb = P.sb(nm("identb"), [128, 128], BF16)
            ones6 = P.sb(nm("ones6"), [128, 65], BF16)
            KTn = P.sb(nm("KTn"), [128, 2, NS], BF16)
            VAn = P.sb(nm("VAn"), [128, 4, 65], BF16)
            vnf = P.sb(nm("vnf"), [128, 256], F32)
            PT = [P.sb(nm("PT%d" % i), [128, 24], BF16) for i in range(2)]
            PTn = P.sb(nm("PTn"), [128, 12], BF16)
            PARTS = P.sb(nm("PARTS"), [128, 9, 12], F32)
            g = P.sb(nm("g"), [128, 2, 8], F32)
            m8 = P.sb(nm("m8"), [128, 2, 8], F32)
            sel = P.sb(nm("sel"), [128, 2, 8], F32)
            selx = P.sb(nm("selx"), [128, 2, 8, 6], BF16)
            tmp = P.sb(nm("tmp"), [128, 8, 6], F32)
            acc = P.sb(nm("acc"), [128, 12], F32)
            rden = P.sb(nm("rden"), [128, 12], F32)
            Osb = P.sb(nm("Osb"), [128, 12], BF16)
            qsq = P.sb(nm("qsq"), [128, 512], BF16)
            qrs = P.sb(nm("qrs"), [128, 512], F32)
            T = SimpleNamespace(qsq=qsq, qrs=qrs)
            c = {'k': 0}
            ckv = cache_k.rearrange("n t c -> (n t) c")
            cvv = cache_v.rearrange("n t c -> (n t) c")

            cp(identb[:], ident[:], [ident], [identb], eng='pool')
            mset(Em[:], 0.0, [Em])
            cp(Em[0:64, 0, 0:64], ident[0:64, 0:64], [ident], [Em], eng='pool')
            cp(Em[64:128, 1, 64:128], ident[64:128, 64:128], [ident], [Em], eng='pool')
            cp(Em[0:64, 2, 64:128], ident[0:64, 0:64], [ident], [Em], eng='pool')
            cp(Em[64:128, 3, 0:64], ident[64:128, 64:128], [ident], [Em], eng='pool')
            mset(Ep1[:], 0.0, [Ep1])
            cp(Ep1[0:64, 64:128], ident[0:64, 0:64], [ident], [Ep1], eng='pool')
            mset(ones6[:], 1.0, [ones6])
            for i in range(3):
                mset(VA[i][:, :, :, 64:65], 1.0, [VA[i]])
            mset(VAn[:, :, 64:65], 1.0, [VAn])
            dma('sp', KTn[:, :, :], knT_d.ap(), R=[dknT], W=[KTn])
            dma('sp', vnf[0:NS, :], snew_d.ap()[:, 256:512], R=[dsnew], W=[vnf])
            cp(VAn[0:NS, :, 0:64], vnf[0:NS, :].rearrange("p (k d) -> p k d", k=4), [vnf], [VAn], eng='pool')

            for slot in range(6):
                headnorm_feat(T, Bm[4], slot, NS, 8 + j, Bm[4], slot)
            for hd in range(12):
                par, kp = hd % 2, (hd // 3) % 2
                vsel = {(0, 0): 0, (1, 1): 1, (0, 1): 2, (1, 0): 3}[(par, kp)]
                pq = bank()
                mm(pq[:, 0:NS], Em[:, vsel, :], Bm[4][:, hd // 2, 0:NS], True, True, [Em, Bm[4]], [pq])
                cp(QS[:, hd, :], pq[:, 0:NS], [pq], [QS])

            yield
            B6 = banks[6]

            def issue_gather(step):
                if step >= NS * 8:
                    return
                kpg, vpg = Kpg[step % 3], Vpg[step % 3]
                for pg in range(2):
                    col = step * 2 + pg
                    P.dma('pool', lambda h, kpg=kpg, pg=pg, col=col: h.indirect_dma_start(out=kpg[:, pg, :], out_offset=None, in_=ckv, in_offset=bass.IndirectOffsetOnAxis(ap=pidx[:, col:col + 1], axis=0)), reads=[pidx], writes=[kpg])
                    P.dma('pool', lambda h, vpg=vpg, pg=pg, col=col: h.indirect_dma_start(out=vpg[:, pg, :], out_offset=None, in_=cvv, in_offset=bass.IndirectOffsetOnAxis(ap=pidx[:, col:col + 1], axis=0)), reads=[pidx], writes=[vpg])
            def combine(s):
                pparts = banks[6]
                KMs = KMs2[s % 2]
                for kvh in range(4):
                    mm(B6[0:NS, 160 + 3 * kvh:160 + 3 * kvh + 3], KTn[:, kvh // 2, :], QS[:, 3 * kvh:3 * kvh + 3, s], True, True, [KTn, QS], [B6])
                act(PTn[0:NS, :], B6[0:NS, 160:172], AF.Exp, [B6], [PTn])
                ts(PTn[0:NS, :], PTn[0:NS, :], ident[0:NS, s:s + 1], ALU.mult, [PTn, ident], [PTn])
                for kvh in range(4):
                    mm(pparts[0:65, 96 + 3 * kvh:96 + 3 * kvh + 3], VAn[0:NS, kvh, :], PTn[0:NS, 3 * kvh:3 * kvh + 3], True, True, [VAn, PTn], [pparts])
                cp(PARTS[0:65, :, :], pparts[0:65, 0:108].rearrange("p (b h) -> p b h", h=12), [pparts], [PARTS])
                cp(KMb[:, :, :], KMs[:, :, :], [KMs], [KMb])
                for pr in range(2):
                    mm(B6[0:6, 176 + pr * 8:176 + pr * 8 + 8], QS[:, 6 * pr:6 * pr + 6, s], KMb[:, pr, :], True, True, [QS, KMb], [B6])
                cp(g[0:6, :, :], B6[0:6, 176:192].rearrange("p (a b) -> p a b", a=2), [B6], [g])
                for pr in range(2):
                    P.op('dve', lambda h, pr=pr: h.max(out=m8[0:6, pr, :], in_=g[0:6, pr, :]), reads=[g], writes=[m8])
                    ts(sel[0:6, pr, :], g[0:6, pr, :], m8[0:6, pr, 2:3], ALU.is_ge, [g, m8], [sel])
                    tt(selx[0:6, pr, :, :], sel[0:6, pr, :].unsqueeze(2).broadcast_to([6, 8, 6]), ident[0:6, 0:6].unsqueeze(1).broadcast_to([6, 8, 6]), ALU.mult, [sel, ident], [selx])
                for pr in range(2):
                    mm(B6[0:65, 256 + pr * 48:256 + pr * 48 + 48], ones6[0:6, :], selx[0:6, pr, :, :].rearrange("p b h -> p (b h)"), True, True, [ones6, selx], [B6])
                for pr in range(2):
                    tt(tmp[0:65, :, :], PARTS[0:65, 0:8, 6 * pr:6 * pr + 6], B6[0:65, 256 + pr * 48:256 + pr * 48 + 48].rearrange("p (b h) -> p b h", h=6), ALU.mult, [PARTS, B6], [tmp])
                    P.op('dve', lambda h, pr=pr: h.tensor_reduce(out=acc[0:65, 6 * pr:6 * pr + 6], in_=tmp[0:65, :, :].rearrange("p b h -> p h b"), axis=AX.X, op=ALU.add), reads=[tmp], writes=[acc])
                tt(acc[0:65, :], acc[0:65, :], PARTS[0:65, 8, :], ALU.add, [acc, PARTS], [acc])
                P.op('dve', lambda h: h.reciprocal(out=rden[64:65, :], in_=acc[64:65, :]), reads=[acc], writes=[rden])
                mm(B6[:, 384:396], selrow[64:65, :], rden[64:65, :], True, True, [selrow, rden], [B6])
                tt(Osb[0:64, :], acc[0:64, :], B6[0:64, 384:396], ALU.mult, [acc, B6], [Osb])
                mm(B6[:, 400:406], identb[0:64, :], Osb[0:64, 0:12:2], True, False, [identb, Osb], [B6])
                mm(B6[:, 400:406], Ep1[0:64, :], Osb[0:64, 1:12:2], False, True, [Ep1, Osb], [B6])
                cp(Bm[4][:, 0:6, s], B6[:, 400:406], [B6], [Bm[4]])

            NST = NS * 8
            pparts = banks[6]

            def stageT(step):
                s_, blk = step // 8, step % 8
                kpg, vpg, ktp, va = Kpg[step % 3], Vpg[step % 3], KTp[step % 2], VA[step % 3]
                KMs = KMs2[s_ % 2]
                pk = banks[7]
                for pg in range(2):
                    for pr in range(2):
                        tr(pk[:, (pr * 2 + pg) * 128:(pr * 2 + pg + 1) * 128], kpg[:, pg, pr * 128:(pr + 1) * 128], ident[:], [kpg, ident], [pk])
                ts(ktp[:, :, :], pk[:, :].rearrange("p (a t) -> p a t", a=2), 0.125, ALU.mult, [pk], [ktp])
                P.op('dve', lambda h, ktp=ktp, blk=blk, KMs=KMs: h.tensor_reduce(out=KMs[:, :, blk], in_=ktp[:, :, :], axis=AX.X, op=ALU.add), reads=[ktp], writes=[KMs])
                for pg in range(2):
                    cp(va[:, pg, :, 0:64], vpg[:, pg, :].rearrange("p (k d) -> p k d", k=4), [vpg], [va], eng='pool')

            def stageQ(step):
                s_ = step // 8
                ktp = KTp[step % 2]
                psS = B6[:, 128:152]
                for pg in range(2):
                    for kvh in range(4):
                        mm(psS[:, pg * 12 + 3 * kvh:pg * 12 + 3 * kvh + 3], ktp[:, kvh // 2, pg * 128:(pg + 1) * 128], QS[:, 3 * kvh:3 * kvh + 3, s_], True, True, [ktp, QS], [B6])
                pt = PT[step % 2]
                act(pt[:, 0:24], psS[:, 0:24], AF.Exp, [B6], [pt])

            def stageV(step):
                s_, blk = step // 8, step % 8
                va, pt = VA[step % 3], PT[step % 2]
                for kvh in range(4):
                    for pg in range(2):
                        mm(pparts[0:65, blk * 12 + 3 * kvh:blk * 12 + 3 * kvh + 3], va[:, pg, kvh, :], pt[:, pg * 12 + 3 * kvh:pg * 12 + 3 * kvh + 3], pg == 0, pg == 1, [va, pt], [pparts])

            for g_ in range(2):
                issue_gather(g_)
            for e in range(-2, NST):
                issue_gather(e + 4)
                if 0 <= e + 2 < NST:
                    stageT(e + 2)
                if 0 <= e + 1 < NST:
                    stageQ(e + 1)
                if 0 <= e < NST:
                    stageV(e)
                    if e % 8 == 7:
                        combine(e // 8)
                yield

        for l in range(DEPTH):
            is_s5 = l < 2
            if l == 0:
                P.phase_begin()
                T = common("i%d" % l)
                inproj(l, T)
                P.phase_end()

            if is_s5 and stage >= 2:
                s5_layer(l)
            elif (not is_s5) and stage >= 3:
                moba_layer(l)
            else:
                P.phase_begin()
                for ch in range(5):
                    mset(Bm[ch][:, 0:8, :], 0.0, [Bm[ch]])
                P.phase_end()

            P.phase_begin()
            T = common("o%d" % l)
            wb1 = P.sb("wb1_%d" % l, [128, 6144], BF16)
            wb2 = P.sb("wb2_%d" % l, [128, 2, 1024], BF16)
            SK = P.sb("SK_%d" % l, [128, 4, 2, 256], BF16)
            SV = P.sb("SV_%d" % l, [128, 4, 2, 256], BF16)
            Os = P.sb("Os_%d" % l, [128, 4, NS], BF16)
            identb3 = P.sb("identb3_%d" % l, [128, 128], BF16)
            Ep13 = P.sb("Ep13_%d" % l, [128, 128], BF16)
            cp(identb3[:], ident[:], [ident], [identb3], eng='pool')
            mset(Ep13[:], 0.0, [Ep13])
            cp(Ep13[0:64, 64:128], ident[0:64, 0:64], [ident], [Ep13], eng='pool')
            nmix = 8 if is_s5 else 6
            pmix = 96 if is_s5 else 128
            if is_s5 and stage >= 2:
                gv = load_sq_weight(w_glu[l], 96, 8, 768)
                for ch, (c0, w) in enumerate(CHUNKS):
                    for m in range(8):
                        pb = bank()
                        for k in range(8):
                            mm(pb[0:96, :w], gv[0:96, k, m * 96:(m + 1) * 96], Bm[ch][0:96, k, :w], k == 0, k == 7, [wb0, Bm[ch]], [pb])
                        act(T.osb[0:96, :w], pb[0:96, :w], AF.Sigmoid, [pb], [T.osb])
                        tt(T.aT[0:96, m, :w], T.osb[0:96, :w], Bm[ch][0:96, m, :w], ALU.mult, [T.osb, Bm[ch]], [T.aT])
                    for m in range(8):
                        cp(Bm[ch][0:96, m, :w], T.aT[0:96, m, :w], [T.aT], [Bm[ch]], eng='pool')
            def mem_part1(ch, hd):
                par, hp = hd % 2, hd // 2
                w = CHUNKS[ch][1]
                pts = []
                for kt in range(2):
                    ps = bank()
                    mm(ps[:, :w], MKT[par * 64:(par + 1) * 64, l, hp, kt * 128:(kt + 1) * 128], T.qn[par * 64:(par + 1) * 64, hp, :w], True, True, [MKT, T.qn], [ps])
                    pt = T.pT[st['pT'] % 4]
                    st['pT'] += 1
                    act(pt[:, :w], ps[:, :w], AF.Exp, [ps], [pt])
                    pts.append(pt)
                return pts

            def mem_part2(ch, hd, pts):
                par, hp = hd % 2, hd // 2
                w = CHUNKS[ch][1]
                po = bank()
                for kt in range(2):
                    mm(po[:, :w], MVA[:, l, kt, hd, :], pts[kt][:, :w], kt == 0, kt == 1, [MVA, pts[kt]], [po])
                lo = par * 64
                drow = 64 if par == 0 else 0
                act(T.osb[lo:lo + 64, :w], po[lo:lo + 64, :w], AF.Copy, [po], [T.osb])
                act(T.rden[drow:drow + 1, :w], po[drow:drow + 1, :w], AF.Ln, [po], [T.rden])
                act(T.rden[drow:drow + 1, :w], T.rden[drow:drow + 1, :w], AF.Exp, [T.rden], [T.rden], scale=-1.0)
                pbc = bank()
                mm(pbc[:, :w], selrow[drow:drow + 1, :], T.rden[drow:drow + 1, :w], True, True, [selrow, T.rden], [pbc])
                tt(Bm[ch][lo:lo + 64, 8 + hp, :w], T.osb[lo:lo + 64, :w], pbc[lo:lo + 64, :w], ALU.mult, [T.osb, pbc], [Bm[ch]])

            prev = None
            for ch in range(4):
                for hp in range(2):
                    headnorm_feat(T, Bm[ch], 8 + hp, 512, l, T.qn, hp)
                for hd in range(4):
                    pts = mem_part1(ch, hd)
                    if prev is not None:
                        mem_part2(*prev)
                    prev = (ch, hd, pts)
            mem_part2(*prev)
            for hp in range(2):
                headnorm_feat(T, Bm[4], 8 + hp, NS, l, T.qn, hp)
            for grp in range(4):
                for si in range(4):
                    sq_ = grp * 4 + si
                    for ti in range(2):
                        sbuf_, o0 = ((T.stg, 0) if (si * 2 + ti) % 2 == 0 else (T.kvtm, 0))
                        dma('sp', sbuf_[:, 0:256], cmk[l, sq_, ti * 128:(ti + 1) * 128, :], W=[sbuf_])
                        dma('sp', sbuf_[:, 256:512], cmv[l, sq_, ti * 128:(ti + 1) * 128, :], W=[sbuf_])
                        pb2 = bank()
                        for hp in range(2):
                            tr(pb2[:, hp * 128:(hp + 1) * 128], sbuf_[:, hp * 128:(hp + 1) * 128], ident[:], [sbuf_, ident], [pb2])
                        act(SK[:, si, :, ti * 128:(ti + 1) * 128], pb2[:, 0:256].rearrange("p (a t) -> p a t", a=2), AF.Copy, [pb2], [SK], scale=0.125)
                        cp(SV[:, si, ti, :], sbuf_[:, 256:512], [sbuf_], [SV], eng='pool')
                for hd in range(4):
                    par, hp = hd % 2, hd // 2
                    ps = bank()
                    for si in range(4):
                        sq_ = grp * 4 + si
                        for kt in range(2):
                            mm(ps[:, kt * 4 + si:kt * 4 + si + 1], SK[par * 64:(par + 1) * 64, si, hp, kt * 128:(kt + 1) * 128], T.qn[par * 64:(par + 1) * 64, hp, sq_:sq_ + 1], True, True, [SK, T.qn], [ps])
                    pt = T.pT[st['pT'] % 4]
                    st['pT'] += 1
                    act(pt[:, 0:8], ps[:, 0:8], AF.Exp, [ps], [pt])
                    po = bank()
                    for si in range(4):
                        for kt in range(2):
                            mm(po[0:64, si:si + 1], SV[:, si, kt, hd * 64:(hd + 1) * 64], pt[:, kt * 4 + si:kt * 4 + si + 1], kt == 0, kt == 1, [SV, pt], [po])
                    for kt in range(2):
                        mm(po[0:64, 8:12], onesb[:, 0:64], pt[:, kt * 4:(kt + 1) * 4], kt == 0, kt == 1, [onesb, pt], [po])
                    P.op('dve', lambda h, po=po: h.reciprocal(out=T.rden[0:64, 0:4], in_=po[0:64, 8:12]), reads=[po], writes=[T.rden])
                    act(T.osb[0:64, 0:4], po[0:64, 0:4], AF.Copy, [po], [T.osb])
                    tt(Os[0:64, hd, grp * 4:(grp + 1) * 4], T.osb[0:64, 0:4], T.rden[0:64, 0:4], ALU.mult, [T.osb, T.rden], [Os])
            for hp in range(2):
                po2 = bank()
                mm(po2[:, 0:NS], identb3[0:64, :], Os[0:64, 2 * hp, :], True, False, [identb3, Os], [po2])
                mm(po2[:, 0:NS], Ep13[0:64, :], Os[0:64, 2 * hp + 1, :], False, True, [Ep13, Os], [po2])
                cp(Bm[4][:, 8 + hp, 0:NS], po2[:, 0:NS], [po2], [Bm[4]])
            ov = load_sq_weight(w_out[l, 0:768, :], pmix, nmix, 1024)
            dma('pool', wb2[:, :, :], w_out[l, 768:1024, :].rearrange("(t p) n -> p t n", p=128), W=[wb2])
            for ch, (c0, w) in enumerate(CHUNKS):
                for m in range(8):
                    pb = bank()
                    for k in range(nmix):
                        mm(pb[:, :w], ov[0:pmix, k, m * 128:(m + 1) * 128], Bm[ch][0:pmix, k, :w], k == 0, False, [wb0, Bm[ch]], [pb])
                    for k in range(2):
                        mm(pb[:, :w], wb2[:, k, m * 128:(m + 1) * 128], Bm[ch][:, 8 + k, :w], False, k == 1, [wb2, Bm[ch]], [pb])
                    tt(hT[ch][:, m, :w], hT[ch][:, m, :w], pb[:, :w], ALU.add, [hT[ch], pb], [hT[ch]])
            for ch, (c0, w) in enumerate(CHUNKS):
                rmsnorm_chunk(T, hT[ch], w, 4 + l, Bm[ch])
            for j in range(DFF // 256):
                wbuf = wb0 if st['ffn'] % 2 == 0 else wb1
                st['ffn'] += 1
                gu = wbuf[:, 0:4096].rearrange("p (a k n) -> p a k n", a=2, k=8)
                dn = wbuf[:, 4096:6144].rearrange("p (t n) -> p t n", t=2)
                dma('pool', gu[:, 0, :, :], w_gu[l, :, 256 * j:256 * (j + 1)].rearrange("(t p) n -> p t n", p=128), W=[wbuf])
                dma('pool', gu[:, 1, :, :], w_gu[l, :, DFF + 256 * j:DFF + 256 * (j + 1)].rearrange("(t p) n -> p t n", p=128), W=[wbuf])
                dma('pool', dn[:, :, :], w_down[l, 256 * j:256 * (j + 1), :].rearrange("(t p) n -> p t n", p=128), W=[wbuf])
                for ch, (c0, w) in enumerate(CHUNKS):
                    hids = [T.pT[st['pT'] % 4], T.pT[(st['pT'] + 1) % 4]]
                    st['pT'] += 2
                    for t in range(2):
                        pg = bank()
                        pu = bank()
                        for k in range(8):
                            mm(pg[:, :w], gu[:, 0, k, t * 128:(t + 1) * 128], Bm[ch][:, k, :w], k == 0, k == 7, [wbuf, Bm[ch]], [pg])
                        for k in range(8):
                            mm(pu[:, :w], gu[:, 1, k, t * 128:(t + 1) * 128], Bm[ch][:, k, :w], k == 0, k == 7, [wbuf, Bm[ch]], [pu])
                        act(T.osb[:, :w], pg[:, :w], AF.Silu, [pg], [T.osb])
                        tt(hids[t][:, :w], T.osb[:, :w], pu[:, :w], ALU.mult, [T.osb, pu], [hids[t]])
                    for m in range(8):
                        pb = bank()
                        for t in range(2):
                            mm(pb[:, :w], dn[:, t, m * 128:(m + 1) * 128], hids[t][:, :w], t == 0, t == 1, [wbuf, hids[t]], [pb])
                        tt(hT[ch][:, m, :w], hT[ch][:, m, :w], pb[:, :w], ALU.add, [hT[ch], pb], [hT[ch]])
            if l == 1:
                kvw = load_sq_weight(w_kv, 128, 8, 512)
                for ch, (c0, w) in enumerate(CHUNKS):
                    rmsnorm_chunk(T, hT[ch], w, 12, T.aT)
                    for ti in range((w + 127) // 128):
                        nt = min(128, w - ti * 128)
                        pb = bank()
                        for k in range(8):
                            mm(pb[0:nt, :], T.aT[:, k, ti * 128:ti * 128 + nt], kvw[:, k, :], k == 0, k == 7, [T.aT, wb0], [pb])
                        act(T.kvtm[0:nt, :], pb[0:nt, :], AF.Copy, [pb], [T.kvtm])
                        headnorm_tokmajor(T, T.kvtm[0:nt, 0:256], [T.kvtm], 10, T.kn, nrows=nt)
                        if ch < 4:
                            r0 = c0 + ti * 128
                            dma('sp', o_pk[r0:r0 + 128, :], T.kn[:, :], R=[T.kn], is_out=True)
                            dma('sp', o_pv[r0:r0 + 128, :], T.kvtm[:, 256:512], R=[T.kvtm], is_out=True)
                            pb2 = bank()
                            for hp in range(2):
                                tr(pb2[:, hp * 128:(hp + 1) * 128], T.kn[:, hp * 128:(hp + 1) * 128], ident[:], [T.kn, ident], [pb2])
                            act(T.qn[:, :, 0:128], pb2[:, 0:256].rearrange("p (a t) -> p a t", a=2), AF.Copy, [pb2], [T.qn], scale=0.125)
                            dma('sp', kv_d.ap()[0:256, r0:r0 + 128].rearrange("(a p) t -> p a t", p=128), T.qn[:, :, 0:128], R=[T.qn], W=[dkv])
                            cp(T.qsq[:, 0:256], T.kvtm[:, 256:512], [T.kvtm], [T.qsq], eng='pool')
                            dma('sp', kv_d.ap()[256:512, :].rearrange("r (e c) -> (r e) c", c=256)[r0:r0 + 128, :], T.qsq[:, 0:256], R=[T.qsq], W=[dkv])
                        else:
                            dma('sp', o_sk[:, :], T.kn[0:NS, :], R=[T.kn], is_out=True)
                            dma('sp', o_sv[:, :], T.kvtm[0:NS, 256:512], R=[T.kvtm], is_out=True)
                            dma('sp', snew_d.ap()[:, 0:256], T.kn[0:NS, :], R=[T.kn], W=[dsnew])
                            dma('sp', snew_d.ap()[:, 256:512], T.kvtm[0:NS, 256:512], R=[T.kvtm], W=[dsnew])
                            pb2 = bank()
                            for hp in range(2):
                                tr(pb2[:, hp * NS:(hp + 1) * NS], T.kn[0:NS, hp * 128:(hp + 1) * 128], ident[0:NS, 0:NS], [T.kn, ident], [pb2])
                            act(T.qn[:, :, 0:NS], pb2[:, 0:2 * NS].rearrange("p (a t) -> p a t", a=2), AF.Copy, [pb2], [T.qn], scale=0.125)
                            dma('sp', knT_d.ap(), T.qn[:, :, 0:NS], R=[T.qn], W=[dknT])
                P.dma('pool', lambda h: h.collective_compute("AllGather", ALU.bypass, replica_groups=[[0, 1], [2, 3], [4, 5], [6, 7]], ins=[kv_d.ap().opt()], outs=[kvall_d.ap().opt()]), reads=[dkv], writes=[dkvall], inc=1)
            if l + 1 < DEPTH:
                inproj(l + 1, T)
            P.phase_end()

        P.phase_begin()
        T = common("Z")
        for ti in range(NPT // 128 + 1):
            ntok = 128 if ti < NPT // 128 else NS
            ch = min(ti // 4, 4)
            c0 = (ti % 4) * 128 if ch < 4 else 0
            s = T.stg
            for half in range(2):
                pb = bank()
                for kk in range(4):
                    k = half * 4 + kk
                    tr(pb[0:ntok, kk * 128:(kk + 1) * 128], hT[ch][:, k, c0:c0 + ntok], ident[:], [hT[ch], ident], [pb])
                if half == 0:
                    act(s[0:ntok, 0:512], pb[0:ntok, :], AF.Copy, [pb], [s])
                else:
                    cp(s[0:ntok, 512:1024], pb[0:ntok, :], [pb], [s])
            dst = y_p[ti * 128:(ti + 1) * 128, :] if ti < NPT // 128 else y_s
            dma('sp', dst, s[0:ntok, :], R=[s], is_out=True)
        P.finish()
        P.phase_end()
    return nc


_NC_CACHE = {}


def _prep_inputs(inp):
    f = np.float32
    g = lambda k: np.asarray(inp[k])
    gcols = np.zeros((128, 13, 8), f)
    vecs = [g('g_mix')[l] for l in range(4)] + [g('g_ffn')[l] for l in range(4)] + [g('g_mem')[l] for l in range(4)] + [g('g_kv')]
    for i, v in enumerate(vecs):
        gcols[:, i, :] = v.reshape(8, 128).T
    hv = [g('g_mq')[l] for l in range(4)] + [g('g_mk')[l] for l in range(4)] + [g('g_q')[j] for j in range(2)] + [g('g_k')] + [np.ones(64, f)]
    ghead = np.zeros((128, 12, 64), f)
    gheadp = np.zeros((128, 12), f)
    for i, v in enumerate(hv):
        ghead[:, i, :] = v[None, :]
        gheadp[:, i] = np.concatenate([v, v])

    def pair_layout(a):
        sh = a.shape
        a = a.reshape((24, 2, 64) + sh[2:])
        a = np.moveaxis(a, 0, 2)
        return np.ascontiguousarray(a.reshape((128, 24) + sh[2:]))
    s5a = np.zeros((2, 128, 3, 24), f)
    s5b = np.zeros((2, 2, 128, 24, 16), f)
    s5c = np.zeros((2, 2, 128, 24, 16), f)
    s5d = np.zeros((2, 96, 8), f)
    for l in range(2):
        s5a[l, :, 0, :] = pair_layout(g('ssm_a_re')[l])
        s5a[l, :, 1, :] = pair_layout(g('ssm_a_im')[l])
        s5a[l, :, 2, :] = pair_layout(np.repeat(g('ssm_log_dt')[l][:, None], 64, axis=1))
        s5b[l, 0] = pair_layout(g('ssm_b_re')[l])
        s5b[l, 1] = pair_layout(g('ssm_b_im')[l])
        s5c[l, 0] = pair_layout(np.swapaxes(g('ssm_c_re')[l], 1, 2))
        s5c[l, 1] = pair_layout(np.swapaxes(g('ssm_c_im')[l], 1, 2))
        s5d[l] = g('ssm_d')[l].reshape(8, 96).T
    shared = dict(w_in=g('w_in'), w_out=g('w_out'), w_gu=g('w_gu'), w_down=g('w_down'), w_mem_kv=g('w_mem_kv'),
                  w_kv=g('w_kv'), w_glu=g('w_glu'), gcols=gcols, ghead=ghead, gheadp=gheadp, s5a=s5a, s5b=s5b, s5c=s5c, s5d=s5d)
    maps = []
    ck = np.ascontiguousarray(g('cache_k').reshape(2560, 128, 256))
    cv = np.ascontiguousarray(g('cache_v').reshape(2560, 128, 256))
    for c in range(8):
        b, half = c // 2, c % 2
        m = dict(shared)
        m['xp'] = np.ascontiguousarray(g('x_prompt')[b, half * NPT:(half + 1) * NPT, :])
        m['xs'] = np.ascontiguousarray(g('x_sample')[c * NS:(c + 1) * NS, 0, :])
        m['mem'] = np.ascontiguousarray(g('mem_prompt')[b])
        m['cmk'] = np.ascontiguousarray(g('cache_mem_k')[:, c * NS:(c + 1) * NS].reshape(4, NS, 256, 256))
        m['cmv'] = np.ascontiguousarray(g('cache_mem_v')[:, c * NS:(c + 1) * NS].reshape(4, NS, 256, 256))
        m['st0'] = np.ascontiguousarray(np.stack([g('state_ssm_re')[:, c * NS:(c + 1) * NS].reshape(2, NS, 3072),
                                                  g('state_ssm_im')[:, c * NS:(c + 1) * NS].reshape(2, NS, 3072)], axis=1))
        m['flagb'] = np.full((128, 1), float(half), f)
        m['cache_k'] = ck
        m['cache_v'] = cv
        m['ptab'] = np.ascontiguousarray(g('page_table')[c * NS:(c + 1) * NS].reshape(1, NS * 16).astype(np.int32))
        maps.append(m)
    return maps


def kernel(**inp):
    if 'nc' not in _NC_CACHE:
        _NC_CACHE['nc'] = build_program()
    nc = _NC_CACHE['nc']
    maps = _prep_inputs(inp)
    res = run_bass_kernel_spmd(nc, maps, core_ids=list(range(8))).results
    f = np.float32
    y_prompt = np.zeros((4, 4096, D), f)
    y_sample = np.zeros((128, 1, D), f)
    p_k = np.zeros((4, 4096, 4, 64), f)
    p_v = np.zeros((4, 4096, 4, 64), f)
    p_mem_k = np.zeros((4, 4, 256, 4, 64), f)
    p_mem_v = np.zeros((4, 4, 256, 4, 64), f)
    s_k = np.zeros((128, 1, 4, 64), f)
    s_v = np.zeros((128, 1, 4, 64), f)
    p_re = np.zeros((2, 4, 48, 64), f)
    p_im = np.zeros((2, 4, 48, 64), f)
    s_re = np.zeros((2, 128, 48, 64), f)
    s_im = np.zeros((2, 128, 48, 64), f)
    for c in range(8):
        b, half = c // 2, c % 2
        r = res[c]
        y_prompt[b, half * NPT:(half + 1) * NPT] = r['y_p']
        y_sample[c * NS:(c + 1) * NS, 0] = r['y_s']
        p_k[b, half * NPT:(half + 1) * NPT] = r['o_pk'].reshape(NPT, 4, 64)
        p_v[b, half * NPT:(half + 1) * NPT] = r['o_pv'].reshape(NPT, 4, 64)
        s_k[c * NS:(c + 1) * NS, 0] = r['o_sk'].reshape(NS, 4, 64)
        s_v[c * NS:(c + 1) * NS, 0] = r['o_sv'].reshape(NS, 4, 64)
        s_re[:, c * NS:(c + 1) * NS] = r['o_sssm'][:, 0].reshape(2, NS, 48, 64)
        s_im[:, c * NS:(c + 1) * NS] = r['o_sssm'][:, 1].reshape(2, NS, 48, 64)
        if half == 0:
            p_mem_k[:, b] = r['o_pmk'].reshape(4, 256, 4, 64)
            p_mem_v[:, b] = r['o_pmv'].reshape(4, 256, 4, 64)
        else:
            p_re[:, b] = r['o_pssm'][:, 0].reshape(2, 48, 64)
            p_im[:, b] = r['o_pssm'][:, 1].reshape(2, 48, 64)
    return (y_prompt, y_sample, p_re, p_im, p_k, p_v, p_mem_k, p_mem_v, s_re, s_im, s_k, s_v)
```

```python
import numpy as np
import concourse.bass as bass
import concourse.mybir as mybir
from concourse.bass_utils import run_bass_kernel_spmd
from contextlib import ExitStack

F32 = mybir.dt.float32
BF16 = mybir.dt.bfloat16
I32 = mybir.dt.int32
U32 = mybir.dt.uint32
AF = mybir.ActivationFunctionType
ALU = mybir.AluOpType
AX = mybir.AxisListType


class Buf:
    __slots__ = ('t', 'w', 'r', 'dsem', 'dcnt', 'name', 'es', 'uid')
    _next = [0]

    def __init__(self, t, name=None):
        self.t = t
        self.w = None
        self.r = {}
        self.dsem = None
        self.dcnt = 0
        self.name = name
        self.es = None
        Buf._next[0] += 1
        self.uid = 'b%d' % Buf._next[0]

    def __getitem__(self, k):
        return self.t[k]


class Prog:
    ENG = ('pe', 'act', 'dve', 'pool', 'sp')

    def __init__(self, nc, es):
        self.nc = nc
        self.es = es
        self.q = {e: [] for e in self.ENG}
        self.cnt = {e: 0 for e in self.ENG}
        self.sem = {e: es.enter_context(nc.semaphore('s_' + e)) for e in self.ENG}
        self.seen = {e: {} for e in self.ENG}
        self.nsem = 0
        self.out_events = []
        self.all_dma = []
        self.sem_es = es

    def sb(self, name, shape, dt):
        b = Buf(self.es.enter_context(self.nc.sbuf_tensor(name, list(shape), dt)), name)
        b.es = self.es
        return b

    def ps(self, name, shape, dt=F32):
        return Buf(self.es.enter_context(self.nc.psum_tensor(name, list(shape), dt)), name)

    def view(self, t, name=None):
        return Buf(t, name)

    def _waits(self, eng, reads, writes):
        need = {}

        def add(ev):
            if ev is None:
                return
            k, v = ev
            if k == 'pe' and eng == 'pe':
                return
            if need.get(k, 0) < v:
                need[k] = v
        for b in reads:
            add(b.w)
        for b in writes:
            add(b.w)
            for k, v in b.r.items():
                add((k, v))
        out = []
        for k, v in need.items():
            if self.seen[eng].get(k, 0) >= v:
                continue
            self.seen[eng][k] = v
            out.append((k, v))
        return out

    def op(self, eng, fn, reads=(), writes=()):
        waits = self._waits(eng, reads, writes)
        self.cnt[eng] += 1
        t = self.cnt[eng]
        for b in reads:
            if b.r.get(eng, 0) < t:
                b.r[eng] = t
        for b in writes:
            b.w = (eng, t)
            b.r = {}
        sem = self.sem

        def run(h):
            for k, v in waits:
                h.wait_ge(sem[k], v)
            fn(h).then_inc(sem[eng], 1)
        self.q[eng].append(run)

    def _dsem(self, owner):
        if owner.dsem is None:
            self.nsem += 1
            owner.dsem = (owner.es or self.sem_es).enter_context(self.nc.semaphore('d%d' % self.nsem))
            self.sem[owner.uid] = owner.dsem
        return owner.dsem

    def dma(self, q, fn, reads=(), writes=(), owner=None, is_out=False, inc=16):
        waits = self._waits(q, reads, writes)
        if owner is None:
            owner = writes[0] if writes else reads[0]
        ds = self._dsem(owner)
        owner.dcnt += inc
        ev = (owner.uid, owner.dcnt)
        for b in reads:
            b.r[ev[0]] = ev[1]
        for b in writes:
            b.w = ev
            b.r = {}
        if is_out:
            self.out_events.append(ev)
        self.all_dma.append(ev)
        sem = self.sem

        def run(h):
            for k, v in waits:
                h.wait_ge(sem[k], v)
            fn(h).then_inc(ds, inc)
        self.q[q].append(run)

    def phase_begin(self):
        self._saved_es = self.es
        self.es = ExitStack()
        self.es.__enter__()
        self._sems_es = self._saved_es

    def phase_end(self):
        last = {}
        for k, v in self.all_dma:
            last[k] = max(last.get(k, 0), v)
        self.all_dma = []
        self.out_events = []
        waits = list(last.items())
        sem = self.sem

        def run(h):
            for k, v in waits:
                h.wait_ge(sem[k], v)
        self.q['sp'].append(run)
        self.build()
        self.q = {e: [] for e in self.ENG}
        self.es.__exit__(None, None, None)
        self.es = self._saved_es

    def finish(self):
        last = {}
        for k, v in self.out_events:
            last[k] = max(last.get(k, 0), v)
        waits = list(last.items())
        sem = self.sem

        def run(h):
            for k, v in waits:
                h.wait_ge(sem[k], v)
        self.q['sp'].append(run)

    def build(self):
        q = self.q
        with self.nc.Block() as blk:
            @blk.tensor
            def _(h):
                for f in q['pe']:
                    f(h)

            @blk.scalar
            def _(h):
                for f in q['act']:
                    f(h)

            @blk.vector
            def _(h):
                for f in q['dve']:
                    f(h)

            @blk.gpsimd
            def _(h):
                for f in q['pool']:
                    f(h)

            @blk.sync
            def _(h):
                for f in q['sp']:
                    f(h)


import math
from types import SimpleNamespace

D = 1024
NPT = 2048
NS = 16
NT = NPT + NS
DFF = 2816
DEPTH = 4
CHUNKS = [(0, 512), (512, 512), (1024, 512), (1536, 512), (2048, 16)]
EPS = 1e-6
PI = math.pi
STAGE = 9


def build_program(stage=STAGE):
    nc = bass.Bass("TRN2", target_bir_lowering=False)

    def din(name, shape, dt=F32):
        return nc.dram_tensor(name, list(shape), dt, kind="ExternalInput").ap()

    def dout(name, shape, dt=F32):
        return nc.dram_tensor(name, list(shape), dt, kind="ExternalOutput").ap()

    xp = din("xp", [NPT, D])
    xs = din("xs", [NS, D])
    mem = din("mem", [256, D])
    w_in = din("w_in", [DEPTH, D, D])
    w_out = din("w_out", [DEPTH, D, D])
    w_gu = din("w_gu", [DEPTH, D, 2 * DFF])
    w_down = din("w_down", [DEPTH, DFF, D])
    w_mem_kv = din("w_mem_kv", [DEPTH, D, 512])
    w_kv = din("w_kv", [D, 512])
    w_glu = din("w_glu", [2, 768, 768])
    gcols = din("gcols", [128, 13, 8])
    ghead = din("ghead", [128, 12, 64])
    gheadp = din("gheadp", [128, 12])
    cmk = din("cmk", [DEPTH, NS, 256, 256])
    cmv = din("cmv", [DEPTH, NS, 256, 256])
    s5a = din("s5a", [2, 128, 3, 24])
    s5b = din("s5b", [2, 2, 128, 24, 16])
    s5c = din("s5c", [2, 2, 128, 24, 16])
    s5d = din("s5d", [2, 96, 8])
    st0 = din("st0", [2, 2, NS, 3072])
    flagb = din("flagb", [128, 1])
    cache_k = din("cache_k", [2560, 128, 256])
    cache_v = din("cache_v", [2560, 128, 256])
    ptab = din("ptab", [1, NS * 16], I32)

    y_p = dout("y_p", [NPT, D])
    y_s = dout("y_s", [NS, D])
    o_pmk = dout("o_pmk", [DEPTH, 256, 256])
    o_pmv = dout("o_pmv", [DEPTH, 256, 256])
    o_pk = dout("o_pk", [NPT, 256])
    o_pv = dout("o_pv", [NPT, 256])
    o_sk = dout("o_sk", [NS, 256])
    o_sv = dout("o_sv", [NS, 256])
    o_pssm = dout("o_pssm", [2, 2, 24, 128])
    o_sssm = dout("o_sssm", [2, 2, NS, 3072])

    xin_d = nc.dram_tensor("xin_d", [128, 48], F32)
    xout_d = nc.dram_tensor("xout_d", [256, 48], F32)
    kv_d = nc.dram_tensor("kv_d", [512, 2048], BF16)
    kvall_d = nc.dram_tensor("kvall_d", [1024, 2048], BF16)
    snew_d = nc.dram_tensor("snew_d", [NS, 512], F32)
    knT_d = nc.dram_tensor("knT_d", [128, 2, NS], BF16)

    with ExitStack() as es:
        P = Prog(nc, es)
        dxin = P.view(xin_d, "xin_d")
        dxout = P.view(xout_d, "xout_d")
        dkv = P.view(kv_d, "kv_d")
        dkvall = P.view(kvall_d, "kvall_d")
        dsnew = P.view(snew_d, "snew_d")
        dknT = P.view(knT_d, "knT_d")

        def tt(out, a, b, op, R, W, eng='dve'):
            P.op(eng, lambda h: h.tensor_tensor(out=out, in0=a, in1=b, op=op), reads=R, writes=W)

        def ts(out, a, s1, op0, R, W, s2=None, op1=None, eng='dve'):
            if op1 is None:
                P.op(eng, lambda h: h.tensor_scalar(out=out, in0=a, scalar1=s1, scalar2=None, op0=op0), reads=R, writes=W)
            else:
                P.op(eng, lambda h: h.tensor_scalar(out=out, in0=a, scalar1=s1, scalar2=s2, op0=op0, op1=op1), reads=R, writes=W)

        def stt(out, a, sc, b, op0, op1, R, W):
            P.op('dve', lambda h: h.scalar_tensor_tensor(out=out, in0=a, scalar=sc, in1=b, op0=op0, op1=op1), reads=R, writes=W)

        def act(out, a, func, R, W, scale=1.0, bias=None):
            if bias is None:
                P.op('act', lambda h: h.activation(out=out, in_=a, func=func, scale=scale), reads=R, writes=W)
            else:
                P.op('act', lambda h: h.activation(out=out, in_=a, func=func, scale=scale, bias=bias), reads=R, writes=W)

        def cp(out, a, R, W, eng='dve'):
            P.op(eng, lambda h: h.tensor_copy(out=out, in_=a), reads=R, writes=W)

        def mset(ap, val, W, eng='pool'):
            P.op(eng, lambda h: h.memset(ap, val), writes=W)

        def mm(out, lhsT, rhs, start, stop, R, W):
            P.op('pe', lambda h: h.matmul(out, lhsT=lhsT, rhs=rhs, start=start, stop=stop), reads=R, writes=W)

        def tr(out, in_, idn, R, W):
            P.op('pe', lambda h: h.transpose(out, in_, idn), reads=R, writes=W)

        def dma(q, out, in_, R=(), W=(), is_out=False):
            P.dma(q, lambda h: h.dma_start(out=out, in_=in_), reads=list(R), writes=list(W), is_out=is_out)

        ident = P.sb("ident", [128, 128], F32)
        onesb = P.sb("onesb", [128, 128], BF16)
        blk64 = P.sb("blk64", [128, 128], BF16)
        epsc = P.sb("epsc", [128, 1], F32)
        hpic = P.sb("hpic", [128, 1], F32)
        gc = P.sb("gc", [128, 13, 8], F32)
        gh = P.sb("gh", [128, 12, 64], F32)
        ghp = P.sb("ghp", [128, 12], F32)
        selrow = P.sb("selrow", [128, 128], F32)
        flg = P.sb("flg", [128, 1], F32)
        pidx = P.sb("pidx", [128, NS * 16], I32)
        hT = [P.sb("hT%d" % i, [128, 8, w], F32) for i, (_, w) in enumerate(CHUNKS)]
        Bm_all = es.enter_context(nc.sbuf_tensor("Bm_all", [128, 10, NT], BF16))
        Bm = [P.view(Bm_all[:, :, c0_:c0_ + w_], "Bm%d" % i_) for i_, (c0_, w_) in enumerate(CHUNKS)]
        wb0 = P.sb("wb0", [128, 8192], BF16)
        MKT = P.sb("MKT", [128, DEPTH, 2, 256], BF16)
        MVA = P.sb("MVA", [128, DEPTH, 2, 4, 128], BF16)
        banks = [P.ps("bank%d" % i, [128, 512], F32) for i in range(8)]
        st = {'bank': 0, 'pT': 0, 'ffn': 0}

        def bank():
            b = banks[st['bank'] % 8]
            st['bank'] += 1
            return b

        def wb0v(k):
            return wb0[:, :].rearrange("p (k n) -> p k n", k=k)

        def common(pfx):
            T = SimpleNamespace()
            T.aT = P.sb(pfx + "aT", [128, 8, 512], BF16)
            T.sqs = [P.sb(pfx + "sq%d" % i, [128, 512], BF16) for i in range(2)]
            T.rs = P.sb(pfx + "rs", [128, 512], F32)
            T.stg = P.sb(pfx + "stg", [128, 1024], F32)
            T.kvtm = P.sb(pfx + "kvtm", [128, 512], F32)
            T.ssq = P.sb(pfx + "ssq", [128, 256], F32)
            T.hs4 = P.sb(pfx + "hs4", [128, 4], F32)
            T.kn = P.sb(pfx + "kn", [128, 256], F32)
            T.qsq = P.sb(pfx + "qsq", [128, 512], BF16)
            T.qrs = P.sb(pfx + "qrs", [128, 512], F32)
            T.pT = [P.sb(pfx + "pT%d" % i, [128, 512], BF16) for i in range(4)]
            T.osb = P.sb(pfx + "osb", [128, 512], F32)
            T.rden = P.sb(pfx + "rden", [128, 512], F32)
            T.qn = P.sb(pfx + "qn", [128, 2, 512], BF16)
            return T

        def rmsnorm_chunk(T, srcbuf, w, gidx, dstbuf, dst_tile0=0):
            pb = bank()
            for k in range(8):
                sq = T.sqs[k % 2]
                act(sq[:, :w], srcbuf[:, k, :w], AF.Square, [srcbuf], [sq])
                mm(pb[:, :w], onesb[:], sq[:, :w], k == 0, k == 7, [onesb, sq], [pb])
            act(T.rs[:, :w], pb[:, :w], AF.Ln, [pb, epsc], [T.rs], scale=1.0 / D, bias=epsc[:, 0:1])
            act(T.rs[:, :w], T.rs[:, :w], AF.Exp, [T.rs], [T.rs], scale=-0.5)
            for k in range(8):
                stt(dstbuf[:, dst_tile0 + k, :w], srcbuf[:, k, :w], gc[:, gidx, k:k + 1], T.rs[:, :w], ALU.mult, ALU.mult, [srcbuf, gc, T.rs], [dstbuf])

        def load_sq_weight(src, p, nt, ncols):
            v = wb0[:, 0:nt * ncols].rearrange("p (t n) -> p t n", t=nt)
            dma('pool', v[0:p, :, :], src.rearrange("(t p) n -> p t n", p=p), W=[wb0])
            return v

        def headnorm_tokmajor(T, src, srcbufs, gidx, dst, nrows=128):
            tt(T.ssq[0:nrows, :], src, src, ALU.mult, srcbufs, [T.ssq])
            P.op('dve', lambda h: h.tensor_reduce(out=T.hs4[0:nrows, :], in_=T.ssq[0:nrows, :].rearrange("p (h d) -> p h d", h=4), axis=AX.X, op=ALU.add), reads=[T.ssq], writes=[T.hs4])
            act(T.hs4[0:nrows, :], T.hs4[0:nrows, :], AF.Sqrt, [T.hs4, epsc], [T.hs4], scale=1.0 / 64, bias=epsc[0:nrows, 0:1])
            P.op('dve', lambda h: h.reciprocal(out=T.hs4[0:nrows, :], in_=T.hs4[0:nrows, :]), reads=[T.hs4], writes=[T.hs4])
            for hd in range(4):
                stt(dst[0:nrows, hd * 64:(hd + 1) * 64], src[:, hd * 64:(hd + 1) * 64], T.hs4[0:nrows, hd:hd + 1], gh[0:nrows, gidx, :], ALU.mult, ALU.mult, srcbufs + [T.hs4, gh], [dst])

        def headnorm_feat(T, buf, tile, w, gidx, dst, dtile):
            act(T.qsq[:, :w], buf[:, tile, :w], AF.Square, [buf], [T.qsq])
            pb = bank()
            mm(pb[:, :w], blk64[:], T.qsq[:, :w], True, True, [blk64, T.qsq], [pb])
            act(T.qrs[:, :w], pb[:, :w], AF.Ln, [pb, epsc], [T.qrs], scale=1.0 / 64, bias=epsc[:, 0:1])
            act(T.qrs[:, :w], T.qrs[:, :w], AF.Exp, [T.qrs], [T.qrs], scale=-0.5)
            stt(dst[:, dtile, :w], buf[:, tile, :w], ghp[:, gidx:gidx + 1], T.qrs[:, :w], ALU.mult, ALU.mult, [buf, ghp, T.qrs], [dst])

        def attend(T, keys, q_ap, qbufs, w, par, dst, dtile, kbufs, dc0=0):
            po = bank()
            n = len(keys)
            for i, (kap, vap) in enumerate(keys):
                ps = bank()
                mm(ps[:, :w], kap, q_ap, True, True, kbufs + qbufs, [ps])
                pt = T.pT[st['pT'] % 4]
                st['pT'] += 1
                act(pt[:, :w], ps[:, :w], AF.Exp, [ps], [pt])
                mm(po[:, :w], vap, pt[:, :w], i == 0, i == n - 1, kbufs + [pt], [po])
            lo = par * 64
            drow = 64 if par == 0 else 0
            act(T.osb[lo:lo + 64, :w], po[lo:lo + 64, :w], AF.Copy, [po], [T.osb])
            P.op('dve', lambda h: h.reciprocal(out=T.rden[drow:drow + 1, :w], in_=po[drow:drow + 1, :w]), reads=[po], writes=[T.rden])
            pbc = bank()
            mm(pbc[:, :w], selrow[drow:drow + 1, :], T.rden[drow:drow + 1, :w], True, True, [selrow, T.rden], [pbc])
            tt(dst[lo:lo + 64, dtile, dc0:dc0 + w], T.osb[lo:lo + 64, :w], pbc[lo:lo + 64, :w], ALU.mult, [T.osb, pbc], [dst])

        P.phase_begin()
        T = common("A")
        stg2 = P.sb("Astg2", [128, 1024], F32)
        memT = P.sb("memT", [128, 8, 128], F32)
        mnT = P.sb("mnT", [128, 8, 128], BF16)
        mset(ident[:], 0.0, [ident])
        P.op('pool', lambda h: h.affine_select(out=ident[:], in_=ident[:], pattern=[[-1, 128]], compare_op=ALU.not_equal, fill=1.0, base=0, channel_multiplier=1), reads=[ident], writes=[ident])
        mset(onesb[:], 1.0, [onesb])
        mset(blk64[:], 0.0, [blk64])
        mset(blk64[0:64, 0:64], 1.0, [blk64])
        mset(blk64[64:128, 64:128], 1.0, [blk64])
        mset(epsc[:], EPS, [epsc])
        mset(hpic[:], PI / 2, [hpic])
        mset(selrow[:], 0.0, [selrow])
        mset(selrow[64:65, 0:64], 1.0, [selrow])
        mset(selrow[0:1, 64:128], 1.0, [selrow])
        dma('sp', gc[:], gcols, W=[gc])
        dma('sp', gh[:], ghead, W=[gh])
        dma('sp', ghp[:], gheadp, W=[ghp])
        dma('sp', flg[:], flagb, W=[flg])
        pti = P.sb("pti", [128, NS * 16], I32)
        ptf = P.sb("ptf", [128, NS * 16], F32)
        iot = P.sb("iot", [128, 1], I32)
        iof = P.sb("iof", [128, 1], F32)
        dma('sp', pti[:], ptab.partition_broadcast(128), W=[pti])
        P.op('pool', lambda h: h.iota(iot[:], pattern=[[0, 1]], base=0, channel_multiplier=1), writes=[iot])
        cp(iof[:], iot[:], [iot], [iof])
        cp(ptf[:], pti[:], [pti], [ptf])
        ts(ptf[:], ptf[:], 128.0, ALU.mult, [ptf, iof], [ptf], s2=iof[:, 0:1], op1=ALU.add)
        cp(pidx[:], ptf[:], [ptf], [pidx])
        mset(MVA[:], 0.0, [MVA])
        for hd in range(4):
            dcol = 64 if hd % 2 == 0 else 0
            mset(MVA[:, :, :, hd, dcol:dcol + 1], 1.0, [MVA])

        for ti in range(NPT // 128 + 1):
            ntok = 128 if ti < NPT // 128 else NS
            s = T.stg if ti % 2 == 0 else stg2
            src = xp[ti * 128:(ti + 1) * 128, :] if ti < NPT // 128 else xs
            dma('sp', s[0:ntok, :], src, W=[s])
            ch = min(ti // 4, 4)
            c0 = (ti % 4) * 128 if ch < 4 else 0
            for half in range(2):
                pb = bank()
                for kk in range(4):
                    k = half * 4 + kk
                    tr(pb[:, kk * 128:kk * 128 + ntok], s[0:ntok, k * 128:(k + 1) * 128], ident[0:ntok, 0:ntok], [s, ident], [pb])
                src_ap = pb[:, :].rearrange("p (k t) -> p k t", k=4)[:, :, 0:ntok]
                dst_ap = hT[ch][:, half * 4:half * 4 + 4, c0:c0 + ntok]
                if half == 0:
                    act(dst_ap, src_ap, AF.Copy, [pb], [hT[ch]])
                else:
                    cp(dst_ap, src_ap, [pb], [hT[ch]])

        for ti in range(2):
            s = T.stg
            dma('sp', s[:, :], mem[ti * 128:(ti + 1) * 128, :], W=[s])
            for half in range(2):
                pb = bank()
                for kk in range(4):
                    k = half * 4 + kk
                    tr(pb[:, kk * 128:(kk + 1) * 128], s[:, k * 128:(k + 1) * 128], ident[:], [s, ident], [pb])
                act(memT[:, half * 4:half * 4 + 4, :], pb[:, :].rearrange("p (k t) -> p k t", k=4), AF.Copy, [pb], [memT])
            for l in range(DEPTH):
                rmsnorm_chunk(T, memT, 128, 8 + l, mnT)
                wv = load_sq_weight(w_mem_kv[l], 128, 8, 512)
                pb = bank()
                for k in range(8):
                    mm(pb[:, :], mnT[:, k, :], wv[:, k, :], k == 0, k == 7, [mnT, wb0], [pb])
                act(T.kvtm[:], pb[:], AF.Copy, [pb], [T.kvtm])
                headnorm_tokmajor(T, T.kvtm[:, 0:256], [T.kvtm], 4 + l, T.kn)
                dma('sp', o_pmk[l, ti * 128:(ti + 1) * 128, :], T.kn[:, :], R=[T.kn], is_out=True)
                dma('sp', o_pmv[l, ti * 128:(ti + 1) * 128, :], T.kvtm[:, 256:512], R=[T.kvtm], is_out=True)
                for hd in range(4):
                    c0 = (hd % 2) * 64
                    cp(MVA[:, l, ti, hd, c0:c0 + 64], T.kvtm[:, 256 + hd * 64:256 + (hd + 1) * 64], [T.kvtm], [MVA], eng='pool')
                pb2 = bank()
                for hp in range(2):
                    tr(pb2[:, hp * 128:(hp + 1) * 128], T.kn[:, hp * 128:(hp + 1) * 128], ident[:], [T.kn, ident], [pb2])
                act(MKT[:, l, :, ti * 128:(ti + 1) * 128], pb2[:, 0:256].rearrange("p (a t) -> p a t", a=2), AF.Copy, [pb2], [MKT], scale=0.125)
        P.phase_end()

        def inproj(l, T):
            is_s5 = l < 2
            wv = load_sq_weight(w_in[l], 128, 8, 1024)
            for ch, (c0, w) in enumerate(CHUNKS):
                rmsnorm_chunk(T, hT[ch], w, l, T.aT)
                if is_s5:
                    outs = [(t, 96 * t, 96) for t in range(8)]
                else:
                    outs = [(t, 128 * t, 128) for t in range(6)]
                outs += [(8, 768, 128), (9, 896, 128)]
                for i, (slot, col0, mw) in enumerate(outs):
                    pb = bank()
                    for k in range(8):
                        mm(pb[0:mw, :w], wv[:, k, col0:col0 + mw], T.aT[:, k, :w], k == 0, k == 7, [wb0, T.aT], [pb])
                    if i % 2 == 0:
                        act(Bm[ch][0:mw, slot, :w], pb[0:mw, :w], AF.Copy, [pb], [Bm[ch]])
                    else:
                        cp(Bm[ch][0:mw, slot, :w], pb[0:mw, :w], [pb], [Bm[ch]])

        def s5_layer(l):
            P.phase_begin()
            nm = lambda n: "s5%s_%d" % (n, l)
            prm = P.sb(nm("prm"), [128, 3, 24], F32)
            PB = [P.sb(nm("PB%d" % i), [128, 24, 16], F32) for i in range(2)]
            PC = [P.sb(nm("PC%d" % i), [128, 24, 16], F32) for i in range(2)]
            dcol = P.sb(nm("dcol"), [128, 8], F32)
            V = {n: P.sb(nm(n), [128, 24], F32) for n in
                 ["dt", "lre", "mag", "ang", "r", "kf", "m1", "c1", "s1", "abr", "abi", "nabi", "den", "nre", "t1", "t2", "fre", "fim", "m8", "c8", "s8"]}
            ki = P.sb(nm("ki"), [128, 24], I32)
            PWr = P.sb(nm("PWr"), [128, 9, 24], F32)
            PWi = P.sb(nm("PWi"), [128, 9, 24], F32)
            P2r = P.sb(nm("P2r"), [128, 8, 24], F32)
            P2i = P.sb(nm("P2i"), [128, 8, 24], F32)
            H0 = [P.sb(nm("H0%d" % i), [128, 24, 16], F32) for i in range(2)]
            XN = [P.sb(nm("XN%d" % i), [128, 24, 16], F32) for i in range(2)]
            H0b = P.sb(nm("H0b"), [128, 24, 2, 16], BF16)
            FIN = P.sb(nm("FIN"), [128, 24, 2], F32)
            XI = P.sb(nm("XI"), [128, 24, 2], F32)
            PFIN = P.sb(nm("PFIN"), [128, 2, 24], F32)
            z0 = P.sb(nm("z0"), [128, 4], F32)
            BsT = P.sb(nm("BsT"), [128, 8, 2, 128], BF16)
            Kf = P.sb(nm("Kf"), [128, 8, 96], BF16)
            K0f = P.sb(nm("K0f"), [128, 96], F32)
            CjT = P.sb(nm("CjT"), [128, 3, 8, 2, 96], BF16)
            BBw = P.sb(nm("BBw"), [128, 3, 2, 96], F32)
            Cw = P.sb(nm("Cw"), [128, 3, 2, 96], F32)
            Rc3 = P.sb(nm("Rc3"), [128, 3, 256], F32)
            Rs3 = P.sb(nm("Rs3"), [128, 3, 256], F32)
            SpB = P.sb(nm("SpB"), [128, 1536], F32)
            ZB = P.sb(nm("ZB"), [128, 1728], F32)
            taB = P.sb(nm("ta"), [128, 768], F32)
            rtmp = XN[0]
            u8 = P.sb(nm("u8"), [128, 8, 256], BF16)
            Xprev = P.sb(nm("Xprev"), [128, 3, 2, 256], BF16)
            TS8 = SpB[:, :].rearrange("p (k r n) -> p k r n", k=8, r=2)
            Sp = [SpB[:, 0:256], SpB[:, 256:512]]
            sstg = SpB
            ACw = ZB[:, :].rearrange("p (k r n) -> p k r n", k=9, r=2)
            Z = [ZB[:, 0:256], ZB[:, 256:512]]
            t8a = taB[:, :].rearrange("p (k n) -> p k n", k=8)
            ta, tb, M8t = taB[:, 0:256], taB[:, 256:512], taB[:, 512:768]
            rtab_d = nc.dram_tensor(nm("rtab"), [24, 128, 2, 256], F32)
            bst_d = nc.dram_tensor(nm("bst"), [8, 128, 8 * 2 * 128], BF16)
            drtab = P.view(rtab_d, "rtab")
            dbst = P.view(bst_d, "bst")
            cnt = {'ac': 0}

            dma('sp', prm[:], s5a[l], W=[prm])
            for i in range(2):
                dma('sp', PB[i][:], s5b[l, i], W=[PB[i]])
                dma('sp', PC[i][:], s5c[l, i], W=[PC[i]])
            dma('sp', dcol[0:96, :], s5d[l], W=[dcol])
            mset(K0f[:], 0.0, [K0f])
            mset(Kf[:], 0.0, [Kf])
            are, aim, ldt = prm[:, 0, :], prm[:, 1, :], prm[:, 2, :]
            v = lambda n: V[n][:, :]
            act(v("dt"), ldt, AF.Exp, [prm], [V["dt"]])
            ts(v("lre"), are, -1e-4, ALU.min, [prm], [V["lre"]])
            tt(v("t1"), v("dt"), v("lre"), ALU.mult, [V["dt"], V["lre"]], [V["t1"]])
            act(v("mag"), v("t1"), AF.Exp, [V["t1"]], [V["mag"]])
            tt(v("ang"), v("dt"), aim, ALU.mult, [V["dt"], prm], [V["ang"]])
            ts(v("kf"), v("ang"), 1.0 / (2 * PI), ALU.mult, [V["ang"]], [V["kf"]])
            cp(ki[:, :], v("kf"), [V["kf"]], [ki])
            cp(v("kf"), ki[:, :], [ki], [V["kf"]])
            stt(v("r"), v("kf"), -2 * PI, v("ang"), ALU.mult, ALU.add, [V["kf"], V["ang"]], [V["r"]])
            ts(v("m1"), v("r"), PI, ALU.is_gt, [V["r"]], [V["m1"]])
            stt(v("r"), v("m1"), -2 * PI, v("r"), ALU.mult, ALU.add, [V["m1"], V["r"]], [V["r"]])
            ts(v("m1"), v("r"), -PI, ALU.is_lt, [V["r"]], [V["m1"]])
            stt(v("r"), v("m1"), 2 * PI, v("r"), ALU.mult, ALU.add, [V["m1"], V["r"]], [V["r"]])
            ts(v("r"), v("r"), PI, ALU.min, [V["r"]], [V["r"]], s2=-PI, op1=ALU.max)
            act(v("s1"), v("r"), AF.Sin, [V["r"]], [V["s1"]])
            stt(v("t1"), v("r"), -1.0, v("r"), ALU.mult, ALU.max, [V["r"]], [V["t1"]])
            act(v("c1"), v("t1"), AF.Sin, [V["t1"], hpic], [V["c1"]], scale=-1.0, bias=hpic[:, 0:1])
            tt(v("abr"), v("mag"), v("c1"), ALU.mult, [V["mag"], V["c1"]], [V["abr"]])
            tt(v("abi"), v("mag"), v("s1"), ALU.mult, [V["mag"], V["s1"]], [V["abi"]])
            ts(v("nabi"), v("abi"), -1.0, ALU.mult, [V["abi"]], [V["nabi"]])
            tt(v("den"), v("lre"), v("lre"), ALU.mult, [V["lre"]], [V["den"]])
            tt(v("t1"), aim, aim, ALU.mult, [prm], [V["t1"]])
            tt(v("den"), v("den"), v("t1"), ALU.add, [V["den"], V["t1"]], [V["den"]])
            P.op('dve', lambda h: h.reciprocal(out=v("den"), in_=v("den")), reads=[V["den"]], writes=[V["den"]])
            ts(v("nre"), v("abr"), -1.0, ALU.add, [V["abr"]], [V["nre"]])
            tt(v("t1"), v("nre"), v("lre"), ALU.mult, [V["nre"], V["lre"]], [V["t1"]])
            tt(v("t2"), v("abi"), aim, ALU.mult, [V["abi"], prm], [V["t2"]])
            tt(v("t1"), v("t1"), v("t2"), ALU.add, [V["t1"], V["t2"]], [V["t1"]])
            tt(v("fre"), v("t1"), v("den"), ALU.mult, [V["t1"], V["den"]], [V["fre"]])
            tt(v("t1"), v("abi"), v("lre"), ALU.mult, [V["abi"], V["lre"]], [V["t1"]])
            tt(v("t2"), v("nre"), aim, ALU.mult, [V["nre"], prm], [V["t2"]])
            tt(v("t1"), v("t1"), v("t2"), ALU.subtract, [V["t1"], V["t2"]], [V["t1"]])
            tt(v("fim"), v("t1"), v("den"), ALU.mult, [V["t1"], V["den"]], [V["fim"]])
            frb = v("fre").unsqueeze(2).broadcast_to([128, 24, 16])
            fib = v("fim").unsqueeze(2).broadcast_to([128, 24, 16])
            tt(XN[0][:], PB[1][:], fib, ALU.mult, [PB[1], V["fim"]], [XN[0]])
            tt(XN[1][:], PB[0][:], fib, ALU.mult, [PB[0], V["fim"]], [XN[1]])
            tt(PB[0][:], PB[0][:], frb, ALU.mult, [PB[0], V["fre"]], [PB[0]])
            tt(PB[0][:], PB[0][:], XN[0][:], ALU.subtract, [PB[0], XN[0]], [PB[0]])
            tt(PB[1][:], PB[1][:], frb, ALU.mult, [PB[1], V["fre"]], [PB[1]])
            tt(PB[1][:], PB[1][:], XN[1][:], ALU.add, [PB[1], XN[1]], [PB[1]])
            mset(PWr[:, 0, :], 1.0, [PWr])
            mset(PWi[:, 0, :], 0.0, [PWi])
            for k in range(1, 9):
                tt(v("t1"), PWr[:, k - 1, :], v("abr"), ALU.mult, [PWr, V["abr"]], [V["t1"]])
                tt(v("t2"), PWi[:, k - 1, :], v("abi"), ALU.mult, [PWi, V["abi"]], [V["t2"]])
                tt(PWr[:, k, :], v("t1"), v("t2"), ALU.subtract, [V["t1"], V["t2"]], [PWr])
                tt(v("t1"), PWr[:, k - 1, :], v("abi"), ALU.mult, [PWr, V["abi"]], [V["t1"]])
                tt(v("t2"), PWi[:, k - 1, :], v("abr"), ALU.mult, [PWi, V["abr"]], [V["t2"]])
                tt(PWi[:, k, :], v("t1"), v("t2"), ALU.add, [V["t1"], V["t2"]], [PWi])
            tt(v("m8"), v("mag"), v("mag"), ALU.mult, [V["mag"]], [V["m8"]])
            tt(v("m8"), v("m8"), v("m8"), ALU.mult, [V["m8"]], [V["m8"]])
            tt(v("m8"), v("m8"), v("m8"), ALU.mult, [V["m8"]], [V["m8"]])

            def csquare(cr_o, ci_o, cr_i, ci_i, R, W):
                tt(v("t1"), cr_i, cr_i, ALU.mult, R, [V["t1"]])
                tt(v("t2"), ci_i, ci_i, ALU.mult, R, [V["t2"]])
                tt(v("den"), cr_i, ci_i, ALU.mult, R, [V["den"]])
                tt(cr_o, v("t1"), v("t2"), ALU.subtract, [V["t1"], V["t2"]], W)
                ts(ci_o, v("den"), 2.0, ALU.mult, [V["den"]], W)
            csquare(v("c8"), v("s8"), v("c1"), v("s1"), [V["c1"], V["s1"]], [V["c8"], V["s8"]])
            csquare(v("c8"), v("s8"), v("c8"), v("s8"), [V["c8"], V["s8"]], [V["c8"], V["s8"]])
            csquare(v("c8"), v("s8"), v("c8"), v("s8"), [V["c8"], V["s8"]], [V["c8"], V["s8"]])
            cp(P2r[:, 0, :], v("c8"), [V["c8"]], [P2r])
            cp(P2i[:, 0, :], v("s8"), [V["s8"]], [P2i])
            for k in range(1, 8):
                csquare(P2r[:, k, :], P2i[:, k, :], P2r[:, k - 1, :], P2i[:, k - 1, :], [P2r, P2i], [P2r, P2i])

            for ri in range(2):
                for blk in range(6):
                    dma('sp', sstg[0:NS, 0:512], st0[l, ri, :, blk * 512:(blk + 1) * 512], W=[sstg])
                    pb = bank()
                    for i in range(4):
                        tr(pb[:, i * 16:(i + 1) * 16], sstg[0:NS, i * 128:(i + 1) * 128], ident[0:NS, 0:NS], [sstg, ident], [pb])
                    cp(H0[ri][:, 4 * blk:4 * blk + 4, :], pb[:, 0:64].rearrange("p (a s) -> p a s", a=4), [pb], [H0[ri]])
                cp(H0b[:, :, ri, :], H0[ri][:], [H0[ri]], [H0b])

            def build_wide(q, qi):
                for ri in range(2):
                    for g2 in range(2):
                        c0 = 32 * qi + 16 * g2
                        cp(BBw[g2 * 64:(g2 + 1) * 64, qi, ri, c0:c0 + 16], PB[ri][g2 * 64:(g2 + 1) * 64, q, :], [PB[ri]], [BBw], eng='pool')
                        cp(Cw[g2 * 64:(g2 + 1) * 64, qi, ri, c0:c0 + 16], PC[ri][g2 * 64:(g2 + 1) * 64, q, :], [PC[ri]], [Cw], eng='pool')

            def build_Bs(q, qi):
                pr = PWr[:, 0:8, q:q + 1].broadcast_to([128, 8, 96])
                pi = PWi[:, 0:8, q:q + 1].broadcast_to([128, 8, 96])
                Br = BBw[:, qi, 0, :].unsqueeze(1).broadcast_to([128, 8, 96])
                Bi = BBw[:, qi, 1, :].unsqueeze(1).broadcast_to([128, 8, 96])
                tt(t8a, Bi, pi, ALU.mult, [BBw, PWi], [taB])
                tt(TS8[:, :, 0, :], Br, pr, ALU.mult, [BBw, PWr], [SpB])
                tt(TS8[:, :, 0, :], TS8[:, :, 0, :], t8a, ALU.subtract, [SpB, taB], [SpB])
                tt(t8a, Br, pi, ALU.mult, [BBw, PWi], [taB])
                tt(TS8[:, :, 1, :], Bi, pr, ALU.mult, [BBw, PWr], [SpB])
                tt(TS8[:, :, 1, :], TS8[:, :, 1, :], t8a, ALU.add, [SpB, taB], [SpB])
                for k in range(8):
                    pb = bank()
                    tr(pb[0:96, 0:128], TS8[:, k, 0, :], ident[:], [SpB, ident], [pb])
                    tr(pb[0:96, 128:256], TS8[:, k, 1, :], ident[:], [SpB, ident], [pb])
                    act(BsT[32 * qi:32 * qi + 32, k, :, :], pb[32 * qi:32 * qi + 32, 0:256].rearrange("p (r n) -> p r n", r=2), AF.Copy, [pb], [BsT])

            def build_AC(q, qi):
                pr = PWr[:, 0:9, q:q + 1].broadcast_to([128, 9, 96])
                pi = PWi[:, 0:9, q:q + 1].broadcast_to([128, 9, 96])
                Cr = Cw[:, qi, 0, :].unsqueeze(1).broadcast_to([128, 9, 96])
                Ci = Cw[:, qi, 1, :].unsqueeze(1).broadcast_to([128, 9, 96])
                t9 = SpB[:, 0:864].rearrange("p (k n) -> p k n", k=9)
                tt(t9, Ci, pi, ALU.mult, [Cw, PWi], [SpB])
                tt(ACw[:, :, 0, :], Cr, pr, ALU.mult, [Cw, PWr], [ZB])
                tt(ACw[:, :, 0, :], ACw[:, :, 0, :], t9, ALU.subtract, [ZB, SpB], [ZB])
                tt(t9, Cr, pi, ALU.mult, [Cw, PWi], [SpB])
                tt(ACw[:, :, 1, :], Ci, pr, ALU.mult, [Cw, PWr], [ZB])
                stt(ACw[:, :, 1, :], ACw[:, :, 1, :], -1.0, t9, ALU.mult, ALU.subtract, [ZB, SpB], [ZB])

            def build_wide_triple(t):
                mset(BBw[:], 0.0, [BBw])
                mset(Cw[:], 0.0, [Cw])
                for qi in range(3):
                    build_wide(3 * t + qi, qi)

            def build_conv_tables(t):
                for qi in range(3):
                    q = 3 * t + qi
                    build_AC(q, qi)
                    cp(CjT[:, qi, :, :, :], ACw[:, 1:9, :, :], [ZB], [CjT], eng='pool')
                    pK = bank()
                    for tau in range(8):
                        for ri in range(2):
                            mm(pK[0:96, tau * 32:(tau + 1) * 32], BBw[:, qi, ri, :], ACw[:, tau, ri, 32 * qi:32 * qi + 32], ri == 0, ri == 1, [BBw, ZB], [pK])
                    cp(Kf[32 * qi:32 * qi + 32, 1:8, 32 * qi:32 * qi + 32], pK[32 * qi:32 * qi + 32, 32:256].rearrange("p (k n) -> p k n", k=7), [pK], [Kf])
                    cp(K0f[32 * qi:32 * qi + 32, 32 * qi:32 * qi + 32], pK[32 * qi:32 * qi + 32, 0:32], [pK], [K0f])
                stt(Kf[0:96, 0, :], ident[0:96, 0:96], dcol[0:96, t:t + 1], K0f[0:96, :], ALU.mult, ALU.add, [ident, dcol, K0f], [Kf])

            def deint(t):
                act(u8[0:96, :, :], Bm_all[0:96, t, 0:NPT].rearrange("p (c s) -> p s c", s=8), AF.Copy, [Bm[0], Bm[1], Bm[2], Bm[3]], [u8])

            def build_rot_triple(t):
                q0 = 3 * t
                mset(Rc3[:, :, 0:1], 1.0, [Rc3])
                mset(Rs3[:, :, 0:1], 0.0, [Rs3])
                for k in range(8):
                    n = 1 << k
                    pr = P2r[:, k, q0:q0 + 3].unsqueeze(2).broadcast_to([128, 3, n])
                    pi = P2i[:, k, q0:q0 + 3].unsqueeze(2).broadcast_to([128, 3, n])
                    tmp = rtmp[:, :, :].rearrange("p a b -> p (a b)")[:, 0:3 * n].rearrange("p (a n) -> p a n", a=3)
                    tt(tmp, Rs3[:, :, 0:n], pi, ALU.mult, [Rs3, P2i], [rtmp], eng='pool')
                    tt(Rc3[:, :, n:2 * n], Rc3[:, :, 0:n], pr, ALU.mult, [Rc3, P2r], [Rc3], eng='pool')
                    tt(Rc3[:, :, n:2 * n], Rc3[:, :, n:2 * n], tmp, ALU.subtract, [Rc3, rtmp], [Rc3], eng='pool')
                    tt(tmp, Rc3[:, :, 0:n], pi, ALU.mult, [Rc3, P2i], [rtmp], eng='pool')
                    tt(Rs3[:, :, n:2 * n], Rs3[:, :, 0:n], pr, ALU.mult, [Rs3, P2r], [Rs3], eng='pool')
                    tt(Rs3[:, :, n:2 * n], Rs3[:, :, n:2 * n], tmp, ALU.add, [Rs3, rtmp], [Rs3], eng='pool')

            def scan_pair(t, qi, use_init, full):
                q = 3 * t + qi
                Rc, Rs = Rc3[:, qi, :], Rs3[:, qi, :]
                cp(M8t, V["m8"][:, q:q + 1].broadcast_to([128, 256]), [V["m8"]], [taB])
                pS = bank()
                for ri in range(2):
                    for s_ in range(8):
                        mm(pS[:, ri * 256:(ri + 1) * 256], BsT[32 * qi:32 * qi + 32, 7 - s_, ri, :], u8[32 * qi:32 * qi + 32, s_, :], s_ == 0, s_ == 7, [BsT, u8], [pS])
                Sr, Si = pS[:, 0:256], pS[:, 256:512]
                tt(ta, Sr, Rc, ALU.mult, [pS, Rc3], [taB])
                tt(tb, Si, Rs, ALU.mult, [pS, Rs3], [taB])
                tt(Sp[0], ta, tb, ALU.add, [taB], [SpB])
                tt(ta, Si, Rc, ALU.mult, [pS, Rc3], [taB])
                tt(tb, Sr, Rs, ALU.mult, [pS, Rs3], [taB])
                tt(Sp[1], ta, tb, ALU.subtract, [taB], [SpB])
                if use_init:
                    xr, xi = XI[:, q, 0:1], XI[:, q, 1:2]
                    c8q, s8q = V["c8"][:, q:q + 1], V["s8"][:, q:q + 1]
                    ts(z0[:, 2:3], xi, s8q, ALU.mult, [XI, V["s8"]], [z0])
                    stt(z0[:, 0:1], xr, c8q, z0[:, 2:3], ALU.mult, ALU.subtract, [XI, V["c8"], z0], [z0])
                    ts(z0[:, 2:3], xr, s8q, ALU.mult, [XI, V["s8"]], [z0])
                    stt(z0[:, 1:2], xi, c8q, z0[:, 2:3], ALU.mult, ALU.add, [XI, V["c8"], z0], [z0])
                for ri in range(2):
                    init = z0[:, ri:ri + 1] if use_init else 0.0
                    P.op('dve', lambda h, ri=ri, init=init: h.tensor_tensor_scan(out=Z[ri], data0=M8t, data1=Sp[ri], initial=init, op0=ALU.mult, op1=ALU.add), reads=[taB, SpB, z0], writes=[ZB])
                if not full:
                    tt(ta[:, 0:1], Z[1][:, 255:256], Rs[:, 255:256], ALU.mult, [ZB, Rs3], [taB])
                    tt(tb[:, 0:1], Z[0][:, 255:256], Rc[:, 255:256], ALU.mult, [ZB, Rc3], [taB])
                    tt(FIN[:, q, 0:1], tb[:, 0:1], ta[:, 0:1], ALU.subtract, [taB], [FIN])
                    tt(ta[:, 0:1], Z[0][:, 255:256], Rs[:, 255:256], ALU.mult, [ZB, Rs3], [taB])
                    tt(tb[:, 0:1], Z[1][:, 255:256], Rc[:, 255:256], ALU.mult, [ZB, Rc3], [taB])
                    tt(FIN[:, q, 1:2], tb[:, 0:1], ta[:, 0:1], ALU.add, [taB], [FIN])
                    return
                tt(ta, Z[0], Rc, ALU.mult, [ZB, Rc3], [taB])
                tt(tb, Z[1], Rs, ALU.mult, [ZB, Rs3], [taB])
                tt(Sp[0], ta, tb, ALU.subtract, [taB], [SpB])
                tt(ta, Z[1], Rc, ALU.mult, [ZB, Rc3], [taB])
                tt(tb, Z[0], Rs, ALU.mult, [ZB, Rs3], [taB])
                tt(Sp[1], ta, tb, ALU.add, [taB], [SpB])
                for ri in range(2):
                    cp(Xprev[:, qi, ri, 1:256], Sp[ri][:, 0:255], [SpB], [Xprev], eng='pool')
                    cp(Xprev[:, qi, ri, 0:1], XI[:, q, ri:ri + 1], [XI], [Xprev], eng='pool')
                    cp(PFIN[:, ri, q:q + 1], Sp[ri][:, 255:256], [SpB], [PFIN], eng='pool')

            for t in range(8):
                deint(t)
                build_wide_triple(t)
                build_rot_triple(t)
                for qi in range(3):
                    build_Bs(3 * t + qi, qi)
                dma('sp', bst_d.ap()[t], BsT[:, :, :, :].rearrange("p k r n -> p (k r n)"), R=[BsT], W=[dbst])
                for qi in range(3):
                    dma('sp', rtab_d.ap()[3 * t + qi, :, 0, :], Rc3[:, qi, :], R=[Rc3], W=[drtab])
                    dma('sp', rtab_d.ap()[3 * t + qi, :, 1, :], Rs3[:, qi, :], R=[Rs3], W=[drtab])
                for qi in range(3):
                    scan_pair(t, qi, False, False)
            dma('sp', xin_d.ap(), FIN[:, :, :].rearrange("p q r -> p (q r)"), R=[FIN], W=[dxin])
            P.dma('pool', lambda h: h.collective_compute("AllGather", ALU.bypass, replica_groups=[[0, 1], [2, 3], [4, 5], [6, 7]], ins=[xin_d.ap().opt()], outs=[xout_d.ap().opt()]), reads=[dxin], writes=[dxout], inc=1)
            dma('sp', XI[:, :, :].rearrange("p q r -> p (q r)"), xout_d.ap()[0:128, :], R=[dxout], W=[XI])
            ts(XI[:, :, :], XI[:, :, :], flg[:, 0:1], ALU.mult, [XI, flg], [XI])
            for t in range(8):
                deint(t)
                dma('sp', BsT[:, :, :, :].rearrange("p k r n -> p (k r n)"), bst_d.ap()[t], R=[dbst], W=[BsT])
                for qi in range(3):
                    dma('sp', Rc3[:, qi, :], rtab_d.ap()[3 * t + qi, :, 0, :], R=[drtab], W=[Rc3])
                    dma('sp', Rs3[:, qi, :], rtab_d.ap()[3 * t + qi, :, 1, :], R=[drtab], W=[Rs3])
                build_wide_triple(t)
                build_conv_tables(t)
                for qi in range(3):
                    scan_pair(t, qi, True, True)
                pos = [bank() for _ in range(4)]
                for j in range(8):
                    po = pos[j // 2]
                    o = po[0:96, (j % 2) * 256:(j % 2 + 1) * 256]
                    nmm = (j + 1) + 6
                    i = 0
                    for tau in range(j + 1):
                        mm(o, Kf[0:96, tau, :], u8[0:96, j - tau, :], i == 0, i == nmm - 1, [Kf, u8], [po])
                        i += 1
                    for qi in range(3):
                        for ri in range(2):
                            mm(o, CjT[:, qi, j, ri, :], Xprev[:, qi, ri, :], i == 0, i == nmm - 1, [CjT, Xprev], [po])
                            i += 1
                for ch in range(4):
                    for b in range(4):
                        src = pos[b][0:96, :].rearrange("p (j c) -> p j c", j=2)[:, :, ch * 64:(ch + 1) * 64]
                        dst = Bm[ch][0:96, t, :].rearrange("p (c j) -> p j c", j=8)[:, 2 * b:2 * b + 2, :]
                        act(dst, src, AF.Gelu, [pos[b]], [Bm[ch]])
                pq = bank()
                i = 0
                mm(pq[0:96, 0:NS], Kf[0:96, 0, :], Bm[4][0:96, t, 0:NS], True, False, [Kf, Bm[4]], [pq])
                for qi in range(3):
                    for ri in range(2):
                        i += 1
                        mm(pq[0:96, 0:NS], CjT[:, qi, 0, ri, :], H0b[:, 3 * t + qi, ri, :], False, i == 6, [CjT, H0b], [pq])
                for qi in range(3):
                    q = 3 * t + qi
                    pu = bank()
                    for ri in range(2):
                        mm(pu[:, ri * NS:(ri + 1) * NS], BsT[32 * qi:32 * qi + 32, 0, ri, :], Bm[4][32 * qi:32 * qi + 32, t, 0:NS], True, True, [BsT, Bm[4]], [pu])
                    abr_q, abi_q, nabi_q = V["abr"][:, q:q + 1], V["abi"][:, q:q + 1], V["nabi"][:, q:q + 1]
                    ts(ta[:, 0:NS], H0[0][:, q, :], abr_q, ALU.mult, [H0[0], V["abr"]], [taB])
                    tt(ta[:, 0:NS], ta[:, 0:NS], pu[:, 0:NS], ALU.add, [taB, pu], [taB])
                    stt(XN[0][:, q, :], H0[1][:, q, :], nabi_q, ta[:, 0:NS], ALU.mult, ALU.add, [H0[1], V["nabi"], taB], [XN[0]])
                    ts(tb[:, 0:NS], H0[1][:, q, :], abr_q, ALU.mult, [H0[1], V["abr"]], [taB])
                    tt(tb[:, 0:NS], tb[:, 0:NS], pu[:, NS:2 * NS], ALU.add, [taB, pu], [taB])
                    stt(XN[1][:, q, :], H0[0][:, q, :], abi_q, tb[:, 0:NS], ALU.mult, ALU.add, [H0[0], V["abi"], taB], [XN[1]])
                act(Bm[4][0:96, t, 0:NS], pq[0:96, 0:NS], AF.Gelu, [pq], [Bm[4]])
            for ri in range(2):
                pb = bank()
                tr(pb[0:24, 0:128], PFIN[:, ri, :], ident[:], [PFIN, ident], [pb])
                cp(sstg[0:24, 0:128], pb[0:24, 0:128], [pb], [sstg])
                dma('sp', o_pssm[l, ri], sstg[0:24, 0:128], R=[sstg], is_out=True)
                for blk in range(6):
                    pb = bank()
                    for i in range(4):
                        tr(pb[0:NS, i * 128:(i + 1) * 128], XN[ri][:, 4 * blk + i, :], ident[:], [XN[ri], ident], [pb])
                    cp(sstg[0:NS, 0:512], pb[0:NS, :], [pb], [sstg])
                    dma('sp', o_sssm[l, ri, :, blk * 512:(blk + 1) * 512], sstg[0:NS, 0:512], R=[sstg], is_out=True)
            P.phase_end()

        def moba_layer(l):
            j = l - 2
            P.phase_begin()
            nm = lambda n: "mb%s_%d" % (n, l)
            KTaug = P.sb(nm("KT"), [128, 4096], BF16)
            VZ = P.sb(nm("VZ"), [128, 32, 192], BF16)
            KMf = P.sb(nm("KMf"), [128, 16], F32)
            KM = P.sb(nm("KM"), [128, 16], BF16)
            qaug = [P.sb(nm("qa%d" % i), [128, 512], BF16) for i in range(2)]
            MBw = [P.sb(nm("MBw%d" % i), [128, 80], F32) for i in range(4)]
            g1 = P.sb(nm("g1"), [128, 16], F32)
            m8 = P.sb(nm("m8"), [128, 8], F32)
            sel = P.sb(nm("sel"), [128, 16], F32)
            VBg = P.sb(nm("VBg"), [128, 8, 16], F32)
            vm = P.sb(nm("vm"), [128, 8, 16], F32)
            pb1 = P.sb(nm("pb1"), [128, 1], F32)
            tri = P.sb(nm("tri"), [128, 128], BF16)
            shiftsel = P.sb(nm("shs"), [128, 64], BF16)
            pTs = [P.sb(nm("pT%d" % i), [128, 512], BF16) for i in range(4)]
            osb = P.sb(nm("osb"), [128, 512], F32)
            rden = P.sb(nm("rden"), [128, 512], F32)
            qsq = P.sb(nm("qsq"), [128, 512], BF16)
            qrs = P.sb(nm("qrs"), [128, 512], F32)
            T = SimpleNamespace(qsq=qsq, qrs=qrs)
            c = {'pt': 0, 'qa': 0, 'mb': 0, 'ps': 0, 'po': 0, 'mi': 0}

            def bank_ps():
                c['ps'] += 1
                return banks[c['ps'] % 3]

            def bank_po():
                c['po'] += 1
                return banks[3 + c['po'] % 2]

            def bank_mi():
                return banks[5]

            sd = sdec_gen(l) if stage >= 4 else None
            pend = {'e': None}

            def step_sd(k):
                nonlocal sd
                for _ in range(k):
                    if sd is None:
                        return
                    try:
                        next(sd)
                    except StopIteration:
                        sd = None

            mset(tri[:], 1.0, [tri])
            P.op('pool', lambda h: h.affine_select(out=tri[:], in_=tri[:], pattern=[[1, 128]], compare_op=ALU.is_ge, fill=0.0, base=0, channel_multiplier=-1), reads=[tri], writes=[tri])
            mset(shiftsel[:], 1.0, [shiftsel])
            P.op('pool', lambda h: h.affine_select(out=shiftsel[:], in_=shiftsel[:], pattern=[[-1, 64]], compare_op=ALU.is_equal, fill=0.0, base=-64, channel_multiplier=1), reads=[shiftsel], writes=[shiftsel])
            mset(KTaug[64:80, :], 1.0, [KTaug])
            P.op('pool', lambda h: h.affine_select(out=KTaug[64:80, :], in_=KTaug[64:80, :], pattern=[[1, 4096]], compare_op=ALU.is_ge, fill=0.0, base=0, channel_multiplier=-256), reads=[KTaug], writes=[KTaug])
            P.op('pool', lambda h: h.affine_select(out=KTaug[64:80, :], in_=KTaug[64:80, :], pattern=[[-1, 4096]], compare_op=ALU.is_ge, fill=0.0, base=255, channel_multiplier=256), reads=[KTaug], writes=[KTaug])
            mset(VZ[:], 0.0, [VZ])
            mset(VZ[:, :, 64:65], 1.0, [VZ])
            for i in range(4):
                mset(MBw[i][:], 0.0, [MBw[i]])
            ts(pb1[:, :], flg[:, :], -1.0, ALU.add, [flg], [pb1], s2=1e9, op1=ALU.mult)
            mset(VBg[:], -1e9, [VBg])
            mset(vm[:], 0.0, [vm])
            for lb in range(1, 8):
                mset(VBg[:, lb, 8:8 + lb], 0.0, [VBg])
                mset(vm[:, lb, 8:8 + lb], 1.0, [vm])
            cp(VBg[:, :, 0:8], pb1[:, 0:1].unsqueeze(2).broadcast_to([128, 8, 8]), [pb1], [VBg])
            cp(vm[:, :, 0:8], flg[:, 0:1].unsqueeze(2).broadcast_to([128, 8, 8]), [flg], [vm])

            step_sd(1)
            for ch in range(4):
                for slot in range(6):
                    headnorm_feat(T, Bm[ch], slot, 512, 8 + j, Bm[ch], slot)

            vall = kvall_d.ap()[256:512, :].rearrange("r (e c) -> (r e) c", c=256)
            vloc = kv_d.ap()[256:512, :].rearrange("r (e c) -> (r e) c", c=256)
            for kvh in range(4):
                dma('sp', KTaug[0:64, 0:2048], kvall_d.ap()[kvh * 64:(kvh + 1) * 64, :], R=[dkvall], W=[KTaug])
                dma('sp', KTaug[0:64, 2048:4096], kv_d.ap()[kvh * 64:(kvh + 1) * 64, :], R=[dkv], W=[KTaug])
                for (src, dsrc, t0) in ((vall, dkvall, 0), (vloc, dkv, 16)):
                    sv = src[:, kvh * 64:(kvh + 1) * 64].rearrange("(t p) c -> p t c", p=128)
                    dma('sp', VZ[:, t0:t0 + 16, 0:64], sv, R=[dsrc], W=[VZ])
                    dma('sp', VZ[:, t0:t0 + 16, 128:192], sv, R=[dsrc], W=[VZ])
                P.op('dve', lambda h: h.tensor_reduce(out=KMf[0:64, :], in_=KTaug[0:64, :].rearrange("p (b t) -> p b t", t=256), axis=AX.X, op=ALU.add), reads=[KTaug], writes=[KMf])
                cp(KM[0:64, :], KMf[0:64, :], [KMf], [KM])
                units = [(hh, ch) for hh in range(3) for ch in range(4)]

                def prep_gen(hh, ch, qa):
                    hd = 3 * kvh + hh
                    slot, par = hd // 2, hd % 2
                    if par == 0:
                        cp(qa[0:64, :], Bm[ch][0:64, slot, :], [Bm[ch]], [qa])
                    else:
                        pq = banks[3 + (c['po'] + 1) % 2]
                        mm(pq[0:64, :], shiftsel[:, :], Bm[ch][:, slot, :], True, True, [shiftsel, Bm[ch]], [pq])
                        act(qa[0:64, :], pq[0:64, :], AF.Copy, [pq], [qa])
                    pg = banks[5]
                    mbws = []
                    for tti in range(4):
                        lb = (ch * 4 + tti) // 2
                        mm(pg[:, tti * 16:(tti + 1) * 16], qa[0:64, tti * 128:(tti + 1) * 128], KM[0:64, :], True, True, [qa, KM], [pg])
                    for tti in range(4):
                        lb = (ch * 4 + tti) // 2
                        tt(g1[:, :], pg[:, tti * 16:(tti + 1) * 16], VBg[:, lb, :], ALU.add, [pg, VBg], [g1])
                        P.op('dve', lambda h: h.max(out=m8[:, :], in_=g1[:, :]), reads=[g1], writes=[m8])
                        ts(sel[:, :], g1[:, :], m8[:, 2:3], ALU.is_ge, [g1, m8], [sel])
                        tt(sel[:, :], sel[:, :], vm[:, lb, :], ALU.mult, [sel, vm], [sel])
                        mbw = MBw[c['mb'] % 4]
                        c['mb'] += 1
                        ts(mbw[:, 64:80], sel[:, :], -1.0, ALU.add, [sel], [mbw], s2=30000.0, op1=ALU.mult)
                        mset(mbw[:, 72 + lb:73 + lb], 0.0, [mbw], eng='dve')
                        mbws.append(mbw)
                    yield
                    for tti in range(4):
                        tr(pg[0:80, tti * 128:(tti + 1) * 128], mbws[tti][:, :], ident[:], [mbws[tti], ident], [pg])
                    cp(qa[64:80, :], pg[64:80, :], [pg], [qa])
                    yield

                def drain(g):
                    for _ in g:
                        pass

                qa_cur = qaug[c['qa'] % 2]
                c['qa'] += 1
                drain(prep_gen(units[0][0], units[0][1], qa_cur))
                for ui, (hh, ch) in enumerate(units):
                    hd = 3 * kvh + hh
                    slot, par = hd // 2, hd % 2
                    lo = par * 64
                    zoff = par * 64
                    drow = 64 if par == 0 else 0
                    qa = qa_cur
                    gnext = None
                    if ui + 1 < len(units):
                        qa_next = qaug[c['qa'] % 2]
                        c['qa'] += 1
                        gnext = prep_gen(units[ui + 1][0], units[ui + 1][1], qa_next)
                    b0 = 2 * ch
                    steps = [(kt, 0, 512, False) for kt in range(16 + 2 * b0)]
                    steps += [(16 + 2 * b0, 0, 512, True), (17 + 2 * b0, 128, 384, True), (18 + 2 * b0, 256, 256, True), (19 + 2 * b0, 384, 128, True)]
                    po = bank_po()
                    n = len(steps)
                    LA = 2
                    live = {}
                    for i in range(n + LA):
                        if i % 8 == 5:
                            step_sd(1)
                        if i == 4 and pend['e'] is not None:
                            pend['e']()
                            pend['e'] = None
                        if gnext is not None and (i == 6 or i == 14):
                            next(gnext)
                        if i < n:
                            kt, q0, nq, dotri = steps[i]
                            ps = bank_ps()
                            pt = pTs[c['pt'] % 4]
                            c['pt'] += 1
                            mm(ps[:, 0:nq], KTaug[0:80, kt * 128:(kt + 1) * 128], qa[0:80, q0:q0 + nq], True, True, [KTaug, qa], [ps])
                            act(pt[:, 0:nq], ps[:, 0:nq], AF.Exp, [ps], [pt])
                            if dotri:
                                tt(pt[:, 0:128], pt[:, 0:128], tri[:, :], ALU.mult, [pt, tri], [pt])
                            live[i] = pt
                        k = i - LA
                        if k >= 0:
                            kt, q0, nq, dotri = steps[k]
                            pt = live.pop(k)
                            mm(po[:, q0:q0 + nq], VZ[:, kt, zoff:zoff + 128], pt[:, 0:nq], k == 0, k == n - 1, [VZ, pt], [po])
                    def epi1(po=po, lo=lo, drow=drow):
                        act(osb[lo:lo + 64, :], po[lo:lo + 64, :], AF.Copy, [po], [osb])
                        act(rden[drow:drow + 1, :], po[drow:drow + 1, :], AF.Ln, [po], [rden])
                        act(rden[drow:drow + 1, :], rden[drow:drow + 1, :], AF.Exp, [rden], [rden], scale=-1.0)

                    def epi2(po=po, lo=lo, drow=drow, ch=ch, slot=slot):
                        pbc = banks[5]
                        mm(pbc[:, :], selrow[drow:drow + 1, :], rden[drow:drow + 1, :], True, True, [selrow, rden], [pbc])
                        tt(Bm[ch][lo:lo + 64, slot, :], osb[lo:lo + 64, :], pbc[lo:lo + 64, :], ALU.mult, [osb, pbc], [Bm[ch]])
                    epi1()
                    pend['e'] = epi2
                    step_sd(1)
                    if gnext is not None:
                        drain(gnext)
                        qa_cur = qa_next
                if pend['e'] is not None:
                    pend['e']()
                    pend['e'] = None
            if stage < 4:
                mset(Bm[4][:, 0:6, :], 0.0, [Bm[4]])
            step_sd(100000)
            P.phase_end()

        def sdec_gen(l):
            j = l - 2
            nm = lambda n: "sd%s_%d" % (n, l)
            Kpg = [P.sb(nm("Kpg%d" % i), [128, 2, 256], F32) for i in range(3)]
            Vpg = [P.sb(nm("Vpg%d" % i), [128, 2, 256], F32) for i in range(3)]
            KTp = [P.sb(nm("KTp%d" % i), [128, 2, 256], BF16) for i in range(2)]
            VA = [P.sb(nm("VA%d" % i), [128, 2, 4, 65], BF16) for i in range(3)]
            KMs2 = [P.sb(nm("KMs%d" % i), [128, 2, 8], F32) for i in range(2)]
            KMb = P.sb(nm("KMb"), [128, 2, 8], BF16)
            QS = P.sb(nm("QS"), [128, 12, NS], BF16)
            Em = P.sb(nm("Em"), [128, 4, 128], BF16)
            Ep1 = P.sb(nm("Ep1"), [128, 128], BF16)
            identb = P.sb(nm("identb"), [128, 128], BF16)
            ones6 = P.sb(nm("ones6"), [128, 65], BF16)
            KTn = P.sb(nm("KTn"), [128, 2, NS], BF16)
            VAn = P.sb(nm("VAn"), [128, 4, 65], BF16)
            vnf = P.sb(nm("vnf"), [128, 256], F32)
            PT = [P.sb(nm("PT%d" % i), [128, 24], BF16) for i in range(2)]
            PTn = P.sb(nm("PTn"), [128, 12], BF16)
            PARTS = P.sb(nm("PARTS"), [128, 9, 12], F32)
            g = P.sb(nm("g"), [128, 2, 8], F32)
            m8 = P.sb(nm("m8"), [128, 2, 8], F32)
            sel = P.sb(nm("sel"), [128, 2, 8], F32)
            selx = P.sb(nm("selx"), [128, 2, 8, 6], BF16)
            tmp = P.sb(nm("tmp"), [128, 8, 6], F32)
            acc = P.sb(nm("acc"), [128, 12], F32)
            rden = P.sb(nm("rden"), [128, 12], F32)
            Osb = P.sb(nm("Osb"), [128, 12], BF16)
            qsq = P.sb(nm("qsq"), [128, 512], BF16)
            qrs = P.sb(nm("qrs"), [128, 512], F32)
            T = SimpleNamespace(qsq=qsq, qrs=qrs)
            c = {'k': 0}
            ckv = cache_k.rearrange("n t c -> (n t) c")
            cvv = cache_v.rearrange("n t c -> (n t) c")

            cp(identb[:], ident[:], [ident], [identb], eng='pool')
            mset(Em[:], 0.0, [Em])
            cp(Em[0:64, 0, 0:64], ident[0:64, 0:64], [ident], [Em], eng='pool')
            cp(Em[64:128, 1, 64:128], ident[64:128, 64:128], [ident], [Em], eng='pool')
            cp(Em[0:64, 2, 64:128], ident[0:64, 0:64], [ident], [Em], eng='pool')
            cp(Em[64:128, 3, 0:64], ident[64:128, 64:128], [ident], [Em], eng='pool')
            mset(Ep1[:], 0.0, [Ep1])
            cp(Ep1[0:64, 64:128], ident[0:64, 0:64], [ident], [Ep1], eng='pool')
            mset(ones6[:], 1.0, [ones6])
            for i in range(3):
                mset(VA[i][:, :, :, 64:65], 1.0, [VA[i]])
            mset(VAn[:, :, 64:65], 1.0, [VAn])
            dma('sp', KTn[:, :, :], knT_d.ap(), R=[dknT], W=[KTn])
            dma('sp', vnf[0:NS, :], snew_d.ap()[:, 256:512], R=[dsnew], W=[vnf])
            cp(VAn[0:NS, :, 0:64], vnf[0:NS, :].rearrange("p (k d) -> p k d", k=4), [vnf], [VAn], eng='pool')

            for slot in range(6):
                headnorm_feat(T, Bm[4], slot, NS, 8 + j, Bm[4], slot)
            for hd in range(12):
                par, kp = hd % 2, (hd // 3) % 2
                vsel = {(0, 0): 0, (1, 1): 1, (0, 1): 2, (1, 0): 3}[(par, kp)]
                pq = bank()
                mm(pq[:, 0:NS], Em[:, vsel, :], Bm[4][:, hd // 2, 0:NS], True, True, [Em, Bm[4]], [pq])
                cp(QS[:, hd, :], pq[:, 0:NS], [pq], [QS])

            yield
            B6 = banks[6]

            def issue_gather(step):
                if step >= NS * 8:
                    return
                kpg, vpg = Kpg[step % 3], Vpg[step % 3]
                for pg in range(2):
                    col = step * 2 + pg
                    P.dma('pool', lambda h, kpg=kpg, pg=pg, col=col: h.indirect_dma_start(out=kpg[:, pg, :], out_offset=None, in_=ckv, in_offset=bass.IndirectOffsetOnAxis(ap=pidx[:, col:col + 1], axis=0)), reads=[pidx], writes=[kpg])
                    P.dma('pool', lambda h, vpg=vpg, pg=pg, col=col: h.indirect_dma_start(out=vpg[:, pg, :], out_offset=None, in_=cvv, in_offset=bass.IndirectOffsetOnAxis(ap=pidx[:, col:col + 1], axis=0)), reads=[pidx], writes=[vpg])
            def combine(s):
                pparts = banks[6]
                KMs = KMs2[s % 2]
                for kvh in range(4):
                    mm(B6[0:NS, 160 + 3 * kvh:160 + 3 * kvh + 3], KTn[:, kvh // 2, :], QS[:, 3 * kvh:3 * kvh + 3, s], True, True, [KTn, QS], [B6])
                act(PTn[0:NS, :], B6[0:NS, 160:172], AF.Exp, [B6], [PTn])
                ts(PTn[0:NS, :], PTn[0:NS, :], ident[0:NS, s:s + 1], ALU.mult, [PTn, ident], [PTn])
                for kvh in range(4):
                    mm(pparts[0:65, 96 + 3 * kvh:96 + 3 * kvh + 3], VAn[0:NS, kvh, :], PTn[0:NS, 3 * kvh:3 * kvh + 3], True, True, [VAn, PTn], [pparts])
                cp(PARTS[0:65, :, :], pparts[0:65, 0:108].rearrange("p (b h) -> p b h", h=12), [pparts], [PARTS])
                cp(KMb[:, :, :], KMs[:, :, :], [KMs], [KMb])
                for pr in range(2):
                    mm(B6[0:6, 176 + pr * 8:176 + pr * 8 + 8], QS[:, 6 * pr:6 * pr + 6, s], KMb[:, pr, :], True, True, [QS, KMb], [B6])
                cp(g[0:6, :, :], B6[0:6, 176:192].rearrange("p (a b) -> p a b", a=2), [B6], [g])
                for pr in range(2):
                    P.op('dve', lambda h, pr=pr: h.max(out=m8[0:6, pr, :], in_=g[0:6, pr, :]), reads=[g], writes=[m8])
                    ts(sel[0:6, pr, :], g[0:6, pr, :], m8[0:6, pr, 2:3], ALU.is_ge, [g, m8], [sel])
                    tt(selx[0:6, pr, :, :], sel[0:6, pr, :].unsqueeze(2).broadcast_to([6, 8, 6]), ident[0:6, 0:6].unsqueeze(1).broadcast_to([6, 8, 6]), ALU.mult, [sel, ident], [selx])
                for pr in range(2):
                    mm(B6[0:65, 256 + pr * 48:256 + pr * 48 + 48], ones6[0:6, :], selx[0:6, pr, :, :].rearrange("p b h -> p (b h)"), True, True, [ones6, selx], [B6])
                for pr in range(2):
                    tt(tmp[0:65, :, :], PARTS[0:65, 0:8, 6 * pr:6 * pr + 6], B6[0:65, 256 + pr * 48:256 + pr * 48 + 48].rearrange("p (b h) -> p b h", h=6), ALU.mult, [PARTS, B6], [tmp])
                    P.op('dve', lambda h, pr=pr: h.tensor_reduce(out=acc[0:65, 6 * pr:6 * pr + 6], in_=tmp[0:65, :, :].rearrange("p b h -> p h b"), axis=AX.X, op=ALU.add), reads=[tmp], writes=[acc])
                tt(acc[0:65, :], acc[0:65, :], PARTS[0:65, 8, :], ALU.add, [acc, PARTS], [acc])
                P.op('dve', lambda h: h.reciprocal(out=rden[64:65, :], in_=acc[64:65, :]), reads=[acc], writes=[rden])
                mm(B6[:, 384:396], selrow[64:65, :], rden[64:65, :], True, True, [selrow, rden], [B6])
                tt(Osb[0:64, :], acc[0:64, :], B6[0:64, 384:396], ALU.mult, [acc, B6], [Osb])
                mm(B6[:, 400:406], identb[0:64, :], Osb[0:64, 0:12:2], True, False, [identb, Osb], [B6])
                mm(B6[:, 400:406], Ep1[0:64, :], Osb[0:64, 1:12:2], False, True, [Ep1, Osb], [B6])
                cp(Bm[4][:, 0:6, s], B6[:, 400:406], [B6], [Bm[4]])

            NST = NS * 8
            pparts = banks[6]

            def stageT(step):
                s_, blk = step // 8, step % 8
                kpg, vpg, ktp, va = Kpg[step % 3], Vpg[step % 3], KTp[step % 2], VA[step % 3]
                KMs = KMs2[s_ % 2]
                pk = banks[7]
                for pg in range(2):
                    for pr in range(2):
                        tr(pk[:, (pr * 2 + pg) * 128:(pr * 2 + pg + 1) * 128], kpg[:, pg, pr * 128:(pr + 1) * 128], ident[:], [kpg, ident], [pk])
                ts(ktp[:, :, :], pk[:, :].rearrange("p (a t) -> p a t", a=2), 0.125, ALU.mult, [pk], [ktp])
                P.op('dve', lambda h, ktp=ktp, blk=blk, KMs=KMs: h.tensor_reduce(out=KMs[:, :, blk], in_=ktp[:, :, :], axis=AX.X, op=ALU.add), reads=[ktp], writes=[KMs])
                cp(va[:, :, :, 0:64], vpg[:, :, :].rearrange("p g (k d) -> p g k d", k=4), [vpg], [va])

            def stageQ(step):
                s_ = step // 8
                ktp = KTp[step % 2]
                psS = B6[:, 128:152]
                for pg in range(2):
                    for kvh in range(4):
                        mm(psS[:, pg * 12 + 3 * kvh:pg * 12 + 3 * kvh + 3], ktp[:, kvh // 2, pg * 128:(pg + 1) * 128], QS[:, 3 * kvh:3 * kvh + 3, s_], True, True, [ktp, QS], [B6])
                pt = PT[step % 2]
                act(pt[:, 0:24], psS[:, 0:24], AF.Exp, [B6], [pt])

            def stageV(step):
                s_, blk = step // 8, step % 8
                va, pt = VA[step % 3], PT[step % 2]
                for kvh in range(4):
                    for pg in range(2):
                        mm(pparts[0:65, blk * 12 + 3 * kvh:blk * 12 + 3 * kvh + 3], va[:, pg, kvh, :], pt[:, pg * 12 + 3 * kvh:pg * 12 + 3 * kvh + 3], pg == 0, pg == 1, [va, pt], [pparts])

            for g_ in range(2):
                issue_gather(g_)
            for e in range(-2, NST):
                issue_gather(e + 4)
                if 0 <= e + 2 < NST:
                    stageT(e + 2)
                if 0 <= e + 1 < NST:
                    stageQ(e + 1)
                if 0 <= e < NST:
                    stageV(e)
                    if e % 8 == 7:
                        combine(e // 8)
                yield

        for l in range(DEPTH):
            is_s5 = l < 2
            if l == 0:
                P.phase_begin()
                T = common("i%d" % l)
                inproj(l, T)
                P.phase_end()

            if is_s5 and stage >= 2:
                s5_layer(l)
            elif (not is_s5) and stage >= 3:
                moba_layer(l)
            else:
                P.phase_begin()
                for ch in range(5):
                    mset(Bm[ch][:, 0:8, :], 0.0, [Bm[ch]])
                P.phase_end()

            P.phase_begin()
            T = common("o%d" % l)
            wb1 = P.sb("wb1_%d" % l, [128, 6144], BF16)
            wb2 = P.sb("wb2_%d" % l, [128, 2, 1024], BF16)
            SK = P.sb("SK_%d" % l, [128, 4, 2, 256], BF16)
            SV = P.sb("SV_%d" % l, [128, 4, 2, 256], BF16)
            Os = P.sb("Os_%d" % l, [128, 4, NS], BF16)
            identb3 = P.sb("identb3_%d" % l, [128, 128], BF16)
            Ep13 = P.sb("Ep13_%d" % l, [128, 128], BF16)
            cp(identb3[:], ident[:], [ident], [identb3], eng='pool')
            mset(Ep13[:], 0.0, [Ep13])
            cp(Ep13[0:64, 64:128], ident[0:64, 0:64], [ident], [Ep13], eng='pool')
            nmix = 8 if is_s5 else 6
            pmix = 96 if is_s5 else 128
            if is_s5 and stage >= 2:
                gv = load_sq_weight(w_glu[l], 96, 8, 768)
                for ch, (c0, w) in enumerate(CHUNKS):
                    for m in range(8):
                        pb = bank()
                        for k in range(8):
                            mm(pb[0:96, :w], gv[0:96, k, m * 96:(m + 1) * 96], Bm[ch][0:96, k, :w], k == 0, k == 7, [wb0, Bm[ch]], [pb])
                        act(T.osb[0:96, :w], pb[0:96, :w], AF.Sigmoid, [pb], [T.osb])
                        tt(T.aT[0:96, m, :w], T.osb[0:96, :w], Bm[ch][0:96, m, :w], ALU.mult, [T.osb, Bm[ch]], [T.aT])
                    for m in range(8):
                        cp(Bm[ch][0:96, m, :w], T.aT[0:96, m, :w], [T.aT], [Bm[ch]], eng='pool')
            def mem_part1(ch, hd):
                par, hp = hd % 2, hd // 2
                w = CHUNKS[ch][1]
                pts = []
                for kt in range(2):
                    ps = bank()
                    mm(ps[:, :w], MKT[par * 64:(par + 1) * 64, l, hp, kt * 128:(kt + 1) * 128], T.qn[par * 64:(par + 1) * 64, hp, :w], True, True, [MKT, T.qn], [ps])
                    pt = T.pT[st['pT'] % 4]
                    st['pT'] += 1
                    act(pt[:, :w], ps[:, :w], AF.Exp, [ps], [pt])
                    pts.append(pt)
                return pts

            def mem_part2(ch, hd, pts):
                par, hp = hd % 2, hd // 2
                w = CHUNKS[ch][1]
                po = bank()
                for kt in range(2):
                    mm(po[:, :w], MVA[:, l, kt, hd, :], pts[kt][:, :w], kt == 0, kt == 1, [MVA, pts[kt]], [po])
                lo = par * 64
                drow = 64 if par == 0 else 0
                act(T.osb[lo:lo + 64, :w], po[lo:lo + 64, :w], AF.Copy, [po], [T.osb])
                act(T.rden[drow:drow + 1, :w], po[drow:drow + 1, :w], AF.Ln, [po], [T.rden])
                act(T.rden[drow:drow + 1, :w], T.rden[drow:drow + 1, :w], AF.Exp, [T.rden], [T.rden], scale=-1.0)
                pbc = bank()
                mm(pbc[:, :w], selrow[drow:drow + 1, :], T.rden[drow:drow + 1, :w], True, True, [selrow, T.rden], [pbc])
                tt(Bm[ch][lo:lo + 64, 8 + hp, :w], T.osb[lo:lo + 64, :w], pbc[lo:lo + 64, :w], ALU.mult, [T.osb, pbc], [Bm[ch]])

            prev = None
            for ch in range(4):
                for hp in range(2):
                    headnorm_feat(T, Bm[ch], 8 + hp, 512, l, T.qn, hp)
                for hd in range(4):
                    pts = mem_part1(ch, hd)
                    if prev is not None:
                        mem_part2(*prev)
                    prev = (ch, hd, pts)
            mem_part2(*prev)
            for hp in range(2):
                headnorm_feat(T, Bm[4], 8 + hp, NS, l, T.qn, hp)
            for grp in range(4):
                for si in range(4):
                    sq_ = grp * 4 + si
                    for ti in range(2):
                        sbuf_, o0 = ((T.stg, 0) if (si * 2 + ti) % 2 == 0 else (T.kvtm, 0))
                        dma('sp', sbuf_[:, 0:256], cmk[l, sq_, ti * 128:(ti + 1) * 128, :], W=[sbuf_])
                        dma('sp', sbuf_[:, 256:512], cmv[l, sq_, ti * 128:(ti + 1) * 128, :], W=[sbuf_])
                        pb2 = bank()
                        for hp in range(2):
                            tr(pb2[:, hp * 128:(hp + 1) * 128], sbuf_[:, hp * 128:(hp + 1) * 128], ident[:], [sbuf_, ident], [pb2])
                        act(SK[:, si, :, ti * 128:(ti + 1) * 128], pb2[:, 0:256].rearrange("p (a t) -> p a t", a=2), AF.Copy, [pb2], [SK], scale=0.125)
                        cp(SV[:, si, ti, :], sbuf_[:, 256:512], [sbuf_], [SV], eng='pool')
                for hd in range(4):
                    par, hp = hd % 2, hd // 2
                    ps = bank()
                    for si in range(4):
                        sq_ = grp * 4 + si
                        for kt in range(2):
                            mm(ps[:, kt * 4 + si:kt * 4 + si + 1], SK[par * 64:(par + 1) * 64, si, hp, kt * 128:(kt + 1) * 128], T.qn[par * 64:(par + 1) * 64, hp, sq_:sq_ + 1], True, True, [SK, T.qn], [ps])
                    pt = T.pT[st['pT'] % 4]
                    st['pT'] += 1
                    act(pt[:, 0:8], ps[:, 0:8], AF.Exp, [ps], [pt])
                    po = bank()
                    for si in range(4):
                        for kt in range(2):
                            mm(po[0:64, si:si + 1], SV[:, si, kt, hd * 64:(hd + 1) * 64], pt[:, kt * 4 + si:kt * 4 + si + 1], kt == 0, kt == 1, [SV, pt], [po])
                    for kt in range(2):
                        mm(po[0:64, 8:12], onesb[:, 0:64], pt[:, kt * 4:(kt + 1) * 4], kt == 0, kt == 1, [onesb, pt], [po])
                    P.op('dve', lambda h, po=po: h.reciprocal(out=T.rden[0:64, 0:4], in_=po[0:64, 8:12]), reads=[po], writes=[T.rden])
                    act(T.osb[0:64, 0:4], po[0:64, 0:4], AF.Copy, [po], [T.osb])
                    tt(Os[0:64, hd, grp * 4:(grp + 1) * 4], T.osb[0:64, 0:4], T.rden[0:64, 0:4], ALU.mult, [T.osb, T.rden], [Os])
            for hp in range(2):
                po2 = bank()
                mm(po2[:, 0:NS], identb3[0:64, :], Os[0:64, 2 * hp, :], True, False, [identb3, Os], [po2])
                mm(po2[:, 0:NS], Ep13[0:64, :], Os[0:64, 2 * hp + 1, :], False, True, [Ep13, Os], [po2])
                cp(Bm[4][:, 8 + hp, 0:NS], po2[:, 0:NS], [po2], [Bm[4]])
            ov = load_sq_weight(w_out[l, 0:768, :], pmix, nmix, 1024)
            dma('pool', wb2[:, :, :], w_out[l, 768:1024, :].rearrange("(t p) n -> p t n", p=128), W=[wb2])
            for ch, (c0, w) in enumerate(CHUNKS):
                for m in range(8):
                    pb = bank()
                    for k in range(nmix):
                        mm(pb[:, :w], ov[0:pmix, k, m * 128:(m + 1) * 128], Bm[ch][0:pmix, k, :w], k == 0, False, [wb0, Bm[ch]], [pb])
                    for k in range(2):
                        mm(pb[:, :w], wb2[:, k, m * 128:(m + 1) * 128], Bm[ch][:, 8 + k, :w], False, k == 1, [wb2, Bm[ch]], [pb])
                    tt(hT[ch][:, m, :w], hT[ch][:, m, :w], pb[:, :w], ALU.add, [hT[ch], pb], [hT[ch]])
            for ch, (c0, w) in enumerate(CHUNKS):
                rmsnorm_chunk(T, hT[ch], w, 4 + l, Bm[ch])
            for j in range(DFF // 256):
                wbuf = wb0 if st['ffn'] % 2 == 0 else wb1
                st['ffn'] += 1
                gu = wbuf[:, 0:4096].rearrange("p (a k n) -> p a k n", a=2, k=8)
                dn = wbuf[:, 4096:6144].rearrange("p (t n) -> p t n", t=2)
                dma('pool', gu[:, 0, :, :], w_gu[l, :, 256 * j:256 * (j + 1)].rearrange("(t p) n -> p t n", p=128), W=[wbuf])
                dma('pool', gu[:, 1, :, :], w_gu[l, :, DFF + 256 * j:DFF + 256 * (j + 1)].rearrange("(t p) n -> p t n", p=128), W=[wbuf])
                dma('pool', dn[:, :, :], w_down[l, 256 * j:256 * (j + 1), :].rearrange("(t p) n -> p t n", p=128), W=[wbuf])
                for ch, (c0, w) in enumerate(CHUNKS):
                    hids = [T.pT[st['pT'] % 4], T.pT[(st['pT'] + 1) % 4]]
                    st['pT'] += 2
                    for t in range(2):
                        pg = bank()
                        pu = bank()
                        for k in range(8):
                            mm(pg[:, :w], gu[:, 0, k, t * 128:(t + 1) * 128], Bm[ch][:, k, :w], k == 0, k == 7, [wbuf, Bm[ch]], [pg])
                        for k in range(8):
                            mm(pu[:, :w], gu[:, 1, k, t * 128:(t + 1) * 128], Bm[ch][:, k, :w], k == 0, k == 7, [wbuf, Bm[ch]], [pu])
                        act(T.osb[:, :w], pg[:, :w], AF.Silu, [pg], [T.osb])
                        tt(hids[t][:, :w], T.osb[:, :w], pu[:, :w], ALU.mult, [T.osb, pu], [hids[t]])
                    for m in range(8):
                        pb = bank()
                        for t in range(2):
                            mm(pb[:, :w], dn[:, t, m * 128:(m + 1) * 128], hids[t][:, :w], t == 0, t == 1, [wbuf, hids[t]], [pb])
                        tt(hT[ch][:, m, :w], hT[ch][:, m, :w], pb[:, :w], ALU.add, [hT[ch], pb], [hT[ch]])
            if l == 1:
                kvw = load_sq_weight(w_kv, 128, 8, 512)
                for ch, (c0, w) in enumerate(CHUNKS):
                    rmsnorm_chunk(T, hT[ch], w, 12, T.aT)
                    for ti in range((w + 127) // 128):
                        nt = min(128, w - ti * 128)
                        pb = bank()
                        for k in range(8):
                            mm(pb[0:nt, :], T.aT[:, k, ti * 128:ti * 128 + nt], kvw[:, k, :], k == 0, k == 7, [T.aT, wb0], [pb])
                        act(T.kvtm[0:nt, :], pb[0:nt, :], AF.Copy, [pb], [T.kvtm])
                        headnorm_tokmajor(T, T.kvtm[0:nt, 0:256], [T.kvtm], 10, T.kn, nrows=nt)
                        if ch < 4:
                            r0 = c0 + ti * 128
                            dma('sp', o_pk[r0:r0 + 128, :], T.kn[:, :], R=[T.kn], is_out=True)
                            dma('sp', o_pv[r0:r0 + 128, :], T.kvtm[:, 256:512], R=[T.kvtm], is_out=True)
                            pb2 = bank()
                            for hp in range(2):
                                tr(pb2[:, hp * 128:(hp + 1) * 128], T.kn[:, hp * 128:(hp + 1) * 128], ident[:], [T.kn, ident], [pb2])
                            act(T.qn[:, :, 0:128], pb2[:, 0:256].rearrange("p (a t) -> p a t", a=2), AF.Copy, [pb2], [T.qn], scale=0.125)
                            dma('sp', kv_d.ap()[0:256, r0:r0 + 128].rearrange("(a p) t -> p a t", p=128), T.qn[:, :, 0:128], R=[T.qn], W=[dkv])
                            cp(T.qsq[:, 0:256], T.kvtm[:, 256:512], [T.kvtm], [T.qsq], eng='pool')
                            dma('sp', kv_d.ap()[256:512, :].rearrange("r (e c) -> (r e) c", c=256)[r0:r0 + 128, :], T.qsq[:, 0:256], R=[T.qsq], W=[dkv])
                        else:
                            dma('sp', o_sk[:, :], T.kn[0:NS, :], R=[T.kn], is_out=True)
                            dma('sp', o_sv[:, :], T.kvtm[0:NS, 256:512], R=[T.kvtm], is_out=True)
                            dma('sp', snew_d.ap()[:, 0:256], T.kn[0:NS, :], R=[T.kn], W=[dsnew])
                            dma('sp', snew_d.ap()[:, 256:512], T.kvtm[0:NS, 256:512], R=[T.kvtm], W=[dsnew])
                            pb2 = bank()
                            for hp in range(2):
                                tr(pb2[:, hp * NS:(hp + 1) * NS], T.kn[0:NS, hp * 128:(hp + 1) * 128], ident[0:NS, 0:NS], [T.kn, ident], [pb2])
                            act(T.qn[:, :, 0:NS], pb2[:, 0:2 * NS].rearrange("p (a t) -> p a t", a=2), AF.Copy, [pb2], [T.qn], scale=0.125)
                            dma('sp', knT_d.ap(), T.qn[:, :, 0:NS], R=[T.qn], W=[dknT])
                P.dma('pool', lambda h: h.collective_compute("AllGather", ALU.bypass, replica_groups=[[0, 1], [2, 3], [4, 5], [6, 7]], ins=[kv_d.ap().opt()], outs=[kvall_d.ap().opt()]), reads=[dkv], writes=[dkvall], inc=1)
            if l + 1 < DEPTH:
                inproj(l + 1, T)
            P.phase_end()

        P.phase_begin()
        T = common("Z")
        for ti in range(NPT // 128 + 1):
            ntok = 128 if ti < NPT // 128 else NS
            ch = min(ti // 4, 4)
            c0 = (ti % 4) * 128 if ch < 4 else 0
            s = T.stg
            for half in range(2):
                pb = bank()
                for kk in range(4):
                    k = half * 4 + kk
                    tr(pb[0:ntok, kk * 128:(kk + 1) * 128], hT[ch][:, k, c0:c0 + ntok], ident[:], [hT[ch], ident], [pb])
                if half == 0:
                    act(s[0:ntok, 0:512], pb[0:ntok, :], AF.Copy, [pb], [s])
                else:
                    cp(s[0:ntok, 512:1024], pb[0:ntok, :], [pb], [s])
            dst = y_p[ti * 128:(ti + 1) * 128, :] if ti < NPT // 128 else y_s
            dma('sp', dst, s[0:ntok, :], R=[s], is_out=True)
        P.finish()
        P.phase_end()
    return nc


_NC_CACHE = {}


def _prep_inputs(inp):
    f = np.float32
    g = lambda k: np.asarray(inp[k])
    gcols = np.zeros((128, 13, 8), f)
    vecs = [g('g_mix')[l] for l in range(4)] + [g('g_ffn')[l] for l in range(4)] + [g('g_mem')[l] for l in range(4)] + [g('g_kv')]
    for i, v in enumerate(vecs):
        gcols[:, i, :] = v.reshape(8, 128).T
    hv = [g('g_mq')[l] for l in range(4)] + [g('g_mk')[l] for l in range(4)] + [g('g_q')[j] for j in range(2)] + [g('g_k')] + [np.ones(64, f)]
    ghead = np.zeros((128, 12, 64), f)
    gheadp = np.zeros((128, 12), f)
    for i, v in enumerate(hv):
        ghead[:, i, :] = v[None, :]
        gheadp[:, i] = np.concatenate([v, v])

    def pair_layout(a):
        sh = a.shape
        a = a.reshape((24, 2, 64) + sh[2:])
        a = np.moveaxis(a, 0, 2)
        return np.ascontiguousarray(a.reshape((128, 24) + sh[2:]))
    s5a = np.zeros((2, 128, 3, 24), f)
    s5b = np.zeros((2, 2, 128, 24, 16), f)
    s5c = np.zeros((2, 2, 128, 24, 16), f)
    s5d = np.zeros((2, 96, 8), f)
    for l in range(2):
        s5a[l, :, 0, :] = pair_layout(g('ssm_a_re')[l])
        s5a[l, :, 1, :] = pair_layout(g('ssm_a_im')[l])
        s5a[l, :, 2, :] = pair_layout(np.repeat(g('ssm_log_dt')[l][:, None], 64, axis=1))
        s5b[l, 0] = pair_layout(g('ssm_b_re')[l])
        s5b[l, 1] = pair_layout(g('ssm_b_im')[l])
        s5c[l, 0] = pair_layout(np.swapaxes(g('ssm_c_re')[l], 1, 2))
        s5c[l, 1] = pair_layout(np.swapaxes(g('ssm_c_im')[l], 1, 2))
        s5d[l] = g('ssm_d')[l].reshape(8, 96).T
    shared = dict(w_in=g('w_in'), w_out=g('w_out'), w_gu=g('w_gu'), w_down=g('w_down'), w_mem_kv=g('w_mem_kv'),
                  w_kv=g('w_kv'), w_glu=g('w_glu'), gcols=gcols, ghead=ghead, gheadp=gheadp, s5a=s5a, s5b=s5b, s5c=s5c, s5d=s5d)
    maps = []
    ck = np.ascontiguousarray(g('cache_k').reshape(2560, 128, 256))
    cv = np.ascontiguousarray(g('cache_v').reshape(2560, 128, 256))
    for c in range(8):
        b, half = c // 2, c % 2
        m = dict(shared)
        m['xp'] = np.ascontiguousarray(g('x_prompt')[b, half * NPT:(half + 1) * NPT, :])
        m['xs'] = np.ascontiguousarray(g('x_sample')[c * NS:(c + 1) * NS, 0, :])
        m['mem'] = np.ascontiguousarray(g('mem_prompt')[b])
        m['cmk'] = np.ascontiguousarray(g('cache_mem_k')[:, c * NS:(c + 1) * NS].reshape(4, NS, 256, 256))
        m['cmv'] = np.ascontiguousarray(g('cache_mem_v')[:, c * NS:(c + 1) * NS].reshape(4, NS, 256, 256))
        m['st0'] = np.ascontiguousarray(np.stack([g('state_ssm_re')[:, c * NS:(c + 1) * NS].reshape(2, NS, 3072),
                                                  g('state_ssm_im')[:, c * NS:(c + 1) * NS].reshape(2, NS, 3072)], axis=1))
        m['flagb'] = np.full((128, 1), float(half), f)
        m['cache_k'] = ck
        m['cache_v'] = cv
        m['ptab'] = np.ascontiguousarray(g('page_table')[c * NS:(c + 1) * NS].reshape(1, NS * 16).astype(np.int32))
        maps.append(m)
    return maps


def kernel(**inp):
    if 'nc' not in _NC_CACHE:
        _NC_CACHE['nc'] = build_program()
    nc = _NC_CACHE['nc']
    maps = _prep_inputs(inp)
    res = run_bass_kernel_spmd(nc, maps, core_ids=list(range(8))).results
    f = np.float32
    y_prompt = np.zeros((4, 4096, D), f)
    y_sample = np.zeros((128, 1, D), f)
    p_k = np.zeros((4, 4096, 4, 64), f)
    p_v = np.zeros((4, 4096, 4, 64), f)
    p_mem_k = np.zeros((4, 4, 256, 4, 64), f)
    p_mem_v = np.zeros((4, 4, 256, 4, 64), f)
    s_k = np.zeros((128, 1, 4, 64), f)
    s_v = np.zeros((128, 1, 4, 64), f)
    p_re = np.zeros((2, 4, 48, 64), f)
    p_im = np.zeros((2, 4, 48, 64), f)
    s_re = np.zeros((2, 128, 48, 64), f)
    s_im = np.zeros((2, 128, 48, 64), f)
    for c in range(8):
        b, half = c // 2, c % 2
        r = res[c]
        y_prompt[b, half * NPT:(half + 1) * NPT] = r['y_p']
        y_sample[c * NS:(c + 1) * NS, 0] = r['y_s']
        p_k[b, half * NPT:(half + 1) * NPT] = r['o_pk'].reshape(NPT, 4, 64)
        p_v[b, half * NPT:(half + 1) * NPT] = r['o_pv'].reshape(NPT, 4, 64)
        s_k[c * NS:(c + 1) * NS, 0] = r['o_sk'].reshape(NS, 4, 64)
        s_v[c * NS:(c + 1) * NS, 0] = r['o_sv'].reshape(NS, 4, 64)
        s_re[:, c * NS:(c + 1) * NS] = r['o_sssm'][:, 0].reshape(2, NS, 48, 64)
        s_im[:, c * NS:(c + 1) * NS] = r['o_sssm'][:, 1].reshape(2, NS, 48, 64)
        if half == 0:
            p_mem_k[:, b] = r['o_pmk'].reshape(4, 256, 4, 64)
            p_mem_v[:, b] = r['o_pmv'].reshape(4, 256, 4, 64)
        else:
            p_re[:, b] = r['o_pssm'][:, 0].reshape(2, 48, 64)
            p_im[:, b] = r['o_pssm'][:, 1].reshape(2, 48, 64)
    return (y_prompt, y_sample, p_re, p_im, p_k, p_v, p_mem_k, p_mem_v, s_re, s_im, s_k, s_v)
```

```python
import numpy as np
import concourse.bass as bass
import concourse.mybir as mybir
from concourse.bass_utils import run_bass_kernel_spmd
from contextlib import ExitStack

F32 = mybir.dt.float32
BF16 = mybir.dt.bfloat16
I32 = mybir.dt.int32
U32 = mybir.dt.uint32
AF = mybir.ActivationFunctionType
ALU = mybir.AluOpType
AX = mybir.AxisListType


class Buf:
    __slots__ = ('t', 'w', 'r', 'dsem', 'dcnt', 'name', 'es', 'uid')
    _next = [0]

    def __init__(self, t, name=None):
        self.t = t
        self.w = None
        self.r = {}
        self.dsem = None
        self.dcnt = 0
        self.name = name
        self.es = None
        Buf._next[0] += 1
        self.uid = 'b%d' % Buf._next[0]

    def __getitem__(self, k):
        return self.t[k]


class Prog:
    ENG = ('pe', 'act', 'dve', 'pool', 'sp')

    def __init__(self, nc, es):
        self.nc = nc
        self.es = es
        self.q = {e: [] for e in self.ENG}
        self.cnt = {e: 0 for e in self.ENG}
        self.sem = {e: es.enter_context(nc.semaphore('s_' + e)) for e in self.ENG}
        self.seen = {e: {} for e in self.ENG}
        self.nsem = 0
        self.out_events = []
        self.all_dma = []
        self.sem_es = es

    def sb(self, name, shape, dt):
        b = Buf(self.es.enter_context(self.nc.sbuf_tensor(name, list(shape), dt)), name)
        b.es = self.es
        return b

    def ps(self, name, shape, dt=F32):
        return Buf(self.es.enter_context(self.nc.psum_tensor(name, list(shape), dt)), name)

    def view(self, t, name=None):
        return Buf(t, name)

    def _waits(self, eng, reads, writes):
        need = {}

        def add(ev):
            if ev is None:
                return
            k, v = ev
            if k == 'pe' and eng == 'pe':
                return
            if need.get(k, 0) < v:
                need[k] = v
        for b in reads:
            add(b.w)
        for b in writes:
            add(b.w)
            for k, v in b.r.items():
                add((k, v))
        out = []
        for k, v in need.items():
            if self.seen[eng].get(k, 0) >= v:
                continue
            self.seen[eng][k] = v
            out.append((k, v))
        return out

    def op(self, eng, fn, reads=(), writes=()):
        waits = self._waits(eng, reads, writes)
        self.cnt[eng] += 1
        t = self.cnt[eng]
        for b in reads:
            if b.r.get(eng, 0) < t:
                b.r[eng] = t
        for b in writes:
            b.w = (eng, t)
            b.r = {}
        sem = self.sem

        def run(h):
            for k, v in waits:
                h.wait_ge(sem[k], v)
            fn(h).then_inc(sem[eng], 1)
        self.q[eng].append(run)

    def _dsem(self, owner):
        if owner.dsem is None:
            self.nsem += 1
            owner.dsem = (owner.es or self.sem_es).enter_context(self.nc.semaphore('d%d' % self.nsem))
            self.sem[owner.uid] = owner.dsem
        return owner.dsem

    def dma(self, q, fn, reads=(), writes=(), owner=None, is_out=False, inc=16):
        waits = self._waits(q, reads, writes)
        if owner is None:
            owner = writes[0] if writes else reads[0]
        ds = self._dsem(owner)
        owner.dcnt += inc
        ev = (owner.uid, owner.dcnt)
        for b in reads:
            b.r[ev[0]] = ev[1]
        for b in writes:
            b.w = ev
            b.r = {}
        if is_out:
            self.out_events.append(ev)
        self.all_dma.append(ev)
        sem = self.sem

        def run(h):
            for k, v in waits:
                h.wait_ge(sem[k], v)
            fn(h).then_inc(ds, inc)
        self.q[q].append(run)

    def phase_begin(self):
        self._saved_es = self.es
        self.es = ExitStack()
        self.es.__enter__()
        self._sems_es = self._saved_es

    def phase_end(self):
        last = {}
        for k, v in self.all_dma:
            last[k] = max(last.get(k, 0), v)
        self.all_dma = []
        self.out_events = []
        waits = list(last.items())
        sem = self.sem

        def run(h):
            for k, v in waits:
                h.wait_ge(sem[k], v)
        self.q['sp'].append(run)
        self.build()
        self.q = {e: [] for e in self.ENG}
        self.es.__exit__(None, None, None)
        self.es = self._saved_es

    def finish(self):
        last = {}
        for k, v in self.out_events:
            last[k] = max(last.get(k, 0), v)
        waits = list(last.items())
        sem = self.sem

        def run(h):
            for k, v in waits:
                h.wait_ge(sem[k], v)
        self.q['sp'].append(run)

    def build(self):
        q = self.q
        with self.nc.Block() as blk:
            @blk.tensor
            def _(h):
                for f in q['pe']:
                    f(h)

            @blk.scalar
            def _(h):
                for f in q['act']:
                    f(h)

            @blk.vector
            def _(h):
                for f in q['dve']:
                    f(h)

            @blk.gpsimd
            def _(h):
                for f in q['pool']:
                    f(h)

            @blk.sync
            def _(h):
                for f in q['sp']:
                    f(h)


import math
from types import SimpleNamespace

D = 1024
NPT = 2048
NS = 16
NT = NPT + NS
DFF = 2816
DEPTH = 4
CHUNKS = [(0, 512), (512, 512), (1024, 512), (1536, 512), (2048, 16)]
EPS = 1e-6
PI = math.pi
STAGE = 9


def build_program(stage=STAGE):
    nc = bass.Bass("TRN2", target_bir_lowering=False)

    def din(name, shape, dt=F32):
        return nc.dram_tensor(name, list(shape), dt, kind="ExternalInput").ap()

    def dout(name, shape, dt=F32):
        return nc.dram_tensor(name, list(shape), dt, kind="ExternalOutput").ap()

    xp = din("xp", [NPT, D])
    xs = din("xs", [NS, D])
    mem = din("mem", [256, D])
    w_in = din("w_in", [DEPTH, D, D])
    w_out = din("w_out", [DEPTH, D, D])
    w_gu = din("w_gu", [DEPTH, D, 2 * DFF])
    w_down = din("w_down", [DEPTH, DFF, D])
    w_mem_kv = din("w_mem_kv", [DEPTH, D, 512])
    w_kv = din("w_kv", [D, 512])
    w_glu = din("w_glu", [2, 768, 768])
    gcols = din("gcols", [128, 13, 8])
    ghead = din("ghead", [128, 12, 64])
    gheadp = din("gheadp", [128, 12])
    cmk = din("cmk", [DEPTH, NS, 256, 256])
    cmv = din("cmv", [DEPTH, NS, 256, 256])
    s5a = din("s5a", [2, 128, 3, 24])
    s5b = din("s5b", [2, 2, 128, 24, 16])
    s5c = din("s5c", [2, 2, 128, 24, 16])
    s5d = din("s5d", [2, 96, 8])
    st0 = din("st0", [2, 2, NS, 3072])
    flagb = din("flagb", [128, 1])
    cache_k = din("cache_k", [2560, 128, 256])
    cache_v = din("cache_v", [2560, 128, 256])
    ptab = din("ptab", [1, NS * 16], I32)

    y_p = dout("y_p", [NPT, D])
    y_s = dout("y_s", [NS, D])
    o_pmk = dout("o_pmk", [DEPTH, 256, 256])
    o_pmv = dout("o_pmv", [DEPTH, 256, 256])
    o_pk = dout("o_pk", [NPT, 256])
    o_pv = dout("o_pv", [NPT, 256])
    o_sk = dout("o_sk", [NS, 256])
    o_sv = dout("o_sv", [NS, 256])
    o_pssm = dout("o_pssm", [2, 2, 24, 128])
    o_sssm = dout("o_sssm", [2, 2, NS, 3072])

    xin_d = nc.dram_tensor("xin_d", [128, 48], F32)
    xout_d = nc.dram_tensor("xout_d", [256, 48], F32)
    kv_d = nc.dram_tensor("kv_d", [512, 2048], BF16)
    kvall_d = nc.dram_tensor("kvall_d", [1024, 2048], BF16)
    snew_d = nc.dram_tensor("snew_d", [NS, 512], F32)
    knT_d = nc.dram_tensor("knT_d", [128, 2, NS], BF16)

    with ExitStack() as es:
        P = Prog(nc, es)
        dxin = P.view(xin_d, "xin_d")
        dxout = P.view(xout_d, "xout_d")
        dkv = P.view(kv_d, "kv_d")
        dkvall = P.view(kvall_d, "kvall_d")
        dsnew = P.view(snew_d, "snew_d")
        dknT = P.view(knT_d, "knT_d")

        def tt(out, a, b, op, R, W, eng='dve'):
            P.op(eng, lambda h: h.tensor_tensor(out=out, in0=a, in1=b, op=op), reads=R, writes=W)

        def ts(out, a, s1, op0, R, W, s2=None, op1=None, eng='dve'):
            if op1 is None:
                P.op(eng, lambda h: h.tensor_scalar(out=out, in0=a, scalar1=s1, scalar2=None, op0=op0), reads=R, writes=W)
            else:
                P.op(eng, lambda h: h.tensor_scalar(out=out, in0=a, scalar1=s1, scalar2=s2, op0=op0, op1=op1), reads=R, writes=W)

        def stt(out, a, sc, b, op0, op1, R, W):
            P.op('dve', lambda h: h.scalar_tensor_tensor(out=out, in0=a, scalar=sc, in1=b, op0=op0, op1=op1), reads=R, writes=W)

        def act(out, a, func, R, W, scale=1.0, bias=None):
            if bias is None:
                P.op('act', lambda h: h.activation(out=out, in_=a, func=func, scale=scale), reads=R, writes=W)
            else:
                P.op('act', lambda h: h.activation(out=out, in_=a, func=func, scale=scale, bias=bias), reads=R, writes=W)

        def cp(out, a, R, W, eng='dve'):
            P.op(eng, lambda h: h.tensor_copy(out=out, in_=a), reads=R, writes=W)

        def mset(ap, val, W, eng='pool'):
            P.op(eng, lambda h: h.memset(ap, val), writes=W)

        def mm(out, lhsT, rhs, start, stop, R, W):
            P.op('pe', lambda h: h.matmul(out, lhsT=lhsT, rhs=rhs, start=start, stop=stop), reads=R, writes=W)

        def tr(out, in_, idn, R, W):
            P.op('pe', lambda h: h.transpose(out, in_, idn), reads=R, writes=W)

        def dma(q, out, in_, R=(), W=(), is_out=False):
            P.dma(q, lambda h: h.dma_start(out=out, in_=in_), reads=list(R), writes=list(W), is_out=is_out)

        ident = P.sb("ident", [128, 128], F32)
        onesb = P.sb("onesb", [128, 128], BF16)
        blk64 = P.sb("blk64", [128, 128], BF16)
        epsc = P.sb("epsc", [128, 1], F32)
        hpic = P.sb("hpic", [128, 1], F32)
        gc = P.sb("gc", [128, 13, 8], F32)
        gh = P.sb("gh", [128, 12, 64], F32)
        ghp = P.sb("ghp", [128, 12], F32)
        selrow = P.sb("selrow", [128, 128], F32)
        flg = P.sb("flg", [128, 1], F32)
        pidx = P.sb("pidx", [128, NS * 16], I32)
        hT = [P.sb("hT%d" % i, [128, 8, w], F32) for i, (_, w) in enumerate(CHUNKS)]
        Bm_all = es.enter_context(nc.sbuf_tensor("Bm_all", [128, 10, NT], BF16))
        Bm = [P.view(Bm_all[:, :, c0_:c0_ + w_], "Bm%d" % i_) for i_, (c0_, w_) in enumerate(CHUNKS)]
        wb0 = P.sb("wb0", [128, 8192], BF16)
        MKT = P.sb("MKT", [128, DEPTH, 2, 256], BF16)
        MVA = P.sb("MVA", [128, DEPTH, 2, 4, 128], BF16)
        banks = [P.ps("bank%d" % i, [128, 512], F32) for i in range(8)]
        st = {'bank': 0, 'pT': 0, 'ffn': 0}

        def bank():
            b = banks[st['bank'] % 8]
            st['bank'] += 1
            return b

        def wb0v(k):
            return wb0[:, :].rearrange("p (k n) -> p k n", k=k)

        def common(pfx):
            T = SimpleNamespace()
            T.aT = P.sb(pfx + "aT", [128, 8, 512], BF16)
            T.sqs = [P.sb(pfx + "sq%d" % i, [128, 512], BF16) for i in range(2)]
            T.rs = P.sb(pfx + "rs", [128, 512], F32)
            T.stg = P.sb(pfx + "stg", [128, 1024], F32)
            T.kvtm = P.sb(pfx + "kvtm", [128, 512], F32)
            T.ssq = P.sb(pfx + "ssq", [128, 256], F32)
            T.hs4 = P.sb(pfx + "hs4", [128, 4], F32)
            T.kn = P.sb(pfx + "kn", [128, 256], F32)
            T.qsq = P.sb(pfx + "qsq", [128, 512], BF16)
            T.qrs = P.sb(pfx + "qrs", [128, 512], F32)
            T.pT = [P.sb(pfx + "pT%d" % i, [128, 512], BF16) for i in range(4)]
            T.osb = P.sb(pfx + "osb", [128, 512], F32)
            T.rden = P.sb(pfx + "rden", [128, 512], F32)
            T.qn = P.sb(pfx + "qn", [128, 2, 512], BF16)
            return T

        def rmsnorm_chunk(T, srcbuf, w, gidx, dstbuf, dst_tile0=0):
            pb = bank()
            for k in range(8):
                sq = T.sqs[k % 2]
                act(sq[:, :w], srcbuf[:, k, :w], AF.Square, [srcbuf], [sq])
                mm(pb[:, :w], onesb[:], sq[:, :w], k == 0, k == 7, [onesb, sq], [pb])
            act(T.rs[:, :w], pb[:, :w], AF.Ln, [pb, epsc], [T.rs], scale=1.0 / D, bias=epsc[:, 0:1])
            act(T.rs[:, :w], T.rs[:, :w], AF.Exp, [T.rs], [T.rs], scale=-0.5)
            for k in range(8):
                stt(dstbuf[:, dst_tile0 + k, :w], srcbuf[:, k, :w], gc[:, gidx, k:k + 1], T.rs[:, :w], ALU.mult, ALU.mult, [srcbuf, gc, T.rs], [dstbuf])

        def load_sq_weight(src, p, nt, ncols):
            v = wb0[:, 0:nt * ncols].rearrange("p (t n) -> p t n", t=nt)
            dma('pool', v[0:p, :, :], src.rearrange("(t p) n -> p t n", p=p), W=[wb0])
            return v

        def headnorm_tokmajor(T, src, srcbufs, gidx, dst, nrows=128):
            tt(T.ssq[0:nrows, :], src, src, ALU.mult, srcbufs, [T.ssq])
            P.op('dve', lambda h: h.tensor_reduce(out=T.hs4[0:nrows, :], in_=T.ssq[0:nrows, :].rearrange("p (h d) -> p h d", h=4), axis=AX.X, op=ALU.add), reads=[T.ssq], writes=[T.hs4])
            act(T.hs4[0:nrows, :], T.hs4[0:nrows, :], AF.Sqrt, [T.hs4, epsc], [T.hs4], scale=1.0 / 64, bias=epsc[0:nrows, 0:1])
            P.op('dve', lambda h: h.reciprocal(out=T.hs4[0:nrows, :], in_=T.hs4[0:nrows, :]), reads=[T.hs4], writes=[T.hs4])
            for hd in range(4):
                stt(dst[0:nrows, hd * 64:(hd + 1) * 64], src[:, hd * 64:(hd + 1) * 64], T.hs4[0:nrows, hd:hd + 1], gh[0:nrows, gidx, :], ALU.mult, ALU.mult, srcbufs + [T.hs4, gh], [dst])

        def headnorm_feat(T, buf, tile, w, gidx, dst, dtile):
            act(T.qsq[:, :w], buf[:, tile, :w], AF.Square, [buf], [T.qsq])
            pb = bank()
            mm(pb[:, :w], blk64[:], T.qsq[:, :w], True, True, [blk64, T.qsq], [pb])
            act(T.qrs[:, :w], pb[:, :w], AF.Ln, [pb, epsc], [T.qrs], scale=1.0 / 64, bias=epsc[:, 0:1])
            act(T.qrs[:, :w], T.qrs[:, :w], AF.Exp, [T.qrs], [T.qrs], scale=-0.5)
            stt(dst[:, dtile, :w], buf[:, tile, :w], ghp[:, gidx:gidx + 1], T.qrs[:, :w], ALU.mult, ALU.mult, [buf, ghp, T.qrs], [dst])

        def attend(T, keys, q_ap, qbufs, w, par, dst, dtile, kbufs, dc0=0):
            po = bank()
            n = len(keys)
            for i, (kap, vap) in enumerate(keys):
                ps = bank()
                mm(ps[:, :w], kap, q_ap, True, True, kbufs + qbufs, [ps])
                pt = T.pT[st['pT'] % 4]
                st['pT'] += 1
                act(pt[:, :w], ps[:, :w], AF.Exp, [ps], [pt])
                mm(po[:, :w], vap, pt[:, :w], i == 0, i == n - 1, kbufs + [pt], [po])
            lo = par * 64
            drow = 64 if par == 0 else 0
            act(T.osb[lo:lo + 64, :w], po[lo:lo + 64, :w], AF.Copy, [po], [T.osb])
            P.op('dve', lambda h: h.reciprocal(out=T.rden[drow:drow + 1, :w], in_=po[drow:drow + 1, :w]), reads=[po], writes=[T.rden])
            pbc = bank()
            mm(pbc[:, :w], selrow[drow:drow + 1, :], T.rden[drow:drow + 1, :w], True, True, [selrow, T.rden], [pbc])
            tt(dst[lo:lo + 64, dtile, dc0:dc0 + w], T.osb[lo:lo + 64, :w], pbc[lo:lo + 64, :w], ALU.mult, [T.osb, pbc], [dst])

        P.phase_begin()
        T = common("A")
        stg2 = P.sb("Astg2", [128, 1024], F32)
        memT = P.sb("memT", [128, 8, 128], F32)
        mnT = P.sb("mnT", [128, 8, 128], BF16)
        mset(ident[:], 0.0, [ident])
        P.op('pool', lambda h: h.affine_select(out=ident[:], in_=ident[:], pattern=[[-1, 128]], compare_op=ALU.not_equal, fill=1.0, base=0, channel_multiplier=1), reads=[ident], writes=[ident])
        mset(onesb[:], 1.0, [onesb])
        mset(blk64[:], 0.0, [blk64])
        mset(blk64[0:64, 0:64], 1.0, [blk64])
        mset(blk64[64:128, 64:128], 1.0, [blk64])
        mset(epsc[:], EPS, [epsc])
        mset(hpic[:], PI / 2, [hpic])
        mset(selrow[:], 0.0, [selrow])
        mset(selrow[64:65, 0:64], 1.0, [selrow])
        mset(selrow[0:1, 64:128], 1.0, [selrow])
        dma('sp', gc[:], gcols, W=[gc])
        dma('sp', gh[:], ghead, W=[gh])
        dma('sp', ghp[:], gheadp, W=[ghp])
        dma('sp', flg[:], flagb, W=[flg])
        pti = P.sb("pti", [128, NS * 16], I32)
        ptf = P.sb("ptf", [128, NS * 16], F32)
        iot = P.sb("iot", [128, 1], I32)
        iof = P.sb("iof", [128, 1], F32)
        dma('sp', pti[:], ptab.partition_broadcast(128), W=[pti])
        P.op('pool', lambda h: h.iota(iot[:], pattern=[[0, 1]], base=0, channel_multiplier=1), writes=[iot])
        cp(iof[:], iot[:], [iot], [iof])
        cp(ptf[:], pti[:], [pti], [ptf])
        ts(ptf[:], ptf[:], 128.0, ALU.mult, [ptf, iof], [ptf], s2=iof[:, 0:1], op1=ALU.add)
        cp(pidx[:], ptf[:], [ptf], [pidx])
        mset(MVA[:], 0.0, [MVA])
        for hd in range(4):
            dcol = 64 if hd % 2 == 0 else 0
            mset(MVA[:, :, :, hd, dcol:dcol + 1], 1.0, [MVA])

        for ti in range(NPT // 128 + 1):
            ntok = 128 if ti < NPT // 128 else NS
            s = T.stg if ti % 2 == 0 else stg2
            src = xp[ti * 128:(ti + 1) * 128, :] if ti < NPT // 128 else xs
            dma('sp', s[0:ntok, :], src, W=[s])
            ch = min(ti // 4, 4)
            c0 = (ti % 4) * 128 if ch < 4 else 0
            for half in range(2):
                pb = bank()
                for kk in range(4):
                    k = half * 4 + kk
                    tr(pb[:, kk * 128:kk * 128 + ntok], s[0:ntok, k * 128:(k + 1) * 128], ident[0:ntok, 0:ntok], [s, ident], [pb])
                src_ap = pb[:, :].rearrange("p (k t) -> p k t", k=4)[:, :, 0:ntok]
                dst_ap = hT[ch][:, half * 4:half * 4 + 4, c0:c0 + ntok]
                if half == 0:
                    act(dst_ap, src_ap, AF.Copy, [pb], [hT[ch]])
                else:
                    cp(dst_ap, src_ap, [pb], [hT[ch]])

        for ti in range(2):
            s = T.stg
            dma('sp', s[:, :], mem[ti * 128:(ti + 1) * 128, :], W=[s])
            for half in range(2):
                pb = bank()
                for kk in range(4):
                    k = half * 4 + kk
                    tr(pb[:, kk * 128:(kk + 1) * 128], s[:, k * 128:(k + 1) * 128], ident[:], [s, ident], [pb])
                act(memT[:, half * 4:half * 4 + 4, :], pb[:, :].rearrange("p (k t) -> p k t", k=4), AF.Copy, [pb], [memT])
            for l in range(DEPTH):
                rmsnorm_chunk(T, memT, 128, 8 + l, mnT)
                wv = load_sq_weight(w_mem_kv[l], 128, 8, 512)
                pb = bank()
                for k in range(8):
                    mm(pb[:, :], mnT[:, k, :], wv[:, k, :], k == 0, k == 7, [mnT, wb0], [pb])
                act(T.kvtm[:], pb[:], AF.Copy, [pb], [T.kvtm])
                headnorm_tokmajor(T, T.kvtm[:, 0:256], [T.kvtm], 4 + l, T.kn)
                dma('sp', o_pmk[l, ti * 128:(ti + 1) * 128, :], T.kn[:, :], R=[T.kn], is_out=True)
                dma('sp', o_pmv[l, ti * 128:(ti + 1) * 128, :], T.kvtm[:, 256:512], R=[T.kvtm], is_out=True)
                for hd in range(4):
                    c0 = (hd % 2) * 64
                    cp(MVA[:, l, ti, hd, c0:c0 + 64], T.kvtm[:, 256 + hd * 64:256 + (hd + 1) * 64], [T.kvtm], [MVA], eng='pool')
                pb2 = bank()
                for hp in range(2):
                    tr(pb2[:, hp * 128:(hp + 1) * 128], T.kn[:, hp * 128:(hp + 1) * 128], ident[:], [T.kn, ident], [pb2])
                act(MKT[:, l, :, ti * 128:(ti + 1) * 128], pb2[:, 0:256].rearrange("p (a t) -> p a t", a=2), AF.Copy, [pb2], [MKT], scale=0.125)
        P.phase_end()

        def inproj(l, T):
            is_s5 = l < 2
            wv = load_sq_weight(w_in[l], 128, 8, 1024)
            for ch, (c0, w) in enumerate(CHUNKS):
                rmsnorm_chunk(T, hT[ch], w, l, T.aT)
                if is_s5:
                    outs = [(t, 96 * t, 96) for t in range(8)]
                else:
                    outs = [(t, 128 * t, 128) for t in range(6)]
                outs += [(8, 768, 128), (9, 896, 128)]
                for i, (slot, col0, mw) in enumerate(outs):
                    pb = bank()
                    for k in range(8):
                        mm(pb[0:mw, :w], wv[:, k, col0:col0 + mw], T.aT[:, k, :w], k == 0, k == 7, [wb0, T.aT], [pb])
                    if i % 2 == 0:
                        act(Bm[ch][0:mw, slot, :w], pb[0:mw, :w], AF.Copy, [pb], [Bm[ch]])
                    else:
                        cp(Bm[ch][0:mw, slot, :w], pb[0:mw, :w], [pb], [Bm[ch]])

        def s5_layer(l):
            P.phase_begin()
            nm = lambda n: "s5%s_%d" % (n, l)
            prm = P.sb(nm("prm"), [128, 3, 24], F32)
            PB = [P.sb(nm("PB%d" % i), [128, 24, 16], F32) for i in range(2)]
            PC = [P.sb(nm("PC%d" % i), [128, 24, 16], F32) for i in range(2)]
            dcol = P.sb(nm("dcol"), [128, 8], F32)
            V = {n: P.sb(nm(n), [128, 24], F32) for n in
                 ["dt", "lre", "mag", "ang", "r", "kf", "m1", "c1", "s1", "abr", "abi", "nabi", "den", "nre", "t1", "t2", "fre", "fim", "m8", "c8", "s8"]}
            ki = P.sb(nm("ki"), [128, 24], I32)
            PWr = P.sb(nm("PWr"), [128, 9, 24], F32)
            PWi = P.sb(nm("PWi"), [128, 9, 24], F32)
            P2r = P.sb(nm("P2r"), [128, 8, 24], F32)
            P2i = P.sb(nm("P2i"), [128, 8, 24], F32)
            H0 = [P.sb(nm("H0%d" % i), [128, 24, 16], F32) for i in range(2)]
            XN = [P.sb(nm("XN%d" % i), [128, 24, 16], F32) for i in range(2)]
            H0b = P.sb(nm("H0b"), [128, 24, 2, 16], BF16)
            FIN = P.sb(nm("FIN"), [128, 24, 2], F32)
            XI = P.sb(nm("XI"), [128, 24, 2], F32)
            PFIN = P.sb(nm("PFIN"), [128, 2, 24], F32)
            z0 = P.sb(nm("z0"), [128, 4], F32)
            BsT = P.sb(nm("BsT"), [128, 8, 2, 128], BF16)
            Kf = P.sb(nm("Kf"), [128, 8, 96], BF16)
            K0f = P.sb(nm("K0f"), [128, 96], F32)
            CjT = P.sb(nm("CjT"), [128, 3, 8, 2, 96], BF16)
            BBw = P.sb(nm("BBw"), [128, 3, 2, 96], F32)
            Cw = P.sb(nm("Cw"), [128, 3, 2, 96], F32)
            Rc3 = P.sb(nm("Rc3"), [128, 3, 256], F32)
            Rs3 = P.sb(nm("Rs3"), [128, 3, 256], F32)
            SpB = P.sb(nm("SpB"), [128, 1536], F32)
            ZB = P.sb(nm("ZB"), [128, 1728], F32)
            taB = P.sb(nm("ta"), [128, 768], F32)
            rtmp = XN[0]
            u8 = P.sb(nm("u8"), [128, 8, 256], BF16)
            Xprev = P.sb(nm("Xprev"), [128, 3, 2, 256], BF16)
            TS8 = SpB[:, :].rearrange("p (k r n) -> p k r n", k=8, r=2)
            Sp = [SpB[:, 0:256], SpB[:, 256:512]]
            sstg = SpB
            ACw = ZB[:, :].rearrange("p (k r n) -> p k r n", k=9, r=2)
            Z = [ZB[:, 0:256], ZB[:, 256:512]]
            t8a = taB[:, :].rearrange("p (k n) -> p k n", k=8)
            ta, tb, M8t = taB[:, 0:256], taB[:, 256:512], taB[:, 512:768]
            rtab_d = nc.dram_tensor(nm("rtab"), [24, 128, 2, 256], F32)
            bst_d = nc.dram_tensor(nm("bst"), [8, 128, 8 * 2 * 128], BF16)
            drtab = P.view(rtab_d, "rtab")
            dbst = P.view(bst_d, "bst")
            cnt = {'ac': 0}

            dma('sp', prm[:], s5a[l], W=[prm])
            for i in range(2):
                dma('sp', PB[i][:], s5b[l, i], W=[PB[i]])
                dma('sp', PC[i][:], s5c[l, i], W=[PC[i]])
            dma('sp', dcol[0:96, :], s5d[l], W=[dcol])
            mset(K0f[:], 0.0, [K0f])
            mset(Kf[:], 0.0, [Kf])
            are, aim, ldt = prm[:, 0, :], prm[:, 1, :], prm[:, 2, :]
            v = lambda n: V[n][:, :]
            act(v("dt"), ldt, AF.Exp, [prm], [V["dt"]])
            ts(v("lre"), are, -1e-4, ALU.min, [prm], [V["lre"]])
            tt(v("t1"), v("dt"), v("lre"), ALU.mult, [V["dt"], V["lre"]], [V["t1"]])
            act(v("mag"), v("t1"), AF.Exp, [V["t1"]], [V["mag"]])
            tt(v("ang"), v("dt"), aim, ALU.mult, [V["dt"], prm], [V["ang"]])
            ts(v("kf"), v("ang"), 1.0 / (2 * PI), ALU.mult, [V["ang"]], [V["kf"]])
            cp(ki[:, :], v("kf"), [V["kf"]], [ki])
            cp(v("kf"), ki[:, :], [ki], [V["kf"]])
            stt(v("r"), v("kf"), -2 * PI, v("ang"), ALU.mult, ALU.add, [V["kf"], V["ang"]], [V["r"]])
            ts(v("m1"), v("r"), PI, ALU.is_gt, [V["r"]], [V["m1"]])
            stt(v("r"), v("m1"), -2 * PI, v("r"), ALU.mult, ALU.add, [V["m1"], V["r"]], [V["r"]])
            ts(v("m1"), v("r"), -PI, ALU.is_lt, [V["r"]], [V["m1"]])
            stt(v("r"), v("m1"), 2 * PI, v("r"), ALU.mult, ALU.add, [V["m1"], V["r"]], [V["r"]])
            ts(v("r"), v("r"), PI, ALU.min, [V["r"]], [V["r"]], s2=-PI, op1=ALU.max)
            act(v("s1"), v("r"), AF.Sin, [V["r"]], [V["s1"]])
            stt(v("t1"), v("r"), -1.0, v("r"), ALU.mult, ALU.max, [V["r"]], [V["t1"]])
            act(v("c1"), v("t1"), AF.Sin, [V["t1"], hpic], [V["c1"]], scale=-1.0, bias=hpic[:, 0:1])
            tt(v("abr"), v("mag"), v("c1"), ALU.mult, [V["mag"], V["c1"]], [V["abr"]])
            tt(v("abi"), v("mag"), v("s1"), ALU.mult, [V["mag"], V["s1"]], [V["abi"]])
            ts(v("nabi"), v("abi"), -1.0, ALU.mult, [V["abi"]], [V["nabi"]])
            tt(v("den"), v("lre"), v("lre"), ALU.mult, [V["lre"]], [V["den"]])
            tt(v("t1"), aim, aim, ALU.mult, [prm], [V["t1"]])
            tt(v("den"), v("den"), v("t1"), ALU.add, [V["den"], V["t1"]], [V["den"]])
            P.op('dve', lambda h: h.reciprocal(out=v("den"), in_=v("den")), reads=[V["den"]], writes=[V["den"]])
            ts(v("nre"), v("abr"), -1.0, ALU.add, [V["abr"]], [V["nre"]])
            tt(v("t1"), v("nre"), v("lre"), ALU.mult, [V["nre"], V["lre"]], [V["t1"]])
            tt(v("t2"), v("abi"), aim, ALU.mult, [V["abi"], prm], [V["t2"]])
            tt(v("t1"), v("t1"), v("t2"), ALU.add, [V["t1"], V["t2"]], [V["t1"]])
            tt(v("fre"), v("t1"), v("den"), ALU.mult, [V["t1"], V["den"]], [V["fre"]])
            tt(v("t1"), v("abi"), v("lre"), ALU.mult, [V["abi"], V["lre"]], [V["t1"]])
            tt(v("t2"), v("nre"), aim, ALU.mult, [V["nre"], prm], [V["t2"]])
            tt(v("t1"), v("t1"), v("t2"), ALU.subtract, [V["t1"], V["t2"]], [V["t1"]])
            tt(v("fim"), v("t1"), v("den"), ALU.mult, [V["t1"], V["den"]], [V["fim"]])
            frb = v("fre").unsqueeze(2).broadcast_to([128, 24, 16])
            fib = v("fim").unsqueeze(2).broadcast_to([128, 24, 16])
            tt(XN[0][:], PB[1][:], fib, ALU.mult, [PB[1], V["fim"]], [XN[0]])
            tt(XN[1][:], PB[0][:], fib, ALU.mult, [PB[0], V["fim"]], [XN[1]])
            tt(PB[0][:], PB[0][:], frb, ALU.mult, [PB[0], V["fre"]], [PB[0]])
            tt(PB[0][:], PB[0][:], XN[0][:], ALU.subtract, [PB[0], XN[0]], [PB[0]])
            tt(PB[1][:], PB[1][:], frb, ALU.mult, [PB[1], V["fre"]], [PB[1]])
            tt(PB[1][:], PB[1][:], XN[1][:], ALU.add, [PB[1], XN[1]], [PB[1]])
            mset(PWr[:, 0, :], 1.0, [PWr])
            mset(PWi[:, 0, :], 0.0, [PWi])
            for k in range(1, 9):
                tt(v("t1"), PWr[:, k - 1, :], v("abr"), ALU.mult, [PWr, V["abr"]], [V["t1"]])
                tt(v("t2"), PWi[:, k - 1, :], v("abi"), ALU.mult, [PWi, V["abi"]], [V["t2"]])
                tt(PWr[:, k, :], v("t1"), v("t2"), ALU.subtract, [V["t1"], V["t2"]], [PWr])
                tt(v("t1"), PWr[:, k - 1, :], v("abi"), ALU.mult, [PWr, V["abi"]], [V["t1"]])
                tt(v("t2"), PWi[:, k - 1, :], v("abr"), ALU.mult, [PWi, V["abr"]], [V["t2"]])
                tt(PWi[:, k, :], v("t1"), v("t2"), ALU.add, [V["t1"], V["t2"]], [PWi])
            tt(v("m8"), v("mag"), v("mag"), ALU.mult, [V["mag"]], [V["m8"]])
            tt(v("m8"), v("m8"), v("m8"), ALU.mult, [V["m8"]], [V["m8"]])
            tt(v("m8"), v("m8"), v("m8"), ALU.mult, [V["m8"]], [V["m8"]])

            def csquare(cr_o, ci_o, cr_i, ci_i, R, W):
                tt(v("t1"), cr_i, cr_i, ALU.mult, R, [V["t1"]])
                tt(v("t2"), ci_i, ci_i, ALU.mult, R, [V["t2"]])
                tt(v("den"), cr_i, ci_i, ALU.mult, R, [V["den"]])
                tt(cr_o, v("t1"), v("t2"), ALU.subtract, [V["t1"], V["t2"]], W)
                ts(ci_o, v("den"), 2.0, ALU.mult, [V["den"]], W)
            csquare(v("c8"), v("s8"), v("c1"), v("s1"), [V["c1"], V["s1"]], [V["c8"], V["s8"]])
            csquare(v("c8"), v("s8"), v("c8"), v("s8"), [V["c8"], V["s8"]], [V["c8"], V["s8"]])
            csquare(v("c8"), v("s8"), v("c8"), v("s8"), [V["c8"], V["s8"]], [V["c8"], V["s8"]])
            cp(P2r[:, 0, :], v("c8"), [V["c8"]], [P2r])
            cp(P2i[:, 0, :], v("s8"), [V["s8"]], [P2i])
            for k in range(1, 8):
                csquare(P2r[:, k, :], P2i[:, k, :], P2r[:, k - 1, :], P2i[:, k - 1, :], [P2r, P2i], [P2r, P2i])

            for ri in range(2):
                for blk in range(6):
                    dma('sp', sstg[0:NS, 0:512], st0[l, ri, :, blk * 512:(blk + 1) * 512], W=[sstg])
                    pb = bank()
                    for i in range(4):
                        tr(pb[:, i * 16:(i + 1) * 16], sstg[0:NS, i * 128:(i + 1) * 128], ident[0:NS, 0:NS], [sstg, ident], [pb])
                    cp(H0[ri][:, 4 * blk:4 * blk + 4, :], pb[:, 0:64].rearrange("p (a s) -> p a s", a=4), [pb], [H0[ri]])
                cp(H0b[:, :, ri, :], H0[ri][:], [H0[ri]], [H0b])

            def build_wide(q, qi):
                for ri in range(2):
                    for g2 in range(2):
                        c0 = 32 * qi + 16 * g2
                        cp(BBw[g2 * 64:(g2 + 1) * 64, qi, ri, c0:c0 + 16], PB[ri][g2 * 64:(g2 + 1) * 64, q, :], [PB[ri]], [BBw], eng='pool')
                        cp(Cw[g2 * 64:(g2 + 1) * 64, qi, ri, c0:c0 + 16], PC[ri][g2 * 64:(g2 + 1) * 64, q, :], [PC[ri]], [Cw], eng='pool')

            def build_Bs(q, qi):
                pr = PWr[:, 0:8, q:q + 1].broadcast_to([128, 8, 96])
                pi = PWi[:, 0:8, q:q + 1].broadcast_to([128, 8, 96])
                Br = BBw[:, qi, 0, :].unsqueeze(1).broadcast_to([128, 8, 96])
                Bi = BBw[:, qi, 1, :].unsqueeze(1).broadcast_to([128, 8, 96])
                tt(t8a, Bi, pi, ALU.mult, [BBw, PWi], [taB])
                tt(TS8[:, :, 0, :], Br, pr, ALU.mult, [BBw, PWr], [SpB])
                tt(TS8[:, :, 0, :], TS8[:, :, 0, :], t8a, ALU.subtract, [SpB, taB], [SpB])
                tt(t8a, Br, pi, ALU.mult, [BBw, PWi], [taB])
                tt(TS8[:, :, 1, :], Bi, pr, ALU.mult, [BBw, PWr], [SpB])
                tt(TS8[:, :, 1, :], TS8[:, :, 1, :], t8a, ALU.add, [SpB, taB], [SpB])
                for k in range(8):
                    pb = bank()
                    tr(pb[0:96, 0:128], TS8[:, k, 0, :], ident[:], [SpB, ident], [pb])
                    tr(pb[0:96, 128:256], TS8[:, k, 1, :], ident[:], [SpB, ident], [pb])
                    act(BsT[32 * qi:32 * qi + 32, k, :, :], pb[32 * qi:32 * qi + 32, 0:256].rearrange("p (r n) -> p r n", r=2), AF.Copy, [pb], [BsT])

            def build_AC(q, qi):
                pr = PWr[:, 0:9, q:q + 1].broadcast_to([128, 9, 96])
                pi = PWi[:, 0:9, q:q + 1].broadcast_to([128, 9, 96])
                Cr = Cw[:, qi, 0, :].unsqueeze(1).broadcast_to([128, 9, 96])
                Ci = Cw[:, qi, 1, :].unsqueeze(1).broadcast_to([128, 9, 96])
                t9 = SpB[:, 0:864].rearrange("p (k n) -> p k n", k=9)
                tt(t9, Ci, pi, ALU.mult, [Cw, PWi], [SpB])
                tt(ACw[:, :, 0, :], Cr, pr, ALU.mult, [Cw, PWr], [ZB])
                tt(ACw[:, :, 0, :], ACw[:, :, 0, :], t9, ALU.subtract, [ZB, SpB], [ZB])
                tt(t9, Cr, pi, ALU.mult, [Cw, PWi], [SpB])
                tt(ACw[:, :, 1, :], Ci, pr, ALU.mult, [Cw, PWr], [ZB])
                stt(ACw[:, :, 1, :], ACw[:, :, 1, :], -1.0, t9, ALU.mult, ALU.subtract, [ZB, SpB], [ZB])

            def build_wide_triple(t):
                mset(BBw[:], 0.0, [BBw])
                mset(Cw[:], 0.0, [Cw])
                for qi in range(3):
                    build_wide(3 * t + qi, qi)

            def build_conv_tables(t):
                for qi in range(3):
                    q = 3 * t + qi
                    build_AC(q, qi)
                    act(CjT[:, qi, :, :, :], ACw[:, 1:9, :, :], AF.Copy, [ZB], [CjT])
                    pK = bank()
                    for tau in range(8):
                        for ri in range(2):
                            mm(pK[0:96, tau * 32:(tau + 1) * 32], BBw[:, qi, ri, :], ACw[:, tau, ri, 32 * qi:32 * qi + 32], ri == 0, ri == 1, [BBw, ZB], [pK])
                    cp(Kf[32 * qi:32 * qi + 32, 1:8, 32 * qi:32 * qi + 32], pK[32 * qi:32 * qi + 32, 32:256].rearrange("p (k n) -> p k n", k=7), [pK], [Kf])
                    cp(K0f[32 * qi:32 * qi + 32, 32 * qi:32 * qi + 32], pK[32 * qi:32 * qi + 32, 0:32], [pK], [K0f])
                stt(Kf[0:96, 0, :], ident[0:96, 0:96], dcol[0:96, t:t + 1], K0f[0:96, :], ALU.mult, ALU.add, [ident, dcol, K0f], [Kf])

            def deint(t):
                act(u8[0:96, :, :], Bm_all[0:96, t, 0:NPT].rearrange("p (c s) -> p s c", s=8), AF.Copy, [Bm[0], Bm[1], Bm[2], Bm[3]], [u8])

            def build_rot_triple(t):
                q0 = 3 * t
                mset(Rc3[:, :, 0:1], 1.0, [Rc3])
                mset(Rs3[:, :, 0:1], 0.0, [Rs3])
                for k in range(8):
                    n = 1 << k
                    pr = P2r[:, k, q0:q0 + 3].unsqueeze(2).broadcast_to([128, 3, n])
                    pi = P2i[:, k, q0:q0 + 3].unsqueeze(2).broadcast_to([128, 3, n])
                    tmp = rtmp[:, :, :].rearrange("p a b -> p (a b)")[:, 0:3 * n].rearrange("p (a n) -> p a n", a=3)
                    tt(tmp, Rs3[:, :, 0:n], pi, ALU.mult, [Rs3, P2i], [rtmp], eng='pool')
                    tt(Rc3[:, :, n:2 * n], Rc3[:, :, 0:n], pr, ALU.mult, [Rc3, P2r], [Rc3], eng='pool')
                    tt(Rc3[:, :, n:2 * n], Rc3[:, :, n:2 * n], tmp, ALU.subtract, [Rc3, rtmp], [Rc3], eng='pool')
                    tt(tmp, Rc3[:, :, 0:n], pi, ALU.mult, [Rc3, P2i], [rtmp], eng='pool')
                    tt(Rs3[:, :, n:2 * n], Rs3[:, :, 0:n], pr, ALU.mult, [Rs3, P2r], [Rs3], eng='pool')
                    tt(Rs3[:, :, n:2 * n], Rs3[:, :, n:2 * n], tmp, ALU.add, [Rs3, rtmp], [Rs3], eng='pool')

            def scan_pair(t, qi, use_init, full):
                q = 3 * t + qi
                Rc, Rs = Rc3[:, qi, :], Rs3[:, qi, :]
                cp(M8t, V["m8"][:, q:q + 1].broadcast_to([128, 256]), [V["m8"]], [taB])
                pS = bank()
                for ri in range(2):
                    for s_ in range(8):
                        mm(pS[:, ri * 256:(ri + 1) * 256], BsT[32 * qi:32 * qi + 32, 7 - s_, ri, :], u8[32 * qi:32 * qi + 32, s_, :], s_ == 0, s_ == 7, [BsT, u8], [pS])
                Sr, Si = pS[:, 0:256], pS[:, 256:512]
                tt(ta, Sr, Rc, ALU.mult, [pS, Rc3], [taB])
                tt(tb, Si, Rs, ALU.mult, [pS, Rs3], [taB])
                tt(Sp[0], ta, tb, ALU.add, [taB], [SpB])
                tt(ta, Si, Rc, ALU.mult, [pS, Rc3], [taB])
                tt(tb, Sr, Rs, ALU.mult, [pS, Rs3], [taB])
                tt(Sp[1], ta, tb, ALU.subtract, [taB], [SpB])
                if use_init:
                    xr, xi = XI[:, q, 0:1], XI[:, q, 1:2]
                    c8q, s8q = V["c8"][:, q:q + 1], V["s8"][:, q:q + 1]
                    ts(z0[:, 2:3], xi, s8q, ALU.mult, [XI, V["s8"]], [z0])
                    stt(z0[:, 0:1], xr, c8q, z0[:, 2:3], ALU.mult, ALU.subtract, [XI, V["c8"], z0], [z0])
                    ts(z0[:, 2:3], xr, s8q, ALU.mult, [XI, V["s8"]], [z0])
                    stt(z0[:, 1:2], xi, c8q, z0[:, 2:3], ALU.mult, ALU.add, [XI, V["c8"], z0], [z0])
                for ri in range(2):
                    init = z0[:, ri:ri + 1] if use_init else 0.0
                    P.op('dve', lambda h, ri=ri, init=init: h.tensor_tensor_scan(out=Z[ri], data0=M8t, data1=Sp[ri], initial=init, op0=ALU.mult, op1=ALU.add), reads=[taB, SpB, z0], writes=[ZB])
                if not full:
                    tt(ta[:, 0:1], Z[1][:, 255:256], Rs[:, 255:256], ALU.mult, [ZB, Rs3], [taB])
                    tt(tb[:, 0:1], Z[0][:, 255:256], Rc[:, 255:256], ALU.mult, [ZB, Rc3], [taB])
                    tt(FIN[:, q, 0:1], tb[:, 0:1], ta[:, 0:1], ALU.subtract, [taB], [FIN])
                    tt(ta[:, 0:1], Z[0][:, 255:256], Rs[:, 255:256], ALU.mult, [ZB, Rs3], [taB])
                    tt(tb[:, 0:1], Z[1][:, 255:256], Rc[:, 255:256], ALU.mult, [ZB, Rc3], [taB])
                    tt(FIN[:, q, 1:2], tb[:, 0:1], ta[:, 0:1], ALU.add, [taB], [FIN])
                    return
                tt(ta, Z[0], Rc, ALU.mult, [ZB, Rc3], [taB])
                tt(tb, Z[1], Rs, ALU.mult, [ZB, Rs3], [taB])
                tt(Sp[0], ta, tb, ALU.subtract, [taB], [SpB])
                tt(ta, Z[1], Rc, ALU.mult, [ZB, Rc3], [taB])
                tt(tb, Z[0], Rs, ALU.mult, [ZB, Rs3], [taB])
                tt(Sp[1], ta, tb, ALU.add, [taB], [SpB])
                for ri in range(2):
                    act(Xprev[:, qi, ri, 1:256], Sp[ri][:, 0:255], AF.Copy, [SpB], [Xprev])
                    cp(Xprev[:, qi, ri, 0:1], XI[:, q, ri:ri + 1], [XI], [Xprev], eng='pool')
                    cp(PFIN[:, ri, q:q + 1], Sp[ri][:, 255:256], [SpB], [PFIN], eng='pool')

            for t in range(8):
                deint(t)
                build_wide_triple(t)
                build_rot_triple(t)
                for qi in range(3):
                    build_Bs(3 * t + qi, qi)
                dma('sp', bst_d.ap()[t], BsT[:, :, :, :].rearrange("p k r n -> p (k r n)"), R=[BsT], W=[dbst])
                for qi in range(3):
                    dma('sp', rtab_d.ap()[3 * t + qi, :, 0, :], Rc3[:, qi, :], R=[Rc3], W=[drtab])
                    dma('sp', rtab_d.ap()[3 * t + qi, :, 1, :], Rs3[:, qi, :], R=[Rs3], W=[drtab])
                for qi in range(3):
                    scan_pair(t, qi, False, False)
            dma('sp', xin_d.ap(), FIN[:, :, :].rearrange("p q r -> p (q r)"), R=[FIN], W=[dxin])
            P.dma('pool', lambda h: h.collective_compute("AllGather", ALU.bypass, replica_groups=[[0, 1], [2, 3], [4, 5], [6, 7]], ins=[xin_d.ap().opt()], outs=[xout_d.ap().opt()]), reads=[dxin], writes=[dxout], inc=1)
            dma('sp', XI[:, :, :].rearrange("p q r -> p (q r)"), xout_d.ap()[0:128, :], R=[dxout], W=[XI])
            ts(XI[:, :, :], XI[:, :, :], flg[:, 0:1], ALU.mult, [XI, flg], [XI])
            for t in range(8):
                deint(t)
                dma('sp', BsT[:, :, :, :].rearrange("p k r n -> p (k r n)"), bst_d.ap()[t], R=[dbst], W=[BsT])
                for qi in range(3):
                    dma('sp', Rc3[:, qi, :], rtab_d.ap()[3 * t + qi, :, 0, :], R=[drtab], W=[Rc3])
                    dma('sp', Rs3[:, qi, :], rtab_d.ap()[3 * t + qi, :, 1, :], R=[drtab], W=[Rs3])
                build_wide_triple(t)
                build_conv_tables(t)
                for qi in range(3):
                    scan_pair(t, qi, True, True)
                pos = [bank() for _ in range(4)]
                for j in range(8):
                    po = pos[j // 2]
                    o = po[0:96, (j % 2) * 256:(j % 2 + 1) * 256]
                    nmm = (j + 1) + 6
                    i = 0
                    for tau in range(j + 1):
                        mm(o, Kf[0:96, tau, :], u8[0:96, j - tau, :], i == 0, i == nmm - 1, [Kf, u8], [po])
                        i += 1
                    for qi in range(3):
                        for ri in range(2):
                            mm(o, CjT[:, qi, j, ri, :], Xprev[:, qi, ri, :], i == 0, i == nmm - 1, [CjT, Xprev], [po])
                            i += 1
                for ch in range(4):
                    for b in range(4):
                        src = pos[b][0:96, :].rearrange("p (j c) -> p j c", j=2)[:, :, ch * 64:(ch + 1) * 64]
                        dst = Bm[ch][0:96, t, :].rearrange("p (c j) -> p j c", j=8)[:, 2 * b:2 * b + 2, :]
                        act(dst, src, AF.Gelu, [pos[b]], [Bm[ch]])
                pq = bank()
                i = 0
                mm(pq[0:96, 0:NS], Kf[0:96, 0, :], Bm[4][0:96, t, 0:NS], True, False, [Kf, Bm[4]], [pq])
                for qi in range(3):
                    for ri in range(2):
                        i += 1
                        mm(pq[0:96, 0:NS], CjT[:, qi, 0, ri, :], H0b[:, 3 * t + qi, ri, :], False, i == 6, [CjT, H0b], [pq])
                for qi in range(3):
                    q = 3 * t + qi
                    pu = bank()
                    for ri in range(2):
                        mm(pu[:, ri * NS:(ri + 1) * NS], BsT[32 * qi:32 * qi + 32, 0, ri, :], Bm[4][32 * qi:32 * qi + 32, t, 0:NS], True, True, [BsT, Bm[4]], [pu])
                    abr_q, abi_q, nabi_q = V["abr"][:, q:q + 1], V["abi"][:, q:q + 1], V["nabi"][:, q:q + 1]
                    ts(ta[:, 0:NS], H0[0][:, q, :], abr_q, ALU.mult, [H0[0], V["abr"]], [taB])
                    tt(ta[:, 0:NS], ta[:, 0:NS], pu[:, 0:NS], ALU.add, [taB, pu], [taB])
                    stt(XN[0][:, q, :], H0[1][:, q, :], nabi_q, ta[:, 0:NS], ALU.mult, ALU.add, [H0[1], V["nabi"], taB], [XN[0]])
                    ts(tb[:, 0:NS], H0[1][:, q, :], abr_q, ALU.mult, [H0[1], V["abr"]], [taB])
                    tt(tb[:, 0:NS], tb[:, 0:NS], pu[:, NS:2 * NS], ALU.add, [taB, pu], [taB])
                    stt(XN[1][:, q, :], H0[0][:, q, :], abi_q, tb[:, 0:NS], ALU.mult, ALU.add, [H0[0], V["abi"], taB], [XN[1]])
                act(Bm[4][0:96, t, 0:NS], pq[0:96, 0:NS], AF.Gelu, [pq], [Bm[4]])
            for ri in range(2):
                pb = bank()
                tr(pb[0:24, 0:128], PFIN[:, ri, :], ident[:], [PFIN, ident], [pb])
                cp(sstg[0:24, 0:128], pb[0:24, 0:128], [pb], [sstg])
                dma('sp', o_pssm[l, ri], sstg[0:24, 0:128], R=[sstg], is_out=True)
                for blk in range(6):
                    pb = bank()
                    for i in range(4):
                        tr(pb[0:NS, i * 128:(i + 1) * 128], XN[ri][:, 4 * blk + i, :], ident[:], [XN[ri], ident], [pb])
                    cp(sstg[0:NS, 0:512], pb[0:NS, :], [pb], [sstg])
                    dma('sp', o_sssm[l, ri, :, blk * 512:(blk + 1) * 512], sstg[0:NS, 0:512], R=[sstg], is_out=True)
            P.phase_end()

        def moba_layer(l):
            j = l - 2
            P.phase_begin()
            nm = lambda n: "mb%s_%d" % (n, l)
            KTaug = P.sb(nm("KT"), [128, 4096], BF16)
            VZ = P.sb(nm("VZ"), [128, 32, 192], BF16)
            KMf = P.sb(nm("KMf"), [128, 16], F32)
            KM = P.sb(nm("KM"), [128, 16], BF16)
            qaug = [P.sb(nm("qa%d" % i), [128, 512], BF16) for i in range(2)]
            MBw = [P.sb(nm("MBw%d" % i), [128, 80], F32) for i in range(4)]
            g1 = P.sb(nm("g1"), [128, 16], F32)
            m8 = P.sb(nm("m8"), [128, 8], F32)
            sel = P.sb(nm("sel"), [128, 16], F32)
            VBg = P.sb(nm("VBg"), [128, 8, 16], F32)
            vm = P.sb(nm("vm"), [128, 8, 16], F32)
            pb1 = P.sb(nm("pb1"), [128, 1], F32)
            tri = P.sb(nm("tri"), [128, 128], BF16)
            shiftsel = P.sb(nm("shs"), [128, 64], BF16)
            pTs = [P.sb(nm("pT%d" % i), [128, 512], BF16) for i in range(4)]
            osb = P.sb(nm("osb"), [128, 512], F32)
            rden = P.sb(nm("rden"), [128, 512], F32)
            qsq = P.sb(nm("qsq"), [128, 512], BF16)
            qrs = P.sb(nm("qrs"), [128, 512], F32)
            T = SimpleNamespace(qsq=qsq, qrs=qrs)
            c = {'pt': 0, 'qa': 0, 'mb': 0, 'ps': 0, 'po': 0, 'mi': 0}

            def bank_ps():
                c['ps'] += 1
                return banks[c['ps'] % 3]

            def bank_po():
                c['po'] += 1
                return banks[3 + c['po'] % 2]

            def bank_mi():
                return banks[5]

            sd = sdec_gen(l) if stage >= 4 else None
            pend = {'e': None}

            def step_sd(k):
                nonlocal sd
                for _ in range(k):
                    if sd is None:
                        return
                    try:
                        next(sd)
                    except StopIteration:
                        sd = None

            mset(tri[:], 1.0, [tri])
            P.op('pool', lambda h: h.affine_select(out=tri[:], in_=tri[:], pattern=[[1, 128]], compare_op=ALU.is_ge, fill=0.0, base=0, channel_multiplier=-1), reads=[tri], writes=[tri])
            mset(shiftsel[:], 1.0, [shiftsel])
            P.op('pool', lambda h: h.affine_select(out=shiftsel[:], in_=shiftsel[:], pattern=[[-1, 64]], compare_op=ALU.is_equal, fill=0.0, base=-64, channel_multiplier=1), reads=[shiftsel], writes=[shiftsel])
            mset(KTaug[64:80, :], 1.0, [KTaug])
            P.op('pool', lambda h: h.affine_select(out=KTaug[64:80, :], in_=KTaug[64:80, :], pattern=[[1, 4096]], compare_op=ALU.is_ge, fill=0.0, base=0, channel_multiplier=-256), reads=[KTaug], writes=[KTaug])
            P.op('pool', lambda h: h.affine_select(out=KTaug[64:80, :], in_=KTaug[64:80, :], pattern=[[-1, 4096]], compare_op=ALU.is_ge, fill=0.0, base=255, channel_multiplier=256), reads=[KTaug], writes=[KTaug])
            mset(VZ[:], 0.0, [VZ])
            mset(VZ[:, :, 64:65], 1.0, [VZ])
            for i in range(4):
                mset(MBw[i][:], 0.0, [MBw[i]])
            ts(pb1[:, :], flg[:, :], -1.0, ALU.add, [flg], [pb1], s2=1e9, op1=ALU.mult)
            mset(VBg[:], -1e9, [VBg])
            mset(vm[:], 0.0, [vm])
            for lb in range(1, 8):
                mset(VBg[:, lb, 8:8 + lb], 0.0, [VBg])
                mset(vm[:, lb, 8:8 + lb], 1.0, [vm])
            cp(VBg[:, :, 0:8], pb1[:, 0:1].unsqueeze(2).broadcast_to([128, 8, 8]), [pb1], [VBg])
            cp(vm[:, :, 0:8], flg[:, 0:1].unsqueeze(2).broadcast_to([128, 8, 8]), [flg], [vm])

            step_sd(1)
            for ch in range(4):
                for slot in range(6):
                    headnorm_feat(T, Bm[ch], slot, 512, 8 + j, Bm[ch], slot)

            vall = kvall_d.ap()[256:512, :].rearrange("r (e c) -> (r e) c", c=256)
            vloc = kv_d.ap()[256:512, :].rearrange("r (e c) -> (r e) c", c=256)
            for kvh in range(4):
                dma('sp', KTaug[0:64, 0:2048], kvall_d.ap()[kvh * 64:(kvh + 1) * 64, :], R=[dkvall], W=[KTaug])
                dma('sp', KTaug[0:64, 2048:4096], kv_d.ap()[kvh * 64:(kvh + 1) * 64, :], R=[dkv], W=[KTaug])
                for (src, dsrc, t0) in ((vall, dkvall, 0), (vloc, dkv, 16)):
                    sv = src[:, kvh * 64:(kvh + 1) * 64].rearrange("(t p) c -> p t c", p=128)
                    dma('sp', VZ[:, t0:t0 + 16, 0:64], sv, R=[dsrc], W=[VZ])
                    dma('sp', VZ[:, t0:t0 + 16, 128:192], sv, R=[dsrc], W=[VZ])
                P.op('dve', lambda h: h.tensor_reduce(out=KMf[0:64, :], in_=KTaug[0:64, :].rearrange("p (b t) -> p b t", t=256), axis=AX.X, op=ALU.add), reads=[KTaug], writes=[KMf])
                cp(KM[0:64, :], KMf[0:64, :], [KMf], [KM])
                units = [(hh, ch) for hh in range(3) for ch in range(4)]

                def prep_gen(hh, ch, qa):
                    hd = 3 * kvh + hh
                    slot, par = hd // 2, hd % 2
                    if par == 0:
                        cp(qa[0:64, :], Bm[ch][0:64, slot, :], [Bm[ch]], [qa])
                    else:
                        pq = banks[3 + (c['po'] + 1) % 2]
                        mm(pq[0:64, :], shiftsel[:, :], Bm[ch][:, slot, :], True, True, [shiftsel, Bm[ch]], [pq])
                        act(qa[0:64, :], pq[0:64, :], AF.Copy, [pq], [qa])
                    pg = banks[5]
                    mbws = []
                    for tti in range(4):
                        lb = (ch * 4 + tti) // 2
                        mm(pg[:, tti * 16:(tti + 1) * 16], qa[0:64, tti * 128:(tti + 1) * 128], KM[0:64, :], True, True, [qa, KM], [pg])
                    for tti in range(4):
                        lb = (ch * 4 + tti) // 2
                        tt(g1[:, :], pg[:, tti * 16:(tti + 1) * 16], VBg[:, lb, :], ALU.add, [pg, VBg], [g1])
                        P.op('dve', lambda h: h.max(out=m8[:, :], in_=g1[:, :]), reads=[g1], writes=[m8])
                        ts(sel[:, :], g1[:, :], m8[:, 2:3], ALU.is_ge, [g1, m8], [sel])
                        tt(sel[:, :], sel[:, :], vm[:, lb, :], ALU.mult, [sel, vm], [sel])
                        mbw = MBw[c['mb'] % 4]
                        c['mb'] += 1
                        ts(mbw[:, 64:80], sel[:, :], -1.0, ALU.add, [sel], [mbw], s2=30000.0, op1=ALU.mult)
                        mset(mbw[:, 72 + lb:73 + lb], 0.0, [mbw], eng='dve')
                        mbws.append(mbw)
                    yield
                    for tti in range(4):
                        tr(pg[0:80, tti * 128:(tti + 1) * 128], mbws[tti][:, :], ident[:], [mbws[tti], ident], [pg])
                    cp(qa[64:80, :], pg[64:80, :], [pg], [qa])
                    yield

                def drain(g):
                    for _ in g:
                        pass

                qa_cur = qaug[c['qa'] % 2]
                c['qa'] += 1
                drain(prep_gen(units[0][0], units[0][1], qa_cur))
                for ui, (hh, ch) in enumerate(units):
                    hd = 3 * kvh + hh
                    slot, par = hd // 2, hd % 2
                    lo = par * 64
                    zoff = par * 64
                    drow = 64 if par == 0 else 0
                    qa = qa_cur
                    gnext = None
                    if ui + 1 < len(units):
                        qa_next = qaug[c['qa'] % 2]
                        c['qa'] += 1
                        gnext = prep_gen(units[ui + 1][0], units[ui + 1][1], qa_next)
                    b0 = 2 * ch
                    steps = [(kt, 0, 512, False) for kt in range(16 + 2 * b0)]
                    steps += [(16 + 2 * b0, 0, 512, True), (17 + 2 * b0, 128, 384, True), (18 + 2 * b0, 256, 256, True), (19 + 2 * b0, 384, 128, True)]
                    po = bank_po()
                    n = len(steps)
                    LA = 2
                    live = {}
                    for i in range(n + LA):
                        if i % 8 == 5:
                            step_sd(1)
                        if i == 4 and pend['e'] is not None:
                            pend['e']()
                            pend['e'] = None
                        if gnext is not None and (i == 6 or i == 14):
                            next(gnext)
                        if i < n:
                            kt, q0, nq, dotri = steps[i]
                            ps = bank_ps()
                            pt = pTs[c['pt'] % 4]
                            c['pt'] += 1
                            mm(ps[:, 0:nq], KTaug[0:80, kt * 128:(kt + 1) * 128], qa[0:80, q0:q0 + nq], True, True, [KTaug, qa], [ps])
                            act(pt[:, 0:nq], ps[:, 0:nq], AF.Exp, [ps], [pt])
                            if dotri:
                                tt(pt[:, 0:128], pt[:, 0:128], tri[:, :], ALU.mult, [pt, tri], [pt])
                            live[i] = pt
                        k = i - LA
                        if k >= 0:
                            kt, q0, nq, dotri = steps[k]
                            pt = live.pop(k)
                            mm(po[:, q0:q0 + nq], VZ[:, kt, zoff:zoff + 128], pt[:, 0:nq], k == 0, k == n - 1, [VZ, pt], [po])
                    def epi1(po=po, lo=lo, drow=drow):
                        act(osb[lo:lo + 64, :], po[lo:lo + 64, :], AF.Copy, [po], [osb])
                        act(rden[drow:drow + 1, :], po[drow:drow + 1, :], AF.Ln, [po], [rden])
                        act(rden[drow:drow + 1, :], rden[drow:drow + 1, :], AF.Exp, [rden], [rden], scale=-1.0)

                    def epi2(po=po, lo=lo, drow=drow, ch=ch, slot=slot):
                        pbc = banks[5]
                        mm(pbc[:, :], selrow[drow:drow + 1, :], rden[drow:drow + 1, :], True, True, [selrow, rden], [pbc])
                        tt(Bm[ch][lo:lo + 64, slot, :], osb[lo:lo + 64, :], pbc[lo:lo + 64, :], ALU.mult, [osb, pbc], [Bm[ch]])
                    epi1()
                    pend['e'] = epi2
                    step_sd(1)
                    if gnext is not None:
                        drain(gnext)
                        qa_cur = qa_next
                if pend['e'] is not None:
                    pend['e']()
                    pend['e'] = None
            if stage < 4:
                mset(Bm[4][:, 0:6, :], 0.0, [Bm[4]])
            step_sd(100000)
            P.phase_end()

        def sdec_gen(l):
            j = l - 2
            nm = lambda n: "sd%s_%d" % (n, l)
            Kpg = [P.sb(nm("Kpg%d" % i), [128, 2, 256], F32) for i in range(3)]
            Vpg = [P.sb(nm("Vpg%d" % i), [128, 2, 256], F32) for i in range(3)]
            KTp = [P.sb(nm("KTp%d" % i), [128, 2, 256], BF16) for i in range(2)]
            VA = [P.sb(nm("VA%d" % i), [128, 2, 4, 65], BF16) for i in range(3)]
            KMs2 = [P.sb(nm("KMs%d" % i), [128, 2, 8], F32) for i in range(2)]
            KMb = P.sb(nm("KMb"), [128, 2, 8], BF16)
            QS = P.sb(nm("QS"), [128, 12, NS], BF16)
            Em = P.sb(nm("Em"), [128, 4, 128], BF16)
            Ep1 = P.sb(nm("Ep1"), [128, 128], BF16)
            identb = P.sb(nm("identb"), [128, 128], BF16)
            ones6 = P.sb(nm("ones6"), [128, 65], BF16)
            KTn = P.sb(nm("KTn"), [128, 2, NS], BF16)
            VAn = P.sb(nm("VAn"), [128, 4, 65], BF16)
            vnf = P.sb(nm("vnf"), [128, 256], F32)
            PT = [P.sb(nm("PT%d" % i), [128, 24], BF16) for i in range(2)]
            PTn = P.sb(nm("PTn"), [128, 12], BF16)
            PARTS = P.sb(nm("PARTS"), [128, 9, 12], F32)
            g = P.sb(nm("g"), [128, 2, 8], F32)
            m8 = P.sb(nm("m8"), [128, 2, 8], F32)
            sel = P.sb(nm("sel"), [128, 2, 8], F32)
            selx = P.sb(nm("selx"), [128, 2, 8, 6], BF16)
            tmp = P.sb(nm("tmp"), [128, 8, 6], F32)
            acc = P.sb(nm("acc"), [128, 12], F32)
            rden = P.sb(nm("rden"), [128, 12], F32)
            Osb = P.sb(nm("Osb"), [128, 12], BF16)
            qsq = P.sb(nm("qsq"), [128, 512], BF16)
            qrs = P.sb(nm("qrs"), [128, 512], F32)
            T = SimpleNamespace(qsq=qsq, qrs=qrs)
            c = {'k': 0}
            ckv = cache_k.rearrange("n t c -> (n t) c")
            cvv = cache_v.rearrange("n t c -> (n t) c")

            cp(identb[:], ident[:], [ident], [identb], eng='pool')
            mset(Em[:], 0.0, [Em])
            cp(Em[0:64, 0, 0:64], ident[0:64, 0:64], [ident], [Em], eng='pool')
            cp(Em[64:128, 1, 64:128], ident[64:128, 64:128], [ident], [Em], eng='pool')
            cp(Em[0:64, 2, 64:128], ident[0:64, 0:64], [ident], [Em], eng='pool')
            cp(Em[64:128, 3, 0:64], ident[64:128, 64:128], [ident], [Em], eng='pool')
            mset(Ep1[:], 0.0, [Ep1])
            cp(Ep1[0:64, 64:128], ident[0:64, 0:64], [ident], [Ep1], eng='pool')
            mset(ones6[:], 1.0, [ones6])
            for i in range(3):
                mset(VA[i][:, :, :, 64:65], 1.0, [VA[i]])
            mset(VAn[:, :, 64:65], 1.0, [VAn])
            dma('sp', KTn[:, :, :], knT_d.ap(), R=[dknT], W=[KTn])
            dma('sp', vnf[0:NS, :], snew_d.ap()[:, 256:512], R=[dsnew], W=[vnf])
            cp(VAn[0:NS, :, 0:64], vnf[0:NS, :].rearrange("p (k d) -> p k d", k=4), [vnf], [VAn], eng='pool')

            for slot in range(6):
                headnorm_feat(T, Bm[4], slot, NS, 8 + j, Bm[4], slot)
            for hd in range(12):
                par, kp = hd % 2, (hd // 3) % 2
                vsel = {(0, 0): 0, (1, 1): 1, (0, 1): 2, (1, 0): 3}[(par, kp)]
                pq = bank()
                mm(pq[:, 0:NS], Em[:, vsel, :], Bm[4][:, hd // 2, 0:NS], True, True, [Em, Bm[4]], [pq])
                cp(QS[:, hd, :], pq[:, 0:NS], [pq], [QS])

            yield
            B6 = banks[6]

            def issue_gather(step):
                if step >= NS * 8:
                    return
                kpg, vpg = Kpg[step % 3], Vpg[step % 3]
                for pg in range(2):
                    col = step * 2 + pg
                    P.dma('pool', lambda h, kpg=kpg, pg=pg, col=col: h.indirect_dma_start(out=kpg[:, pg, :], out_offset=None, in_=ckv, in_offset=bass.IndirectOffsetOnAxis(ap=pidx[:, col:col + 1], axis=0)), reads=[pidx], writes=[kpg])
                    P.dma('pool', lambda h, vpg=vpg, pg=pg, col=col: h.indirect_dma_start(out=vpg[:, pg, :], out_offset=None, in_=cvv, in_offset=bass.IndirectOffsetOnAxis(ap=pidx[:, col:col + 1], axis=0)), reads=[pidx], writes=[vpg])
            def combine(s):
                pparts = banks[6]
                KMs = KMs2[s % 2]
                for kvh in range(4):
                    mm(B6[0:NS, 160 + 3 * kvh:160 + 3 * kvh + 3], KTn[:, kvh // 2, :], QS[:, 3 * kvh:3 * kvh + 3, s], True, True, [KTn, QS], [B6])
                act(PTn[0:NS, :], B6[0:NS, 160:172], AF.Exp, [B6], [PTn])
                ts(PTn[0:NS, :], PTn[0:NS, :], ident[0:NS, s:s + 1], ALU.mult, [PTn, ident], [PTn])
                for kvh in range(4):
                    mm(pparts[0:65, 96 + 3 * kvh:96 + 3 * kvh + 3], VAn[0:NS, kvh, :], PTn[0:NS, 3 * kvh:3 * kvh + 3], True, True, [VAn, PTn], [pparts])
                cp(PARTS[0:65, :, :], pparts[0:65, 0:108].rearrange("p (b h) -> p b h", h=12), [pparts], [PARTS])
                cp(KMb[:, :, :], KMs[:, :, :], [KMs], [KMb])
                for pr in range(2):
                    mm(B6[0:6, 176 + pr * 8:176 + pr * 8 + 8], QS[:, 6 * pr:6 * pr + 6, s], KMb[:, pr, :], True, True, [QS, KMb], [B6])
                cp(g[0:6, :, :], B6[0:6, 176:192].rearrange("p (a b) -> p a b", a=2), [B6], [g])
                for pr in range(2):
                    P.op('dve', lambda h, pr=pr: h.max(out=m8[0:6, pr, :], in_=g[0:6, pr, :]), reads=[g], writes=[m8])
                    ts(sel[0:6, pr, :], g[0:6, pr, :], m8[0:6, pr, 2:3], ALU.is_ge, [g, m8], [sel])
                    tt(selx[0:6, pr, :, :], sel[0:6, pr, :].unsqueeze(2).broadcast_to([6, 8, 6]), ident[0:6, 0:6].unsqueeze(1).broadcast_to([6, 8, 6]), ALU.mult, [sel, ident], [selx])
                for pr in range(2):
                    mm(B6[0:65, 256 + pr * 48:256 + pr * 48 + 48], ones6[0:6, :], selx[0:6, pr, :, :].rearrange("p b h -> p (b h)"), True, True, [ones6, selx], [B6])
                for pr in range(2):
                    tt(tmp[0:65, :, :], PARTS[0:65, 0:8, 6 * pr:6 * pr + 6], B6[0:65, 256 + pr * 48:256 + pr * 48 + 48].rearrange("p (b h) -> p b h", h=6), ALU.mult, [PARTS, B6], [tmp])
                    P.op('dve', lambda h, pr=pr: h.tensor_reduce(out=acc[0:65, 6 * pr:6 * pr + 6], in_=tmp[0:65, :, :].rearrange("p b h -> p h b"), axis=AX.X, op=ALU.add), reads=[tmp], writes=[acc])
                tt(acc[0:65, :], acc[0:65, :], PARTS[0:65, 8, :], ALU.add, [acc, PARTS], [acc])
                P.op('dve', lambda h: h.reciprocal(out=rden[64:65, :], in_=acc[64:65, :]), reads=[acc], writes=[rden])
                mm(B6[:, 384:396], selrow[64:65, :], rden[64:65, :], True, True, [selrow, rden], [B6])
                tt(Osb[0:64, :], acc[0:64, :], B6[0:64, 384:396], ALU.mult, [acc, B6], [Osb])
                mm(B6[:, 400:406], identb[0:64, :], Osb[0:64, 0:12:2], True, False, [identb, Osb], [B6])
                mm(B6[:, 400:406], Ep1[0:64, :], Osb[0:64, 1:12:2], False, True, [Ep1, Osb], [B6])
                cp(Bm[4][:, 0:6, s], B6[:, 400:406], [B6], [Bm[4]])

            NST = NS * 8
            pparts = banks[6]

            def stageT(step):
                s_, blk = step // 8, step % 8
                kpg, vpg, ktp, va = Kpg[step % 3], Vpg[step % 3], KTp[step % 2], VA[step % 3]
                KMs = KMs2[s_ % 2]
                pk = banks[7]
                for pg in range(2):
                    for pr in range(2):
                        tr(pk[:, (pr * 2 + pg) * 128:(pr * 2 + pg + 1) * 128], kpg[:, pg, pr * 128:(pr + 1) * 128], ident[:], [kpg, ident], [pk])
                ts(ktp[:, :, :], pk[:, :].rearrange("p (a t) -> p a t", a=2), 0.125, ALU.mult, [pk], [ktp])
                P.op('dve', lambda h, ktp=ktp, blk=blk, KMs=KMs: h.tensor_reduce(out=KMs[:, :, blk], in_=ktp[:, :, :], axis=AX.X, op=ALU.add), reads=[ktp], writes=[KMs])
                cp(va[:, :, :, 0:64], vpg[:, :, :].rearrange("p g (k d) -> p g k d", k=4), [vpg], [va])

            def stageQ(step):
                s_ = step // 8
                ktp = KTp[step % 2]
                psS = B6[:, 128:152]
                for pg in range(2):
                    for kvh in range(4):
                        mm(psS[:, pg * 12 + 3 * kvh:pg * 12 + 3 * kvh + 3], ktp[:, kvh // 2, pg * 128:(pg + 1) * 128], QS[:, 3 * kvh:3 * kvh + 3, s_], True, True, [ktp, QS], [B6])
                pt = PT[step % 2]
                act(pt[:, 0:24], psS[:, 0:24], AF.Exp, [B6], [pt])

            def stageV(step):
                s_, blk = step // 8, step % 8
                va, pt = VA[step % 3], PT[step % 2]
                for kvh in range(4):
                    for pg in range(2):
                        mm(pparts[0:65, blk * 12 + 3 * kvh:blk * 12 + 3 * kvh + 3], va[:, pg, kvh, :], pt[:, pg * 12 + 3 * kvh:pg * 12 + 3 * kvh + 3], pg == 0, pg == 1, [va, pt], [pparts])

            for g_ in range(2):
                issue_gather(g_)
            for e in range(-2, NST):
                issue_gather(e + 4)
                if 0 <= e + 2 < NST:
                    stageT(e + 2)
                if 0 <= e + 1 < NST:
                    stageQ(e + 1)
                if 0 <= e < NST:
                    stageV(e)
                    if e % 8 == 7:
                        combine(e // 8)
                yield

        for l in range(DEPTH):
            is_s5 = l < 2
            if l == 0:
                P.phase_begin()
                T = common("i%d" % l)
                inproj(l, T)
                P.phase_end()

            if is_s5 and stage >= 2:
                s5_layer(l)
            elif (not is_s5) and stage >= 3:
                moba_layer(l)
            else:
                P.phase_begin()
                for ch in range(5):
                    mset(Bm[ch][:, 0:8, :], 0.0, [Bm[ch]])
                P.phase_end()

            P.phase_begin()
            T = common("o%d" % l)
            wb1 = P.sb("wb1_%d" % l, [128, 6144], BF16)
            wb2 = P.sb("wb2_%d" % l, [128, 2, 1024], BF16)
            SK = P.sb("SK_%d" % l, [128, 4, 2, 256], BF16)
            SV = P.sb("SV_%d" % l, [128, 4, 2, 256], BF16)
            Os = P.sb("Os_%d" % l, [128, 4, NS], BF16)
            identb3 = P.sb("identb3_%d" % l, [128, 128], BF16)
            Ep13 = P.sb("Ep13_%d" % l, [128, 128], BF16)
            cp(identb3[:], ident[:], [ident], [identb3], eng='pool')
            mset(Ep13[:], 0.0, [Ep13])
            cp(Ep13[0:64, 64:128], ident[0:64, 0:64], [ident], [Ep13], eng='pool')
            nmix = 8 if is_s5 else 6
            pmix = 96 if is_s5 else 128
            if is_s5 and stage >= 2:
                gv = load_sq_weight(w_glu[l], 96, 8, 768)
                for ch, (c0, w) in enumerate(CHUNKS):
                    for m in range(8):
                        pb = bank()
                        for k in range(8):
                            mm(pb[0:96, :w], gv[0:96, k, m * 96:(m + 1) * 96], Bm[ch][0:96, k, :w], k == 0, k == 7, [wb0, Bm[ch]], [pb])
                        act(T.osb[0:96, :w], pb[0:96, :w], AF.Sigmoid, [pb], [T.osb])
                        tt(T.aT[0:96, m, :w], T.osb[0:96, :w], Bm[ch][0:96, m, :w], ALU.mult, [T.osb, Bm[ch]], [T.aT])
                    for m in range(8):
                        cp(Bm[ch][0:96, m, :w], T.aT[0:96, m, :w], [T.aT], [Bm[ch]], eng='pool')
            def mem_part1(ch, hd):
                par, hp = hd % 2, hd // 2
                w = CHUNKS[ch][1]
                pts = []
                for kt in range(2):
                    ps = bank()
                    mm(ps[:, :w], MKT[par * 64:(par + 1) * 64, l, hp, kt * 128:(kt + 1) * 128], T.qn[par * 64:(par + 1) * 64, hp, :w], True, True, [MKT, T.qn], [ps])
                    pt = T.pT[st['pT'] % 4]
                    st['pT'] += 1
                    act(pt[:, :w], ps[:, :w], AF.Exp, [ps], [pt])
                    pts.append(pt)
                return pts

            def mem_part2(ch, hd, pts):
                par, hp = hd % 2, hd // 2
                w = CHUNKS[ch][1]
                po = bank()
                for kt in range(2):
                    mm(po[:, :w], MVA[:, l, kt, hd, :], pts[kt][:, :w], kt == 0, kt == 1, [MVA, pts[kt]], [po])
                lo = par * 64
                drow = 64 if par == 0 else 0
                act(T.osb[lo:lo + 64, :w], po[lo:lo + 64, :w], AF.Copy, [po], [T.osb])
                act(T.rden[drow:drow + 1, :w], po[drow:drow + 1, :w], AF.Ln, [po], [T.rden])
                act(T.rden[drow:drow + 1, :w], T.rden[drow:drow + 1, :w], AF.Exp, [T.rden], [T.rden], scale=-1.0)
                pbc = bank()
                mm(pbc[:, :w], selrow[drow:drow + 1, :], T.rden[drow:drow + 1, :w], True, True, [selrow, T.rden], [pbc])
                tt(Bm[ch][lo:lo + 64, 8 + hp, :w], T.osb[lo:lo + 64, :w], pbc[lo:lo + 64, :w], ALU.mult, [T.osb, pbc], [Bm[ch]])

            prev = None
            for ch in range(4):
                for hp in range(2):
                    headnorm_feat(T, Bm[ch], 8 + hp, 512, l, T.qn, hp)
                for hd in range(4):
                    pts = mem_part1(ch, hd)
                    if prev is not None:
                        mem_part2(*prev)
                    prev = (ch, hd, pts)
            mem_part2(*prev)
            for hp in range(2):
                headnorm_feat(T, Bm[4], 8 + hp, NS, l, T.qn, hp)
            for grp in range(4):
                for si in range(4):
                    sq_ = grp * 4 + si
                    for ti in range(2):
                        sbuf_, o0 = ((T.stg, 0) if (si * 2 + ti) % 2 == 0 else (T.kvtm, 0))
                        dma('sp', sbuf_[:, 0:256], cmk[l, sq_, ti * 128:(ti + 1) * 128, :], W=[sbuf_])
                        dma('sp', sbuf_[:, 256:512], cmv[l, sq_, ti * 128:(ti + 1) * 128, :], W=[sbuf_])
                        pb2 = bank()
                        for hp in range(2):
                            tr(pb2[:, hp * 128:(hp + 1) * 128], sbuf_[:, hp * 128:(hp + 1) * 128], ident[:], [sbuf_, ident], [pb2])
                        act(SK[:, si, :, ti * 128:(ti + 1) * 128], pb2[:, 0:256].rearrange("p (a t) -> p a t", a=2), AF.Copy, [pb2], [SK], scale=0.125)
                        cp(SV[:, si, ti, :], sbuf_[:, 256:512], [sbuf_], [SV], eng='pool')
                for hd in range(4):
                    par, hp = hd % 2, hd // 2
                    ps = bank()
                    for si in range(4):
                        sq_ = grp * 4 + si
                        for kt in range(2):
                            mm(ps[:, kt * 4 + si:kt * 4 + si + 1], SK[par * 64:(par + 1) * 64, si, hp, kt * 128:(kt + 1) * 128], T.qn[par * 64:(par + 1) * 64, hp, sq_:sq_ + 1], True, True, [SK, T.qn], [ps])
                    pt = T.pT[st['pT'] % 4]
                    st['pT'] += 1
                    act(pt[:, 0:8], ps[:, 0:8], AF.Exp, [ps], [pt])
                    po = bank()
                    for si in range(4):
                        for kt in range(2):
                            mm(po[0:64, si:si + 1], SV[:, si, kt, hd * 64:(hd + 1) * 64], pt[:, kt * 4 + si:kt * 4 + si + 1], kt == 0, kt == 1, [SV, pt], [po])
                    for kt in range(2):
                        mm(po[0:64, 8:12], onesb[:, 0:64], pt[:, kt * 4:(kt + 1) * 4], kt == 0, kt == 1, [onesb, pt], [po])
                    P.op('dve', lambda h, po=po: h.reciprocal(out=T.rden[0:64, 0:4], in_=po[0:64, 8:12]), reads=[po], writes=[T.rden])
                    act(T.osb[0:64, 0:4], po[0:64, 0:4], AF.Copy, [po], [T.osb])
                    tt(Os[0:64, hd, grp * 4:(grp + 1) * 4], T.osb[0:64, 0:4], T.rden[0:64, 0:4], ALU.mult, [T.osb, T.rden], [Os])
            for hp in range(2):
                po2 = bank()
                mm(po2[:, 0:NS], identb3[0:64, :], Os[0:64, 2 * hp, :], True, False, [identb3, Os], [po2])
                mm(po2[:, 0:NS], Ep13[0:64, :], Os[0:64, 2 * hp + 1, :], False, True, [Ep13, Os], [po2])
                cp(Bm[4][:, 8 + hp, 0:NS], po2[:, 0:NS], [po2], [Bm[4]])
            ov = load_sq_weight(w_out[l, 0:768, :], pmix, nmix, 1024)
            dma('pool', wb2[:, :, :], w_out[l, 768:1024, :].rearrange("(t p) n -> p t n", p=128), W=[wb2])
            for ch, (c0, w) in enumerate(CHUNKS):
                for m in range(8):
                    pb = bank()
                    for k in range(nmix):
                        mm(pb[:, :w], ov[0:pmix, k, m * 128:(m + 1) * 128], Bm[ch][0:pmix, k, :w], k == 0, False, [wb0, Bm[ch]], [pb])
                    for k in range(2):
                        mm(pb[:, :w], wb2[:, k, m * 128:(m + 1) * 128], Bm[ch][:, 8 + k, :w], False, k == 1, [wb2, Bm[ch]], [pb])
                    tt(hT[ch][:, m, :w], hT[ch][:, m, :w], pb[:, :w], ALU.add, [hT[ch], pb], [hT[ch]])
            for ch, (c0, w) in enumerate(CHUNKS):
                rmsnorm_chunk(T, hT[ch], w, 4 + l, Bm[ch])
            for j in range(DFF // 256):
                wbuf = wb0 if st['ffn'] % 2 == 0 else wb1
                st['ffn'] += 1
                gu = wbuf[:, 0:4096].rearrange("p (a k n) -> p a k n", a=2, k=8)
                dn = wbuf[:, 4096:6144].rearrange("p (t n) -> p t n", t=2)
                dma('pool', gu[:, 0, :, :], w_gu[l, :, 256 * j:256 * (j + 1)].rearrange("(t p) n -> p t n", p=128), W=[wbuf])
                dma('pool', gu[:, 1, :, :], w_gu[l, :, DFF + 256 * j:DFF + 256 * (j + 1)].rearrange("(t p) n -> p t n", p=128), W=[wbuf])
                dma('pool', dn[:, :, :], w_down[l, 256 * j:256 * (j + 1), :].rearrange("(t p) n -> p t n", p=128), W=[wbuf])
                for ch, (c0, w) in enumerate(CHUNKS):
                    hids = [T.pT[st['pT'] % 4], T.pT[(st['pT'] + 1) % 4]]
                    st['pT'] += 2
                    for t in range(2):
                        pg = bank()
                        pu = bank()
                        for k in range(8):
                            mm(pg[:, :w], gu[:, 0, k, t * 128:(t + 1) * 128], Bm[ch][:, k, :w], k == 0, k == 7, [wbuf, Bm[ch]], [pg])
                        for k in range(8):
                            mm(pu[:, :w], gu[:, 1, k, t * 128:(t + 1) * 128], Bm[ch][:, k, :w], k == 0, k == 7, [wbuf, Bm[ch]], [pu])
                        act(T.osb[:, :w], pg[:, :w], AF.Silu, [pg], [T.osb])
                        tt(hids[t][:, :w], T.osb[:, :w], pu[:, :w], ALU.mult, [T.osb, pu], [hids[t]])
                    for m in range(8):
                        pb = bank()
                        for t in range(2):
                            mm(pb[:, :w], dn[:, t, m * 128:(m + 1) * 128], hids[t][:, :w], t == 0, t == 1, [wbuf, hids[t]], [pb])
                        tt(hT[ch][:, m, :w], hT[ch][:, m, :w], pb[:, :w], ALU.add, [hT[ch], pb], [hT[ch]])
            if l == 1:
                kvw = load_sq_weight(w_kv, 128, 8, 512)
                for ch, (c0, w) in enumerate(CHUNKS):
                    rmsnorm_chunk(T, hT[ch], w, 12, T.aT)
                    for ti in range((w + 127) // 128):
                        nt = min(128, w - ti * 128)
                        pb = bank()
                        for k in range(8):
                            mm(pb[0:nt, :], T.aT[:, k, ti * 128:ti * 128 + nt], kvw[:, k, :], k == 0, k == 7, [T.aT, wb0], [pb])
                        act(T.kvtm[0:nt, :], pb[0:nt, :], AF.Copy, [pb], [T.kvtm])
                        headnorm_tokmajor(T, T.kvtm[0:nt, 0:256], [T.kvtm], 10, T.kn, nrows=nt)
                        if ch < 4:
                            r0 = c0 + ti * 128
                            dma('sp', o_pk[r0:r0 + 128, :], T.kn[:, :], R=[T.kn], is_out=True)
                            dma('sp', o_pv[r0:r0 + 128, :], T.kvtm[:, 256:512], R=[T.kvtm], is_out=True)
                            pb2 = bank()
                            for hp in range(2):
                                tr(pb2[:, hp * 128:(hp + 1) * 128], T.kn[:, hp * 128:(hp + 1) * 128], ident[:], [T.kn, ident], [pb2])
                            act(T.qn[:, :, 0:128], pb2[:, 0:256].rearrange("p (a t) -> p a t", a=2), AF.Copy, [pb2], [T.qn], scale=0.125)
                            dma('sp', kv_d.ap()[0:256, r0:r0 + 128].rearrange("(a p) t -> p a t", p=128), T.qn[:, :, 0:128], R=[T.qn], W=[dkv])
                            cp(T.qsq[:, 0:256], T.kvtm[:, 256:512], [T.kvtm], [T.qsq], eng='pool')
                            dma('sp', kv_d.ap()[256:512, :].rearrange("r (e c) -> (r e) c", c=256)[r0:r0 + 128, :], T.qsq[:, 0:256], R=[T.qsq], W=[dkv])
                        else:
                            dma('sp', o_sk[:, :], T.kn[0:NS, :], R=[T.kn], is_out=True)
                            dma('sp', o_sv[:, :], T.kvtm[0:NS, 256:512], R=[T.kvtm], is_out=True)
                            dma('sp', snew_d.ap()[:, 0:256], T.kn[0:NS, :], R=[T.kn], W=[dsnew])
                            dma('sp', snew_d.ap()[:, 256:512], T.kvtm[0:NS, 256:512], R=[T.kvtm], W=[dsnew])
                            pb2 = bank()
                            for hp in range(2):
                                tr(pb2[:, hp * NS:(hp + 1) * NS], T.kn[0:NS, hp * 128:(hp + 1) * 128], ident[0:NS, 0:NS], [T.kn, ident], [pb2])
                            act(T.qn[:, :, 0:NS], pb2[:, 0:2 * NS].rearrange("p (a t) -> p a t", a=2), AF.Copy, [pb2], [T.qn], scale=0.125)
                            dma('sp', knT_d.ap(), T.qn[:, :, 0:NS], R=[T.qn], W=[dknT])
                P.dma('pool', lambda h: h.collective_compute("AllGather", ALU.bypass, replica_groups=[[0, 1], [2, 3], [4, 5], [6, 7]], ins=[kv_d.ap().opt()], outs=[kvall_d.ap().opt()]), reads=[dkv], writes=[dkvall], inc=1)
            if l + 1 < DEPTH:
                inproj(l + 1, T)
            P.phase_end()

        P.phase_begin()
        T = common("Z")
        for ti in range(NPT // 128 + 1):
            ntok = 128 if ti < NPT // 128 else NS
            ch = min(ti // 4, 4)
            c0 = (ti % 4) * 128 if ch < 4 else 0
            s = T.stg
            for half in range(2):
                pb = bank()
                for kk in range(4):
                    k = half * 4 + kk
                    tr(pb[0:ntok, kk * 128:(kk + 1) * 128], hT[ch][:, k, c0:c0 + ntok], ident[:], [hT[ch], ident], [pb])
                if half == 0:
                    act(s[0:ntok, 0:512], pb[0:ntok, :], AF.Copy, [pb], [s])
                else:
                    cp(s[0:ntok, 512:1024], pb[0:ntok, :], [pb], [s])
            dst = y_p[ti * 128:(ti + 1) * 128, :] if ti < NPT // 128 else y_s
            dma('sp', dst, s[0:ntok, :], R=[s], is_out=True)
        P.finish()
        P.phase_end()
    return nc


_NC_CACHE = {}


def _prep_inputs(inp):
    f = np.float32
    g = lambda k: np.asarray(inp[k])
    gcols = np.zeros((128, 13, 8), f)
    vecs = [g('g_mix')[l] for l in range(4)] + [g('g_ffn')[l] for l in range(4)] + [g('g_mem')[l] for l in range(4)] + [g('g_kv')]
    for i, v in enumerate(vecs):
        gcols[:, i, :] = v.reshape(8, 128).T
    hv = [g('g_mq')[l] for l in range(4)] + [g('g_mk')[l] for l in range(4)] + [g('g_q')[j] for j in range(2)] + [g('g_k')] + [np.ones(64, f)]
    ghead = np.zeros((128, 12, 64), f)
    gheadp = np.zeros((128, 12), f)
    for i, v in enumerate(hv):
        ghead[:, i, :] = v[None, :]
        gheadp[:, i] = np.concatenate([v, v])

    def pair_layout(a):
        sh = a.shape
        a = a.reshape((24, 2, 64) + sh[2:])
        a = np.moveaxis(a, 0, 2)
        return np.ascontiguousarray(a.reshape((128, 24) + sh[2:]))
    s5a = np.zeros((2, 128, 3, 24), f)
    s5b = np.zeros((2, 2, 128, 24, 16), f)
    s5c = np.zeros((2, 2, 128, 24, 16), f)
    s5d = np.zeros((2, 96, 8), f)
    for l in range(2):
        s5a[l, :, 0, :] = pair_layout(g('ssm_a_re')[l])
        s5a[l, :, 1, :] = pair_layout(g('ssm_a_im')[l])
        s5a[l, :, 2, :] = pair_layout(np.repeat(g('ssm_log_dt')[l][:, None], 64, axis=1))
        s5b[l, 0] = pair_layout(g('ssm_b_re')[l])
        s5b[l, 1] = pair_layout(g('ssm_b_im')[l])
        s5c[l, 0] = pair_layout(np.swapaxes(g('ssm_c_re')[l], 1, 2))
        s5c[l, 1] = pair_layout(np.swapaxes(g('ssm_c_im')[l], 1, 2))
        s5d[l] = g('ssm_d')[l].reshape(8, 96).T
    shared = dict(w_in=g('w_in'), w_out=g('w_out'), w_gu=g('w_gu'), w_down=g('w_down'), w_mem_kv=g('w_mem_kv'),
                  w_kv=g('w_kv'), w_glu=g('w_glu'), gcols=gcols, ghead=ghead, gheadp=gheadp, s5a=s5a, s5b=s5b, s5c=s5c, s5d=s5d)
    maps = []
    ck = np.ascontiguousarray(g('cache_k').reshape(2560, 128, 256))
    cv = np.ascontiguousarray(g('cache_v').reshape(2560, 128, 256))
    for c in range(8):
        b, half = c // 2, c % 2
        m = dict(shared)
        m['xp'] = np.ascontiguousarray(g('x_prompt')[b, half * NPT:(half + 1) * NPT, :])
        m['xs'] = np.ascontiguousarray(g('x_sample')[c * NS:(c + 1) * NS, 0, :])
        m['mem'] = np.ascontiguousarray(g('mem_prompt')[b])
        m['cmk'] = np.ascontiguousarray(g('cache_mem_k')[:, c * NS:(c + 1) * NS].reshape(4, NS, 256, 256))
        m['cmv'] = np.ascontiguousarray(g('cache_mem_v')[:, c * NS:(c + 1) * NS].reshape(4, NS, 256, 256))
        m['st0'] = np.ascontiguousarray(np.stack([g('state_ssm_re')[:, c * NS:(c + 1) * NS].reshape(2, NS, 3072),
                                                  g('state_ssm_im')[:, c * NS:(c + 1) * NS].reshape(2, NS, 3072)], axis=1))
        m['flagb'] = np.full((128, 1), float(half), f)
        m['cache_k'] = ck
        m['cache_v'] = cv
        m['ptab'] = np.ascontiguousarray(g('page_table')[c * NS:(c + 1) * NS].reshape(1, NS * 16).astype(np.int32))
        maps.append(m)
    return maps


def kernel(**inp):
    if 'nc' not in _NC_CACHE:
        _NC_CACHE['nc'] = build_program()
    nc = _NC_CACHE['nc']
    maps = _prep_inputs(inp)
    res = run_bass_kernel_spmd(nc, maps, core_ids=list(range(8))).results
    f = np.float32
    y_prompt = np.zeros((4, 4096, D), f)
    y_sample = np.zeros((128, 1, D), f)
    p_k = np.zeros((4, 4096, 4, 64), f)
    p_v = np.zeros((4, 4096, 4, 64), f)
    p_mem_k = np.zeros((4, 4, 256, 4, 64), f)
    p_mem_v = np.zeros((4, 4, 256, 4, 64), f)
    s_k = np.zeros((128, 1, 4, 64), f)
    s_v = np.zeros((128, 1, 4, 64), f)
    p_re = np.zeros((2, 4, 48, 64), f)
    p_im = np.zeros((2, 4, 48, 64), f)
    s_re = np.zeros((2, 128, 48, 64), f)
    s_im = np.zeros((2, 128, 48, 64), f)
    for c in range(8):
        b, half = c // 2, c % 2
        r = res[c]
        y_prompt[b, half * NPT:(half + 1) * NPT] = r['y_p']
        y_sample[c * NS:(c + 1) * NS, 0] = r['y_s']
        p_k[b, half * NPT:(half + 1) * NPT] = r['o_pk'].reshape(NPT, 4, 64)
        p_v[b, half * NPT:(half + 1) * NPT] = r['o_pv'].reshape(NPT, 4, 64)
        s_k[c * NS:(c + 1) * NS, 0] = r['o_sk'].reshape(NS, 4, 64)
        s_v[c * NS:(c + 1) * NS, 0] = r['o_sv'].reshape(NS, 4, 64)
        s_re[:, c * NS:(c + 1) * NS] = r['o_sssm'][:, 0].reshape(2, NS, 48, 64)
        s_im[:, c * NS:(c + 1) * NS] = r['o_sssm'][:, 1].reshape(2, NS, 48, 64)
        if half == 0:
            p_mem_k[:, b] = r['o_pmk'].reshape(4, 256, 4, 64)
            p_mem_v[:, b] = r['o_pmv'].reshape(4, 256, 4, 64)
        else:
            p_re[:, b] = r['o_pssm'][:, 0].reshape(2, 48, 64)
            p_im[:, b] = r['o_pssm'][:, 1].reshape(2, 48, 64)
    return (y_prompt, y_sample, p_re, p_im, p_k, p_v, p_mem_k, p_mem_v, s_re, s_im, s_k, s_v)
```
